# Optimizing a Trainium2 kernel written in Bass

```python
import math
import jax
import jax.numpy as jnp
from jax import lax
import numpy as np

D_MODEL = 1024
BATCH = 32
SEQ = 256
DEPTH = 2
DEC_BATCH = 2
DEC_SEQ = 2048
PAST_LEN = 256

GRID_W = 64
HA = 4
DK_A = 64
DV_A = 64
WA = HA * DK_A
HB = 4
DQK_B = 32
DV_B = 64
WB = HB * 2 * DQK_B
HC = 4
DK_C = 64
DV_C = 64
WC = HC * DK_C
HD = 4
DK_D = 64
DV_D = 64
WD = HD * DK_D
N_BRANCH = 4
BRANCH_W = 256
FFN_DIM = 2816
SHORT_CONV = 3
CHUNK = 64
Q_BLOCK = 128
ROPE_BASE = 10000.0
EPS = 1e-6
N_MOD = 9
IN_SIZES = (WA, WA, WA, WA, 2 * HA, 2 * HA,
            WB, WB, HB * DV_B,
            WC, WC, WC, WC, 2 * HC, 2 * HC,
            WD, WD, WD, WD,
            N_BRANCH * D_MODEL)
N_IN = sum(IN_SIZES)

kernel_name = 'hybrid_diffusion_trunk_step'


def rms_norm(x, g):
    xf = x.astype(jnp.float32)
    y = xf * lax.rsqrt(jnp.mean(xf * xf, axis=-1, keepdims=True) + EPS)
    return (y * g.astype(jnp.float32)).astype(x.dtype)


def l2_norm(x):
    xf = x.astype(jnp.float32)
    return (xf * lax.rsqrt(jnp.sum(xf * xf, axis=-1, keepdims=True) + EPS)).astype(x.dtype)


def to_heads(x, n_heads):
    b, t, w = x.shape
    return x.reshape(b, t, n_heads, w // n_heads).transpose(0, 2, 1, 3)


def from_heads(x):
    b, h, t, d = x.shape
    return x.transpose(0, 2, 1, 3).reshape(b, t, h * d)


def bidir(x):
    return jnp.stack([x, jnp.flip(x, axis=-2)])


def bidir_gate(a):
    a = a.transpose(2, 0, 3, 1)
    return jnp.stack([a[0], jnp.flip(a[1], axis=-1)])


def merge_dirs(o):
    return o[0] + jnp.flip(o[1], axis=-2)


def short_conv(x, w):
    k = w.shape[0]
    pad = k // 2
    t = x.shape[1]
    xp = jnp.pad(x, ((0, 0), (pad, k - 1 - pad), (0, 0)))
    return sum(w[j] * xp[:, j:j + t] for j in range(k))


def swiglu(h, w_gate, w_up, w_down):
    return jnp.einsum('btf,fd->btd', jax.nn.silu(h @ w_gate) * (h @ w_up), w_down)


def rope_2d(x):
    t = x.shape[1]
    n_rows = t // GRID_W
    rows = jnp.repeat(jnp.arange(n_rows), GRID_W).astype(jnp.float32)
    cols = jnp.tile(jnp.arange(GRID_W), n_rows).astype(jnp.float32)
    n_freq = DQK_B // 4
    freqs = ROPE_BASE ** (-jnp.arange(n_freq, dtype=jnp.float32) / n_freq)

    def rot(xh, pos):
        ang = pos[:, None] * freqs
        cos = jnp.cos(ang)[None, :, None, None, :]
        sin = jnp.sin(ang)[None, :, None, None, :]
        x1, x2 = xh[..., :n_freq], xh[..., n_freq:]
        return jnp.concatenate([x1 * cos - x2 * sin, x2 * cos + x1 * sin], axis=-1)

    half = DQK_B // 2
    out = jnp.concatenate([rot(x[..., :half], rows), rot(x[..., half:], cols)], axis=-1)
    return out.astype(x.dtype)


def _tril(strict):
    return jnp.tril(jnp.ones((CHUNK, CHUNK), dtype=bool), -1 if strict else 0)


def _chunk(a):
    n = a.shape[-2] // CHUNK
    return jnp.moveaxis(a.reshape(a.shape[:-2] + (n, CHUNK, a.shape[-1])), -3, 0)


def _chunk_gate(a):
    n = a.shape[-1] // CHUNK
    return jnp.moveaxis(a.reshape(a.shape[:-1] + (n, CHUNK)), -2, 0)


def _unchunk(o):
    o = jnp.moveaxis(o, 0, -3)
    return o.reshape(o.shape[:-3] + (o.shape[-3] * CHUNK, o.shape[-1]))


def gated_delta_chunked(q, k, v, beta, g, s0):
    f32 = jnp.float32
    q, k, v = _chunk(q.astype(f32)), _chunk(k.astype(f32)), _chunk(v.astype(f32))
    beta, g = _chunk_gate(beta.astype(f32)), _chunk_gate(g.astype(f32))
    cg = jnp.cumsum(g, axis=-1)
    decay = jnp.exp(jnp.where(_tril(False), cg[..., :, None] - cg[..., None, :], -jnp.inf))
    kb = k * beta[..., None]
    vb = v * beta[..., None]
    a_low = jnp.where(_tril(True), jnp.einsum('...id,...jd->...ij', kb, k) * decay, 0.0)
    m_unit = a_low + jnp.eye(CHUNK, dtype=f32)
    u = lax.linalg.triangular_solve(m_unit, vb, left_side=True, lower=True, unit_diagonal=True)
    w = lax.linalg.triangular_solve(m_unit, kb * jnp.exp(cg)[..., None], left_side=True,
                                    lower=True, unit_diagonal=True)

    def step(s, inp):
        qc, kc, uc, wc, gc, dc = inp
        v_new = uc - jnp.einsum('...cd,...de->...ce', wc, s)
        attn = jnp.einsum('...id,...jd->...ij', qc, kc) * dc
        o = (jnp.einsum('...id,...de->...ie', qc * jnp.exp(gc)[..., None], s)
             + jnp.einsum('...ij,...je->...ie', attn, v_new))
        g_last = gc[..., -1]
        s = (s * jnp.exp(g_last)[..., None, None]
             + jnp.einsum('...cd,...ce->...de', kc * jnp.exp(g_last[..., None] - gc)[..., None], v_new))
        return s, o

    s_fin, o = lax.scan(step, s0.astype(f32), (q, k, u, w, cg, decay))
    return _unchunk(o), s_fin


def mlstm_chunked(q, k, v, ig, lf, c0, n0, m0):
    f32 = jnp.float32
    q, k, v = _chunk(q.astype(f32)), _chunk(k.astype(f32)), _chunk(v.astype(f32))
    ig, lf = _chunk_gate(ig.astype(f32)), _chunk_gate(lf.astype(f32))
    b = jnp.cumsum(lf, axis=-1)
    dmat = jnp.where(_tril(False), b[..., :, None] - b[..., None, :] + ig[..., None, :], -jnp.inf)

    def step(carry, inp):
        cp, np_, mp = carry
        qc, kc, vc, bc, igc, dc = inp
        inter = bc + mp[..., None]
        m_i = jnp.maximum(inter, jnp.max(dc, axis=-1))
        s = jnp.einsum('...id,...jd->...ij', qc, kc) * jnp.exp(dc - m_i[..., None])
        si = jnp.exp(inter - m_i)
        num = (si[..., None] * jnp.einsum('...id,...de->...ie', qc, cp)
               + jnp.einsum('...ij,...je->...ie', s, vc))
        den = si * jnp.einsum('...id,...d->...i', qc, np_) + jnp.sum(s, axis=-1)
        h = num / jnp.maximum(jnp.abs(den), jnp.exp(-m_i))[..., None]
        bl = bc[..., -1]
        logw = bl[..., None] - bc + igc
        m_new = jnp.maximum(bl + mp, jnp.max(logw, axis=-1))
        wk = kc * jnp.exp(logw - m_new[..., None])[..., None]
        dec = jnp.exp(bl + mp - m_new)
        c_new = dec[..., None, None] * cp + jnp.einsum('...cd,...ce->...de', wk, vc)
        n_new = dec[..., None] * np_ + jnp.sum(wk, axis=-2)
        return (c_new, n_new, m_new), h

    init = (c0.astype(f32), n0.astype(f32), m0.astype(f32))
    fin, h = lax.scan(step, init, (q, k, v, b, ig, dmat))
    return _unchunk(h), fin


def retention_chunked(q, k, v, log_gamma, r0):
    f32 = jnp.float32
    q, k, v = _chunk(q.astype(f32)), _chunk(k.astype(f32)), _chunk(v.astype(f32))
    pos = jnp.arange(CHUNK, dtype=f32)
    lg = log_gamma[..., None]
    rel = pos[:, None] - pos[None, :]
    decay = jnp.exp(jnp.where(_tril(False), rel * lg[..., None], -jnp.inf))
    q_dec = jnp.exp((pos + 1.0) * lg)
    k_dec = jnp.exp((CHUNK - 1.0 - pos) * lg)
    c_dec = jnp.exp(CHUNK * log_gamma)[..., None, None]

    def step(r, inp):
        qc, kc, vc = inp
        attn = jnp.einsum('...id,...jd->...ij', qc, kc) * decay
        o = (jnp.einsum('...ij,...je->...ie', attn, vc)
             + jnp.einsum('...id,...de->...ie', qc * q_dec[..., None], r))
        r = c_dec * r + jnp.einsum('...cd,...ce->...de', kc * k_dec[..., None], vc)
        return r, o

    r_fin, o = lax.scan(step, r0.astype(f32), (q, k, v))
    return _unchunk(o), r_fin


def diff_attention_blocked(q, keys, vals, lam):
    b, h, t, dq = q.shape
    nb = t // Q_BLOCK
    qb = jnp.moveaxis(q.reshape(b, h, nb, Q_BLOCK, dq), 2, 0)
    k1, k2 = keys[..., :DQK_B], keys[..., DQK_B:]
    scale = DQK_B ** -0.5

    def one(qblk):
        q1, q2 = qblk[..., :DQK_B], qblk[..., DQK_B:]
        s1 = jnp.einsum('bhqd,bhkd->bhqk', q1, k1).astype(jnp.float32) * scale
        s2 = jnp.einsum('bhqd,bhkd->bhqk', q2, k2).astype(jnp.float32) * scale
        p = jax.nn.softmax(s1, axis=-1) - lam * jax.nn.softmax(s2, axis=-1)
        return jnp.einsum('bhqk,bhkd->bhqd', p.astype(vals.dtype), vals)

    o = lax.map(one, qb)
    return jnp.moveaxis(o, 0, 2).reshape(b, h, t, vals.shape[-1])


def token_mix(h, lp, l, cache):
    f32 = jnp.float32
    b, t, _ = h.shape
    proj = jnp.einsum('btd,dn->btn', h, lp['w_in'])
    (a_q, a_k, a_v, a_z, a_beta, a_alpha,
     b_q, b_k, b_v,
     c_q, c_k, c_v, c_o, c_i, c_f,
     d_q, d_k, d_v, d_g, merge_logits) = jnp.split(proj, np.cumsum(IN_SIZES)[:-1].tolist(), axis=-1)
    if cache is None:
        ctx_k = None
        ctx_v = None
        s_dn0 = jnp.zeros((2, b, HA, DK_A, DV_A), f32)
        c0 = jnp.zeros((2, b, HC, DK_C, DV_C), f32)
        n0 = jnp.zeros((2, b, HC, DK_C), f32)
        m0 = jnp.zeros((2, b, HC), f32)
        r0 = jnp.zeros((2, b, HD, DK_D, DV_D), f32)
    else:
        ctx_k, ctx_v, st_dn, st_c, st_n, st_m, st_r = cache
        s_dn0 = jnp.moveaxis(st_dn, 1, 0)
        c0 = jnp.moveaxis(st_c, 1, 0)
        n0 = jnp.moveaxis(st_n, 1, 0)
        m0 = jnp.moveaxis(st_m, 1, 0)
        r0 = jnp.moveaxis(st_r, 1, 0)

    qkv = jax.nn.silu(short_conv(jnp.concatenate([a_q, a_k, a_v], axis=-1), lp['dn_conv_w']))
    qa, ka, va = jnp.split(qkv, 3, axis=-1)
    qa = l2_norm(to_heads(qa, HA)) * DK_A ** -0.5
    ka = l2_norm(to_heads(ka, HA))
    va = to_heads(va, HA)
    beta = bidir_gate(jax.nn.sigmoid(a_beta.astype(f32)).reshape(b, t, 2, HA))
    g = bidir_gate(-jnp.exp(lp['dn_a_log'].astype(f32))
                   * jax.nn.softplus(a_alpha.astype(f32).reshape(b, t, 2, HA) + lp['dn_dt_bias'].astype(f32)))
    o_a, s_dn = gated_delta_chunked(bidir(qa), bidir(ka), bidir(va), beta, g, s_dn0)
    out_a = from_heads(rms_norm(merge_dirs(o_a).astype(h.dtype), lp['dn_norm_g'])
                       * jax.nn.silu(to_heads(a_z, HA)))

    qd = rms_norm(b_q.reshape(b, t, HB, 2, DQK_B), lp['da_qn_g'])
    kd = rms_norm(b_k.reshape(b, t, HB, 2, DQK_B), lp['da_kn_g'])
    if cache is not None:
        qd = rope_2d(qd)
        kd = rope_2d(kd)
    qd = qd.reshape(b, t, HB, 2 * DQK_B).transpose(0, 2, 1, 3)
    kd = kd.reshape(b, t, HB, 2 * DQK_B).transpose(0, 2, 1, 3)
    vd = to_heads(b_v, HB)
    if ctx_k is None:
        keys, vals = kd, vd
    else:
        keys = jnp.concatenate([ctx_k.astype(kd.dtype), kd], axis=2)
        vals = jnp.concatenate([ctx_v.astype(vd.dtype), vd], axis=2)
    lam_init = 0.8 - 0.6 * math.exp(-0.3 * l)
    lam_p = lp['da_lambda'].astype(f32)
    lam = jnp.exp(jnp.sum(lam_p[0] * lam_p[1])) - jnp.exp(jnp.sum(lam_p[2] * lam_p[3])) + lam_init
    o_b = diff_attention_blocked(qd, keys, vals, lam)
    out_b = from_heads(rms_norm(o_b, lp['da_norm_g']) * (1.0 - lam_init))

    qc = to_heads(c_q, HC) * DK_C ** -0.5
    kc = to_heads(c_k, HC)
    vc = to_heads(c_v, HC)
    ig = bidir_gate(c_i.astype(f32).reshape(b, t, 2, HC) + lp['ml_i_bias'].astype(f32))
    lf = bidir_gate(jax.nn.log_sigmoid(c_f.astype(f32).reshape(b, t, 2, HC) + lp['ml_f_bias'].astype(f32)))
    o_c, (s_c, s_n, s_m) = mlstm_chunked(bidir(qc), bidir(kc), bidir(vc), ig, lf, c0, n0, m0)
    out_c = from_heads(jax.nn.sigmoid(to_heads(c_o, HC))
                       * rms_norm(merge_dirs(o_c).astype(h.dtype), lp['ml_norm_g']))

    qr = to_heads(d_q, HD)
    kr = to_heads(d_k, HD) * DK_D ** -0.5
    vr = to_heads(d_v, HD)
    log_gamma = jax.nn.log_sigmoid(lp['ret_decay_logit'].astype(f32))[:, None, :]
    o_d, s_r = retention_chunked(bidir(qr), bidir(kr), bidir(vr), log_gamma, r0)
    out_d = from_heads(jax.nn.silu(to_heads(d_g, HD))
                       * rms_norm(merge_dirs(o_d).astype(h.dtype), lp['ret_norm_g']))

    branches = jnp.stack([out_a, out_b, out_c, out_d])
    proj_b = jnp.einsum('mbtc,mcd->btmd', branches, lp['w_branch'])
    gates = jax.nn.sigmoid(merge_logits.reshape(b, t, N_BRANCH, D_MODEL))
    y = jnp.einsum('btd,de->bte', jnp.sum(gates * proj_b, axis=2), lp['w_out'])
    dt = h.dtype
    state = (kd, vd,
             jnp.moveaxis(s_dn, 0, 1).astype(dt), jnp.moveaxis(s_c, 0, 1).astype(dt),
             jnp.moveaxis(s_n, 0, 1).astype(dt), jnp.moveaxis(s_m, 0, 1).astype(dt),
             jnp.moveaxis(s_r, 0, 1).astype(dt))
    return y, state


def trunk_layer(x, cond, lp, l, cache):
    mod = jnp.einsum('bd,de->be', jax.nn.silu(cond), lp['w_ada']) + lp['b_ada']
    mod = mod.reshape(cond.shape[0], N_MOD, 1, D_MODEL)
    h = rms_norm(x, lp['norm_g'][0]) * (1.0 + mod[:, 1]) + mod[:, 0]
    x = x + 0.5 * mod[:, 2] * swiglu(h, lp['ffn_w_gate'][0], lp['ffn_w_up'][0], lp['ffn_w_down'][0])
    h = rms_norm(x, lp['norm_g'][1]) * (1.0 + mod[:, 4]) + mod[:, 3]
    mix, state = token_mix(h, lp, l, cache)
    x = x + mod[:, 5] * mix
    h = rms_norm(x, lp['norm_g'][2]) * (1.0 + mod[:, 7]) + mod[:, 6]
    x = x + 0.5 * mod[:, 8] * swiglu(h, lp['ffn_w_gate'][1], lp['ffn_w_up'][1], lp['ffn_w_down'][1])
    return x, state


def setup_inputs(seed: int = 0) -> dict:
    key = jax.random.key(seed)
    ks = iter(jax.random.split(key, 40))
    f32 = jnp.float32
    D = D_MODEL

    def nrm(shape, s):
        return jax.random.normal(next(ks), shape, f32) * s

    dt0 = jax.random.uniform(next(ks), (DEPTH, 2, HA), f32, 0.001, 0.1)
    return {
        'x_prompt': nrm((BATCH, SEQ, D), 1.0),
        'x_sample': nrm((DEC_BATCH, DEC_SEQ, D), 1.0),
        'cache_diff_k': nrm((DEC_BATCH, DEPTH, HB, PAST_LEN, 2 * DQK_B), 1.0),
        'cache_diff_v': nrm((DEC_BATCH, DEPTH, HB, PAST_LEN, DV_B), 1.0),
        'state_delta': nrm((DEC_BATCH, DEPTH, 2, HA, DK_A, DV_A), 0.1),
        'state_mlstm_C': nrm((DEC_BATCH, DEPTH, 2, HC, DK_C, DV_C), 0.5),
        'state_mlstm_n': nrm((DEC_BATCH, DEPTH, 2, HC, DK_C), 0.5),
        'state_mlstm_m': nrm((DEC_BATCH, DEPTH, 2, HC), 1.0),
        'state_ret': nrm((DEC_BATCH, DEPTH, 2, HD, DK_D, DV_D), 0.1),
        'c': nrm((DEC_BATCH, D), 1.0),
        'c_ctx': nrm((D,), 1.0),
        'w_ada': nrm((DEPTH, D, N_MOD * D), 0.5 * D ** -0.5),
        'b_ada': nrm((DEPTH, N_MOD * D), 0.02),
        'norm_g': 1.0 + nrm((DEPTH, 3, D), 0.02),
        'ffn_w_gate': nrm((DEPTH, 2, D, FFN_DIM), D ** -0.5),
        'ffn_w_up': nrm((DEPTH, 2, D, FFN_DIM), D ** -0.5),
        'ffn_w_down': nrm((DEPTH, 2, FFN_DIM, D), FFN_DIM ** -0.5),
        'w_in': nrm((DEPTH, D, N_IN), D ** -0.5),
        'dn_conv_w': nrm((DEPTH, SHORT_CONV, 3 * WA), 0.5),
        'dn_a_log': jnp.log(jax.random.uniform(next(ks), (DEPTH, 2, HA), f32, 1.0, 16.0)),
        'dn_dt_bias': dt0 + jnp.log(-jnp.expm1(-dt0)),
        'dn_norm_g': 1.0 + nrm((DEPTH, DV_A), 0.02),
        'da_qn_g': 1.0 + nrm((DEPTH, DQK_B), 0.02),
        'da_kn_g': 1.0 + nrm((DEPTH, DQK_B), 0.02),
        'da_lambda': nrm((DEPTH, 4, DQK_B), 0.1),
        'da_norm_g': 1.0 + nrm((DEPTH, DV_B), 0.02),
        'ml_i_bias': nrm((DEPTH, 2, HC), 0.1),
        'ml_f_bias': jnp.linspace(3.0, 6.0, HC, dtype=f32) + nrm((DEPTH, 2, HC), 0.1),
        'ml_norm_g': 1.0 + nrm((DEPTH, DV_C), 0.02),
        'ret_decay_logit': jnp.log(2.0 ** (5.0 + jnp.arange(HD, dtype=f32)) - 1.0) + nrm((DEPTH, 2, HD), 0.1),
        'ret_norm_g': 1.0 + nrm((DEPTH, DV_D), 0.02),
        'w_branch': nrm((DEPTH, N_BRANCH, BRANCH_W, D), BRANCH_W ** -0.5),
        'w_out': nrm((DEPTH, D, D), D ** -0.5),
    }


def reference(x_prompt, x_sample, cache_diff_k, cache_diff_v, state_delta, state_mlstm_C,
              state_mlstm_n, state_mlstm_m, state_ret, c, c_ctx, w_ada, b_ada, norm_g,
              ffn_w_gate, ffn_w_up, ffn_w_down, w_in, dn_conv_w, dn_a_log, dn_dt_bias, dn_norm_g,
              da_qn_g, da_kn_g, da_lambda, da_norm_g, ml_i_bias, ml_f_bias, ml_norm_g,
              ret_decay_logit, ret_norm_g, w_branch, w_out):
    yp = x_prompt
    ys = x_sample
    st_k, st_v, st_dn, st_c, st_n, st_m, st_r = [], [], [], [], [], [], []
    for l in range(DEPTH):
        lp = {
            'w_ada': w_ada[l], 'b_ada': b_ada[l], 'norm_g': norm_g[l],
            'ffn_w_gate': ffn_w_gate[l], 'ffn_w_up': ffn_w_up[l], 'ffn_w_down': ffn_w_down[l],
            'w_in': w_in[l], 'dn_conv_w': dn_conv_w[l], 'dn_a_log': dn_a_log[l],
            'dn_dt_bias': dn_dt_bias[l], 'dn_norm_g': dn_norm_g[l], 'da_qn_g': da_qn_g[l],
            'da_kn_g': da_kn_g[l], 'da_lambda': da_lambda[l], 'da_norm_g': da_norm_g[l],
            'ml_i_bias': ml_i_bias[l], 'ml_f_bias': ml_f_bias[l], 'ml_norm_g': ml_norm_g[l],
            'ret_decay_logit': ret_decay_logit[l], 'ret_norm_g': ret_norm_g[l],
            'w_branch': w_branch[l], 'w_out': w_out[l],
        }
        yp, st = trunk_layer(yp, c_ctx[None, :], lp, l, None)
        st_k.append(st[0])
        st_v.append(st[1])
        st_dn.append(st[2])
        st_c.append(st[3])
        st_n.append(st[4])
        st_m.append(st[5])
        st_r.append(st[6])
        cache_l = (cache_diff_k[:, l], cache_diff_v[:, l], state_delta[:, l], state_mlstm_C[:, l],
                   state_mlstm_n[:, l], state_mlstm_m[:, l], state_ret[:, l])
        ys, _ = trunk_layer(ys, c, lp, l, cache_l)
    new_diff_k = jnp.stack(st_k, axis=1)
    new_diff_v = jnp.stack(st_v, axis=1)
    new_delta = jnp.stack(st_dn, axis=1)
    new_mlstm_C = jnp.stack(st_c, axis=1)
    new_mlstm_n = jnp.stack(st_n, axis=1)
    new_mlstm_m = jnp.stack(st_m, axis=1)
    new_ret = jnp.stack(st_r, axis=1)
    return (yp, ys, new_diff_k, new_diff_v, new_delta, new_mlstm_C, new_mlstm_n, new_mlstm_m, new_ret)
```

```python
import numpy as np
import concourse.bass as bass
import concourse.mybir as mybir

F32 = mybir.dt.float32
BF16 = mybir.dt.bfloat16
AF = mybir.ActivationFunctionType
ALU = mybir.AluOpType
AX = mybir.AxisListType

PE, ACT, DVE, POOL, SP = "pe", "act", "dve", "pool", "sp"


class Tl:
    def __init__(self, ap, name=""):
        self.ap = ap
        self.name = name
        self.lw = None
        self.rd = []
        self.dram_out = False
        self.writers = []
        self.grp = None

    def __getitem__(self, idx):
        return Vw(self, self.ap[idx])

    def v(self):
        return Vw(self, self.ap)


class Vw:
    def __init__(self, t, ap):
        self.t = t
        self.ap = ap

    def __getitem__(self, idx):
        return Vw(self.t, self.ap[idx])

    def rr(self, s, **kw):
        return Vw(self.t, self.ap.rearrange(s, **kw))

    def bc(self, shape):
        return Vw(self.t, self.ap.broadcast_to(shape))

    def bitcast(self, dt):
        return Vw(self.t, self.ap.bitcast(dt))


class Op:
    __slots__ = ("eng", "fn", "deps", "id", "is_dma", "sig", "cnt", "dsem", "dval", "dprev")

    def __init__(self, eng, fn, deps, is_dma):
        self.eng = eng
        self.fn = fn
        self.deps = deps
        self.is_dma = is_dma
        self.sig = False
        self.cnt = 0
        self.dsem = None
        self.dval = 0
        self.dprev = 0


class Prog:
    def __init__(self, nc, n_dma_sems=20, same_engine_sync=True):
        self.nc = nc
        self.ops = []
        self.same_engine_sync = same_engine_sync
        self.esem = {}
        self.n_dma_sems = n_dma_sems
        self.dma_rr = {}
        self.final_tokens = []
        self.out_tiles = []

    def sb(self, name, shape, dt=F32):
        h = self.nc.alloc_sbuf_tensor(name, list(shape), dt)
        return Tl(h.ap(), name)

    def ps(self, name, shape, dt=F32):
        h = self.nc.alloc_psum_tensor(name, list(shape), dt)
        return Tl(h.ap(), name)

    def wrap(self, ap, name=""):
        return Tl(ap, name)

    def carve(self, arena_tl, specs, precise=True):
        out = {}
        if getattr(arena_tl, "conservative", False):
            precise = False
        if not hasattr(arena_tl, "members"):
            arena_tl.members = []
        for name, off, shape, dt in specs:
            nb = 4 if dt == F32 else 2
            n = 1
            for d in shape[1:]:
                n *= d
            a = arena_tl.ap[0:shape[0], off // 2: off // 2 + n * nb // 2]
            if dt == F32:
                a = a.bitcast(F32)
            if len(shape) == 3:
                a = a.rearrange("p (a b) -> p a b", a=shape[1])
            elif len(shape) == 4:
                a = a.rearrange("p (a b c) -> p a b c", a=shape[1], b=shape[2])
            t = Tl(a, name)
            lo, hi = off, off + n * nb
            t.grp = [t]
            for (ot, olo, ohi) in arena_tl.members:
                if (not precise) or (lo < ohi and olo < hi):
                    t.grp.append(ot)
                    ot.grp.append(t)
            arena_tl.members.append((t, lo, hi))
            out[name] = t
        return out

    def _rec(self, eng, fn, outs, ins, is_dma=False):
        deps = set()
        for v in ins:
            if v is None:
                continue
            for t in (v.t.grp or [v.t]):
                if t.lw is not None:
                    deps.add(t.lw)
        for v in outs:
            if v.t.dram_out:
                continue
            for t in (v.t.grp or [v.t]):
                if t.lw is not None:
                    deps.add(t.lw)
                deps.update(t.rd)
        op = Op(eng, fn, deps, is_dma)
        op.id = len(self.ops)
        self.ops.append(op)
        for v in ins:
            if v is None:
                continue
            v.t.rd.append(op.id)
        for v in outs:
            if v.t.dram_out:
                v.t.writers.append(op.id)
                continue
            v.t.lw = op.id
            v.t.rd = []
        return op

    def op(self, eng, fn, outs, ins):
        return self._rec(eng, fn, outs, ins)

    def dma(self, q, out, in_, **kw):
        o, i = out.ap, in_.ap
        return self._rec(q, lambda e: e.dma_start(out=o, in_=i, **kw), [out], [in_], is_dma=True)

    def mm(self, out, lhsT, rhs, start=True, stop=True, **kw):
        o, l, r = out.ap, lhsT.ap, rhs.ap
        return self._rec(PE, lambda e: e.matmul(o, l, r, start=start, stop=stop, **kw), [out], [lhsT, rhs])

    def tr(self, out, in_, ident):
        o, i, d = out.ap, in_.ap, ident.ap
        return self._rec(PE, lambda e: e.transpose(o, i, d), [out], [in_, ident])

    def act(self, out, in_, func, bias=None, scale=None, accum=None, eng=ACT):
        o, i = out.ap, in_.ap
        kw = {}
        ins = [in_]
        outs = [out]
        if bias is not None:
            if isinstance(bias, Vw):
                kw["bias"] = bias.ap
                ins.append(bias)
            else:
                kw["bias"] = bias
        if scale is not None:
            if isinstance(scale, Vw):
                kw["scale"] = scale.ap
                ins.append(scale)
            else:
                kw["scale"] = scale
        if accum is not None:
            kw["accum_out"] = accum.ap
            outs.append(accum)
        return self._rec(eng, lambda e: e.activation(o, i, func, **kw), outs, ins)

    def tt(self, out, a, b, op, eng=DVE):
        o, x, y = out.ap, a.ap, b.ap
        return self._rec(eng, lambda e: e.tensor_tensor(o, x, y, op), [out], [a, b])

    def ts(self, out, a, s1, op0, s2=None, op1=None, eng=DVE, accum=None):
        o, x = out.ap, a.ap
        ins = [a]
        outs = [out]
        a1 = s1.ap if isinstance(s1, Vw) else s1
        a2 = s2.ap if isinstance(s2, Vw) else s2
        if isinstance(s1, Vw):
            ins.append(s1)
        if isinstance(s2, Vw):
            ins.append(s2)
        kw = {}
        if op1 is not None:
            kw["op1"] = op1
        if accum is not None:
            kw["accum_out"] = accum.ap
            outs.append(accum)
        return self._rec(eng, lambda e: e.tensor_scalar(o, x, a1, a2, op0, **kw), outs, ins)

    def stt(self, out, a, s, b, op0, op1, eng=DVE):
        o, x, y = out.ap, a.ap, b.ap
        ins = [a, b]
        sa = s.ap if isinstance(s, Vw) else s
        if isinstance(s, Vw):
            ins.append(s)
        return self._rec(eng, lambda e: e.scalar_tensor_tensor(o, x, sa, y, op0, op1), [out], ins)

    def copy(self, out, in_, eng=DVE):
        o, i = out.ap, in_.ap
        if eng == ACT:
            return self._rec(eng, lambda e: e.copy(o, i), [out], [in_])
        return self._rec(eng, lambda e: e.tensor_copy(o, i), [out], [in_])

    def memset(self, out, val, eng=DVE):
        o = out.ap
        return self._rec(eng, lambda e: e.memset(o, val), [out], [])

    def reduce(self, out, in_, op, axis=None, eng=DVE):
        o, i = out.ap, in_.ap
        ax = axis if axis is not None else AX.X
        return self._rec(eng, lambda e: e.tensor_reduce(o, i, ax, op), [out], [in_])

    def recip(self, out, in_):
        o, i = out.ap, in_.ap
        return self._rec(DVE, lambda e: e.reciprocal(o, i), [out], [in_])

    def scan(self, out, d0, d1, init, op0, op1, eng=DVE):
        o, a, b = out.ap, d0.ap, d1.ap
        ins = [d0, d1]
        iv = init.ap if isinstance(init, Vw) else init
        if isinstance(init, Vw):
            ins.append(init)
        return self._rec(eng, lambda e: e.tensor_tensor_scan(o, a, b, iv, op0, op1), [out], ins)

    def mark_output(self, tl):
        tl.dram_out = True
        self.out_tiles.append(tl)

    def emit(self):
        nc = self.nc
        ops = self.ops
        for op in ops:
            for d in op.deps:
                p = ops[d]
                if p.is_dma:
                    continue
                if p.eng == op.eng and not op.is_dma:
                    if p.eng == PE:
                        continue
                    if not self.same_engine_sync:
                        continue
                p.sig = True
        final_deps = set()
        for tl in self.out_tiles:
            final_deps.update(tl.writers)
        for d in final_deps:
            if not ops[d].is_dma:
                ops[d].sig = True
        engs = [PE, ACT, DVE, POOL, SP]
        cnt = {e: 0 for e in engs}
        for op in ops:
            if op.is_dma:
                continue
            if op.sig:
                cnt[op.eng] += 1
                op.cnt = cnt[op.eng]
        from contextlib import ExitStack

        with ExitStack() as st:
            for e in engs:
                self.esem[e] = st.enter_context(nc.semaphore("es_" + e))
            qs = sorted(set(op.eng for op in ops if op.is_dma))
            pools = {}
            for q in qs:
                pools[q] = [st.enter_context(nc.semaphore("ds_%s_%d" % (q, i))) for i in range(self.n_dma_sems)]
            rr = {q: 0 for q in qs}
            tot = {}
            for op in ops:
                if not op.is_dma:
                    continue
                pool = pools[op.eng]
                s = pool[rr[op.eng] % len(pool)]
                rr[op.eng] += 1
                key = (op.eng, rr[op.eng] % len(pool) if False else id(s))
                prev = tot.get(id(s), 0)
                op.dsem = s
                op.dprev = prev
                op.dval = prev + 16
                tot[id(s)] = op.dval
            block = st.enter_context(nc.Block())
            by_eng = {e: [op for op in ops if op.eng == e] for e in engs}

            def run(eng_name, e):
                waited = {}

                def wait(sem, val):
                    k = id(sem)
                    if waited.get(k, 0) >= val:
                        return
                    waited[k] = val
                    e.wait_ge(sem, val)

                for op in by_eng[eng_name]:
                    for d in sorted(op.deps):
                        p = ops[d]
                        if p.is_dma:
                            wait(p.dsem, p.dval)
                        else:
                            if p.eng == eng_name and not op.is_dma:
                                if p.eng == PE or not self.same_engine_sync:
                                    continue
                            wait(self.esem[p.eng], p.cnt)
                    if op.is_dma:
                        if op.dprev > 0:
                            wait(op.dsem, op.dprev)
                        ins = op.fn(e)
                        ins.then_inc(op.dsem, 16)
                    else:
                        ins = op.fn(e)
                        if op.sig:
                            ins.then_inc(self.esem[eng_name], 1)
                if eng_name == SP:
                    for d in sorted(final_deps):
                        p = ops[d]
                        if p.is_dma:
                            wait(p.dsem, p.dval)
                        else:
                            wait(self.esem[p.eng], p.cnt)

            @block.tensor
            def _(e):
                run(PE, e)

            @block.scalar
            def _(e):
                run(ACT, e)

            @block.vector
            def _(e):
                run(DVE, e)

            @block.gpsimd
            def _(e):
                run(POOL, e)

            @block.sync
            def _(e):
                run(SP, e)

from concourse.bass_utils import run_bass_kernel_spmd

D = 1024
NK = 8
FF = 2816
NF = 22
NMOD = 9
EPS = 1e-6
WIN_BLOCKS = 45
BLK = dict(a_q=0, a_k=1, a_v=2, a_z=3, a_g=4, b_q=5, b_k=6, b_v=7, c_q=8, c_k=9, c_v=10, c_o=11, c_g=12,
           d_q=13, d_k=14, d_v=15, d_g=16, merge=17, b_tm=33, c_tm=37, d_tm=41)
CST_W = 1216
LP_W = 544
C_ID, C_NEGF, C_NEGB, C_RELF, C_RELB, C_POS, C_KEEP = 0, 128, 256, 384, 512, 640, 644
C_PFLAG = 676
C_ONES, C_TRIF, C_TRIB, C_NOTI = 704, 832, 960, 1088
L_RETL, L_RETG, L_MLI, L_MLF, L_MLG, L_ISF, L_DNG, L_DNA, L_DNB, L_CONV, L_QNG, L_KNG, L_DAG, L_LAM = 0, 8, 72, 80, 88, 152, 160, 224, 232, 240, 276, 308, 340, 404
BIGNEG = -30000.0


class Ring:
    def __init__(self, tiles):
        self.tiles = tiles
        self.i = 0

    def next(self):
        t = self.tiles[self.i % len(self.tiles)]
        self.i += 1
        return t


def build(T, NL, stage="full", MIXSEL=(0, 1, 2, 3)):
    NTT = T // 512
    NH = max(1, NTT // 2)
    TPH = NTT // NH
    nc = bass.Bass("TRN2", target_bir_lowering=False)
    P = Prog(nc)

    def din(name, shape):
        return P.wrap(nc.dram_tensor(name, list(shape), F32, kind="ExternalInput").ap(), name)

    def dout(name, shape):
        t = P.wrap(nc.dram_tensor(name, list(shape), F32, kind="ExternalOutput").ap(), name)
        P.mark_output(t)
        return t

    x_in = din("xT", [D, T])
    cond_in = din("cond", [128, NK])
    wada_in = din("wada", [NL * 36, 128, NK * 256])
    bada_in = din("bada", [NL, 128, 72])
    ng_in = din("ng", [NL, 128, 24])
    wgu_in = din("wgu", [NL * 2 * NF, 128, NK * 256])
    wd_in = din("wd", [NL * 2 * NK, 128, NF * 128])
    y_out = dout("yT", [D, T])
    NCH = T // 128
    win_in = din("win", [NL * WIN_BLOCKS, 128, NK * 256])
    wbr_in = din("wbr", [NL * 4, 128, 2 * 1024])
    wout_in = din("wout", [NL * 4, 128, NK * 256])
    cst_in = din("cst", [128, CST_W])
    lp_in = din("lpar", [NL, 128, LP_W])
    st_ret_in = din("st_ret", [NL, 8, 64, 64])
    o_ret = dout("o_ret", [NL, NCH // 2, 8, 64, 64])
    NSEG = NCH // 2
    sel_in = din("sel", [8, 520])
    st_C_in = din("st_Cn", [NL, 8, 64, 65])
    st_dl_in = din("st_delta", [NL, 8, 64, 64])
    ckT_in = din("ckT", [NL, 4, 2, 32, 256])
    cv_in = din("cv", [NL, 4, 256, 64])
    rope_in = din("rope", [128, 2, NCH * 16])
    qmask_in = din("qmask", [9, T])
    kmask_in = din("kmask", [9, T + 256])
    o_k = dout("o_k", [NL, 4, T, 64])
    o_v = dout("o_v", [NL, 4, T, 64])
    o_dl = dout("o_delta", [NL, NSEG, 8, 64, 64])
    st_m_in = din("st_m", [NL, 128, 8])
    o_C = dout("o_Cn", [NL, NSEG, 8, 64, 65])
    o_m = dout("o_m", [NL, 8, NSEG])

    xT = [P.sb("xT%d" % t, [128, NK, 512]) for t in range(NTT)]
    hT = [P.sb("hT%d" % t, [128, NK, 512], BF16) for t in range(NTT)]
    ARENA_B = 2 * NF * 512 * TPH + 12 * 1024
    arena = P.sb("arena", [128, ARENA_B // 2], BF16)
    import os
    arena.conservative = os.environ.get("A1_CONS", "0") == "1"
    cv = P.carve(arena, [("aT%d" % t, t * NF * 1024, [128, NF, 512], BF16) for t in range(TPH)])
    aT = [cv["aT%d" % t] for t in range(TPH)]
    cva = P.carve(arena, [("wAda%d" % i, i * NK * 1024, [128, NK, 256], F32) for i in range(2)])
    TB = T * 4
    o = 0
    mspec = []
    for nm, shp in [("QT", [64, T]), ("KT", [64, T]), ("Ktm", [128, NCH, 64]), ("Vtm", [128, NCH, 65]),
                    ("Gtm", [128, NCH, 64]), ("Od0", [128, NCH, 65]), ("Od1", [128, NCH, 65]),
                    ("Opair", [128, NCH, 128]), ("sqo", [128, NCH, 64])]:
        n = 4
        for d in shp[1:]:
            n *= d
        mspec.append((nm, o, shp, F32))
        if nm == "Gtm":
            mspec.append(("raw", o, [64, T + 2], F32))
            for p_ in range(2):
                mspec.append(("wtW_%d" % p_, o + p_ * 1024, [128, 2, 128], F32))
                mspec.append(("wTW_%d" % p_, o + 2048 + p_ * 1024, [64, 2, 128], F32))
        if nm == "sqo":
            for i in range(2):
                mspec.append(("dg%d" % i, o + i * 1024, [128, 128], F32))
                mspec.append(("wt%d" % i, o + i * 1024 + 512, [128, 128], F32))
            for i, xn in enumerate(["Xa", "Xb", "XTa", "XTb", "Yd", "khT"]):
                mspec.append((xn, o + 1024 + i * 512, [128, 128], F32))
            mspec.append(("atW", o, [128, 2, 128], F32))
            n = max(n, 4096)
        o += n
    for i in range(2):
        mspec.append(("attn%d" % i, o, [128, 128], F32)); o += 512
        mspec.append(("kw%d" % i, o, [128, 64], F32)); o += 256
        mspec.append(("o2_%d" % i, o, [128, 65], F32)); o += 260
        mspec.append(("o3_%d" % i, o, [128, 65], F32)); o += 260
    mspec.append(("WTd", o, [128, 8, 128], F32))
    mspec.append(("GT", o, [128, 10, NCH * 8], F32))
    o += max(4096, 10 * NCH * 32)
    assert o <= ARENA_B, (o, ARENA_B)
    mx = P.carve(arena, mspec)
    NKT = NCH + 2
    offs_ = dict((nm_, off_) for nm_, off_, _, _ in mspec)
    bo = 0
    bspec = []
    for nm, shp, dt_ in [("qaug0", [128, T], BF16), ("qaug1", [128, T], BF16), ("kaug", [128, T + 256], BF16),
                         ("Vext", [128, NKT, 65], BF16), ("Qb", [128, NCH, 64], F32),
                         ("Kb", [128, NCH, 64], F32), ("Ob", [128, NCH, 64], F32), ("octmp", [128, 4, 65], F32),
                         ("oc2", [128, 4, 65], F32), ("ocT", [65, 512], F32)]:
        n = 4 if dt_ == F32 else 2
        for d in shp[1:]:
            n *= d
        n = (n + 3) // 4 * 4
        bspec.append((nm, bo, shp, dt_))
        if nm == "Ob":
            bspec.append(("Vf", bo, [128, NCH, 64], F32))
        bo += n
    for i in range(4):
        bspec.append(("PT%d" % i, bo, [128, 512], BF16))
        bo += 1024
    assert (1 not in MIXSEL) or bo <= offs_["Opair"], (bo, offs_["Opair"])
    so = offs_["sqo"]
    bspec.append(("ropet", so, [128, 2, NCH * 16], F32))
    to = offs_["attn0"]
    for i in range(4):
        bspec.append(("rt%d" % i, to + i * NCH * 64, [128, NCH * 16], F32))
    assert to + 4 * NCH * 64 <= ARENA_B
    bx = P.carve(arena, bspec)
    zsp = [("z", 0, [128, NK, 512], F32), ("zb", NK * 2048, [128, NK, 512], BF16),
           ("brm", NK * 2048 + NK * 1024, [128, 8, 512], BF16), ("gsb0", NK * 2048 + 2 * NK * 1024, [128, 512], F32),
           ("gsb1", NK * 2048 + 2 * NK * 1024 + 2048, [128, 512], F32), ("gp0", NK * 2048 + 2 * NK * 1024 + 4096, [128, 512], F32)]
    mz = P.carve(arena, zsp)
    wA = Ring([P.sb("wA%d" % i, [128, NK, 256], BF16) for i in range(3)])
    A2_B = 2 * NF * 256 + 2 * 2048 + 2 * 1024
    arena2 = P.sb("arena2", [128, A2_B // 2], BF16)
    import os
    arena2.conservative = os.environ.get("A2_CONS", "0") == "1"
    c2 = P.carve(arena2, [("wD%d" % i, i * NF * 256, [128, NF, 128], BF16) for i in range(2)]
                 + [("rstd%d" % i, 2 * NF * 256 + i * 2048, [128, 512], F32) for i in range(2)]
                 + [("sq%d" % i, 2 * NF * 256 + 4096 + i * 1024, [128, 512], BF16) for i in range(2)])
    wD = Ring([c2["wD%d" % i] for i in range(2)])
    a2spec = []
    o2_ = 0
    for p_ in range(2):
        for nm_, sz_, shp_ in [("dg", 1024, [128, 2, 128]), ("XTa", 1024, [128, 2, 128]), ("XTb", 1024, [128, 2, 128]),
                               ("Xa", 1024, [128, 2, 128]), ("Xb", 1024, [128, 2, 128]), ("Y", 1024, [128, 2, 128]),
                               ("khT", 1024, [64, 2, 128]), ("kh", 512, [128, 2, 64]), ("u", 512, [128, 2, 64])]:
            a2spec.append(("%s_%d" % (nm_, p_), o2_, shp_, F32))
            o2_ += sz_
    a2spec.append(("vnew", o2_, [128, 2, 64], F32))
    o2_ += 512
    assert o2_ <= A2_B, (o2_, A2_B)
    ax = P.carve(arena2, a2spec)
    wBR = Ring([P.sb("wBR%d" % i, [128, 2, 1024], BF16) for i in range(1)])
    wAda = Ring([cva["wAda%d" % i] for i in range(2)])
    cst = P.sb("cst_sb", [128, CST_W])
    lpar1 = P.sb("lpar1", [128, LP_W])
    lpar = [lpar1 for l in range(NL)]
    sel = P.sb("sel_sb", [8, 520])
    rows = P.sb("rows_sb", [8, 256])
    embc = P.sb("embc", [64, 64])
    rows_b = P.sb("rows_b", [128, 32]).v()
    em0 = P.sb("em0", [128, 8])
    brst = Ring([P.sb("brst%d" % i, [128, T], BF16) for i in range(1)])
    smal = P.sb("smal", [128, 64])
    Sst = [P.sb("Sst%d" % i, [64, 65]) for i in range(2)]
    Sfin = Ring([P.sb("Sfin%d" % i, [64, 65]) for i in range(2)])
    br_dram = P.wrap(nc.dram_tensor("br_scratch", [4, 2, 128, T], BF16,
                                    kind=("ExternalOutput" if stage == "mix" else "Internal")).ap(), "br_scratch")
    ps = Ring([P.ps("pb%d" % i, [128, 512]) for i in range(6)])
    pacc_r = Ring([P.ps("pacc%d" % i, [128, 512]) for i in range(2)])
    sq = Ring([c2["sq%d" % i] for i in range(2)])
    tmpf = Ring([P.sb("tmpf%d" % i, [128, 512]) for i in range(2)])
    rstd_r = Ring([c2["rstd%d" % i] for i in range(2)])
    ones_bf = P.sb("ones_bf", [128, 128], BF16)
    scond = P.sb("scond", [128, NK])
    bada = [P.sb("bada%d" % l, [128, 72]) for l in range(NL)]
    ng = [P.sb("ng%d" % l, [128, 24]) for l in range(NL)]
    mod = [P.sb("mod%d" % l, [128, 72]) for l in range(NL)]
    scale = [P.sb("scale%d" % l, [128, 24]) for l in range(NL)]
    gate = [P.sb("gate%d" % l, [128, 24]) for l in range(NL)]

    P.memset(ones_bf.v(), 1.0)

    xv = x_in.v().rr("(kc p) t -> p kc t", p=128)
    for t in range(NTT):
        P.dma(SP, xT[t].v(), xv[:, :, t * 512:(t + 1) * 512])
    P.dma(SP, scond.v(), cond_in.v())
    P.dma(SP, cst.v(), cst_in.v())
    P.dma(SP, sel.v(), sel_in.v())
    for l in range(NL):
        P.dma(SP, bada[l].v(), bada_in[l])
        P.dma(SP, ng[l].v(), ng_in[l])
    P.act(scond.v(), scond.v(), AF.Silu)

    for l in range(NL):
        pm = ps.next()
        for blk in range(36):
            wb = wAda.next()
            P.dma(SP, wb.v(), wada_in[l * 36 + blk].rr("p (k c) -> p k c", k=NK))
            for cc in range(2):
                col = blk * 2 + cc
                for kc in range(NK):
                    P.mm(pm[:, col:col + 1], wb[:, kc, cc * 128:(cc + 1) * 128], scond[:, kc:kc + 1],
                         start=(kc == 0), stop=(kc == NK - 1))
        P.tt(mod[l].v(), pm[:, 0:72], bada[l].v(), ALU.add)
        for i in range(3):
            P.stt(scale[l][:, i * 8:(i + 1) * 8], mod[l][:, (3 * i + 1) * 8:(3 * i + 2) * 8], 1.0,
                  ng[l][:, i * 8:(i + 1) * 8], ALU.add, ALU.mult)
            P.ts(gate[l][:, i * 8:(i + 1) * 8], mod[l][:, (3 * i + 2) * 8:(3 * i + 3) * 8],
                 0.5 if i != 1 else 1.0, ALU.mult)

    def norm_mod(l, i, t):
        pn = ps.next()
        for kc in range(NK):
            s = sq.next()
            P.act(s.v(), xT[t][:, kc, :], AF.Square)
            P.mm(pn.v(), ones_bf.v(), s.v(), start=(kc == 0), stop=(kc == NK - 1))
        r = rstd_r.next()
        P.ts(r.v(), pn.v(), 1.0 / D, ALU.mult, EPS, ALU.add)
        P.act(r.v(), r.v(), AF.Sqrt)
        P.recip(r.v(), r.v())
        for kc in range(NK):
            tm = tmpf.next()
            P.stt(tm.v(), xT[t][:, kc, :], scale[l][:, i * 8 + kc:i * 8 + kc + 1], r.v(), ALU.mult, ALU.mult)
            P.act(hT[t][:, kc, :], tm.v(), AF.Identity, bias=mod[l][:, 3 * i * 8 + kc:3 * i * 8 + kc + 1])

    def ffn(l, i):
        ni = 0 if i == 0 else 2
        for h in range(NH):
            tts = list(range(h * TPH, (h + 1) * TPH))
            for t in tts:
                norm_mod(l, ni, t)
            for f in range(NF):
                wb = wA.next()
                P.dma(POOL, wb.v(), wgu_in[(l * 2 + i) * NF + f].rr("p (k c) -> p k c", k=NK))
                for j, t in enumerate(tts):
                    pg = ps.next()
                    pu = ps.next()
                    for kc in range(NK):
                        P.mm(pg.v(), wb[:, kc, 0:128], hT[t][:, kc, :], start=(kc == 0), stop=(kc == NK - 1))
                    for kc in range(NK):
                        P.mm(pu.v(), wb[:, kc, 128:256], hT[t][:, kc, :], start=(kc == 0), stop=(kc == NK - 1))
                    sg = tmpf.next()
                    P.act(sg.v(), pg.v(), AF.Silu)
                    P.tt(aT[j][:, f, :], sg.v(), pu.v(), ALU.mult)
            for dc in range(NK):
                wdb = wD.next()
                P.dma(POOL, wdb.v(), wd_in[(l * 2 + i) * NK + dc].rr("p (f c) -> p f c", f=NF))
                for j, t in enumerate(tts):
                    py = ps.next()
                    for fc in range(NF):
                        P.mm(py.v(), wdb[:, fc, :], aT[j][:, fc, :], start=(fc == 0), stop=(fc == NF - 1))
                    P.stt(xT[t][:, dc, :], py.v(), gate[l][:, ni * 8 + dc:ni * 8 + dc + 1], xT[t][:, dc, :],
                          ALU.mult, ALU.add)


    ident = cst[:, C_ID:C_ID + 128]

    def load_blk(l, b):
        wb = wA.next()
        P.dma(POOL, wb.v(), win_in[l * WIN_BLOCKS + b].rr("p (k c) -> p k c", k=NK))
        return wb

    def proj_fm(wb, c0, M, dst, scale=None):
        for t in range(NTT):
            pp = ps.next()
            for kc in range(NK):
                P.mm(pp[0:M, :], wb[:, kc, c0:c0 + M], hT[t][:, kc, :], start=(kc == 0), stop=(kc == NK - 1))
            if scale is None:
                P.copy(dst[0:M, t * 512:(t + 1) * 512], pp[0:M, :], eng=ACT)
            else:
                P.act(dst[0:M, t * 512:(t + 1) * 512], pp[0:M, :], AF.Copy, scale=scale)

    def proj_tm(wb, c0, N, dst, ncol, scale=None):
        for g in range(NCH // 4):
            pp = ps.next()
            for q in range(4):
                n = g * 4 + q
                t, off = n // 4, (n % 4) * 128
                for kc in range(NK):
                    P.mm(pp[:, q * 64:q * 64 + N], hT[t][:, kc, off:off + 128], wb[:, kc, c0:c0 + N],
                         start=(kc == 0), stop=(kc == NK - 1))
            src = pp[:, 0:256].rr("p (q c) -> p q c", q=4)[:, :, 0:N]
            if scale is None:
                P.copy(dst[:, g * 4:(g + 1) * 4, 0:N], src, eng=ACT)
            else:
                P.act(dst[:, g * 4:(g + 1) * 4, 0:N], src, AF.Copy, scale=scale)

    def proj_tm3(wb, dsts):
        for g in range(NCH // 2):
            pp = ps.next()
            for q in range(2):
                n = g * 2 + q
                t, off = n // 4, (n % 4) * 128
                for kc in range(NK):
                    P.mm(pp[:, q * 192:(q + 1) * 192], hT[t][:, kc, off:off + 128], wb[:, kc, 0:192],
                         start=(kc == 0), stop=(kc == NK - 1))
            for j, (dst, scale) in enumerate(dsts):
                src = pp[:, 0:384].rr("p (q c) -> p q c", q=2)[:, :, j * 64:(j + 1) * 64]
                if scale is None:
                    P.copy(dst[:, g * 2:(g + 1) * 2, 0:64], src, eng=ACT)
                else:
                    P.act(dst[:, g * 2:(g + 1) * 2, 0:64], src, AF.Copy, scale=scale)

    rr2 = [0]

    def run_rr(gens, delays=None):
        gens = list(gens)
        delays = list(delays) if delays is not None else [0] * len(gens)
        live = list(range(len(gens)))
        while live:
            nxt = []
            for i in live:
                if delays[i] > 0:
                    delays[i] -= 1
                    nxt.append(i)
                    continue
                try:
                    next(gens[i])
                    nxt.append(i)
                except StopIteration:
                    pass
            live = nxt

    def lin_chunk_g(n, E, WT, qs, ks, lam, S, Vx, o_out, keepcol, sfin_dst, fix=None):
        i2 = rr2[0] % 2
        rr2[0] += 1
        if fix is not None:
            i2 = fix
        QTn = mx["QT"][:, n * 128:(n + 1) * 128]
        KTn = mx["KT"][:, n * 128:(n + 1) * 128]
        p1 = ps.next()
        P.mm(p1[:, 0:128], KTn, QTn)
        at = mx["attn%d" % i2]
        P.tt(at.v(), p1[:, 0:128], WT, ALU.mult)
        kw = mx["kw%d" % i2]
        P.ts(kw.v(), mx["Ktm"][:, n, :], ks, ALU.mult)
        yield
        p2 = ps.next()
        P.mm(p2[:, 0:E], at.v(), Vx)
        P.mm(p2[:, 128:128 + E], QTn, S[:, 0:E])
        o2 = mx["o2_%d" % i2]
        P.copy(o2[:, 0:E], p2[:, 0:E], eng=ACT)
        P.stt(o_out, p2[:, 128:128 + E], qs, o2[:, 0:E], ALU.mult, ALU.add)
        yield
        p3 = ps.next()
        P.mm(p3[0:64, 0:E], kw.v(), Vx)
        P.stt(S[:, 0:E], S[:, 0:E], lam, p3[0:64, 0:E], ALU.mult, ALU.add)
        if sfin_dst is not None:
            sf = Sfin.next()
            P.copy(sf[:, 0:E], S[:, 0:E])
            sfin_dst(sf)
        P.ts(S[:, 0:E], S[:, 0:E], keepcol, ALU.mult)
        yield

    def lin_chunk(*a, **k):
        for _ in lin_chunk_g(*a, **k):
            pass

    def branch_out(m, pair):
        st = brst.next()
        for g in range(NCH // 4):
            pp = ps.next()
            for q in range(4):
                n = g * 4 + q
                P.tr(pp[:, q * 128:(q + 1) * 128], mx["Opair"][:, n, :], ident)
            P.copy(st[:, g * 512:(g + 1) * 512], pp.v())
        P.dma(SP, br_dram[m, pair], st.v())

    def mixer_D(l):
        lgt = smal[:, 0:8]
        qsd = smal[:, 8:16]
        ksd = smal[:, 16:24]
        lamd = smal[:, 24:32]
        P.act(lgt, lpar[l][:, 0:8], AF.Exp, scale=-1.0)
        P.act(lgt, lgt, AF.Ln, bias=1.0)
        P.ts(lgt, lgt, -1.0, ALU.mult)
        for r in range(8):
            d = r // 4
            rel = cst[:, (C_RELF if d == 0 else C_RELB):(C_RELF if d == 0 else C_RELB) + 128]
            P.act(mx["WTd"][:, r, :], rel, AF.Exp, scale=lgt[:, r:r + 1])
            P.act(qsd[:, r:r + 1], cst[:, C_POS + d:C_POS + d + 1], AF.Exp, scale=lgt[:, r:r + 1])
            P.act(ksd[:, r:r + 1], cst[:, C_POS + 2 + d:C_POS + 3 + d], AF.Exp, scale=lgt[:, r:r + 1])
        P.act(lamd, lgt, AF.Exp, scale=128.0)
        for h in range(4):
            wb = load_blk(l, BLK["d_q"])
            proj_fm(wb, h * 64, 64, mx["QT"])
            wb = load_blk(l, BLK["d_k"])
            proj_fm(wb, h * 64, 64, mx["KT"], scale=0.125)
            wb = load_blk(l, BLK["d_tm"] + h)
            proj_tm3(wb, [(mx["Ktm"], 0.125), (mx["Vtm"], None), (mx["Gtm"], None)])
            def chain_D(d):
                r = d * 4 + h
                S = Sst[d]
                P.dma(SP, S[:, 0:64], st_ret_in[l, r])
                od = mx["Od%d" % d]
                order = range(NCH) if d == 0 else range(NCH - 1, -1, -1)
                for n in order:
                    isend = (n % 2 == 1) if d == 0 else (n % 2 == 0)
                    dst = None
                    if isend:
                        seg = n // 2
                        dst = (lambda sf, seg=seg, r=r: P.dma(SP, o_ret[l, seg, r], sf[:, 0:64]))
                    kc = cst[0:64, C_KEEP + d * NCH + n:C_KEEP + d * NCH + n + 1]
                    yield from lin_chunk_g(n, 64, mx["WTd"][:, r, :], qsd[:, r:r + 1], ksd[:, r:r + 1],
                                           lamd[0:64, r:r + 1], S, mx["Vtm"][:, n, 0:64], od[:, n, 0:64], kc, dst, fix=d)
            run_rr([chain_D(0), chain_D(1)])
            o0 = mx["Od0"][:, :, 0:64]
            P.tt(o0, o0, mx["Od1"][:, :, 0:64], ALU.add)
            post_norm_gate(l, o0, 8, AF.Silu, h)
            if h % 2 == 1:
                branch_out(3, h // 2)

    def post_norm_gate(l, o, gcol, gfunc, h):
        sqo = mx["sqo"]
        P.tt(sqo.v(), o, o, ALU.mult)
        ss = smal[:, 32:32 + NCH]
        P.reduce(ss, sqo.v(), ALU.add)
        P.ts(ss, ss, 1.0 / 64, ALU.mult, EPS, ALU.add)
        P.act(ss, ss, AF.Sqrt)
        P.recip(ss, ss)
        P.tt(sqo.v(), o, ss.rr("p (n o) -> p n o", o=1).bc([128, NCH, 64]), ALU.mult)
        gn = lpar[l][:, gcol:gcol + 64].rr("p (o e) -> p o e", o=1).bc([128, NCH, 64])
        P.tt(sqo.v(), sqo.v(), gn, ALU.mult)
        P.act(mx["Gtm"].v(), mx["Gtm"].v(), gfunc)
        P.tt(mx["Opair"][:, :, (h % 2) * 64:(h % 2) * 64 + 64], sqo.v(), mx["Gtm"].v(), ALU.mult)

    def zero_branch(m):
        for pair in range(2):
            st = brst.next()
            P.memset(st.v(), 0.0)
            P.dma(SP, br_dram[m, pair], st.v())

    def merge(l):
        z, zb, brm = mz["z"], mz["zb"], mz["brm"]
        gs = Ring([mz["gsb0"], mz["gsb1"]])
        for t in range(NTT):
            for m in range(4):
                for pair in range(2):
                    P.dma(SP, brm[:, m * 2 + pair, :], br_dram[m, pair][:, t * 512:(t + 1) * 512])
            for m in range(4):
                wbr = wBR.next()
                P.dma(POOL, wbr.v(), wbr_in[l * 4 + m].rr("p (k c) -> p k c", k=2))
                wbv = wbr.v()
                for gb in range(4):
                    wg = load_blk(l, BLK["merge"] + m * 4 + gb)
                    for cc in range(2):
                        dc = gb * 2 + cc
                        pg = ps.next()
                        for kc in range(NK):
                            P.mm(pg.v(), wg[:, kc, cc * 128:(cc + 1) * 128], hT[t][:, kc, :],
                                 start=(kc == 0), stop=(kc == NK - 1))
                        g = gs.next()
                        P.act(g.v(), pg.v(), AF.Sigmoid)
                        pp = ps.next()
                        for kc in range(2):
                            P.mm(pp.v(), wbv[:, kc, dc * 128:(dc + 1) * 128], brm[:, m * 2 + kc, :],
                                 start=(kc == 0), stop=(kc == 1))
                        if m == 0:
                            P.tt(z[:, dc, :], g.v(), pp.v(), ALU.mult)
                        else:
                            P.tt(g.v(), g.v(), pp.v(), ALU.mult)
                            P.tt(z[:, dc, :], z[:, dc, :], g.v(), ALU.add)
            P.copy(zb.v(), z.v(), eng=ACT)
            for ob in range(4):
                wo = wA.next()
                P.dma(POOL, wo.v(), wout_in[l * 4 + ob].rr("p (k c) -> p k c", k=NK))
                for cc in range(2):
                    dc = ob * 2 + cc
                    py = ps.next()
                    for kc in range(NK):
                        P.mm(py.v(), wo[:, kc, cc * 128:(cc + 1) * 128], zb[:, kc, :], start=(kc == 0), stop=(kc == NK - 1))
                    P.stt(xT[t][:, dc, :], py.v(), gate[l][:, 8 + dc:8 + dc + 1], xT[t][:, dc, :], ALU.mult, ALU.add)


    ONESF = cst[:, C_ONES:C_ONES + 128]
    TRIF = cst[:, C_TRIF:C_TRIF + 128]
    TRIB = cst[:, C_TRIB:C_TRIB + 128]
    NEGM = [cst[:, C_NEGF:C_NEGF + 128], cst[:, C_NEGB:C_NEGB + 128]]

    def GTs(i, w=8):
        return mx["GT"][:, i, 0:NCH * 8].rr("p (n r) -> p n r", r=8) if w == 8 else None

    def bc8(col):
        return lpar1[:, col:col + 8].rr("p (o r) -> p o r", o=1).bc([128, NCH, 8])

    def cums(src, dP, dB, dT):
        for lhs, dst in ((TRIF, dP), (TRIB, dB), (ONESF, dT)):
            pp = ps.next()
            P.mm(pp[:, 0:NCH * 8], lhs, src)
            P.copy(dst, pp[:, 0:NCH * 8], eng=ACT)

    def blend_dir(dst, aF, aB):
        P.tt(dst, aF, aB, ALU.subtract)
        P.tt(dst, dst, bc8(L_ISF), ALU.mult)
        P.tt(dst, dst, aB, ALU.add)

    def rows_of(src_slot_view_fn, dst_rows, reduce_max):
        for g in range(NCH // 4):
            pp = ps.next()
            for q in range(4):
                n = g * 4 + q
                P.tr(pp[0:8, q * 128:(q + 1) * 128], src_slot_view_fn(n), ident)
            v = pp[0:8, :].rr("p (q t) -> p q t", q=4)
            if reduce_max:
                P.reduce(dst_rows[:, g * 4:(g + 1) * 4], v, ALU.max)
            else:
                P.copy(dst_rows[:, g * 4:(g + 1) * 4], v[:, :, 0])

    def mixer_C(l):
        G16 = mx["GT"][:, 0:2, :].rr("p a b -> p (a b)").rr("p (n c) -> p n c", c=16)
        IG, LF, bP, bb, TOT, QS, KS, LAM = [GTs(i) for i in range(2, 10)]
        flat = lambda i: mx["GT"][:, i, 0:NCH * 8]
        wb = load_blk(l, BLK["c_g"])
        proj_tm(wb, 0, 16, G16, 16)
        P.tt(IG, G16[:, :, 0:8], bc8(L_MLI), ALU.add)
        P.tt(LF, G16[:, :, 8:16], bc8(L_MLF), ALU.add)
        P.act(LF, LF, AF.Exp, scale=-1.0)
        P.act(LF, LF, AF.Ln, bias=1.0)
        P.ts(LF, LF, -1.0, ALU.mult)
        cums(flat(3), flat(4), flat(7), flat(6))
        blend_dir(bb, bP, QS)
        P.act(QS, bb, AF.Exp)
        P.tt(IG, IG, bb, ALU.subtract)
        P.tt(KS, IG, TOT, ALU.add)
        import os
        DBG = int(os.environ.get("DBG_C", "9"))
        if DBG < 2:
            return
        Xr = rows[:, 0:NCH]
        Tr = rows[:, 16:16 + NCH]
        rows_of(lambda n: KS[:, n, :], Xr, True)
        rows_of(lambda n: TOT[:, n, :], Tr, False)
        X2 = Xr.rr("p (s c) -> p s c", c=2)
        T2 = Tr.rr("p (s c) -> p s c", c=2)
        Bt = rows[:, 32:32 + NSEG]
        ta = rows[:, 48:48 + NSEG]
        mF = rows[:, 64:64 + NSEG]
        mB = rows[:, 80:80 + NSEG]
        mr = rows[:, 96:96 + NSEG]
        P.tt(Bt, T2[:, :, 0], T2[:, :, 1], ALU.add)
        P.tt(ta, X2[:, :, 0], T2[:, :, 1], ALU.add)
        P.tt(mF, Bt, X2[:, :, 1], ALU.max)
        P.tt(mF, mF, ta, ALU.max)
        P.tt(ta, X2[:, :, 1], T2[:, :, 0], ALU.add)
        P.tt(mB, Bt, X2[:, :, 0], ALU.max)
        P.tt(mB, mB, ta, ALU.max)
        P.tt(mr, mF, mB, ALU.subtract)
        P.ts(mr, mr, sel[:, 512:513], ALU.mult)
        P.tt(mr, mr, mB, ALU.add)
        if DBG < 3:
            return
        P.dma(SP, o_m[l], mr)
        if DBG < 4:
            return
        em = rows[:, 112:112 + NSEG]
        P.act(em, mr, AF.Exp, scale=-1.0)
        pp = ps.next()
        for r in range(8):
            P.mm(pp[0:64, r * NSEG:(r + 1) * NSEG], sel[:, r * 64:(r + 1) * 64], em)
        P.copy(embc[:, 0:8 * NSEG], pp[0:64, 0:8 * NSEG])
        P.act(KS, KS, AF.Exp)
        P.act(LAM, TOT, AF.Exp)
        P.dma(SP, em0.v(), st_m_in[l])
        P.act(em0.v(), em0.v(), AF.Exp)
        P.memset(mx["Vtm"][:, :, 64:65], 1.0)
        if DBG < 5:
            return
        for h in range(4):
            wb = load_blk(l, BLK["c_q"])
            proj_fm(wb, h * 64, 64, mx["QT"], scale=0.125)
            wb = load_blk(l, BLK["c_k"])
            proj_fm(wb, h * 64, 64, mx["KT"])
            wb = load_blk(l, BLK["c_tm"] + h)
            proj_tm3(wb, [(mx["Ktm"], None), (mx["Vtm"], None), (mx["Gtm"], None)])
            def chain_C(d):
                r = d * 4 + h
                S = Sst[d]
                P.dma(SP, S.v(), st_C_in[l, r])
                P.ts(S.v(), S.v(), em0[0:64, r:r + 1], ALU.mult)
                od = mx["Od%d" % d]
                order = range(NCH) if d == 0 else range(NCH - 1, -1, -1)
                for n in order:
                    dg = mx["dg%d" % d]
                    wt = mx["wt%d" % d]
                    P.ts(dg.v(), ident, bb[:, n, r:r + 1], ALU.mult)
                    pw = ps.next()
                    P.mm(pw[:, 0:128], ONESF, dg.v(), start=True, stop=False)
                    P.mm(pw[:, 0:128], ident, NEGM[d], start=False, stop=True)
                    P.act(wt.v(), pw[:, 0:128], AF.Exp, bias=IG[:, n, r:r + 1])
                    yield
                    isend = (n % 2 == 1) if d == 0 else (n % 2 == 0)
                    dst = None
                    if isend:
                        seg = n // 2

                        def dst(sf, seg=seg, r=r):
                            P.ts(sf.v(), sf.v(), embc[:, r * NSEG + seg:r * NSEG + seg + 1], ALU.mult)
                            P.dma(SP, o_C[l, seg, r], sf.v())
                    kc = cst[0:64, C_KEEP + d * NCH + n:C_KEEP + d * NCH + n + 1]
                    yield from lin_chunk_g(n, 65, wt.v(), QS[:, n, r:r + 1], KS[:, n, r:r + 1], LAM[0:64, n, r:r + 1],
                                           S, mx["Vtm"][:, n, 0:65], od[:, n, 0:65], kc, dst, fix=d)
            run_rr([chain_C(0), chain_C(1)])
            for d in range(2):
                od = mx["Od%d" % d]
                den = od[:, :, 64:65]
                P.act(den, den, AF.Abs)
                P.ts(den, den, 1.0, ALU.max)
                P.recip(den, den)
                P.tt(od[:, :, 0:64], od[:, :, 0:64], den.bc([128, NCH, 64]), ALU.mult)
            o0 = mx["Od0"][:, :, 0:64]
            P.tt(o0, o0, mx["Od1"][:, :, 0:64], ALU.add)
            post_norm_gate(l, o0, L_MLG, AF.Sigmoid, h)
            if h % 2 == 1:
                branch_out(2, h // 2)


    NOTI = cst[:, C_NOTI:C_NOTI + 128]

    def mixer_A(l):
        G16 = mx["GT"][:, 0:2, :].rr("p a b -> p (a b)").rr("p (n c) -> p n c", c=16)
        SQB, NCG, SQBE, CG, TOT, QS, KS, LAM = [GTs(i) for i in range(2, 10)]
        flat = lambda i: mx["GT"][:, i, 0:NCH * 8]
        wb = load_blk(l, BLK["a_g"])
        proj_tm(wb, 0, 16, G16, 16)
        negA = smal[:, 40:48]
        P.act(negA, lpar1[:, L_DNA:L_DNA + 8], AF.Exp)
        P.ts(negA, negA, -1.0, ALU.mult)
        P.act(SQB, G16[:, :, 0:8], AF.Sigmoid)
        P.act(SQB, SQB, AF.Sqrt)
        P.tt(NCG, G16[:, :, 8:16], bc8(L_DNB), ALU.add)
        P.act(NCG, NCG, AF.Exp)
        P.act(NCG, NCG, AF.Ln, bias=1.0)
        P.tt(NCG, NCG, negA.rr("p (o r) -> p o r", o=1).bc([128, NCH, 8]), ALU.mult)
        cums(flat(3), flat(4), flat(7), flat(6))
        blend_dir(CG, SQBE, QS)
        P.act(QS, CG, AF.Exp)
        P.tt(SQBE, SQB, QS, ALU.mult)
        P.tt(KS, TOT, CG, ALU.subtract)
        P.act(KS, KS, AF.Exp)
        P.act(LAM, TOT, AF.Exp)
        P.ts(NCG, CG, -1.0, ALU.mult)
        I64 = cst[0:64, C_ID:C_ID + 64]
        O64 = cst[0:64, C_ONES:C_ONES + 64]
        for h in range(4):
            for xi, (bn, dstn) in enumerate((("a_q", "QT"), ("a_k", "KT"), ("a_v", None))):
                raw = mx["raw"]
                P.memset(raw[:, 0:1], 0.0)
                P.memset(raw[:, T + 1:T + 2], 0.0)
                wb = load_blk(l, BLK[bn])
                proj_fm(wb, h * 64, 64, raw[:, 1:T + 1])
                cw = lambda j: lpar1[0:64, L_CONV + (xi * 4 + h) * 3 + j:L_CONV + (xi * 4 + h) * 3 + j + 1]
                nw0 = smal[0:64, 48:49]
                nw2 = smal[0:64, 49:50]
                P.ts(nw0, cw(0), cst[0:64, C_PFLAG:C_PFLAG + 1], ALU.mult, -1.0, ALU.mult)
                P.ts(nw2, cw(2), cst[0:64, C_PFLAG:C_PFLAG + 1], ALU.mult, -1.0, ALU.mult)
                for t in range(NTT):
                    off = t * 512
                    st_ = tmpf.next()
                    st = st_[0:64, :]
                    P.ts(st, raw[:, off + 1:off + 513], cw(1), ALU.mult)
                    P.stt(st, raw[:, off:off + 512], cw(0), st, ALU.mult, ALU.add)
                    P.stt(st, raw[:, off + 2:off + 514], cw(2), st, ALU.mult, ALU.add)
                    sv = st.rr("p (a b) -> p a b", b=256)
                    r0 = raw[:, off:off + 512].rr("p (a b) -> p a b", b=256)[:, :, 0]
                    r2 = raw[:, off + 2:off + 514].rr("p (a b) -> p a b", b=256)[:, :, 255]
                    P.stt(sv[:, :, 0], r0, nw0, sv[:, :, 0], ALU.mult, ALU.add)
                    P.stt(sv[:, :, 255], r2, nw2, sv[:, :, 255], ALU.mult, ALU.add)
                    if dstn is not None:
                        dv = mx[dstn][:, off:off + 512]
                        P.act(dv, st, AF.Silu)
                        s2_ = tmpf.next()
                        s2 = s2_[0:64, :]
                        P.act(s2, dv, AF.Square)
                        pn = ps.next()
                        P.mm(pn[0:64, :], O64, s2)
                        P.ts(s2, pn[0:64, :], EPS, ALU.add)
                        P.act(s2, s2, AF.Sqrt)
                        P.recip(s2, s2)
                        if xi == 0:
                            P.stt(dv, dv, 0.125, s2, ALU.mult, ALU.mult)
                        else:
                            P.tt(dv, dv, s2, ALU.mult)
                    else:
                        P.act(st, st, AF.Silu)
                        pt = ps.next()
                        for q in range(4):
                            P.tr(pt[:, q * 64:(q + 1) * 64], st[:, q * 128:(q + 1) * 128], I64)
                        P.copy(mx["Vtm"][:, t * 4:(t + 1) * 4, 0:64], pt[:, 0:256].rr("p (q c) -> p q c", q=4), eng=ACT)
            for g in range(NCH // 4):
                pt = ps.next()
                for q in range(4):
                    n = g * 4 + q
                    P.tr(pt[:, q * 64:(q + 1) * 64], mx["KT"][:, n * 128:(n + 1) * 128], I64)
                P.copy(mx["Ktm"][:, g * 4:(g + 1) * 4, :], pt[:, 0:256].rr("p (q c) -> p q c", q=4), eng=ACT)
            stt_ = {"pre": [0] * (NCH + 2), "scan": 0}
            ident2 = Vw(cst, cst.ap[:, C_ID:C_ID + 128].rearrange("p (o c) -> p o c", o=1).broadcast_to([128, 2, 128]))
            noti2 = Vw(cst, cst.ap[:, C_NOTI:C_NOTI + 128].rearrange("p (o c) -> p o c", o=1).broadcast_to([128, 2, 128]))

            def chunks_of(k):
                return (k, NCH - 1 - k)

            def pre_W(p):
                psp = Ring(ps.tiles[2 * p:2 * p + 2])
                for k in range(p, NCH, 2):
                    while stt_["scan"] < k - 1:
                        yield
                    ns = chunks_of(k)
                    rs = (h, 4 + h)
                    dg, Y, kh, khT, u = [ax["%s_%d" % (nm_, p)] for nm_ in ("dg", "Y", "kh", "khT", "u")]
                    wt = mx["wtW_%d" % p]
                    wT = mx["wTW_%d" % p]
                    for d in range(2):
                        P.ts(dg[:, d, :], ident, CG[:, ns[d], rs[d]:rs[d] + 1], ALU.mult)
                        P.ts(kh[:, d, :], mx["Ktm"][:, ns[d], :], SQB[:, ns[d], rs[d]:rs[d] + 1], ALU.mult)
                    yield
                    pw = psp.next()
                    pk = psp.next()
                    for d in range(2):
                        P.mm(pw[:, d * 128:(d + 1) * 128], ONESF, dg[:, d, :], start=True, stop=False)
                        P.mm(pw[:, d * 128:(d + 1) * 128], ident, NEGM[d], start=False, stop=True)
                        P.tr(pk[0:64, d * 128:(d + 1) * 128], kh[:, d, :], ident)
                    yield
                    for d in range(2):
                        P.act(wt[:, d, :], pw[:, d * 128:(d + 1) * 128], AF.Exp, bias=NCG[:, ns[d], rs[d]:rs[d] + 1])
                    P.copy(khT.v(), pk[0:64, 0:256].rr("p (d c) -> p d c", d=2))
                    for d in range(2):
                        P.ts(Y[:, d, 0:64], mx["Vtm"][:, ns[d], 0:64], SQB[:, ns[d], rs[d]:rs[d] + 1], ALU.mult)
                        P.ts(Y[:, d, 64:128], mx["Ktm"][:, ns[d], :], SQBE[:, ns[d], rs[d]:rs[d] + 1], ALU.mult)
                    yield
                    pg = psp.next()
                    for d in range(2):
                        P.mm(pg[:, d * 128:(d + 1) * 128], khT[:, d, :], khT[:, d, :])
                    yield
                    XT = ax["XTa_%d" % p]
                    X = ax["Xa_%d" % p]
                    P.tt(XT.v(), pg[:, 0:256].rr("p (d c) -> p d c", d=2), wt.v(), ALU.mult)
                    P.tt(XT.v(), XT.v(), noti2, ALU.mult)
                    yield
                    px = psp.next()
                    py = psp.next()
                    for d in range(2):
                        P.tr(px[:, d * 128:(d + 1) * 128], XT[:, d, :], ident)
                        P.mm(py[:, d * 128:(d + 1) * 128], XT[:, d, :], Y[:, d, :])
                    yield
                    P.copy(X.v(), px[:, 0:256].rr("p (d c) -> p d c", d=2), eng=ACT)
                    P.tt(Y.v(), Y.v(), py[:, 0:256].rr("p (d c) -> p d c", d=2), ALU.subtract)
                    yield
                    for lev in range(6):
                        Xn = ax["Xb_%d" % p] if X is ax["Xa_%d" % p] else ax["Xa_%d" % p]
                        XTn = ax["XTb_%d" % p] if XT is ax["XTa_%d" % p] else ax["XTa_%d" % p]
                        p2x = psp.next()
                        p2y = psp.next()
                        for d in range(2):
                            if lev < 5:
                                P.mm(p2x[:, d * 128:(d + 1) * 128], XT[:, d, :], X[:, d, :])
                            P.mm(p2y[:, d * 128:(d + 1) * 128], X[:, d, :], XT[:, d, :])
                        yield
                        if lev < 5:
                            P.copy(Xn.v(), p2x[:, 0:256].rr("p (d c) -> p d c", d=2), eng=ACT)
                        P.copy(XTn.v(), p2y[:, 0:256].rr("p (d c) -> p d c", d=2))
                        X, XT = Xn, XTn
                        yield
                        py = psp.next()
                        for d in range(2):
                            P.mm(py[:, d * 128:(d + 1) * 128], XT[:, d, :], Y[:, d, :])
                        yield
                        P.tt(Y.v(), Y.v(), py[:, 0:256].rr("p (d c) -> p d c", d=2), ALU.add)
                        yield
                    for d in range(2):
                        P.ts(u[:, d, :], Y[:, d, 0:64], SQB[:, ns[d], rs[d]:rs[d] + 1], ALU.mult)
                        P.ts(kh[:, d, :], Y[:, d, 64:128], SQB[:, ns[d], rs[d]:rs[d] + 1], ALU.mult)
                    yield
                    pk2 = psp.next()
                    for d in range(2):
                        P.tr(pk2[0:64, d * 128:(d + 1) * 128], kh[:, d, :], ident)
                    yield
                    P.copy(wT.v(), pk2[0:64, 0:256].rr("p (d c) -> p d c", d=2), eng=ACT)
                    stt_["pre"][k] = 1
                    yield

            def scan_W():
                pss = Ring(ps.tiles[4:6])
                rs = (h, 4 + h)
                Ss = (Sst[0], Sst[1])
                for d in range(2):
                    P.dma(SP, Ss[d][:, 0:64], st_dl_in[l, rs[d]])
                ods = (mx["Od0"], mx["Od1"])
                for k in range(NCH):
                    while not stt_["pre"][k]:
                        yield
                    p = k % 2
                    ns = chunks_of(k)
                    wt = mx["wtW_%d" % p]
                    wT = mx["wTW_%d" % p]
                    u = ax["u_%d" % p]
                    vnew = ax["vnew"]
                    at = mx["atW"]
                    pv = pss.next()
                    p1 = pss.next()
                    for d in range(2):
                        P.mm(pv[:, d * 64:(d + 1) * 64], wT[:, d, :], Ss[d][:, 0:64])
                        P.mm(p1[:, d * 128:(d + 1) * 128], mx["KT"][:, ns[d] * 128:(ns[d] + 1) * 128],
                             mx["QT"][:, ns[d] * 128:(ns[d] + 1) * 128])
                    yield
                    P.tt(vnew.v(), u.v(), pv[:, 0:128].rr("p (d c) -> p d c", d=2), ALU.subtract)
                    P.tt(at.v(), p1[:, 0:256].rr("p (d c) -> p d c", d=2), wt.v(), ALU.mult)
                    kws = (mx["kw0"], mx["kw1"])
                    for d in range(2):
                        P.ts(kws[d].v(), mx["Ktm"][:, ns[d], :], KS[:, ns[d], rs[d]:rs[d] + 1], ALU.mult)
                    yield
                    p2 = pss.next()
                    p3 = pss.next()
                    for d in range(2):
                        P.mm(p2[:, d * 64:(d + 1) * 64], at[:, d, :], vnew[:, d, :])
                        P.mm(p2[:, 128 + d * 64:128 + (d + 1) * 64], mx["QT"][:, ns[d] * 128:(ns[d] + 1) * 128], Ss[d][:, 0:64])
                        P.mm(p3[0:64, d * 64:(d + 1) * 64], kws[d].v(), vnew[:, d, :])
                    yield
                    o2 = mx["attn0"]
                    P.copy(o2[:, 0:128], p2[:, 0:128], eng=ACT)
                    for d in range(2):
                        n = ns[d]
                        r = rs[d]
                        S = Ss[d]
                        P.stt(ods[d][:, n, 0:64], p2[:, 128 + d * 64:128 + (d + 1) * 64], QS[:, n, r:r + 1],
                              o2[:, d * 64:(d + 1) * 64], ALU.mult, ALU.add)
                        P.stt(S[:, 0:64], S[:, 0:64], LAM[0:64, n, r:r + 1], p3[0:64, d * 64:(d + 1) * 64], ALU.mult, ALU.add)
                        isend = (n % 2 == 1) if d == 0 else (n % 2 == 0)
                        if isend:
                            sf = Sfin.next()
                            P.copy(sf[:, 0:64], S[:, 0:64])
                            P.dma(SP, o_dl[l, n // 2, r], sf[:, 0:64])
                        P.ts(S[:, 0:64], S[:, 0:64], cst[0:64, C_KEEP + d * NCH + n:C_KEEP + d * NCH + n + 1], ALU.mult)
                    stt_["scan"] = k + 1
                    yield
            run_rr([pre_W(0), pre_W(1), scan_W()], delays=[0, 2, 0])
            wb = load_blk(l, BLK["a_z"])
            proj_tm(wb, h * 64, 64, mx["Gtm"], 64)
            o0 = mx["Od0"][:, :, 0:64]
            P.tt(o0, o0, mx["Od1"][:, :, 0:64], ALU.add)
            post_norm_gate(l, o0, L_DNG, AF.Silu, h)
            if h % 2 == 1:
                branch_out(0, h // 2)


    def mixer_B(l):
        import math
        lam_init = 0.8 - 0.6 * math.exp(-0.3 * l)
        lt = smal[:, 50:52]
        tmpl = bx["rt0"][:, 0:32]
        P.tt(tmpl, lpar1[:, L_LAM:L_LAM + 32], lpar1[:, L_LAM + 32:L_LAM + 64], ALU.mult)
        P.reduce(lt[:, 0:1], tmpl, ALU.add)
        P.tt(tmpl, lpar1[:, L_LAM + 64:L_LAM + 96], lpar1[:, L_LAM + 96:L_LAM + 128], ALU.mult)
        P.reduce(lt[:, 1:2], tmpl, ALU.add)
        P.act(lt, lt, AF.Exp)
        nlam = smal[:, 52:53]
        P.tt(nlam, lt[:, 1:2], lt[:, 0:1], ALU.subtract)
        P.ts(nlam, nlam, -lam_init, ALU.add)
        P.dma(SP, bx["ropet"].v(), rope_in.v())
        qaugs, kaug, Vext = (bx["qaug0"], bx["qaug1"]), bx["kaug"], bx["Vext"]
        for h in range(4):
            P.memset(kaug.v(), 0.0, eng=POOL)
            P.memset(qaugs[0].v(), 0.0, eng=POOL)
            P.memset(qaugs[1].v(), 0.0, eng=POOL)
            for c in range(2):
                qaug = qaugs[c]
                P.dma(POOL, qaug[c * 64 + 32:c * 64 + 41, :], qmask_in.v())
                P.dma(POOL, kaug[c * 64 + 32:c * 64 + 41, :], kmask_in.v())
                P.dma(POOL, kaug[c * 64:c * 64 + 32, 0:256], ckT_in[l, h, c])
            P.dma(POOL, Vext[:, 0:2, 0:64], cv_in[l, h].rr("(j p) c -> p j c", p=128))
            P.memset(Vext[:, :, 64:65], 1.0)
            wb = load_blk(l, BLK["b_tm"] + h)
            Vf = bx["Vf"]
            proj_tm3(wb, [(Vf, None), (bx["Qb"], None), (bx["Kb"], None)])
            P.dma(SP, o_v[l, h].rr("(n p) c -> p n c", p=128), Vf.v())
            P.copy(Vext[:, 2:NKT, 0:64], Vf.v())
            for xi, (bn, xb, gcol) in enumerate((("b_q", "Qb", L_QNG), ("b_k", "Kb", L_KNG))):
                X = bx[xb]
                sqo = bx["Ob"]
                Xg = X.v().rr("p n (c e) -> p (n c) e", c=2)
                Sg = sqo.v().rr("p n (c e) -> p (n c) e", c=2)
                P.tt(sqo.v(), X.v(), X.v(), ALU.mult)
                ss = rows_b
                P.reduce(ss, Sg, ALU.add)
                P.ts(ss, ss, 1.0 / 32, ALU.mult, EPS, ALU.add)
                P.act(ss, ss, AF.Sqrt)
                P.recip(ss, ss)
                P.tt(Xg, Xg, ss.rr("p (m o) -> p m o", o=1).bc([128, NCH * 2, 32]), ALU.mult)
                P.tt(Xg, Xg, lpar1[:, gcol:gcol + 32].rr("p (o e) -> p o e", o=1).bc([128, NCH * 2, 32]), ALU.mult)
                rp = bx["ropet"]
                cosv = rp[:, 0, :].rr("p (n a f) -> p n a f", a=2, f=8)
                sinv = rp[:, 1, :].rr("p (n a f) -> p n a f", a=2, f=8)
                t1, t2, t3, t4 = [bx["rt%d" % i].v().rr("p (n a f) -> p n a f", a=2, f=8) for i in range(4)]
                for c in range(2):
                    xv = X[:, :, c * 32:(c + 1) * 32].rr("p n (a q f) -> p n a q f", a=2, q=2)
                    x1 = xv[:, :, :, 0, :]
                    x2 = xv[:, :, :, 1, :]
                    P.tt(t1, x1, cosv, ALU.mult)
                    P.tt(t2, x2, sinv, ALU.mult)
                    P.tt(t3, x2, cosv, ALU.mult)
                    P.tt(t4, x1, sinv, ALU.mult)
                    P.tt(x1, t1, t2, ALU.subtract)
                    P.tt(x2, t3, t4, ALU.add)
                if xi == 1:
                    P.dma(SP, o_k[l, h].rr("(n p) c -> p n c", p=128), X.v())
                coff = 0 if xi == 0 else 256
                for c in range(2):
                    dstT = qaugs[c] if xi == 0 else kaug
                    for g in range(NCH // 4):
                        pt = ps.next()
                        for q in range(4):
                            n = g * 4 + q
                            P.tr(pt[0:32, q * 128:(q + 1) * 128], X[:, n, c * 32:(c + 1) * 32], ident)
                        dd = dstT[c * 64:c * 64 + 32, coff + g * 512:coff + (g + 1) * 512]
                        if xi == 0:
                            P.act(dd, pt[0:32, :], AF.Copy, scale=32 ** -0.5)
                        else:
                            P.copy(dd, pt[0:32, :], eng=ACT)
            PTr = Ring([bx["PT%d" % i] for i in range(4)])
            Ob = bx["Ob"]
            for qb in range(NTT):
                for c in range(2):
                    pacc = pacc_r.next()
                    pend = None
                    for kt in range(NKT + 1):
                        cur = None
                        if kt < NKT:
                            psc = ps.next()
                            P.mm(psc.v(), kaug[:, kt * 128:(kt + 1) * 128], qaugs[c][:, qb * 512:(qb + 1) * 512])
                            pt_ = PTr.next()
                            P.act(pt_.v(), psc.v(), AF.Exp, bias=-4.0)
                            cur = (kt, pt_)
                        if pend is not None:
                            k0, p0 = pend
                            P.mm(pacc[0:65, :], Vext[:, k0, :], p0.v(), start=(k0 == 0), stop=(k0 == NKT - 1))
                        pend = cur
                    ocT = bx["ocT"]
                    P.copy(ocT.v(), pacc[0:65, :], eng=ACT)
                    ptr = ps.next()
                    for qs in range(4):
                        P.tr(ptr[:, qs * 65:(qs + 1) * 65], ocT[:, qs * 128:(qs + 1) * 128], cst[0:65, C_ID:C_ID + 65])
                    oc = bx["octmp"] if c == 0 else bx["oc2"]
                    P.copy(oc.v(), ptr[:, 0:260].rr("p (q e) -> p q e", q=4))
                o1, o2 = bx["octmp"], bx["oc2"]
                P.recip(o1[:, :, 64:65], o1[:, :, 64:65])
                P.recip(o2[:, :, 64:65], o2[:, :, 64:65])
                P.ts(o2[:, :, 64:65], o2[:, :, 64:65], nlam, ALU.mult)
                P.tt(o1[:, :, 0:64], o1[:, :, 0:64], o1[:, :, 64:65].bc([128, 4, 64]), ALU.mult)
                P.tt(o2[:, :, 0:64], o2[:, :, 0:64], o2[:, :, 64:65].bc([128, 4, 64]), ALU.mult)
                P.tt(Ob[:, qb * 4:(qb + 1) * 4, :], o1[:, :, 0:64], o2[:, :, 0:64], ALU.add)
            sqo = bx["Qb"]
            P.tt(sqo.v(), Ob.v(), Ob.v(), ALU.mult)
            ss = smal[:, 32:32 + NCH]
            P.reduce(ss, sqo.v(), ALU.add)
            P.ts(ss, ss, 1.0 / 64, ALU.mult, EPS, ALU.add)
            P.act(ss, ss, AF.Sqrt)
            P.recip(ss, ss)
            P.ts(ss, ss, 1.0 - lam_init, ALU.mult)
            P.tt(sqo.v(), Ob.v(), ss.rr("p (n o) -> p n o", o=1).bc([128, NCH, 64]), ALU.mult)
            gn = lpar1[:, L_DAG:L_DAG + 64].rr("p (o e) -> p o e", o=1).bc([128, NCH, 64])
            P.tt(mx["Opair"][:, :, (h % 2) * 64:(h % 2) * 64 + 64], sqo.v(), gn, ALU.mult)
            if h % 2 == 1:
                branch_out(1, h // 2)

    def mixers(l):
        P.dma(SP, lpar1.v(), lp_in[l])
        for t in range(NTT):
            norm_mod(l, 1, t)
        for m in range(4):
            if m not in MIX_IMPL:
                zero_branch(m)
        if 0 in MIX_IMPL:
            mixer_A(l)
        if 1 in MIX_IMPL:
            mixer_B(l)
        if 2 in MIX_IMPL:
            mixer_C(l)
        if 3 in MIX_IMPL:
            mixer_D(l)
        merge(l)

    MIX_IMPL = set(MIXSEL)
    for l in range(NL):
        ffn(l, 0)
        if stage == "ffn1":
            break
        mixers(l)
        if stage == "mix":
            break
        ffn(l, 1)

    yv = y_out.v().rr("(kc p) t -> p kc t", p=128)
    for t in range(NTT):
        P.dma(SP, yv[:, :, t * 512:(t + 1) * 512], xT[t].v())
    P.emit()
    return nc


def kblocks(W, width):
    K, N = W.shape
    kc = K // 128
    nb = N // width
    return np.ascontiguousarray(W.reshape(kc, 128, nb, width).transpose(2, 1, 0, 3).reshape(nb, 128, kc * width))


def host_weights(inp, NL):
    w = {}
    w["wada"] = np.concatenate([kblocks(inp["w_ada"][l], 256) for l in range(NL)], 0)
    w["bada"] = np.stack([np.ascontiguousarray(inp["b_ada"][l].reshape(72, 128).T) for l in range(NL)])
    w["ng"] = np.stack([np.ascontiguousarray(inp["norm_g"][l].reshape(24, 128).T) for l in range(NL)])
    gu = []
    wd = []
    for l in range(NL):
        for i in range(2):
            g = kblocks(inp["ffn_w_gate"][l, i], 128).reshape(NF, 128, NK, 128)
            u = kblocks(inp["ffn_w_up"][l, i], 128).reshape(NF, 128, NK, 128)
            gu.append(np.concatenate([g, u], axis=3).reshape(NF, 128, NK * 256))
            wd.append(kblocks(inp["ffn_w_down"][l, i], 128))
    w["wgu"] = np.ascontiguousarray(np.concatenate(gu, 0))
    w["wd"] = np.ascontiguousarray(np.concatenate(wd, 0))
    return w


IN_SIZES = (256, 256, 256, 256, 8, 8, 256, 256, 256, 256, 256, 256, 256, 8, 8, 256, 256, 256, 256, 4096)


def host_weights2(inp, NL, w):
    offs = np.concatenate([[0], np.cumsum(IN_SIZES)])
    names = ["a_q", "a_k", "a_v", "a_z", "a_beta", "a_alpha", "b_q", "b_k", "b_v", "c_q", "c_k", "c_v", "c_o",
             "c_i", "c_f", "d_q", "d_k", "d_v", "d_g", "merge"]
    col = {n: int(offs[i]) for i, n in enumerate(names)}
    blocks = []
    for l in range(NL):
        W = inp["w_in"][l]

        def blk(c0, n=256):
            b = np.zeros((1024, 256), np.float32)
            b[:, :n] = W[:, c0:c0 + n]
            return b

        def gblk(c0, c1):
            b = np.zeros((1024, 256), np.float32)
            b[:, 0:8] = W[:, c0:c0 + 8]
            b[:, 8:16] = W[:, c1:c1 + 8]
            return b
        bl = [blk(col["a_q"]), blk(col["a_k"]), blk(col["a_v"]), blk(col["a_z"]), gblk(col["a_beta"], col["a_alpha"]),
              blk(col["b_q"]), blk(col["b_k"]), blk(col["b_v"]),
              blk(col["c_q"]), blk(col["c_k"]), blk(col["c_v"]), blk(col["c_o"]), gblk(col["c_i"], col["c_f"]),
              blk(col["d_q"]), blk(col["d_k"]), blk(col["d_v"]), blk(col["d_g"])]
        bl += [blk(col["merge"] + 256 * i) for i in range(16)]

        def tmblk(names, hh):
            b = np.zeros((1024, 256), np.float32)
            for j, nm in enumerate(names):
                b[:, j * 64:(j + 1) * 64] = W[:, col[nm] + hh * 64:col[nm] + (hh + 1) * 64]
            return b
        for names in (("b_v", "b_q", "b_k"), ("c_k", "c_v", "c_o"), ("d_k", "d_v", "d_g")):
            bl += [tmblk(names, hh) for hh in range(4)]
        blocks += [kblocks(b, 256)[0] for b in bl]
    w["win"] = np.ascontiguousarray(np.stack(blocks))
    w["wbr"] = np.ascontiguousarray(np.stack([inp["w_branch"][l, m].reshape(2, 128, 1024).transpose(1, 0, 2).reshape(128, 2048)
                                              for l in range(NL) for m in range(4)]))
    w["wout"] = np.concatenate([kblocks(inp["w_out"][l], 256) for l in range(NL)], 0)
    lp = np.zeros((NL, 128, LP_W), np.float32)
    for l in range(NL):
        lp[l, :, 0:8] = inp["ret_decay_logit"][l].reshape(8)[None, :]
        lp[l, :, 8:72] = inp["ret_norm_g"][l][None, :]
        lp[l, :, L_MLI:L_MLI + 8] = inp["ml_i_bias"][l].reshape(8)[None, :]
        lp[l, :, L_MLF:L_MLF + 8] = inp["ml_f_bias"][l].reshape(8)[None, :]
        lp[l, :, L_MLG:L_MLG + 64] = inp["ml_norm_g"][l][None, :]
        lp[l, :, L_ISF:L_ISF + 4] = 1.0
        lp[l, :, L_QNG:L_QNG + 32] = inp["da_qn_g"][l][None, :]
        lp[l, :, L_KNG:L_KNG + 32] = inp["da_kn_g"][l][None, :]
        lp[l, :, L_DAG:L_DAG + 64] = inp["da_norm_g"][l][None, :]
        lp[l, :, L_LAM:L_LAM + 128] = inp["da_lambda"][l].reshape(128)[None, :]
        lp[l, :, L_DNG:L_DNG + 64] = inp["dn_norm_g"][l][None, :]
        lp[l, :, L_DNA:L_DNA + 8] = inp["dn_a_log"][l].reshape(8)[None, :]
        lp[l, :, L_DNB:L_DNB + 8] = inp["dn_dt_bias"][l].reshape(8)[None, :]
        cw = inp["dn_conv_w"][l]
        for xi in range(3):
            for hh in range(4):
                for jj in range(3):
                    lp[l, 0:64, L_CONV + (xi * 4 + hh) * 3 + jj] = cw[jj, xi * 256 + hh * 64: xi * 256 + hh * 64 + 64]
    w["lpar"] = lp
    return w


def host_consts(T, is_sample):
    NCH = T // 128
    c = np.zeros((128, CST_W), np.float32)
    c[:, C_ID:C_ID + 128] = np.eye(128, dtype=np.float32)
    j = np.arange(128)[:, None]
    i = np.arange(128)[None, :]
    c[:, C_NEGF:C_NEGF + 128] = np.where(j <= i, 0.0, BIGNEG)
    c[:, C_NEGB:C_NEGB + 128] = np.where(j >= i, 0.0, BIGNEG)
    c[:, C_RELF:C_RELF + 128] = np.where(j <= i, (i - j).astype(np.float32), 1e6)
    c[:, C_RELB:C_RELB + 128] = np.where(j >= i, (j - i).astype(np.float32), 1e6)
    p = np.arange(128, dtype=np.float32)
    c[:, C_POS + 0] = p + 1
    c[:, C_POS + 1] = 128 - p
    c[:, C_POS + 2] = 127 - p
    c[:, C_POS + 3] = p
    c[:, C_PFLAG] = 0.0 if is_sample else 1.0
    c[:, C_ONES:C_ONES + 128] = 1.0
    c[:, C_TRIF:C_TRIF + 128] = (j <= i)
    c[:, C_TRIB:C_TRIB + 128] = (j >= i)
    c[:, C_NOTI:C_NOTI + 128] = (j != i)
    for n in range(NCH):
        c[:, C_KEEP + n] = 1.0 if (is_sample or n % 2 == 0) else 0.0
        c[:, C_KEEP + NCH + n] = 1.0 if (is_sample or n % 2 == 1) else 0.0
    return c


def host_sel():
    s = np.zeros((8, 520), np.float32)
    for r in range(8):
        s[r, r * 64:(r + 1) * 64] = 1.0
    s[0:4, 512] = 1.0
    return s


def host_attn_consts(T, is_sample):
    NCH = T // 128
    t = np.arange(T)
    rope = np.zeros((128, 2, NCH, 2, 8), np.float32)
    rope[:, 0] = 1.0
    qmask = np.zeros((9, T), np.float32)
    kmask = np.zeros((9, T + 256), np.float32)
    if is_sample:
        freqs = (10000.0 ** (-np.arange(8, dtype=np.float32) / 8)).astype(np.float32)
        rows = (t // 64).astype(np.float32)
        cols = (t % 64).astype(np.float32)
        for hf, pos in enumerate((rows, cols)):
            ang = (pos[:, None] * freqs[None, :]).astype(np.float32)
            rope[:, 0, :, hf, :] = np.cos(ang).reshape(NCH, 128, 8).transpose(1, 0, 2)
            rope[:, 1, :, hf, :] = np.sin(ang).reshape(NCH, 128, 8).transpose(1, 0, 2)
    else:
        seg = t // 256
        for a in range(8):
            qmask[a] = np.where(seg == a, 0.0, BIGNEG)
            kmask[a, 256:] = (seg == a)
        qmask[8] = BIGNEG
        kmask[8, 0:256] = 1.0
    return rope.reshape(128, 2, NCH * 16), qmask, kmask


T_SLAB = 2048
N_LAYERS = 2
_NC_CACHE = {}


def kernel(**inp):
    inp = {k: np.asarray(v) for k, v in inp.items()}
    NL = N_LAYERS
    T = T_SLAB
    xp = inp["x_prompt"]
    xs = inp["x_sample"]
    w = host_weights(inp, NL)
    host_weights2(inp, NL, w)
    cst_s = host_consts(T, True)
    cst_p = host_consts(T, False)
    att_s = host_attn_consts(T, True)
    att_p = host_attn_consts(T, False)
    w["sel"] = host_sel()
    zf = lambda *s: np.zeros(s, np.float32)
    in_maps = []
    for c in range(8):
        m = dict(w)
        if c < 2:
            x = xs[c]
            cond = inp["c"][c]
            m["cst"] = cst_s
            m["rope"], m["qmask"], m["kmask"] = att_s
            m["st_ret"] = np.ascontiguousarray(inp["state_ret"][c].reshape(NL, 8, 64, 64))
            m["st_delta"] = np.ascontiguousarray(inp["state_delta"][c].reshape(NL, 8, 64, 64))
            m["st_Cn"] = np.ascontiguousarray(np.concatenate(
                [inp["state_mlstm_C"][c].reshape(NL, 8, 64, 64), inp["state_mlstm_n"][c].reshape(NL, 8, 64, 1)], -1))
            m["st_m"] = np.ascontiguousarray(np.broadcast_to(inp["state_mlstm_m"][c].reshape(NL, 1, 8), (NL, 128, 8)))
            m["ckT"] = np.ascontiguousarray(inp["cache_diff_k"][c].reshape(NL, 4, 256, 2, 32).transpose(0, 1, 3, 4, 2))
            m["cv"] = np.ascontiguousarray(inp["cache_diff_v"][c])
        else:
            s0 = (c - 2) * 8
            if s0 < 32:
                x = xp[s0:s0 + 8].reshape(T, D)
            else:
                x = np.zeros((T, D), np.float32)
            cond = inp["c_ctx"]
            m["cst"] = cst_p
            m["rope"], m["qmask"], m["kmask"] = att_p
            m["st_ret"] = zf(NL, 8, 64, 64)
            m["st_delta"] = zf(NL, 8, 64, 64)
            m["st_Cn"] = zf(NL, 8, 64, 65)
            m["st_m"] = zf(NL, 128, 8)
            m["ckT"] = zf(NL, 4, 2, 32, 256)
            m["cv"] = zf(NL, 4, 256, 64)
        m["xT"] = np.ascontiguousarray(x.T)
        m["cond"] = np.ascontiguousarray(cond.reshape(8, 128).T)
        in_maps.append(m)
    if "nc" not in _NC_CACHE:
        _NC_CACHE["nc"] = build(T, NL, "full")
    nc = _NC_CACHE["nc"]
    res = run_bass_kernel_spmd(nc, in_maps, core_ids=list(range(8)))
    R = res.results
    y_prompt = np.concatenate([R[c]["yT"].T.reshape(8, 256, D) for c in range(2, 6)], 0).astype(np.float32)
    y_sample = np.stack([R[c]["yT"].T for c in range(2)]).astype(np.float32)
    B = 32
    pc = range(2, 6)
    cat = lambda f: np.ascontiguousarray(np.concatenate([f(R[c]) for c in pc], 0)).astype(np.float32)
    new_ret = cat(lambda r: r["o_ret"].transpose(1, 0, 2, 3, 4).reshape(8, NL, 2, 4, 64, 64))
    new_delta = cat(lambda r: r["o_delta"].transpose(1, 0, 2, 3, 4).reshape(8, NL, 2, 4, 64, 64))
    new_C = cat(lambda r: r["o_Cn"][..., 0:64].transpose(1, 0, 2, 3, 4).reshape(8, NL, 2, 4, 64, 64))
    new_n = cat(lambda r: r["o_Cn"][..., 64].transpose(1, 0, 2, 3).reshape(8, NL, 2, 4, 64))
    new_m = cat(lambda r: r["o_m"].transpose(2, 0, 1).reshape(8, NL, 2, 4))
    new_k = cat(lambda r: r["o_k"].reshape(NL, 4, 8, 256, 64).transpose(2, 0, 1, 3, 4))
    new_v = cat(lambda r: r["o_v"].reshape(NL, 4, 8, 256, 64).transpose(2, 0, 1, 3, 4))
    return (y_prompt, y_sample, new_k, new_v, new_delta, new_C, new_n, new_m, new_ret)
```

```python
import numpy as np
import concourse.bass as bass
import concourse.mybir as mybir

F32 = mybir.dt.float32
BF16 = mybir.dt.bfloat16
AF = mybir.ActivationFunctionType
ALU = mybir.AluOpType
AX = mybir.AxisListType

PE, ACT, DVE, POOL, SP = "pe", "act", "dve", "pool", "sp"


class Tl:
    def __init__(self, ap, name=""):
        self.ap = ap
        self.name = name
        self.lw = None
        self.rd = []
        self.dram_out = False
        self.writers = []
        self.grp = None

    def __getitem__(self, idx):
        return Vw(self, self.ap[idx])

    def v(self):
        return Vw(self, self.ap)


class Vw:
    def __init__(self, t, ap):
        self.t = t
        self.ap = ap

    def __getitem__(self, idx):
        return Vw(self.t, self.ap[idx])

    def rr(self, s, **kw):
        return Vw(self.t, self.ap.rearrange(s, **kw))

    def bc(self, shape):
        return Vw(self.t, self.ap.broadcast_to(shape))

    def bitcast(self, dt):
        return Vw(self.t, self.ap.bitcast(dt))


class Op:
    __slots__ = ("eng", "fn", "deps", "id", "is_dma", "sig", "cnt", "dsem", "dval", "dprev")

    def __init__(self, eng, fn, deps, is_dma):
        self.eng = eng
        self.fn = fn
        self.deps = deps
        self.is_dma = is_dma
        self.sig = False
        self.cnt = 0
        self.dsem = None
        self.dval = 0
        self.dprev = 0


class Prog:
    def __init__(self, nc, n_dma_sems=20, same_engine_sync=True):
        self.nc = nc
        self.ops = []
        self.same_engine_sync = same_engine_sync
        self.esem = {}
        self.n_dma_sems = n_dma_sems
        self.dma_rr = {}
        self.final_tokens = []
        self.out_tiles = []

    def sb(self, name, shape, dt=F32):
        h = self.nc.alloc_sbuf_tensor(name, list(shape), dt)
        return Tl(h.ap(), name)

    def ps(self, name, shape, dt=F32):
        h = self.nc.alloc_psum_tensor(name, list(shape), dt)
        return Tl(h.ap(), name)

    def wrap(self, ap, name=""):
        return Tl(ap, name)

    def carve(self, arena_tl, specs, precise=True):
        out = {}
        if getattr(arena_tl, "conservative", False):
            precise = False
        if not hasattr(arena_tl, "members"):
            arena_tl.members = []
        for name, off, shape, dt in specs:
            nb = 4 if dt == F32 else 2
            n = 1
            for d in shape[1:]:
                n *= d
            a = arena_tl.ap[0:shape[0], off // 2: off // 2 + n * nb // 2]
            if dt == F32:
                a = a.bitcast(F32)
            if len(shape) == 3:
                a = a.rearrange("p (a b) -> p a b", a=shape[1])
            elif len(shape) == 4:
                a = a.rearrange("p (a b c) -> p a b c", a=shape[1], b=shape[2])
            t = Tl(a, name)
            lo, hi = off, off + n * nb
            t.grp = [t]
            for (ot, olo, ohi) in arena_tl.members:
                if (not precise) or (lo < ohi and olo < hi):
                    t.grp.append(ot)
                    ot.grp.append(t)
            arena_tl.members.append((t, lo, hi))
            out[name] = t
        return out

    def _rec(self, eng, fn, outs, ins, is_dma=False):
        deps = set()
        for v in ins:
            if v is None:
                continue
            for t in (v.t.grp or [v.t]):
                if t.lw is not None:
                    deps.add(t.lw)
        for v in outs:
            if v.t.dram_out:
                continue
            for t in (v.t.grp or [v.t]):
                if t.lw is not None:
                    deps.add(t.lw)
                deps.update(t.rd)
        op = Op(eng, fn, deps, is_dma)
        op.id = len(self.ops)
        self.ops.append(op)
        for v in ins:
            if v is None:
                continue
            v.t.rd.append(op.id)
        for v in outs:
            if v.t.dram_out:
                v.t.writers.append(op.id)
                continue
            v.t.lw = op.id
            v.t.rd = []
        return op

    def op(self, eng, fn, outs, ins):
        return self._rec(eng, fn, outs, ins)

    def dma(self, q, out, in_, **kw):
        o, i = out.ap, in_.ap
        return self._rec(q, lambda e: e.dma_start(out=o, in_=i, **kw), [out], [in_], is_dma=True)

    def mm(self, out, lhsT, rhs, start=True, stop=True, **kw):
        o, l, r = out.ap, lhsT.ap, rhs.ap
        return self._rec(PE, lambda e: e.matmul(o, l, r, start=start, stop=stop, **kw), [out], [lhsT, rhs])

    def tr(self, out, in_, ident):
        o, i, d = out.ap, in_.ap, ident.ap
        return self._rec(PE, lambda e: e.transpose(o, i, d), [out], [in_, ident])

    def act(self, out, in_, func, bias=None, scale=None, accum=None, eng=ACT):
        o, i = out.ap, in_.ap
        kw = {}
        ins = [in_]
        outs = [out]
        if bias is not None:
            if isinstance(bias, Vw):
                kw["bias"] = bias.ap
                ins.append(bias)
            else:
                kw["bias"] = bias
        if scale is not None:
            if isinstance(scale, Vw):
                kw["scale"] = scale.ap
                ins.append(scale)
            else:
                kw["scale"] = scale
        if accum is not None:
            kw["accum_out"] = accum.ap
            outs.append(accum)
        return self._rec(eng, lambda e: e.activation(o, i, func, **kw), outs, ins)

    def tt(self, out, a, b, op, eng=DVE):
        o, x, y = out.ap, a.ap, b.ap
        return self._rec(eng, lambda e: e.tensor_tensor(o, x, y, op), [out], [a, b])

    def ts(self, out, a, s1, op0, s2=None, op1=None, eng=DVE, accum=None):
        o, x = out.ap, a.ap
        ins = [a]
        outs = [out]
        a1 = s1.ap if isinstance(s1, Vw) else s1
        a2 = s2.ap if isinstance(s2, Vw) else s2
        if isinstance(s1, Vw):
            ins.append(s1)
        if isinstance(s2, Vw):
            ins.append(s2)
        kw = {}
        if op1 is not None:
            kw["op1"] = op1
        if accum is not None:
            kw["accum_out"] = accum.ap
            outs.append(accum)
        return self._rec(eng, lambda e: e.tensor_scalar(o, x, a1, a2, op0, **kw), outs, ins)

    def stt(self, out, a, s, b, op0, op1, eng=DVE):
        o, x, y = out.ap, a.ap, b.ap
        ins = [a, b]
        sa = s.ap if isinstance(s, Vw) else s
        if isinstance(s, Vw):
            ins.append(s)
        return self._rec(eng, lambda e: e.scalar_tensor_tensor(o, x, sa, y, op0, op1), [out], ins)

    def copy(self, out, in_, eng=DVE):
        o, i = out.ap, in_.ap
        if eng == ACT:
            return self._rec(eng, lambda e: e.copy(o, i), [out], [in_])
        return self._rec(eng, lambda e: e.tensor_copy(o, i), [out], [in_])

    def memset(self, out, val, eng=DVE):
        o = out.ap
        return self._rec(eng, lambda e: e.memset(o, val), [out], [])

    def reduce(self, out, in_, op, axis=None, eng=DVE):
        o, i = out.ap, in_.ap
        ax = axis if axis is not None else AX.X
        return self._rec(eng, lambda e: e.tensor_reduce(o, i, ax, op), [out], [in_])

    def recip(self, out, in_):
        o, i = out.ap, in_.ap
        return self._rec(DVE, lambda e: e.reciprocal(o, i), [out], [in_])

    def scan(self, out, d0, d1, init, op0, op1, eng=DVE):
        o, a, b = out.ap, d0.ap, d1.ap
        ins = [d0, d1]
        iv = init.ap if isinstance(init, Vw) else init
        if isinstance(init, Vw):
            ins.append(init)
        return self._rec(eng, lambda e: e.tensor_tensor_scan(o, a, b, iv, op0, op1), [out], ins)

    def mark_output(self, tl):
        tl.dram_out = True
        self.out_tiles.append(tl)

    def emit(self):
        nc = self.nc
        ops = self.ops
        for op in ops:
            for d in op.deps:
                p = ops[d]
                if p.is_dma:
                    continue
                if p.eng == op.eng and not op.is_dma:
                    if p.eng == PE:
                        continue
                    if not self.same_engine_sync:
                        continue
                p.sig = True
        final_deps = set()
        for tl in self.out_tiles:
            final_deps.update(tl.writers)
        for d in final_deps:
            if not ops[d].is_dma:
                ops[d].sig = True
        engs = [PE, ACT, DVE, POOL, SP]
        cnt = {e: 0 for e in engs}
        for op in ops:
            if op.is_dma:
                continue
            if op.sig:
                cnt[op.eng] += 1
                op.cnt = cnt[op.eng]
        from contextlib import ExitStack

        with ExitStack() as st:
            for e in engs:
                self.esem[e] = st.enter_context(nc.semaphore("es_" + e))
            qs = sorted(set(op.eng for op in ops if op.is_dma))
            pools = {}
            for q in qs:
                pools[q] = [st.enter_context(nc.semaphore("ds_%s_%d" % (q, i))) for i in range(self.n_dma_sems)]
            rr = {q: 0 for q in qs}
            tot = {}
            for op in ops:
                if not op.is_dma:
                    continue
                pool = pools[op.eng]
                s = pool[rr[op.eng] % len(pool)]
                rr[op.eng] += 1
                key = (op.eng, rr[op.eng] % len(pool) if False else id(s))
                prev = tot.get(id(s), 0)
                op.dsem = s
                op.dprev = prev
                op.dval = prev + 16
                tot[id(s)] = op.dval
            block = st.enter_context(nc.Block())
            by_eng = {e: [op for op in ops if op.eng == e] for e in engs}

            def run(eng_name, e):
                waited = {}

                def wait(sem, val):
                    k = id(sem)
                    if waited.get(k, 0) >= val:
                        return
                    waited[k] = val
                    e.wait_ge(sem, val)

                for op in by_eng[eng_name]:
                    for d in sorted(op.deps):
                        p = ops[d]
                        if p.is_dma:
                            wait(p.dsem, p.dval)
                        else:
                            if p.eng == eng_name and not op.is_dma:
                                if p.eng == PE or not self.same_engine_sync:
                                    continue
                            wait(self.esem[p.eng], p.cnt)
                    if op.is_dma:
                        if op.dprev > 0:
                            wait(op.dsem, op.dprev)
                        ins = op.fn(e)
                        ins.then_inc(op.dsem, 16)
                    else:
                        ins = op.fn(e)
                        if op.sig:
                            ins.then_inc(self.esem[eng_name], 1)
                if eng_name == SP:
                    for d in sorted(final_deps):
                        p = ops[d]
                        if p.is_dma:
                            wait(p.dsem, p.dval)
                        else:
                            wait(self.esem[p.eng], p.cnt)

            @block.tensor
            def _(e):
                run(PE, e)

            @block.scalar
            def _(e):
                run(ACT, e)

            @block.vector
            def _(e):
                run(DVE, e)

            @block.gpsimd
            def _(e):
                run(POOL, e)

            @block.sync
            def _(e):
                run(SP, e)

from concourse.bass_utils import run_bass_kernel_spmd

D = 1024
NK = 8
FF = 2816
NF = 22
NMOD = 9
EPS = 1e-6
WIN_BLOCKS = 45
BLK = dict(a_q=0, a_k=1, a_v=2, a_z=3, a_g=4, b_q=5, b_k=6, b_v=7, c_q=8, c_k=9, c_v=10, c_o=11, c_g=12,
           d_q=13, d_k=14, d_v=15, d_g=16, merge=17, b_tm=33, c_tm=37, d_tm=41)
CST_W = 1216
LP_W = 544
C_ID, C_NEGF, C_NEGB, C_RELF, C_RELB, C_POS, C_KEEP = 0, 128, 256, 384, 512, 640, 644
C_PFLAG = 676
C_ONES, C_TRIF, C_TRIB, C_NOTI = 704, 832, 960, 1088
L_RETL, L_RETG, L_MLI, L_MLF, L_MLG, L_ISF, L_DNG, L_DNA, L_DNB, L_CONV, L_QNG, L_KNG, L_DAG, L_LAM = 0, 8, 72, 80, 88, 152, 160, 224, 232, 240, 276, 308, 340, 404
BIGNEG = -30000.0


class Ring:
    def __init__(self, tiles):
        self.tiles = tiles
        self.i = 0

    def next(self):
        t = self.tiles[self.i % len(self.tiles)]
        self.i += 1
        return t


def build(T, NL, stage="full", MIXSEL=(0, 1, 2, 3)):
    NTT = T // 512
    NH = max(1, NTT // 2)
    TPH = NTT // NH
    nc = bass.Bass("TRN2", target_bir_lowering=False)
    P = Prog(nc)

    def din(name, shape):
        return P.wrap(nc.dram_tensor(name, list(shape), F32, kind="ExternalInput").ap(), name)

    def dout(name, shape):
        t = P.wrap(nc.dram_tensor(name, list(shape), F32, kind="ExternalOutput").ap(), name)
        P.mark_output(t)
        return t

    x_in = din("xT", [D, T])
    cond_in = din("cond", [128, NK])
    wada_in = din("wada", [NL * 36, 128, NK * 256])
    bada_in = din("bada", [NL, 128, 72])
    ng_in = din("ng", [NL, 128, 24])
    wgu_in = din("wgu", [NL * 2 * NF, 128, NK * 256])
    wd_in = din("wd", [NL * 2 * NK, 128, NF * 128])
    y_out = dout("yT", [D, T])
    NCH = T // 128
    win_in = din("win", [NL * WIN_BLOCKS, 128, NK * 256])
    wbr_in = din("wbr", [NL * 4, 128, 2 * 1024])
    wout_in = din("wout", [NL * 4, 128, NK * 256])
    cst_in = din("cst", [128, CST_W])
    lp_in = din("lpar", [NL, 128, LP_W])
    st_ret_in = din("st_ret", [NL, 8, 64, 64])
    o_ret = dout("o_ret", [NL, NCH // 2, 8, 64, 64])
    NSEG = NCH // 2
    sel_in = din("sel", [8, 520])
    st_C_in = din("st_Cn", [NL, 8, 64, 65])
    st_dl_in = din("st_delta", [NL, 8, 64, 64])
    ckT_in = din("ckT", [NL, 4, 2, 32, 256])
    cv_in = din("cv", [NL, 4, 256, 64])
    rope_in = din("rope", [128, 2, NCH * 16])
    qmask_in = din("qmask", [9, T])
    kmask_in = din("kmask", [9, T + 256])
    o_k = dout("o_k", [NL, 4, T, 64])
    o_v = dout("o_v", [NL, 4, T, 64])
    o_dl = dout("o_delta", [NL, NSEG, 8, 64, 64])
    st_m_in = din("st_m", [NL, 128, 8])
    o_C = dout("o_Cn", [NL, NSEG, 8, 64, 65])
    o_m = dout("o_m", [NL, 8, NSEG])

    xT = [P.sb("xT%d" % t, [128, NK, 512]) for t in range(NTT)]
    hT = [P.sb("hT%d" % t, [128, NK, 512], BF16) for t in range(NTT)]
    ARENA_B = 2 * NF * 512 * TPH + 12 * 1024
    arena = P.sb("arena", [128, ARENA_B // 2], BF16)
    import os
    arena.conservative = os.environ.get("A1_CONS", "0") == "1"
    cv = P.carve(arena, [("aT%d" % t, t * NF * 1024, [128, NF, 512], BF16) for t in range(TPH)])
    aT = [cv["aT%d" % t] for t in range(TPH)]
    cva = P.carve(arena, [("wAda%d" % i, i * NK * 1024, [128, NK, 256], F32) for i in range(2)])
    TB = T * 4
    o = 0
    mspec = []
    for nm, shp in [("QT", [64, T]), ("KT", [64, T]), ("Ktm", [128, NCH, 64]), ("Vtm", [128, NCH, 65]),
                    ("Gtm", [128, NCH, 64]), ("Od0", [128, NCH, 65]), ("Od1", [128, NCH, 65]),
                    ("Opair", [128, NCH, 128]), ("sqo", [128, NCH, 64])]:
        n = 4
        for d in shp[1:]:
            n *= d
        mspec.append((nm, o, shp, F32))
        if nm in ("QT", "KT"):
            mspec.append((nm + "b", o, [64, T], BF16))
        if nm == "Vtm":
            mspec.append(("Vtb", o, [128, NCH, 66], BF16))
        if nm == "Gtm":
            mspec.append(("raw", o, [64, T + 2], F32))
            for p_ in range(2):
                mspec.append(("wtW_%d" % p_, o + p_ * 1024, [128, 2, 128], F32))
                mspec.append(("wTW_%d" % p_, o + 2048 + p_ * 1024, [64, 2, 128], F32))
        if nm == "sqo":
            for i in range(2):
                mspec.append(("dg%d" % i, o + i * 1024, [128, 128], F32))
                mspec.append(("wt%d" % i, o + i * 1024 + 512, [128, 128], F32))
            for i, xn in enumerate(["Xa", "Xb", "XTa", "XTb", "Yd", "khT"]):
                mspec.append((xn, o + 1024 + i * 512, [128, 128], F32))
            mspec.append(("atW", o, [128, 2, 128], F32))
            n = max(n, 4096)
        o += n
    for i in range(2):
        mspec.append(("attn%d" % i, o, [128, 128], F32))
        mspec.append(("atb%d" % i, o, [128, 128], BF16)); o += 512
        mspec.append(("kw%d" % i, o, [128, 64], F32))
        mspec.append(("kwb%d" % i, o, [128, 64], BF16)); o += 256
        mspec.append(("o2_%d" % i, o, [128, 65], F32)); o += 260
        mspec.append(("o3_%d" % i, o, [128, 65], F32)); o += 260
    mspec.append(("WTd", o, [128, 8, 128], F32))
    mspec.append(("GT", o, [128, 10, NCH * 8], F32))
    o += max(4096, 10 * NCH * 32)
    assert o <= ARENA_B, (o, ARENA_B)
    mx = P.carve(arena, mspec)
    NKT = NCH + 2
    offs_ = dict((nm_, off_) for nm_, off_, _, _ in mspec)
    bo = 0
    bspec = []
    for nm, shp, dt_ in [("qaug0", [128, T], BF16), ("qaug1", [128, T], BF16), ("kaug", [128, T + 256], BF16),
                         ("Vext", [128, NKT, 65], BF16), ("Qb", [128, NCH, 64], F32),
                         ("Kb", [128, NCH, 64], F32), ("Ob", [128, NCH, 64], F32), ("octmp", [128, 4, 65], F32),
                         ("oc2", [128, 4, 65], F32), ("ocT", [65, 512], F32)]:
        n = 4 if dt_ == F32 else 2
        for d in shp[1:]:
            n *= d
        n = (n + 3) // 4 * 4
        bspec.append((nm, bo, shp, dt_))
        if nm == "Ob":
            bspec.append(("Vf", bo, [128, NCH, 64], F32))
        bo += n
    for i in range(4):
        bspec.append(("PT%d" % i, bo, [128, 512], BF16))
        bo += 1024
    assert (1 not in MIXSEL) or bo <= offs_["Opair"], (bo, offs_["Opair"])
    so = offs_["sqo"]
    bspec.append(("ropet", so, [128, 2, NCH * 16], F32))
    to = offs_["attn0"]
    for i in range(4):
        bspec.append(("rt%d" % i, to + i * NCH * 64, [128, NCH * 16], F32))
    assert to + 4 * NCH * 64 <= ARENA_B
    bx = P.carve(arena, bspec)
    zsp = [("z", 0, [128, NK, 512], F32), ("zb", NK * 2048, [128, NK, 512], BF16),
           ("brm", NK * 2048 + NK * 1024, [128, 8, 512], BF16), ("gsb0", NK * 2048 + 2 * NK * 1024, [128, 512], F32),
           ("gsb1", NK * 2048 + 2 * NK * 1024 + 2048, [128, 512], F32), ("gp0", NK * 2048 + 2 * NK * 1024 + 4096, [128, 512], F32)]
    mz = P.carve(arena, zsp)
    wA = Ring([P.sb("wA%d" % i, [128, NK, 256], BF16) for i in range(3)])
    A2_B = 2 * NF * 256 + 2 * 2048 + 2 * 1024
    arena2 = P.sb("arena2", [128, A2_B // 2], BF16)
    import os
    arena2.conservative = os.environ.get("A2_CONS", "0") == "1"
    c2 = P.carve(arena2, [("wD%d" % i, i * NF * 256, [128, NF, 128], BF16) for i in range(2)]
                 + [("rstd%d" % i, 2 * NF * 256 + i * 2048, [128, 512], F32) for i in range(2)]
                 + [("sq%d" % i, 2 * NF * 256 + 4096 + i * 1024, [128, 512], BF16) for i in range(2)])
    wD = Ring([c2["wD%d" % i] for i in range(2)])
    a2spec = []
    o2_ = 0
    for p_ in range(2):
        for nm_, sz_, shp_ in [("dg", 1024, [128, 2, 128]), ("XTa", 1024, [128, 2, 128]), ("XTb", 1024, [128, 2, 128]),
                               ("Xa", 1024, [128, 2, 128]), ("Xb", 1024, [128, 2, 128]), ("Y", 1024, [128, 2, 128]),
                               ("khT", 1024, [64, 2, 128]), ("kh", 512, [128, 2, 64]), ("u", 512, [128, 2, 64])]:
            a2spec.append(("%s_%d" % (nm_, p_), o2_, shp_, F32))
            o2_ += sz_
    a2spec.append(("vnew", o2_, [128, 2, 64], F32))
    o2_ += 512
    assert o2_ <= A2_B, (o2_, A2_B)
    ax = P.carve(arena2, a2spec)
    wBR = Ring([P.sb("wBR%d" % i, [128, 2, 1024], BF16) for i in range(1)])
    wAda = Ring([cva["wAda%d" % i] for i in range(2)])
    cst = P.sb("cst_sb", [128, CST_W])
    lpar1 = P.sb("lpar1", [128, LP_W])
    lpar = [lpar1 for l in range(NL)]
    sel = P.sb("sel_sb", [8, 520])
    rows = P.sb("rows_sb", [8, 256])
    embc = P.sb("embc", [64, 64])
    rows_b = P.sb("rows_b", [128, 32]).v()
    em0 = P.sb("em0", [128, 8])
    brst = Ring([P.sb("brst%d" % i, [128, T], BF16) for i in range(1)])
    smal = P.sb("smal", [128, 64])
    Sst = [P.sb("Sst%d" % i, [64, 65]) for i in range(2)]
    Sbf = [P.sb("Sbf%d" % i, [64, 66], BF16) for i in range(2)]
    Sfin = Ring([P.sb("Sfin%d" % i, [64, 65]) for i in range(2)])
    br_dram = P.wrap(nc.dram_tensor("br_scratch", [4, 2, 128, T], BF16,
                                    kind=("ExternalOutput" if stage == "mix" else "Internal")).ap(), "br_scratch")
    ps = Ring([P.ps("pb%d" % i, [128, 512]) for i in range(6)])
    pacc_r = Ring([P.ps("pacc%d" % i, [128, 512]) for i in range(2)])
    sq = Ring([c2["sq%d" % i] for i in range(2)])
    tmpf = Ring([P.sb("tmpf%d" % i, [128, 512]) for i in range(2)])
    rstd_r = Ring([c2["rstd%d" % i] for i in range(2)])
    ones_bf = P.sb("ones_bf", [128, 128], BF16)
    scond = P.sb("scond", [128, NK])
    bada = [P.sb("bada%d" % l, [128, 72]) for l in range(NL)]
    ng = [P.sb("ng%d" % l, [128, 24]) for l in range(NL)]
    mod = [P.sb("mod%d" % l, [128, 72]) for l in range(NL)]
    scale = [P.sb("scale%d" % l, [128, 24]) for l in range(NL)]
    gate = [P.sb("gate%d" % l, [128, 24]) for l in range(NL)]

    P.memset(ones_bf.v(), 1.0)

    xv = x_in.v().rr("(kc p) t -> p kc t", p=128)
    for t in range(NTT):
        P.dma(SP, xT[t].v(), xv[:, :, t * 512:(t + 1) * 512])
    P.dma(SP, scond.v(), cond_in.v())
    P.dma(SP, cst.v(), cst_in.v())
    P.dma(SP, sel.v(), sel_in.v())
    for l in range(NL):
        P.dma(SP, bada[l].v(), bada_in[l])
        P.dma(SP, ng[l].v(), ng_in[l])
    P.act(scond.v(), scond.v(), AF.Silu)

    for l in range(NL):
        pm = ps.next()
        for blk in range(36):
            wb = wAda.next()
            P.dma(SP, wb.v(), wada_in[l * 36 + blk].rr("p (k c) -> p k c", k=NK))
            for cc in range(2):
                col = blk * 2 + cc
                for kc in range(NK):
                    P.mm(pm[:, col:col + 1], wb[:, kc, cc * 128:(cc + 1) * 128], scond[:, kc:kc + 1],
                         start=(kc == 0), stop=(kc == NK - 1))
        P.tt(mod[l].v(), pm[:, 0:72], bada[l].v(), ALU.add)
        for i in range(3):
            P.stt(scale[l][:, i * 8:(i + 1) * 8], mod[l][:, (3 * i + 1) * 8:(3 * i + 2) * 8], 1.0,
                  ng[l][:, i * 8:(i + 1) * 8], ALU.add, ALU.mult)
            P.ts(gate[l][:, i * 8:(i + 1) * 8], mod[l][:, (3 * i + 2) * 8:(3 * i + 3) * 8],
                 0.5 if i != 1 else 1.0, ALU.mult)

    def norm_mod(l, i, t):
        pn = ps.next()
        for kc in range(NK):
            s = sq.next()
            P.act(s.v(), xT[t][:, kc, :], AF.Square)
            P.mm(pn.v(), ones_bf.v(), s.v(), start=(kc == 0), stop=(kc == NK - 1))
        r = rstd_r.next()
        P.ts(r.v(), pn.v(), 1.0 / D, ALU.mult, EPS, ALU.add)
        P.act(r.v(), r.v(), AF.Sqrt)
        P.recip(r.v(), r.v())
        for kc in range(NK):
            tm = tmpf.next()
            P.stt(tm.v(), xT[t][:, kc, :], scale[l][:, i * 8 + kc:i * 8 + kc + 1], r.v(), ALU.mult, ALU.mult)
            P.act(hT[t][:, kc, :], tm.v(), AF.Identity, bias=mod[l][:, 3 * i * 8 + kc:3 * i * 8 + kc + 1])

    def ffn(l, i):
        ni = 0 if i == 0 else 2
        for h in range(NH):
            tts = list(range(h * TPH, (h + 1) * TPH))
            for t in tts:
                norm_mod(l, ni, t)
            for f in range(NF):
                wb = wA.next()
                P.dma(POOL, wb.v(), wgu_in[(l * 2 + i) * NF + f].rr("p (k c) -> p k c", k=NK))
                for j, t in enumerate(tts):
                    pg = ps.next()
                    pu = ps.next()
                    for kc in range(NK):
                        P.mm(pg.v(), wb[:, kc, 0:128], hT[t][:, kc, :], start=(kc == 0), stop=(kc == NK - 1))
                    for kc in range(NK):
                        P.mm(pu.v(), wb[:, kc, 128:256], hT[t][:, kc, :], start=(kc == 0), stop=(kc == NK - 1))
                    sg = tmpf.next()
                    P.act(sg.v(), pg.v(), AF.Silu)
                    P.tt(aT[j][:, f, :], sg.v(), pu.v(), ALU.mult)
            for dc in range(NK):
                wdb = wD.next()
                P.dma(POOL, wdb.v(), wd_in[(l * 2 + i) * NK + dc].rr("p (f c) -> p f c", f=NF))
                for j, t in enumerate(tts):
                    py = ps.next()
                    for fc in range(NF):
                        P.mm(py.v(), wdb[:, fc, :], aT[j][:, fc, :], start=(fc == 0), stop=(fc == NF - 1))
                    P.stt(xT[t][:, dc, :], py.v(), gate[l][:, ni * 8 + dc:ni * 8 + dc + 1], xT[t][:, dc, :],
                          ALU.mult, ALU.add)


    ident = cst[:, C_ID:C_ID + 128]

    def load_blk(l, b):
        wb = wA.next()
        P.dma(POOL, wb.v(), win_in[l * WIN_BLOCKS + b].rr("p (k c) -> p k c", k=NK))
        return wb

    def proj_fm(wb, c0, M, dst, scale=None):
        for t in range(NTT):
            pp = ps.next()
            for kc in range(NK):
                P.mm(pp[0:M, :], wb[:, kc, c0:c0 + M], hT[t][:, kc, :], start=(kc == 0), stop=(kc == NK - 1))
            if scale is None:
                P.copy(dst[0:M, t * 512:(t + 1) * 512], pp[0:M, :], eng=ACT)
            else:
                P.act(dst[0:M, t * 512:(t + 1) * 512], pp[0:M, :], AF.Copy, scale=scale)

    def proj_tm(wb, c0, N, dst, ncol, scale=None):
        for g in range(NCH // 4):
            pp = ps.next()
            for q in range(4):
                n = g * 4 + q
                t, off = n // 4, (n % 4) * 128
                for kc in range(NK):
                    P.mm(pp[:, q * 64:q * 64 + N], hT[t][:, kc, off:off + 128], wb[:, kc, c0:c0 + N],
                         start=(kc == 0), stop=(kc == NK - 1))
            src = pp[:, 0:256].rr("p (q c) -> p q c", q=4)[:, :, 0:N]
            if scale is None:
                P.copy(dst[:, g * 4:(g + 1) * 4, 0:N], src, eng=ACT)
            else:
                P.act(dst[:, g * 4:(g + 1) * 4, 0:N], src, AF.Copy, scale=scale)

    def proj_tm3(wb, dsts):
        for g in range(NCH // 2):
            pp = ps.next()
            for q in range(2):
                n = g * 2 + q
                t, off = n // 4, (n % 4) * 128
                for kc in range(NK):
                    P.mm(pp[:, q * 192:(q + 1) * 192], hT[t][:, kc, off:off + 128], wb[:, kc, 0:192],
                         start=(kc == 0), stop=(kc == NK - 1))
            for j, (dst, scale) in enumerate(dsts):
                src = pp[:, 0:384].rr("p (q c) -> p q c", q=2)[:, :, j * 64:(j + 1) * 64]
                if scale is None:
                    P.copy(dst[:, g * 2:(g + 1) * 2, 0:64], src, eng=ACT)
                else:
                    P.act(dst[:, g * 2:(g + 1) * 2, 0:64], src, AF.Copy, scale=scale)

    rr2 = [0]

    def run_rr(gens, delays=None):
        gens = list(gens)
        delays = list(delays) if delays is not None else [0] * len(gens)
        live = list(range(len(gens)))
        while live:
            nxt = []
            for i in live:
                if delays[i] > 0:
                    delays[i] -= 1
                    nxt.append(i)
                    continue
                try:
                    next(gens[i])
                    nxt.append(i)
                except StopIteration:
                    pass
            live = nxt

    def lin_chunk_g(n, E, WT, qs, ks, lam, S, Vx, o_out, keepcol, sfin_dst, fix=None, bf=False):
        i2 = rr2[0] % 2
        rr2[0] += 1
        if fix is not None:
            i2 = fix
        sfx = "b" if bf else ""
        QTn = mx["QT" + sfx][:, n * 128:(n + 1) * 128]
        KTn = mx["KT" + sfx][:, n * 128:(n + 1) * 128]
        p1 = ps.next()
        P.mm(p1[:, 0:128], KTn, QTn)
        at = mx[("atb%d" if bf else "attn%d") % i2]
        P.tt(at.v(), p1[:, 0:128], WT, ALU.mult)
        kw = mx[("kwb%d" if bf else "kw%d") % i2]
        P.ts(kw.v(), mx["Ktm"][:, n, :], ks, ALU.mult)
        Sr = S[:, 0:E]
        if bf:
            Vx = mx["Vtb"][:, n, 0:E]
            P.copy(Sbf[i2][:, 0:E], S[:, 0:E], eng=ACT)
            Sr = Sbf[i2][:, 0:E]
        yield
        p2 = ps.next()
        P.mm(p2[:, 0:E], at.v(), Vx)
        P.mm(p2[:, 128:128 + E], QTn, Sr)
        o2 = mx["o2_%d" % i2]
        P.copy(o2[:, 0:E], p2[:, 0:E], eng=ACT)
        P.stt(o_out, p2[:, 128:128 + E], qs, o2[:, 0:E], ALU.mult, ALU.add)
        yield
        p3 = ps.next()
        P.mm(p3[0:64, 0:E], kw.v(), Vx)
        P.stt(S[:, 0:E], S[:, 0:E], lam, p3[0:64, 0:E], ALU.mult, ALU.add)
        if sfin_dst is not None:
            sf = Sfin.next()
            P.copy(sf[:, 0:E], S[:, 0:E])
            sfin_dst(sf)
        P.ts(S[:, 0:E], S[:, 0:E], keepcol, ALU.mult)
        yield

    def lin_chunk(*a, **k):
        for _ in lin_chunk_g(*a, **k):
            pass

    def branch_out(m, pair):
        st = brst.next()
        for g in range(NCH // 4):
            pp = ps.next()
            for q in range(4):
                n = g * 4 + q
                P.tr(pp[:, q * 128:(q + 1) * 128], mx["Opair"][:, n, :], ident)
            P.copy(st[:, g * 512:(g + 1) * 512], pp.v())
        P.dma(SP, br_dram[m, pair], st.v())

    def mixer_D(l):
        lgt = smal[:, 0:8]
        qsd = smal[:, 8:16]
        ksd = smal[:, 16:24]
        lamd = smal[:, 24:32]
        P.act(lgt, lpar[l][:, 0:8], AF.Exp, scale=-1.0)
        P.act(lgt, lgt, AF.Ln, bias=1.0)
        P.ts(lgt, lgt, -1.0, ALU.mult)
        for r in range(8):
            d = r // 4
            rel = cst[:, (C_RELF if d == 0 else C_RELB):(C_RELF if d == 0 else C_RELB) + 128]
            P.act(mx["WTd"][:, r, :], rel, AF.Exp, scale=lgt[:, r:r + 1])
            P.act(qsd[:, r:r + 1], cst[:, C_POS + d:C_POS + d + 1], AF.Exp, scale=lgt[:, r:r + 1])
            P.act(ksd[:, r:r + 1], cst[:, C_POS + 2 + d:C_POS + 3 + d], AF.Exp, scale=lgt[:, r:r + 1])
        P.act(lamd, lgt, AF.Exp, scale=128.0)
        for h in range(4):
            wb = load_blk(l, BLK["d_q"])
            proj_fm(wb, h * 64, 64, mx["QTb"])
            wb = load_blk(l, BLK["d_k"])
            proj_fm(wb, h * 64, 64, mx["KTb"], scale=0.125)
            wb = load_blk(l, BLK["d_tm"] + h)
            proj_tm3(wb, [(mx["Ktm"], 0.125), (mx["Vtb"], None), (mx["Gtm"], None)])
            def chain_D(d):
                r = d * 4 + h
                S = Sst[d]
                P.dma(SP, S[:, 0:64], st_ret_in[l, r])
                od = mx["Od%d" % d]
                order = range(NCH) if d == 0 else range(NCH - 1, -1, -1)
                for n in order:
                    isend = (n % 2 == 1) if d == 0 else (n % 2 == 0)
                    dst = None
                    if isend:
                        seg = n // 2
                        dst = (lambda sf, seg=seg, r=r: P.dma(SP, o_ret[l, seg, r], sf[:, 0:64]))
                    kc = cst[0:64, C_KEEP + d * NCH + n:C_KEEP + d * NCH + n + 1]
                    yield from lin_chunk_g(n, 64, mx["WTd"][:, r, :], qsd[:, r:r + 1], ksd[:, r:r + 1],
                                           lamd[0:64, r:r + 1], S, None, od[:, n, 0:64], kc, dst, fix=d, bf=True)
            run_rr([chain_D(0), chain_D(1)])
            o0 = mx["Od0"][:, :, 0:64]
            P.tt(o0, o0, mx["Od1"][:, :, 0:64], ALU.add)
            post_norm_gate(l, o0, 8, AF.Silu, h)
            if h % 2 == 1:
                branch_out(3, h // 2)

    def post_norm_gate(l, o, gcol, gfunc, h):
        sqo = mx["sqo"]
        P.tt(sqo.v(), o, o, ALU.mult)
        ss = smal[:, 32:32 + NCH]
        P.reduce(ss, sqo.v(), ALU.add)
        P.ts(ss, ss, 1.0 / 64, ALU.mult, EPS, ALU.add)
        P.act(ss, ss, AF.Sqrt)
        P.recip(ss, ss)
        P.tt(sqo.v(), o, ss.rr("p (n o) -> p n o", o=1).bc([128, NCH, 64]), ALU.mult)
        gn = lpar[l][:, gcol:gcol + 64].rr("p (o e) -> p o e", o=1).bc([128, NCH, 64])
        P.tt(sqo.v(), sqo.v(), gn, ALU.mult)
        P.act(mx["Gtm"].v(), mx["Gtm"].v(), gfunc)
        P.tt(mx["Opair"][:, :, (h % 2) * 64:(h % 2) * 64 + 64], sqo.v(), mx["Gtm"].v(), ALU.mult)

    def zero_branch(m):
        for pair in range(2):
            st = brst.next()
            P.memset(st.v(), 0.0)
            P.dma(SP, br_dram[m, pair], st.v())

    def merge(l):
        z, zb, brm = mz["z"], mz["zb"], mz["brm"]
        gs = Ring([mz["gsb0"], mz["gsb1"]])
        for t in range(NTT):
            for m in range(4):
                for pair in range(2):
                    P.dma(SP, brm[:, m * 2 + pair, :], br_dram[m, pair][:, t * 512:(t + 1) * 512])
            for m in range(4):
                wbr = wBR.next()
                P.dma(POOL, wbr.v(), wbr_in[l * 4 + m].rr("p (k c) -> p k c", k=2))
                wbv = wbr.v()
                for gb in range(4):
                    wg = load_blk(l, BLK["merge"] + m * 4 + gb)
                    for cc in range(2):
                        dc = gb * 2 + cc
                        pg = ps.next()
                        for kc in range(NK):
                            P.mm(pg.v(), wg[:, kc, cc * 128:(cc + 1) * 128], hT[t][:, kc, :],
                                 start=(kc == 0), stop=(kc == NK - 1))
                        g = gs.next()
                        P.act(g.v(), pg.v(), AF.Sigmoid)
                        pp = ps.next()
                        for kc in range(2):
                            P.mm(pp.v(), wbv[:, kc, dc * 128:(dc + 1) * 128], brm[:, m * 2 + kc, :],
                                 start=(kc == 0), stop=(kc == 1))
                        if m == 0:
                            P.tt(z[:, dc, :], g.v(), pp.v(), ALU.mult)
                        else:
                            P.tt(g.v(), g.v(), pp.v(), ALU.mult)
                            P.tt(z[:, dc, :], z[:, dc, :], g.v(), ALU.add)
            P.copy(zb.v(), z.v(), eng=ACT)
            for ob in range(4):
                wo = wA.next()
                P.dma(POOL, wo.v(), wout_in[l * 4 + ob].rr("p (k c) -> p k c", k=NK))
                for cc in range(2):
                    dc = ob * 2 + cc
                    py = ps.next()
                    for kc in range(NK):
                        P.mm(py.v(), wo[:, kc, cc * 128:(cc + 1) * 128], zb[:, kc, :], start=(kc == 0), stop=(kc == NK - 1))
                    P.stt(xT[t][:, dc, :], py.v(), gate[l][:, 8 + dc:8 + dc + 1], xT[t][:, dc, :], ALU.mult, ALU.add)


    ONESF = cst[:, C_ONES:C_ONES + 128]
    TRIF = cst[:, C_TRIF:C_TRIF + 128]
    TRIB = cst[:, C_TRIB:C_TRIB + 128]
    NEGM = [cst[:, C_NEGF:C_NEGF + 128], cst[:, C_NEGB:C_NEGB + 128]]

    def GTs(i, w=8):
        return mx["GT"][:, i, 0:NCH * 8].rr("p (n r) -> p n r", r=8) if w == 8 else None

    def bc8(col):
        return lpar1[:, col:col + 8].rr("p (o r) -> p o r", o=1).bc([128, NCH, 8])

    def cums(src, dP, dB, dT):
        for lhs, dst in ((TRIF, dP), (TRIB, dB), (ONESF, dT)):
            pp = ps.next()
            P.mm(pp[:, 0:NCH * 8], lhs, src)
            P.copy(dst, pp[:, 0:NCH * 8], eng=ACT)

    def blend_dir(dst, aF, aB):
        P.tt(dst, aF, aB, ALU.subtract)
        P.tt(dst, dst, bc8(L_ISF), ALU.mult)
        P.tt(dst, dst, aB, ALU.add)

    def rows_of(src_slot_view_fn, dst_rows, reduce_max):
        for g in range(NCH // 4):
            pp = ps.next()
            for q in range(4):
                n = g * 4 + q
                P.tr(pp[0:8, q * 128:(q + 1) * 128], src_slot_view_fn(n), ident)
            v = pp[0:8, :].rr("p (q t) -> p q t", q=4)
            if reduce_max:
                P.reduce(dst_rows[:, g * 4:(g + 1) * 4], v, ALU.max)
            else:
                P.copy(dst_rows[:, g * 4:(g + 1) * 4], v[:, :, 0])

    def mixer_C(l):
        G16 = mx["GT"][:, 0:2, :].rr("p a b -> p (a b)").rr("p (n c) -> p n c", c=16)
        IG, LF, bP, bb, TOT, QS, KS, LAM = [GTs(i) for i in range(2, 10)]
        flat = lambda i: mx["GT"][:, i, 0:NCH * 8]
        wb = load_blk(l, BLK["c_g"])
        proj_tm(wb, 0, 16, G16, 16)
        P.tt(IG, G16[:, :, 0:8], bc8(L_MLI), ALU.add)
        P.tt(LF, G16[:, :, 8:16], bc8(L_MLF), ALU.add)
        P.act(LF, LF, AF.Exp, scale=-1.0)
        P.act(LF, LF, AF.Ln, bias=1.0)
        P.ts(LF, LF, -1.0, ALU.mult)
        cums(flat(3), flat(4), flat(7), flat(6))
        blend_dir(bb, bP, QS)
        P.act(QS, bb, AF.Exp)
        P.tt(IG, IG, bb, ALU.subtract)
        P.tt(KS, IG, TOT, ALU.add)
        import os
        DBG = int(os.environ.get("DBG_C", "9"))
        if DBG < 2:
            return
        Xr = rows[:, 0:NCH]
        Tr = rows[:, 16:16 + NCH]
        rows_of(lambda n: KS[:, n, :], Xr, True)
        rows_of(lambda n: TOT[:, n, :], Tr, False)
        X2 = Xr.rr("p (s c) -> p s c", c=2)
        T2 = Tr.rr("p (s c) -> p s c", c=2)
        Bt = rows[:, 32:32 + NSEG]
        ta = rows[:, 48:48 + NSEG]
        mF = rows[:, 64:64 + NSEG]
        mB = rows[:, 80:80 + NSEG]
        mr = rows[:, 96:96 + NSEG]
        P.tt(Bt, T2[:, :, 0], T2[:, :, 1], ALU.add)
        P.tt(ta, X2[:, :, 0], T2[:, :, 1], ALU.add)
        P.tt(mF, Bt, X2[:, :, 1], ALU.max)
        P.tt(mF, mF, ta, ALU.max)
        P.tt(ta, X2[:, :, 1], T2[:, :, 0], ALU.add)
        P.tt(mB, Bt, X2[:, :, 0], ALU.max)
        P.tt(mB, mB, ta, ALU.max)
        P.tt(mr, mF, mB, ALU.subtract)
        P.ts(mr, mr, sel[:, 512:513], ALU.mult)
        P.tt(mr, mr, mB, ALU.add)
        if DBG < 3:
            return
        P.dma(SP, o_m[l], mr)
        if DBG < 4:
            return
        em = rows[:, 112:112 + NSEG]
        P.act(em, mr, AF.Exp, scale=-1.0)
        pp = ps.next()
        for r in range(8):
            P.mm(pp[0:64, r * NSEG:(r + 1) * NSEG], sel[:, r * 64:(r + 1) * 64], em)
        P.copy(embc[:, 0:8 * NSEG], pp[0:64, 0:8 * NSEG])
        P.act(KS, KS, AF.Exp)
        P.act(LAM, TOT, AF.Exp)
        P.dma(SP, em0.v(), st_m_in[l])
        P.act(em0.v(), em0.v(), AF.Exp)
        if DBG < 5:
            return
        for h in range(4):
            wb = load_blk(l, BLK["c_q"])
            proj_fm(wb, h * 64, 64, mx["QTb"], scale=0.125)
            wb = load_blk(l, BLK["c_k"])
            proj_fm(wb, h * 64, 64, mx["KTb"])
            wb = load_blk(l, BLK["c_tm"] + h)
            P.memset(mx["Vtb"][:, :, 64:65], 1.0)
            proj_tm3(wb, [(mx["Ktm"], None), (mx["Vtb"], None), (mx["Gtm"], None)])
            def chain_C(d):
                r = d * 4 + h
                S = Sst[d]
                P.dma(SP, S.v(), st_C_in[l, r])
                P.ts(S.v(), S.v(), em0[0:64, r:r + 1], ALU.mult)
                od = mx["Od%d" % d]
                order = range(NCH) if d == 0 else range(NCH - 1, -1, -1)
                for n in order:
                    dg = mx["dg%d" % d]
                    wt = mx["wt%d" % d]
                    P.ts(dg.v(), ident, bb[:, n, r:r + 1], ALU.mult)
                    pw = ps.next()
                    P.mm(pw[:, 0:128], ONESF, dg.v(), start=True, stop=False)
                    P.mm(pw[:, 0:128], ident, NEGM[d], start=False, stop=True)
                    P.act(wt.v(), pw[:, 0:128], AF.Exp, bias=IG[:, n, r:r + 1])
                    yield
                    isend = (n % 2 == 1) if d == 0 else (n % 2 == 0)
                    dst = None
                    if isend:
                        seg = n // 2

                        def dst(sf, seg=seg, r=r):
                            P.ts(sf.v(), sf.v(), embc[:, r * NSEG + seg:r * NSEG + seg + 1], ALU.mult)
                            P.dma(SP, o_C[l, seg, r], sf.v())
                    kc = cst[0:64, C_KEEP + d * NCH + n:C_KEEP + d * NCH + n + 1]
                    yield from lin_chunk_g(n, 65, wt.v(), QS[:, n, r:r + 1], KS[:, n, r:r + 1], LAM[0:64, n, r:r + 1],
                                           S, None, od[:, n, 0:65], kc, dst, fix=d, bf=True)
            run_rr([chain_C(0), chain_C(1)])
            for d in range(2):
                od = mx["Od%d" % d]
                den = od[:, :, 64:65]
                P.act(den, den, AF.Abs)
                P.ts(den, den, 1.0, ALU.max)
                P.recip(den, den)
                P.tt(od[:, :, 0:64], od[:, :, 0:64], den.bc([128, NCH, 64]), ALU.mult)
            o0 = mx["Od0"][:, :, 0:64]
            P.tt(o0, o0, mx["Od1"][:, :, 0:64], ALU.add)
            post_norm_gate(l, o0, L_MLG, AF.Sigmoid, h)
            if h % 2 == 1:
                branch_out(2, h // 2)


    NOTI = cst[:, C_NOTI:C_NOTI + 128]

    def mixer_A(l):
        G16 = mx["GT"][:, 0:2, :].rr("p a b -> p (a b)").rr("p (n c) -> p n c", c=16)
        SQB, NCG, SQBE, CG, TOT, QS, KS, LAM = [GTs(i) for i in range(2, 10)]
        flat = lambda i: mx["GT"][:, i, 0:NCH * 8]
        wb = load_blk(l, BLK["a_g"])
        proj_tm(wb, 0, 16, G16, 16)
        negA = smal[:, 40:48]
        P.act(negA, lpar1[:, L_DNA:L_DNA + 8], AF.Exp)
        P.ts(negA, negA, -1.0, ALU.mult)
        P.act(SQB, G16[:, :, 0:8], AF.Sigmoid)
        P.act(SQB, SQB, AF.Sqrt)
        P.tt(NCG, G16[:, :, 8:16], bc8(L_DNB), ALU.add)
        P.act(NCG, NCG, AF.Exp)
        P.act(NCG, NCG, AF.Ln, bias=1.0)
        P.tt(NCG, NCG, negA.rr("p (o r) -> p o r", o=1).bc([128, NCH, 8]), ALU.mult)
        cums(flat(3), flat(4), flat(7), flat(6))
        blend_dir(CG, SQBE, QS)
        P.act(QS, CG, AF.Exp)
        P.tt(SQBE, SQB, QS, ALU.mult)
        P.tt(KS, TOT, CG, ALU.subtract)
        P.act(KS, KS, AF.Exp)
        P.act(LAM, TOT, AF.Exp)
        P.ts(NCG, CG, -1.0, ALU.mult)
        I64 = cst[0:64, C_ID:C_ID + 64]
        O64 = cst[0:64, C_ONES:C_ONES + 64]
        for h in range(4):
            for xi, (bn, dstn) in enumerate((("a_q", "QT"), ("a_k", "KT"), ("a_v", None))):
                raw = mx["raw"]
                P.memset(raw[:, 0:1], 0.0)
                P.memset(raw[:, T + 1:T + 2], 0.0)
                wb = load_blk(l, BLK[bn])
                proj_fm(wb, h * 64, 64, raw[:, 1:T + 1])
                cw = lambda j: lpar1[0:64, L_CONV + (xi * 4 + h) * 3 + j:L_CONV + (xi * 4 + h) * 3 + j + 1]
                nw0 = smal[0:64, 48:49]
                nw2 = smal[0:64, 49:50]
                P.ts(nw0, cw(0), cst[0:64, C_PFLAG:C_PFLAG + 1], ALU.mult, -1.0, ALU.mult)
                P.ts(nw2, cw(2), cst[0:64, C_PFLAG:C_PFLAG + 1], ALU.mult, -1.0, ALU.mult)
                for t in range(NTT):
                    off = t * 512
                    st_ = tmpf.next()
                    st = st_[0:64, :]
                    P.ts(st, raw[:, off + 1:off + 513], cw(1), ALU.mult)
                    P.stt(st, raw[:, off:off + 512], cw(0), st, ALU.mult, ALU.add)
                    P.stt(st, raw[:, off + 2:off + 514], cw(2), st, ALU.mult, ALU.add)
                    sv = st.rr("p (a b) -> p a b", b=256)
                    r0 = raw[:, off:off + 512].rr("p (a b) -> p a b", b=256)[:, :, 0]
                    r2 = raw[:, off + 2:off + 514].rr("p (a b) -> p a b", b=256)[:, :, 255]
                    P.stt(sv[:, :, 0], r0, nw0, sv[:, :, 0], ALU.mult, ALU.add)
                    P.stt(sv[:, :, 255], r2, nw2, sv[:, :, 255], ALU.mult, ALU.add)
                    if dstn is not None:
                        dv = mx[dstn][:, off:off + 512]
                        P.act(dv, st, AF.Silu)
                        s2_ = tmpf.next()
                        s2 = s2_[0:64, :]
                        P.act(s2, dv, AF.Square)
                        pn = ps.next()
                        P.mm(pn[0:64, :], O64, s2)
                        P.ts(s2, pn[0:64, :], EPS, ALU.add)
                        P.act(s2, s2, AF.Sqrt)
                        P.recip(s2, s2)
                        if xi == 0:
                            P.stt(dv, dv, 0.125, s2, ALU.mult, ALU.mult)
                        else:
                            P.tt(dv, dv, s2, ALU.mult)
                    else:
                        P.act(st, st, AF.Silu)
                        pt = ps.next()
                        for q in range(4):
                            P.tr(pt[:, q * 64:(q + 1) * 64], st[:, q * 128:(q + 1) * 128], I64)
                        P.copy(mx["Vtm"][:, t * 4:(t + 1) * 4, 0:64], pt[:, 0:256].rr("p (q c) -> p q c", q=4), eng=ACT)
            for g in range(NCH // 4):
                pt = ps.next()
                for q in range(4):
                    n = g * 4 + q
                    P.tr(pt[:, q * 64:(q + 1) * 64], mx["KT"][:, n * 128:(n + 1) * 128], I64)
                P.copy(mx["Ktm"][:, g * 4:(g + 1) * 4, :], pt[:, 0:256].rr("p (q c) -> p q c", q=4), eng=ACT)
            stt_ = {"pre": [0] * (NCH + 2), "scan": 0}
            ident2 = Vw(cst, cst.ap[:, C_ID:C_ID + 128].rearrange("p (o c) -> p o c", o=1).broadcast_to([128, 2, 128]))
            noti2 = Vw(cst, cst.ap[:, C_NOTI:C_NOTI + 128].rearrange("p (o c) -> p o c", o=1).broadcast_to([128, 2, 128]))

            def chunks_of(k):
                return (k, NCH - 1 - k)

            def pre_W(p):
                psp = Ring(ps.tiles[2 * p:2 * p + 2])
                for k in range(p, NCH, 2):
                    while stt_["scan"] < k - 1:
                        yield
                    ns = chunks_of(k)
                    rs = (h, 4 + h)
                    dg, Y, kh, khT, u = [ax["%s_%d" % (nm_, p)] for nm_ in ("dg", "Y", "kh", "khT", "u")]
                    wt = mx["wtW_%d" % p]
                    wT = mx["wTW_%d" % p]
                    for d in range(2):
                        P.ts(dg[:, d, :], ident, CG[:, ns[d], rs[d]:rs[d] + 1], ALU.mult)
                        P.ts(kh[:, d, :], mx["Ktm"][:, ns[d], :], SQB[:, ns[d], rs[d]:rs[d] + 1], ALU.mult)
                    yield
                    pw = psp.next()
                    pk = psp.next()
                    for d in range(2):
                        P.mm(pw[:, d * 128:(d + 1) * 128], ONESF, dg[:, d, :], start=True, stop=False)
                        P.mm(pw[:, d * 128:(d + 1) * 128], ident, NEGM[d], start=False, stop=True)
                        P.tr(pk[0:64, d * 128:(d + 1) * 128], kh[:, d, :], ident)
                    yield
                    for d in range(2):
                        P.act(wt[:, d, :], pw[:, d * 128:(d + 1) * 128], AF.Exp, bias=NCG[:, ns[d], rs[d]:rs[d] + 1])
                    P.copy(khT.v(), pk[0:64, 0:256].rr("p (d c) -> p d c", d=2))
                    for d in range(2):
                        P.ts(Y[:, d, 0:64], mx["Vtm"][:, ns[d], 0:64], SQB[:, ns[d], rs[d]:rs[d] + 1], ALU.mult)
                        P.ts(Y[:, d, 64:128], mx["Ktm"][:, ns[d], :], SQBE[:, ns[d], rs[d]:rs[d] + 1], ALU.mult)
                    yield
                    pg = psp.next()
                    for d in range(2):
                        P.mm(pg[:, d * 128:(d + 1) * 128], khT[:, d, :], khT[:, d, :])
                    yield
                    XT = ax["XTa_%d" % p]
                    X = ax["Xa_%d" % p]
                    P.tt(XT.v(), pg[:, 0:256].rr("p (d c) -> p d c", d=2), wt.v(), ALU.mult)
                    P.tt(XT.v(), XT.v(), noti2, ALU.mult)
                    yield
                    px = psp.next()
                    py = psp.next()
                    for d in range(2):
                        P.tr(px[:, d * 128:(d + 1) * 128], XT[:, d, :], ident)
                        P.mm(py[:, d * 128:(d + 1) * 128], XT[:, d, :], Y[:, d, :])
                    yield
                    P.copy(X.v(), px[:, 0:256].rr("p (d c) -> p d c", d=2), eng=ACT)
                    P.tt(Y.v(), Y.v(), py[:, 0:256].rr("p (d c) -> p d c", d=2), ALU.subtract)
                    yield
                    for lev in range(6):
                        Xn = ax["Xb_%d" % p] if X is ax["Xa_%d" % p] else ax["Xa_%d" % p]
                        XTn = ax["XTb_%d" % p] if XT is ax["XTa_%d" % p] else ax["XTa_%d" % p]
                        p2x = psp.next()
                        p2y = psp.next()
                        for d in range(2):
                            if lev < 5:
                                P.mm(p2x[:, d * 128:(d + 1) * 128], XT[:, d, :], X[:, d, :])
                            P.mm(p2y[:, d * 128:(d + 1) * 128], X[:, d, :], XT[:, d, :])
                        yield
                        if lev < 5:
                            P.copy(Xn.v(), p2x[:, 0:256].rr("p (d c) -> p d c", d=2), eng=ACT)
                        P.copy(XTn.v(), p2y[:, 0:256].rr("p (d c) -> p d c", d=2))
                        X, XT = Xn, XTn
                        yield
                        py = psp.next()
                        for d in range(2):
                            P.mm(py[:, d * 128:(d + 1) * 128], XT[:, d, :], Y[:, d, :])
                        yield
                        P.tt(Y.v(), Y.v(), py[:, 0:256].rr("p (d c) -> p d c", d=2), ALU.add)
                        yield
                    for d in range(2):
                        P.ts(u[:, d, :], Y[:, d, 0:64], SQB[:, ns[d], rs[d]:rs[d] + 1], ALU.mult)
                        P.ts(kh[:, d, :], Y[:, d, 64:128], SQB[:, ns[d], rs[d]:rs[d] + 1], ALU.mult)
                    yield
                    pk2 = psp.next()
                    for d in range(2):
                        P.tr(pk2[0:64, d * 128:(d + 1) * 128], kh[:, d, :], ident)
                    yield
                    P.copy(wT.v(), pk2[0:64, 0:256].rr("p (d c) -> p d c", d=2), eng=ACT)
                    stt_["pre"][k] = 1
                    yield

            def scan_W():
                pss = Ring(ps.tiles[4:6])
                rs = (h, 4 + h)
                Ss = (Sst[0], Sst[1])
                for d in range(2):
                    P.dma(SP, Ss[d][:, 0:64], st_dl_in[l, rs[d]])
                ods = (mx["Od0"], mx["Od1"])
                for k in range(NCH):
                    while not stt_["pre"][k]:
                        yield
                    p = k % 2
                    ns = chunks_of(k)
                    wt = mx["wtW_%d" % p]
                    wT = mx["wTW_%d" % p]
                    u = ax["u_%d" % p]
                    vnew = ax["vnew"]
                    at = mx["atW"]
                    pv = pss.next()
                    p1 = pss.next()
                    for d in range(2):
                        P.mm(pv[:, d * 64:(d + 1) * 64], wT[:, d, :], Ss[d][:, 0:64])
                        P.mm(p1[:, d * 128:(d + 1) * 128], mx["KT"][:, ns[d] * 128:(ns[d] + 1) * 128],
                             mx["QT"][:, ns[d] * 128:(ns[d] + 1) * 128])
                    yield
                    P.tt(vnew.v(), u.v(), pv[:, 0:128].rr("p (d c) -> p d c", d=2), ALU.subtract)
                    P.tt(at.v(), p1[:, 0:256].rr("p (d c) -> p d c", d=2), wt.v(), ALU.mult)
                    kws = (mx["kw0"], mx["kw1"])
                    for d in range(2):
                        P.ts(kws[d].v(), mx["Ktm"][:, ns[d], :], KS[:, ns[d], rs[d]:rs[d] + 1], ALU.mult)
                    yield
                    p2 = pss.next()
                    p3 = pss.next()
                    for d in range(2):
                        P.mm(p2[:, d * 64:(d + 1) * 64], at[:, d, :], vnew[:, d, :])
                        P.mm(p2[:, 128 + d * 64:128 + (d + 1) * 64], mx["QT"][:, ns[d] * 128:(ns[d] + 1) * 128], Ss[d][:, 0:64])
                        P.mm(p3[0:64, d * 64:(d + 1) * 64], kws[d].v(), vnew[:, d, :])
                    yield
                    o2 = mx["attn0"]
                    P.copy(o2[:, 0:128], p2[:, 0:128], eng=ACT)
                    for d in range(2):
                        n = ns[d]
                        r = rs[d]
                        S = Ss[d]
                        P.stt(ods[d][:, n, 0:64], p2[:, 128 + d * 64:128 + (d + 1) * 64], QS[:, n, r:r + 1],
                              o2[:, d * 64:(d + 1) * 64], ALU.mult, ALU.add)
                        P.stt(S[:, 0:64], S[:, 0:64], LAM[0:64, n, r:r + 1], p3[0:64, d * 64:(d + 1) * 64], ALU.mult, ALU.add)
                        isend = (n % 2 == 1) if d == 0 else (n % 2 == 0)
                        if isend:
                            sf = Sfin.next()
                            P.copy(sf[:, 0:64], S[:, 0:64])
                            P.dma(SP, o_dl[l, n // 2, r], sf[:, 0:64])
                        P.ts(S[:, 0:64], S[:, 0:64], cst[0:64, C_KEEP + d * NCH + n:C_KEEP + d * NCH + n + 1], ALU.mult)
                    stt_["scan"] = k + 1
                    yield
            run_rr([pre_W(0), pre_W(1), scan_W()], delays=[0, 2, 0])
            wb = load_blk(l, BLK["a_z"])
            proj_tm(wb, h * 64, 64, mx["Gtm"], 64)
            o0 = mx["Od0"][:, :, 0:64]
            P.tt(o0, o0, mx["Od1"][:, :, 0:64], ALU.add)
            post_norm_gate(l, o0, L_DNG, AF.Silu, h)
            if h % 2 == 1:
                branch_out(0, h // 2)


    def mixer_B(l):
        import math
        lam_init = 0.8 - 0.6 * math.exp(-0.3 * l)
        lt = smal[:, 50:52]
        tmpl = bx["rt0"][:, 0:32]
        P.tt(tmpl, lpar1[:, L_LAM:L_LAM + 32], lpar1[:, L_LAM + 32:L_LAM + 64], ALU.mult)
        P.reduce(lt[:, 0:1], tmpl, ALU.add)
        P.tt(tmpl, lpar1[:, L_LAM + 64:L_LAM + 96], lpar1[:, L_LAM + 96:L_LAM + 128], ALU.mult)
        P.reduce(lt[:, 1:2], tmpl, ALU.add)
        P.act(lt, lt, AF.Exp)
        nlam = smal[:, 52:53]
        P.tt(nlam, lt[:, 1:2], lt[:, 0:1], ALU.subtract)
        P.ts(nlam, nlam, -lam_init, ALU.add)
        P.dma(SP, bx["ropet"].v(), rope_in.v())
        qaugs, kaug, Vext = (bx["qaug0"], bx["qaug1"]), bx["kaug"], bx["Vext"]
        for h in range(4):
            P.memset(kaug.v(), 0.0, eng=POOL)
            P.memset(qaugs[0].v(), 0.0, eng=POOL)
            P.memset(qaugs[1].v(), 0.0, eng=POOL)
            for c in range(2):
                qaug = qaugs[c]
                P.dma(POOL, qaug[c * 64 + 32:c * 64 + 41, :], qmask_in.v())
                P.dma(POOL, kaug[c * 64 + 32:c * 64 + 41, :], kmask_in.v())
                P.dma(POOL, kaug[c * 64:c * 64 + 32, 0:256], ckT_in[l, h, c])
            P.dma(POOL, Vext[:, 0:2, 0:64], cv_in[l, h].rr("(j p) c -> p j c", p=128))
            P.memset(Vext[:, :, 64:65], 1.0)
            wb = load_blk(l, BLK["b_tm"] + h)
            Vf = bx["Vf"]
            proj_tm3(wb, [(Vf, None), (bx["Qb"], None), (bx["Kb"], None)])
            P.dma(SP, o_v[l, h].rr("(n p) c -> p n c", p=128), Vf.v())
            P.copy(Vext[:, 2:NKT, 0:64], Vf.v())
            for xi, (bn, xb, gcol) in enumerate((("b_q", "Qb", L_QNG), ("b_k", "Kb", L_KNG))):
                X = bx[xb]
                sqo = bx["Ob"]
                Xg = X.v().rr("p n (c e) -> p (n c) e", c=2)
                Sg = sqo.v().rr("p n (c e) -> p (n c) e", c=2)
                P.tt(sqo.v(), X.v(), X.v(), ALU.mult)
                ss = rows_b
                P.reduce(ss, Sg, ALU.add)
                P.ts(ss, ss, 1.0 / 32, ALU.mult, EPS, ALU.add)
                P.act(ss, ss, AF.Sqrt)
                P.recip(ss, ss)
                P.tt(Xg, Xg, ss.rr("p (m o) -> p m o", o=1).bc([128, NCH * 2, 32]), ALU.mult)
                P.tt(Xg, Xg, lpar1[:, gcol:gcol + 32].rr("p (o e) -> p o e", o=1).bc([128, NCH * 2, 32]), ALU.mult)
                rp = bx["ropet"]
                cosv = rp[:, 0, :].rr("p (n a f) -> p n a f", a=2, f=8)
                sinv = rp[:, 1, :].rr("p (n a f) -> p n a f", a=2, f=8)
                t1, t2, t3, t4 = [bx["rt%d" % i].v().rr("p (n a f) -> p n a f", a=2, f=8) for i in range(4)]
                for c in range(2):
                    xv = X[:, :, c * 32:(c + 1) * 32].rr("p n (a q f) -> p n a q f", a=2, q=2)
                    x1 = xv[:, :, :, 0, :]
                    x2 = xv[:, :, :, 1, :]
                    P.tt(t1, x1, cosv, ALU.mult)
                    P.tt(t2, x2, sinv, ALU.mult)
                    P.tt(t3, x2, cosv, ALU.mult)
                    P.tt(t4, x1, sinv, ALU.mult)
                    P.tt(x1, t1, t2, ALU.subtract)
                    P.tt(x2, t3, t4, ALU.add)
                if xi == 1:
                    P.dma(SP, o_k[l, h].rr("(n p) c -> p n c", p=128), X.v())
                coff = 0 if xi == 0 else 256
                for c in range(2):
                    dstT = qaugs[c] if xi == 0 else kaug
                    for g in range(NCH // 4):
                        pt = ps.next()
                        for q in range(4):
                            n = g * 4 + q
                            P.tr(pt[0:32, q * 128:(q + 1) * 128], X[:, n, c * 32:(c + 1) * 32], ident)
                        dd = dstT[c * 64:c * 64 + 32, coff + g * 512:coff + (g + 1) * 512]
                        if xi == 0:
                            P.act(dd, pt[0:32, :], AF.Copy, scale=32 ** -0.5)
                        else:
                            P.copy(dd, pt[0:32, :], eng=ACT)
            PTr = Ring([bx["PT%d" % i] for i in range(4)])
            Ob = bx["Ob"]
            for qb in range(NTT):
                for c in range(2):
                    pacc = pacc_r.next()
                    pend = None
                    for kt in range(NKT + 1):
                        cur = None
                        if kt < NKT:
                            psc = ps.next()
                            P.mm(psc.v(), kaug[:, kt * 128:(kt + 1) * 128], qaugs[c][:, qb * 512:(qb + 1) * 512])
                            pt_ = PTr.next()
                            P.act(pt_.v(), psc.v(), AF.Exp, bias=-4.0)
                            cur = (kt, pt_)
                        if pend is not None:
                            k0, p0 = pend
                            P.mm(pacc[0:65, :], Vext[:, k0, :], p0.v(), start=(k0 == 0), stop=(k0 == NKT - 1))
                        pend = cur
                    ocT = bx["ocT"]
                    P.copy(ocT.v(), pacc[0:65, :], eng=ACT)
                    ptr = ps.next()
                    for qs in range(4):
                        P.tr(ptr[:, qs * 65:(qs + 1) * 65], ocT[:, qs * 128:(qs + 1) * 128], cst[0:65, C_ID:C_ID + 65])
                    oc = bx["octmp"] if c == 0 else bx["oc2"]
                    P.copy(oc.v(), ptr[:, 0:260].rr("p (q e) -> p q e", q=4))
                o1, o2 = bx["octmp"], bx["oc2"]
                P.recip(o1[:, :, 64:65], o1[:, :, 64:65])
                P.recip(o2[:, :, 64:65], o2[:, :, 64:65])
                P.ts(o2[:, :, 64:65], o2[:, :, 64:65], nlam, ALU.mult)
                P.tt(o1[:, :, 0:64], o1[:, :, 0:64], o1[:, :, 64:65].bc([128, 4, 64]), ALU.mult)
                P.tt(o2[:, :, 0:64], o2[:, :, 0:64], o2[:, :, 64:65].bc([128, 4, 64]), ALU.mult)
                P.tt(Ob[:, qb * 4:(qb + 1) * 4, :], o1[:, :, 0:64], o2[:, :, 0:64], ALU.add)
            sqo = bx["Qb"]
            P.tt(sqo.v(), Ob.v(), Ob.v(), ALU.mult)
            ss = smal[:, 32:32 + NCH]
            P.reduce(ss, sqo.v(), ALU.add)
            P.ts(ss, ss, 1.0 / 64, ALU.mult, EPS, ALU.add)
            P.act(ss, ss, AF.Sqrt)
            P.recip(ss, ss)
            P.ts(ss, ss, 1.0 - lam_init, ALU.mult)
            P.tt(sqo.v(), Ob.v(), ss.rr("p (n o) -> p n o", o=1).bc([128, NCH, 64]), ALU.mult)
            gn = lpar1[:, L_DAG:L_DAG + 64].rr("p (o e) -> p o e", o=1).bc([128, NCH, 64])
            P.tt(mx["Opair"][:, :, (h % 2) * 64:(h % 2) * 64 + 64], sqo.v(), gn, ALU.mult)
            if h % 2 == 1:
                branch_out(1, h // 2)

    def mixers(l):
        P.dma(SP, lpar1.v(), lp_in[l])
        for t in range(NTT):
            norm_mod(l, 1, t)
        for m in range(4):
            if m not in MIX_IMPL:
                zero_branch(m)
        if 0 in MIX_IMPL:
            mixer_A(l)
        if 1 in MIX_IMPL:
            mixer_B(l)
        if 2 in MIX_IMPL:
            mixer_C(l)
        if 3 in MIX_IMPL:
            mixer_D(l)
        merge(l)

    MIX_IMPL = set(MIXSEL)
    for l in range(NL):
        ffn(l, 0)
        if stage == "ffn1":
            break
        mixers(l)
        if stage == "mix":
            break
        ffn(l, 1)

    yv = y_out.v().rr("(kc p) t -> p kc t", p=128)
    for t in range(NTT):
        P.dma(SP, yv[:, :, t * 512:(t + 1) * 512], xT[t].v())
    P.emit()
    return nc


def kblocks(W, width):
    K, N = W.shape
    kc = K // 128
    nb = N // width
    return np.ascontiguousarray(W.reshape(kc, 128, nb, width).transpose(2, 1, 0, 3).reshape(nb, 128, kc * width))


def host_weights(inp, NL):
    w = {}
    w["wada"] = np.concatenate([kblocks(inp["w_ada"][l], 256) for l in range(NL)], 0)
    w["bada"] = np.stack([np.ascontiguousarray(inp["b_ada"][l].reshape(72, 128).T) for l in range(NL)])
    w["ng"] = np.stack([np.ascontiguousarray(inp["norm_g"][l].reshape(24, 128).T) for l in range(NL)])
    gu = []
    wd = []
    for l in range(NL):
        for i in range(2):
            g = kblocks(inp["ffn_w_gate"][l, i], 128).reshape(NF, 128, NK, 128)
            u = kblocks(inp["ffn_w_up"][l, i], 128).reshape(NF, 128, NK, 128)
            gu.append(np.concatenate([g, u], axis=3).reshape(NF, 128, NK * 256))
            wd.append(kblocks(inp["ffn_w_down"][l, i], 128))
    w["wgu"] = np.ascontiguousarray(np.concatenate(gu, 0))
    w["wd"] = np.ascontiguousarray(np.concatenate(wd, 0))
    return w


IN_SIZES = (256, 256, 256, 256, 8, 8, 256, 256, 256, 256, 256, 256, 256, 8, 8, 256, 256, 256, 256, 4096)


def host_weights2(inp, NL, w):
    offs = np.concatenate([[0], np.cumsum(IN_SIZES)])
    names = ["a_q", "a_k", "a_v", "a_z", "a_beta", "a_alpha", "b_q", "b_k", "b_v", "c_q", "c_k", "c_v", "c_o",
             "c_i", "c_f", "d_q", "d_k", "d_v", "d_g", "merge"]
    col = {n: int(offs[i]) for i, n in enumerate(names)}
    blocks = []
    for l in range(NL):
        W = inp["w_in"][l]

        def blk(c0, n=256):
            b = np.zeros((1024, 256), np.float32)
            b[:, :n] = W[:, c0:c0 + n]
            return b

        def gblk(c0, c1):
            b = np.zeros((1024, 256), np.float32)
            b[:, 0:8] = W[:, c0:c0 + 8]
            b[:, 8:16] = W[:, c1:c1 + 8]
            return b
        bl = [blk(col["a_q"]), blk(col["a_k"]), blk(col["a_v"]), blk(col["a_z"]), gblk(col["a_beta"], col["a_alpha"]),
              blk(col["b_q"]), blk(col["b_k"]), blk(col["b_v"]),
              blk(col["c_q"]), blk(col["c_k"]), blk(col["c_v"]), blk(col["c_o"]), gblk(col["c_i"], col["c_f"]),
              blk(col["d_q"]), blk(col["d_k"]), blk(col["d_v"]), blk(col["d_g"])]
        bl += [blk(col["merge"] + 256 * i) for i in range(16)]

        def tmblk(names, hh):
            b = np.zeros((1024, 256), np.float32)
            for j, nm in enumerate(names):
                b[:, j * 64:(j + 1) * 64] = W[:, col[nm] + hh * 64:col[nm] + (hh + 1) * 64]
            return b
        for names in (("b_v", "b_q", "b_k"), ("c_k", "c_v", "c_o"), ("d_k", "d_v", "d_g")):
            bl += [tmblk(names, hh) for hh in range(4)]
        blocks += [kblocks(b, 256)[0] for b in bl]
    w["win"] = np.ascontiguousarray(np.stack(blocks))
    w["wbr"] = np.ascontiguousarray(np.stack([inp["w_branch"][l, m].reshape(2, 128, 1024).transpose(1, 0, 2).reshape(128, 2048)
                                              for l in range(NL) for m in range(4)]))
    w["wout"] = np.concatenate([kblocks(inp["w_out"][l], 256) for l in range(NL)], 0)
    lp = np.zeros((NL, 128, LP_W), np.float32)
    for l in range(NL):
        lp[l, :, 0:8] = inp["ret_decay_logit"][l].reshape(8)[None, :]
        lp[l, :, 8:72] = inp["ret_norm_g"][l][None, :]
        lp[l, :, L_MLI:L_MLI + 8] = inp["ml_i_bias"][l].reshape(8)[None, :]
        lp[l, :, L_MLF:L_MLF + 8] = inp["ml_f_bias"][l].reshape(8)[None, :]
        lp[l, :, L_MLG:L_MLG + 64] = inp["ml_norm_g"][l][None, :]
        lp[l, :, L_ISF:L_ISF + 4] = 1.0
        lp[l, :, L_QNG:L_QNG + 32] = inp["da_qn_g"][l][None, :]
        lp[l, :, L_KNG:L_KNG + 32] = inp["da_kn_g"][l][None, :]
        lp[l, :, L_DAG:L_DAG + 64] = inp["da_norm_g"][l][None, :]
        lp[l, :, L_LAM:L_LAM + 128] = inp["da_lambda"][l].reshape(128)[None, :]
        lp[l, :, L_DNG:L_DNG + 64] = inp["dn_norm_g"][l][None, :]
        lp[l, :, L_DNA:L_DNA + 8] = inp["dn_a_log"][l].reshape(8)[None, :]
        lp[l, :, L_DNB:L_DNB + 8] = inp["dn_dt_bias"][l].reshape(8)[None, :]
        cw = inp["dn_conv_w"][l]
        for xi in range(3):
            for hh in range(4):
                for jj in range(3):
                    lp[l, 0:64, L_CONV + (xi * 4 + hh) * 3 + jj] = cw[jj, xi * 256 + hh * 64: xi * 256 + hh * 64 + 64]
    w["lpar"] = lp
    return w


def host_consts(T, is_sample):
    NCH = T // 128
    c = np.zeros((128, CST_W), np.float32)
    c[:, C_ID:C_ID + 128] = np.eye(128, dtype=np.float32)
    j = np.arange(128)[:, None]
    i = np.arange(128)[None, :]
    c[:, C_NEGF:C_NEGF + 128] = np.where(j <= i, 0.0, BIGNEG)
    c[:, C_NEGB:C_NEGB + 128] = np.where(j >= i, 0.0, BIGNEG)
    c[:, C_RELF:C_RELF + 128] = np.where(j <= i, (i - j).astype(np.float32), 1e6)
    c[:, C_RELB:C_RELB + 128] = np.where(j >= i, (j - i).astype(np.float32), 1e6)
    p = np.arange(128, dtype=np.float32)
    c[:, C_POS + 0] = p + 1
    c[:, C_POS + 1] = 128 - p
    c[:, C_POS + 2] = 127 - p
    c[:, C_POS + 3] = p
    c[:, C_PFLAG] = 0.0 if is_sample else 1.0
    c[:, C_ONES:C_ONES + 128] = 1.0
    c[:, C_TRIF:C_TRIF + 128] = (j <= i)
    c[:, C_TRIB:C_TRIB + 128] = (j >= i)
    c[:, C_NOTI:C_NOTI + 128] = (j != i)
    for n in range(NCH):
        c[:, C_KEEP + n] = 1.0 if (is_sample or n % 2 == 0) else 0.0
        c[:, C_KEEP + NCH + n] = 1.0 if (is_sample or n % 2 == 1) else 0.0
    return c


def host_sel():
    s = np.zeros((8, 520), np.float32)
    for r in range(8):
        s[r, r * 64:(r + 1) * 64] = 1.0
    s[0:4, 512] = 1.0
    return s


def host_attn_consts(T, is_sample):
    NCH = T // 128
    t = np.arange(T)
    rope = np.zeros((128, 2, NCH, 2, 8), np.float32)
    rope[:, 0] = 1.0
    qmask = np.zeros((9, T), np.float32)
    kmask = np.zeros((9, T + 256), np.float32)
    if is_sample:
        freqs = (10000.0 ** (-np.arange(8, dtype=np.float32) / 8)).astype(np.float32)
        rows = (t // 64).astype(np.float32)
        cols = (t % 64).astype(np.float32)
        for hf, pos in enumerate((rows, cols)):
            ang = (pos[:, None] * freqs[None, :]).astype(np.float32)
            rope[:, 0, :, hf, :] = np.cos(ang).reshape(NCH, 128, 8).transpose(1, 0, 2)
            rope[:, 1, :, hf, :] = np.sin(ang).reshape(NCH, 128, 8).transpose(1, 0, 2)
    else:
        seg = t // 256
        for a in range(8):
            qmask[a] = np.where(seg == a, 0.0, BIGNEG)
            kmask[a, 256:] = (seg == a)
        qmask[8] = BIGNEG
        kmask[8, 0:256] = 1.0
    return rope.reshape(128, 2, NCH * 16), qmask, kmask


T_SLAB = 2048
N_LAYERS = 2
_NC_CACHE = {}


def kernel(**inp):
    inp = {k: np.asarray(v) for k, v in inp.items()}
    NL = N_LAYERS
    T = T_SLAB
    xp = inp["x_prompt"]
    xs = inp["x_sample"]
    w = host_weights(inp, NL)
    host_weights2(inp, NL, w)
    cst_s = host_consts(T, True)
    cst_p = host_consts(T, False)
    att_s = host_attn_consts(T, True)
    att_p = host_attn_consts(T, False)
    w["sel"] = host_sel()
    zf = lambda *s: np.zeros(s, np.float32)
    in_maps = []
    for c in range(8):
        m = dict(w)
        if c < 2:
            x = xs[c]
            cond = inp["c"][c]
            m["cst"] = cst_s
            m["rope"], m["qmask"], m["kmask"] = att_s
            m["st_ret"] = np.ascontiguousarray(inp["state_ret"][c].reshape(NL, 8, 64, 64))
            m["st_delta"] = np.ascontiguousarray(inp["state_delta"][c].reshape(NL, 8, 64, 64))
            m["st_Cn"] = np.ascontiguousarray(np.concatenate(
                [inp["state_mlstm_C"][c].reshape(NL, 8, 64, 64), inp["state_mlstm_n"][c].reshape(NL, 8, 64, 1)], -1))
            m["st_m"] = np.ascontiguousarray(np.broadcast_to(inp["state_mlstm_m"][c].reshape(NL, 1, 8), (NL, 128, 8)))
            m["ckT"] = np.ascontiguousarray(inp["cache_diff_k"][c].reshape(NL, 4, 256, 2, 32).transpose(0, 1, 3, 4, 2))
            m["cv"] = np.ascontiguousarray(inp["cache_diff_v"][c])
        else:
            s0 = (c - 2) * 8
            if s0 < 32:
                x = xp[s0:s0 + 8].reshape(T, D)
            else:
                x = np.zeros((T, D), np.float32)
            cond = inp["c_ctx"]
            m["cst"] = cst_p
            m["rope"], m["qmask"], m["kmask"] = att_p
            m["st_ret"] = zf(NL, 8, 64, 64)
            m["st_delta"] = zf(NL, 8, 64, 64)
            m["st_Cn"] = zf(NL, 8, 64, 65)
            m["st_m"] = zf(NL, 128, 8)
            m["ckT"] = zf(NL, 4, 2, 32, 256)
            m["cv"] = zf(NL, 4, 256, 64)
        m["xT"] = np.ascontiguousarray(x.T)
        m["cond"] = np.ascontiguousarray(cond.reshape(8, 128).T)
        in_maps.append(m)
    if "nc" not in _NC_CACHE:
        _NC_CACHE["nc"] = build(T, NL, "full")
    nc = _NC_CACHE["nc"]
    res = run_bass_kernel_spmd(nc, in_maps, core_ids=list(range(8)))
    R = res.results
    y_prompt = np.concatenate([R[c]["yT"].T.reshape(8, 256, D) for c in range(2, 6)], 0).astype(np.float32)
    y_sample = np.stack([R[c]["yT"].T for c in range(2)]).astype(np.float32)
    B = 32
    pc = range(2, 6)
    cat = lambda f: np.ascontiguousarray(np.concatenate([f(R[c]) for c in pc], 0)).astype(np.float32)
    new_ret = cat(lambda r: r["o_ret"].transpose(1, 0, 2, 3, 4).reshape(8, NL, 2, 4, 64, 64))
    new_delta = cat(lambda r: r["o_delta"].transpose(1, 0, 2, 3, 4).reshape(8, NL, 2, 4, 64, 64))
    new_C = cat(lambda r: r["o_Cn"][..., 0:64].transpose(1, 0, 2, 3, 4).reshape(8, NL, 2, 4, 64, 64))
    new_n = cat(lambda r: r["o_Cn"][..., 64].transpose(1, 0, 2, 3).reshape(8, NL, 2, 4, 64))
    new_m = cat(lambda r: r["o_m"].transpose(2, 0, 1).reshape(8, NL, 2, 4))
    new_k = cat(lambda r: r["o_k"].reshape(NL, 4, 8, 256, 64).transpose(2, 0, 1, 3, 4))
    new_v = cat(lambda r: r["o_v"].reshape(NL, 4, 8, 256, 64).transpose(2, 0, 1, 3, 4))
    return (y_prompt, y_sample, new_k, new_v, new_delta, new_C, new_n, new_m, new_ret)
```

```python
import numpy as np
import concourse.bass as bass
import concourse.mybir as mybir

F32 = mybir.dt.float32
BF16 = mybir.dt.bfloat16
AF = mybir.ActivationFunctionType
ALU = mybir.AluOpType
AX = mybir.AxisListType

PE, ACT, DVE, POOL, SP = "pe", "act", "dve", "pool", "sp"


class Tl:
    def __init__(self, ap, name=""):
        self.ap = ap
        self.name = name
        self.lw = None
        self.rd = []
        self.dram_out = False
        self.writers = []
        self.grp = None

    def __getitem__(self, idx):
        return Vw(self, self.ap[idx])

    def v(self):
        return Vw(self, self.ap)


class Vw:
    def __init__(self, t, ap):
        self.t = t
        self.ap = ap

    def __getitem__(self, idx):
        return Vw(self.t, self.ap[idx])

    def rr(self, s, **kw):
        return Vw(self.t, self.ap.rearrange(s, **kw))

    def bc(self, shape):
        return Vw(self.t, self.ap.broadcast_to(shape))

    def bitcast(self, dt):
        return Vw(self.t, self.ap.bitcast(dt))


class Op:
    __slots__ = ("eng", "fn", "deps", "id", "is_dma", "sig", "cnt", "dsem", "dval", "dprev")

    def __init__(self, eng, fn, deps, is_dma):
        self.eng = eng
        self.fn = fn
        self.deps = deps
        self.is_dma = is_dma
        self.sig = False
        self.cnt = 0
        self.dsem = None
        self.dval = 0
        self.dprev = 0


class Prog:
    def __init__(self, nc, n_dma_sems=20, same_engine_sync=True):
        self.nc = nc
        self.ops = []
        self.same_engine_sync = same_engine_sync
        self.esem = {}
        self.n_dma_sems = n_dma_sems
        self.dma_rr = {}
        self.final_tokens = []
        self.out_tiles = []

    def sb(self, name, shape, dt=F32):
        h = self.nc.alloc_sbuf_tensor(name, list(shape), dt)
        return Tl(h.ap(), name)

    def ps(self, name, shape, dt=F32):
        h = self.nc.alloc_psum_tensor(name, list(shape), dt)
        return Tl(h.ap(), name)

    def wrap(self, ap, name=""):
        return Tl(ap, name)

    def carve(self, arena_tl, specs, precise=True):
        out = {}
        if getattr(arena_tl, "conservative", False):
            precise = False
        if not hasattr(arena_tl, "members"):
            arena_tl.members = []
        for name, off, shape, dt in specs:
            nb = 4 if dt == F32 else 2
            n = 1
            for d in shape[1:]:
                n *= d
            a = arena_tl.ap[0:shape[0], off // 2: off // 2 + n * nb // 2]
            if dt == F32:
                a = a.bitcast(F32)
            if len(shape) == 3:
                a = a.rearrange("p (a b) -> p a b", a=shape[1])
            elif len(shape) == 4:
                a = a.rearrange("p (a b c) -> p a b c", a=shape[1], b=shape[2])
            t = Tl(a, name)
            lo, hi = off, off + n * nb
            t.grp = [t]
            for (ot, olo, ohi) in arena_tl.members:
                if (not precise) or (lo < ohi and olo < hi):
                    t.grp.append(ot)
                    ot.grp.append(t)
            arena_tl.members.append((t, lo, hi))
            out[name] = t
        return out

    def _rec(self, eng, fn, outs, ins, is_dma=False):
        deps = set()
        for v in ins:
            if v is None:
                continue
            for t in (v.t.grp or [v.t]):
                if t.lw is not None:
                    deps.add(t.lw)
        for v in outs:
            if v.t.dram_out:
                continue
            for t in (v.t.grp or [v.t]):
                if t.lw is not None:
                    deps.add(t.lw)
                deps.update(t.rd)
        op = Op(eng, fn, deps, is_dma)
        op.id = len(self.ops)
        self.ops.append(op)
        for v in ins:
            if v is None:
                continue
            v.t.rd.append(op.id)
        for v in outs:
            if v.t.dram_out:
                v.t.writers.append(op.id)
                continue
            v.t.lw = op.id
            v.t.rd = []
        return op

    def op(self, eng, fn, outs, ins):
        return self._rec(eng, fn, outs, ins)

    def dma(self, q, out, in_, **kw):
        o, i = out.ap, in_.ap
        return self._rec(q, lambda e: e.dma_start(out=o, in_=i, **kw), [out], [in_], is_dma=True)

    def mm(self, out, lhsT, rhs, start=True, stop=True, **kw):
        o, l, r = out.ap, lhsT.ap, rhs.ap
        return self._rec(PE, lambda e: e.matmul(o, l, r, start=start, stop=stop, **kw), [out], [lhsT, rhs])

    def tr(self, out, in_, ident):
        o, i, d = out.ap, in_.ap, ident.ap
        return self._rec(PE, lambda e: e.transpose(o, i, d), [out], [in_, ident])

    def act(self, out, in_, func, bias=None, scale=None, accum=None, eng=ACT):
        o, i = out.ap, in_.ap
        kw = {}
        ins = [in_]
        outs = [out]
        if bias is not None:
            if isinstance(bias, Vw):
                kw["bias"] = bias.ap
                ins.append(bias)
            else:
                kw["bias"] = bias
        if scale is not None:
            if isinstance(scale, Vw):
                kw["scale"] = scale.ap
                ins.append(scale)
            else:
                kw["scale"] = scale
        if accum is not None:
            kw["accum_out"] = accum.ap
            outs.append(accum)
        return self._rec(eng, lambda e: e.activation(o, i, func, **kw), outs, ins)

    def tt(self, out, a, b, op, eng=DVE):
        o, x, y = out.ap, a.ap, b.ap
        return self._rec(eng, lambda e: e.tensor_tensor(o, x, y, op), [out], [a, b])

    def ts(self, out, a, s1, op0, s2=None, op1=None, eng=DVE, accum=None):
        o, x = out.ap, a.ap
        ins = [a]
        outs = [out]
        a1 = s1.ap if isinstance(s1, Vw) else s1
        a2 = s2.ap if isinstance(s2, Vw) else s2
        if isinstance(s1, Vw):
            ins.append(s1)
        if isinstance(s2, Vw):
            ins.append(s2)
        kw = {}
        if op1 is not None:
            kw["op1"] = op1
        if accum is not None:
            kw["accum_out"] = accum.ap
            outs.append(accum)
        return self._rec(eng, lambda e: e.tensor_scalar(o, x, a1, a2, op0, **kw), outs, ins)

    def stt(self, out, a, s, b, op0, op1, eng=DVE):
        o, x, y = out.ap, a.ap, b.ap
        ins = [a, b]
        sa = s.ap if isinstance(s, Vw) else s
        if isinstance(s, Vw):
            ins.append(s)
        return self._rec(eng, lambda e: e.scalar_tensor_tensor(o, x, sa, y, op0, op1), [out], ins)

    def copy(self, out, in_, eng=DVE):
        o, i = out.ap, in_.ap
        if eng == ACT:
            return self._rec(eng, lambda e: e.copy(o, i), [out], [in_])
        return self._rec(eng, lambda e: e.tensor_copy(o, i), [out], [in_])

    def memset(self, out, val, eng=DVE):
        o = out.ap
        return self._rec(eng, lambda e: e.memset(o, val), [out], [])

    def reduce(self, out, in_, op, axis=None, eng=DVE):
        o, i = out.ap, in_.ap
        ax = axis if axis is not None else AX.X
        return self._rec(eng, lambda e: e.tensor_reduce(o, i, ax, op), [out], [in_])

    def recip(self, out, in_):
        o, i = out.ap, in_.ap
        return self._rec(DVE, lambda e: e.reciprocal(o, i), [out], [in_])

    def scan(self, out, d0, d1, init, op0, op1, eng=DVE):
        o, a, b = out.ap, d0.ap, d1.ap
        ins = [d0, d1]
        iv = init.ap if isinstance(init, Vw) else init
        if isinstance(init, Vw):
            ins.append(init)
        return self._rec(eng, lambda e: e.tensor_tensor_scan(o, a, b, iv, op0, op1), [out], ins)

    def mark_output(self, tl):
        tl.dram_out = True
        self.out_tiles.append(tl)

    def emit(self):
        nc = self.nc
        ops = self.ops
        for op in ops:
            for d in op.deps:
                p = ops[d]
                if p.is_dma:
                    continue
                if p.eng == op.eng and not op.is_dma:
                    if p.eng == PE:
                        continue
                    if not self.same_engine_sync:
                        continue
                p.sig = True
        final_deps = set()
        for tl in self.out_tiles:
            final_deps.update(tl.writers)
        for d in final_deps:
            if not ops[d].is_dma:
                ops[d].sig = True
        engs = [PE, ACT, DVE, POOL, SP]
        cnt = {e: 0 for e in engs}
        for op in ops:
            if op.is_dma:
                continue
            if op.sig:
                cnt[op.eng] += 1
                op.cnt = cnt[op.eng]
        from contextlib import ExitStack

        with ExitStack() as st:
            for e in engs:
                self.esem[e] = st.enter_context(nc.semaphore("es_" + e))
            qs = sorted(set(op.eng for op in ops if op.is_dma))
            pools = {}
            for q in qs:
                pools[q] = [st.enter_context(nc.semaphore("ds_%s_%d" % (q, i))) for i in range(self.n_dma_sems)]
            rr = {q: 0 for q in qs}
            tot = {}
            for op in ops:
                if not op.is_dma:
                    continue
                pool = pools[op.eng]
                s = pool[rr[op.eng] % len(pool)]
                rr[op.eng] += 1
                key = (op.eng, rr[op.eng] % len(pool) if False else id(s))
                prev = tot.get(id(s), 0)
                op.dsem = s
                op.dprev = prev
                op.dval = prev + 16
                tot[id(s)] = op.dval
            block = st.enter_context(nc.Block())
            by_eng = {e: [op for op in ops if op.eng == e] for e in engs}

            def run(eng_name, e):
                waited = {}

                def wait(sem, val):
                    k = id(sem)
                    if waited.get(k, 0) >= val:
                        return
                    waited[k] = val
                    e.wait_ge(sem, val)

                for op in by_eng[eng_name]:
                    for d in sorted(op.deps):
                        p = ops[d]
                        if p.is_dma:
                            wait(p.dsem, p.dval)
                        else:
                            if p.eng == eng_name and not op.is_dma:
                                if p.eng == PE or not self.same_engine_sync:
                                    continue
                            wait(self.esem[p.eng], p.cnt)
                    if op.is_dma:
                        if op.dprev > 0:
                            wait(op.dsem, op.dprev)
                        ins = op.fn(e)
                        ins.then_inc(op.dsem, 16)
                    else:
                        ins = op.fn(e)
                        if op.sig:
                            ins.then_inc(self.esem[eng_name], 1)
                if eng_name == SP:
                    for d in sorted(final_deps):
                        p = ops[d]
                        if p.is_dma:
                            wait(p.dsem, p.dval)
                        else:
                            wait(self.esem[p.eng], p.cnt)

            @block.tensor
            def _(e):
                run(PE, e)

            @block.scalar
            def _(e):
                run(ACT, e)

            @block.vector
            def _(e):
                run(DVE, e)

            @block.gpsimd
            def _(e):
                run(POOL, e)

            @block.sync
            def _(e):
                run(SP, e)

from concourse.bass_utils import run_bass_kernel_spmd

D = 1024
NK = 8
FF = 2816
NF = 22
NMOD = 9
EPS = 1e-6
WIN_BLOCKS = 45
BLK = dict(a_q=0, a_k=1, a_v=2, a_z=3, a_g=4, b_q=5, b_k=6, b_v=7, c_q=8, c_k=9, c_v=10, c_o=11, c_g=12,
           d_q=13, d_k=14, d_v=15, d_g=16, merge=17, b_tm=33, c_tm=37, d_tm=41)
CST_W = 1216
LP_W = 544
C_ID, C_NEGF, C_NEGB, C_RELF, C_RELB, C_POS, C_KEEP = 0, 128, 256, 384, 512, 640, 644
C_PFLAG = 676
C_ONES, C_TRIF, C_TRIB, C_NOTI = 704, 832, 960, 1088
L_RETL, L_RETG, L_MLI, L_MLF, L_MLG, L_ISF, L_DNG, L_DNA, L_DNB, L_CONV, L_QNG, L_KNG, L_DAG, L_LAM = 0, 8, 72, 80, 88, 152, 160, 224, 232, 240, 276, 308, 340, 404
BIGNEG = -30000.0


class Ring:
    def __init__(self, tiles):
        self.tiles = tiles
        self.i = 0

    def next(self):
        t = self.tiles[self.i % len(self.tiles)]
        self.i += 1
        return t


def build(T, NL, stage="full", MIXSEL=(0, 1, 2, 3)):
    NTT = T // 512
    NH = max(1, NTT // 2)
    TPH = NTT // NH
    nc = bass.Bass("TRN2", target_bir_lowering=False)
    P = Prog(nc)

    def din(name, shape):
        return P.wrap(nc.dram_tensor(name, list(shape), F32, kind="ExternalInput").ap(), name)

    def dout(name, shape):
        t = P.wrap(nc.dram_tensor(name, list(shape), F32, kind="ExternalOutput").ap(), name)
        P.mark_output(t)
        return t

    x_in = din("xT", [D, T])
    cond_in = din("cond", [128, NK])
    wada_in = din("wada", [NL * 36, 128, NK * 256])
    bada_in = din("bada", [NL, 128, 72])
    ng_in = din("ng", [NL, 128, 24])
    wgu_in = din("wgu", [NL * 2 * NF, 128, NK * 256])
    wd_in = din("wd", [NL * 2 * NK, 128, NF * 128])
    y_out = dout("yT", [D, T])
    NCH = T // 128
    win_in = din("win", [NL * WIN_BLOCKS, 128, NK * 256])
    wbr_in = din("wbr", [NL * 4, 128, 2 * 1024])
    wout_in = din("wout", [NL * 4, 128, NK * 256])
    cst_in = din("cst", [128, CST_W])
    lp_in = din("lpar", [NL, 128, LP_W])
    st_ret_in = din("st_ret", [NL, 8, 64, 64])
    o_ret = dout("o_ret", [NL, NCH // 2, 8, 64, 64])
    NSEG = NCH // 2
    sel_in = din("sel", [8, 520])
    st_C_in = din("st_Cn", [NL, 8, 64, 65])
    st_dl_in = din("st_delta", [NL, 8, 64, 64])
    ckT_in = din("ckT", [NL, 4, 2, 32, 256])
    cv_in = din("cv", [NL, 4, 256, 64])
    rope_in = din("rope", [128, 2, NCH * 16])
    qmask_in = din("qmask", [9, T])
    kmask_in = din("kmask", [9, T + 256])
    o_k = dout("o_k", [NL, 4, T, 64])
    o_v = dout("o_v", [NL, 4, T, 64])
    o_dl = dout("o_delta", [NL, NSEG, 8, 64, 64])
    st_m_in = din("st_m", [NL, 128, 8])
    o_C = dout("o_Cn", [NL, NSEG, 8, 64, 65])
    o_m = dout("o_m", [NL, 8, NSEG])

    xT = [P.sb("xT%d" % t, [128, NK, 512]) for t in range(NTT)]
    hT = [P.sb("hT%d" % t, [128, NK, 512], BF16) for t in range(NTT)]
    ARENA_B = 2 * NF * 512 * TPH + 12 * 1024
    arena = P.sb("arena", [128, ARENA_B // 2], BF16)
    import os
    arena.conservative = os.environ.get("A1_CONS", "0") == "1"
    cv = P.carve(arena, [("aT%d" % t, t * NF * 1024, [128, NF, 512], BF16) for t in range(TPH)])
    aT = [cv["aT%d" % t] for t in range(TPH)]
    cva = P.carve(arena, [("wAda%d" % i, i * NK * 1024, [128, NK, 256], F32) for i in range(2)])
    TB = T * 4
    o = 0
    mspec = []
    for nm, shp in [("QT", [64, T]), ("KT", [64, T]), ("Ktm", [128, NCH, 64]), ("Vtm", [128, NCH, 65]),
                    ("Gtm", [128, NCH, 64]), ("Od0", [128, NCH, 65]), ("Od1", [128, NCH, 65]),
                    ("Opair", [128, NCH, 128]), ("sqo", [128, NCH, 64])]:
        n = 4
        for d in shp[1:]:
            n *= d
        mspec.append((nm, o, shp, F32))
        if nm in ("QT", "KT"):
            mspec.append((nm + "b", o, [64, T], BF16))
        if nm == "Vtm":
            mspec.append(("Vtb", o, [128, NCH, 66], BF16))
        if nm == "Gtm":
            mspec.append(("raw", o, [64, T + 2], F32))
            for p_ in range(2):
                mspec.append(("wtW_%d" % p_, o + p_ * 1024, [128, 2, 128], F32))
                mspec.append(("wTW_%d" % p_, o + 2048 + p_ * 1024, [64, 2, 128], F32))
        if nm == "sqo":
            for i in range(2):
                mspec.append(("dg%d" % i, o + i * 1024, [128, 128], F32))
                mspec.append(("wt%d" % i, o + i * 1024 + 512, [128, 128], F32))
            for i, xn in enumerate(["Xa", "Xb", "XTa", "XTb", "Yd", "khT"]):
                mspec.append((xn, o + 1024 + i * 512, [128, 128], F32))
            mspec.append(("atW", o, [128, 2, 128], F32))
            n = max(n, 4096)
        o += n
    for i in range(2):
        mspec.append(("attn%d" % i, o, [128, 128], F32))
        mspec.append(("atb%d" % i, o, [128, 128], BF16)); o += 512
        mspec.append(("kw%d" % i, o, [128, 64], F32))
        mspec.append(("kwb%d" % i, o, [128, 64], BF16)); o += 256
        mspec.append(("o2_%d" % i, o, [128, 65], F32)); o += 260
        mspec.append(("o3_%d" % i, o, [128, 65], F32)); o += 260
    mspec.append(("WTd", o, [128, 8, 128], F32))
    mspec.append(("GT", o, [128, 10, NCH * 8], F32))
    o += max(4096, 10 * NCH * 32)
    assert o <= ARENA_B, (o, ARENA_B)
    mx = P.carve(arena, mspec)
    NKT = NCH + 2
    offs_ = dict((nm_, off_) for nm_, off_, _, _ in mspec)
    bo = 0
    bspec = []
    for nm, shp, dt_ in [("qaug0", [128, T], BF16), ("qaug1", [128, T], BF16), ("kaug", [128, T + 256], BF16),
                         ("Vext", [128, NKT, 65], BF16), ("Qb", [128, NCH, 64], F32),
                         ("Kb", [128, NCH, 64], F32), ("Ob", [128, NCH, 64], F32), ("octmp", [128, 4, 65], F32),
                         ("oc2", [128, 4, 65], F32), ("ocT", [65, 512], F32)]:
        n = 4 if dt_ == F32 else 2
        for d in shp[1:]:
            n *= d
        n = (n + 3) // 4 * 4
        bspec.append((nm, bo, shp, dt_))
        if nm == "Ob":
            bspec.append(("Vf", bo, [128, NCH, 64], F32))
        bo += n
    for i in range(4):
        bspec.append(("PT%d" % i, bo, [128, 512], BF16))
        bo += 1024
    assert (1 not in MIXSEL) or bo <= offs_["Opair"], (bo, offs_["Opair"])
    so = offs_["sqo"]
    bspec.append(("ropet", so, [128, 2, NCH * 16], F32))
    to = offs_["attn0"]
    for i in range(4):
        bspec.append(("rt%d" % i, to + i * NCH * 64, [128, NCH * 16], F32))
    assert to + 4 * NCH * 64 <= ARENA_B
    bx = P.carve(arena, bspec)
    zsp = [("z", 0, [128, NK, 512], F32), ("zb", NK * 2048, [128, NK, 512], BF16),
           ("brm", NK * 2048 + NK * 1024, [128, 8, 512], BF16), ("gsb0", NK * 2048 + 2 * NK * 1024, [128, 512], F32),
           ("gsb1", NK * 2048 + 2 * NK * 1024 + 2048, [128, 512], F32), ("gp0", NK * 2048 + 2 * NK * 1024 + 4096, [128, 512], F32)]
    mz = P.carve(arena, zsp)
    wA = Ring([P.sb("wA%d" % i, [128, NK, 256], BF16) for i in range(3)])
    A2_B = 2 * NF * 256 + 2 * 2048 + 2 * 1024
    arena2 = P.sb("arena2", [128, A2_B // 2], BF16)
    import os
    arena2.conservative = os.environ.get("A2_CONS", "0") == "1"
    c2 = P.carve(arena2, [("wD%d" % i, i * NF * 256, [128, NF, 128], BF16) for i in range(2)]
                 + [("rstd%d" % i, 2 * NF * 256 + i * 2048, [128, 512], F32) for i in range(2)]
                 + [("sq%d" % i, 2 * NF * 256 + 4096 + i * 1024, [128, 512], BF16) for i in range(2)])
    wD = Ring([c2["wD%d" % i] for i in range(2)])
    a2spec = []
    o2_ = 0
    for p_ in range(2):
        for nm_, sz_, shp_ in [("dg", 1024, [128, 2, 128]), ("XTa", 1024, [128, 2, 128]), ("XTb", 1024, [128, 2, 128]),
                               ("Xa", 1024, [128, 2, 128]), ("Xb", 1024, [128, 2, 128]), ("Y", 1024, [128, 2, 128]),
                               ("khT", 1024, [64, 2, 128]), ("kh", 512, [128, 2, 64]), ("u", 512, [128, 2, 64])]:
            a2spec.append(("%s_%d" % (nm_, p_), o2_, shp_, F32))
            o2_ += sz_
    a2spec.append(("vnew", o2_, [128, 2, 64], F32))
    o2_ += 512
    assert o2_ <= A2_B, (o2_, A2_B)
    ax = P.carve(arena2, a2spec)
    wBR = Ring([P.sb("wBR%d" % i, [128, 2, 1024], BF16) for i in range(1)])
    wAda = Ring([cva["wAda%d" % i] for i in range(2)])
    cst = P.sb("cst_sb", [128, CST_W])
    lpar1 = P.sb("lpar1", [128, LP_W])
    lpar = [lpar1 for l in range(NL)]
    sel = P.sb("sel_sb", [8, 520])
    rows = P.sb("rows_sb", [8, 256])
    embc = P.sb("embc", [64, 64])
    rows_b = P.sb("rows_b", [128, 32]).v()
    em0 = P.sb("em0", [128, 8])
    brst = Ring([P.sb("brst%d" % i, [128, T], BF16) for i in range(1)])
    smal = P.sb("smal", [128, 64])
    Sst = [P.sb("Sst%d" % i, [64, 65]) for i in range(2)]
    Sbf = [P.sb("Sbf%d" % i, [64, 66], BF16) for i in range(2)]
    Sfin = Ring([P.sb("Sfin%d" % i, [64, 65]) for i in range(2)])
    br_dram = P.wrap(nc.dram_tensor("br_scratch", [4, 2, 128, T], BF16,
                                    kind=("ExternalOutput" if stage == "mix" else "Internal")).ap(), "br_scratch")
    ps = Ring([P.ps("pb%d" % i, [128, 512]) for i in range(6)])
    pacc_r = Ring([P.ps("pacc%d" % i, [128, 512]) for i in range(2)])
    sq = Ring([c2["sq%d" % i] for i in range(2)])
    tmpf = Ring([P.sb("tmpf%d" % i, [128, 512]) for i in range(2)])
    rstd_r = Ring([c2["rstd%d" % i] for i in range(2)])
    ones_bf = P.sb("ones_bf", [128, 128], BF16)
    scond = P.sb("scond", [128, NK])
    bada = [P.sb("bada%d" % l, [128, 72]) for l in range(NL)]
    ng = [P.sb("ng%d" % l, [128, 24]) for l in range(NL)]
    mod = [P.sb("mod%d" % l, [128, 72]) for l in range(NL)]
    scale = [P.sb("scale%d" % l, [128, 24]) for l in range(NL)]
    gate = [P.sb("gate%d" % l, [128, 24]) for l in range(NL)]

    P.memset(ones_bf.v(), 1.0)

    xv = x_in.v().rr("(kc p) t -> p kc t", p=128)
    for t in range(NTT):
        P.dma(SP, xT[t].v(), xv[:, :, t * 512:(t + 1) * 512])
    P.dma(SP, scond.v(), cond_in.v())
    P.dma(SP, cst.v(), cst_in.v())
    P.dma(SP, sel.v(), sel_in.v())
    for l in range(NL):
        P.dma(SP, bada[l].v(), bada_in[l])
        P.dma(SP, ng[l].v(), ng_in[l])
    P.act(scond.v(), scond.v(), AF.Silu)

    for l in range(NL):
        pm = pacc_r.next()
        for blk in range(36):
            wb = wAda.next()
            P.dma(SP, wb.v(), wada_in[l * 36 + blk].rr("p (k c) -> p k c", k=NK))
            pr = ps.next()
            for kc in range(NK):
                P.mm(pr[0:1, 0:256], scond[:, kc:kc + 1], wb[:, kc, :], start=(kc == 0), stop=(kc == NK - 1))
            rb = rows[0:1, 0:256]
            P.copy(rb, pr[0:1, 0:256], eng=ACT)
            for cc in range(2):
                col = blk * 2 + cc
                P.mm(pm[:, col:col + 1], rb[0:1, cc * 128:(cc + 1) * 128], cst[0:1, C_ONES:C_ONES + 1])
        P.tt(mod[l].v(), pm[:, 0:72], bada[l].v(), ALU.add)
        for i in range(3):
            P.stt(scale[l][:, i * 8:(i + 1) * 8], mod[l][:, (3 * i + 1) * 8:(3 * i + 2) * 8], 1.0,
                  ng[l][:, i * 8:(i + 1) * 8], ALU.add, ALU.mult)
            P.ts(gate[l][:, i * 8:(i + 1) * 8], mod[l][:, (3 * i + 2) * 8:(3 * i + 3) * 8],
                 0.5 if i != 1 else 1.0, ALU.mult)

    def norm_mod(l, i, t):
        pn = ps.next()
        for kc in range(NK):
            s = sq.next()
            P.act(s.v(), xT[t][:, kc, :], AF.Square)
            P.mm(pn.v(), ones_bf.v(), s.v(), start=(kc == 0), stop=(kc == NK - 1))
        r = rstd_r.next()
        P.ts(r.v(), pn.v(), 1.0 / D, ALU.mult, EPS, ALU.add)
        P.act(r.v(), r.v(), AF.Sqrt)
        P.recip(r.v(), r.v())
        for kc in range(NK):
            tm = tmpf.next()
            P.stt(tm.v(), xT[t][:, kc, :], scale[l][:, i * 8 + kc:i * 8 + kc + 1], r.v(), ALU.mult, ALU.mult)
            P.act(hT[t][:, kc, :], tm.v(), AF.Identity, bias=mod[l][:, 3 * i * 8 + kc:3 * i * 8 + kc + 1])

    def ffn(l, i):
        ni = 0 if i == 0 else 2
        for h in range(NH):
            tts = list(range(h * TPH, (h + 1) * TPH))
            for t in tts:
                norm_mod(l, ni, t)
            for f in range(NF):
                wb = wA.next()
                P.dma(POOL, wb.v(), wgu_in[(l * 2 + i) * NF + f].rr("p (k c) -> p k c", k=NK))
                for j, t in enumerate(tts):
                    pg = ps.next()
                    pu = ps.next()
                    for kc in range(NK):
                        P.mm(pg.v(), wb[:, kc, 0:128], hT[t][:, kc, :], start=(kc == 0), stop=(kc == NK - 1))
                    for kc in range(NK):
                        P.mm(pu.v(), wb[:, kc, 128:256], hT[t][:, kc, :], start=(kc == 0), stop=(kc == NK - 1))
                    sg = tmpf.next()
                    P.act(sg.v(), pg.v(), AF.Silu)
                    P.tt(aT[j][:, f, :], sg.v(), pu.v(), ALU.mult)
            for dc in range(NK):
                wdb = wD.next()
                P.dma(POOL, wdb.v(), wd_in[(l * 2 + i) * NK + dc].rr("p (f c) -> p f c", f=NF))
                for j, t in enumerate(tts):
                    py = ps.next()
                    for fc in range(NF):
                        P.mm(py.v(), wdb[:, fc, :], aT[j][:, fc, :], start=(fc == 0), stop=(fc == NF - 1))
                    P.stt(xT[t][:, dc, :], py.v(), gate[l][:, ni * 8 + dc:ni * 8 + dc + 1], xT[t][:, dc, :],
                          ALU.mult, ALU.add)


    ident = cst[:, C_ID:C_ID + 128]

    def load_blk(l, b):
        wb = wA.next()
        P.dma(POOL, wb.v(), win_in[l * WIN_BLOCKS + b].rr("p (k c) -> p k c", k=NK))
        return wb

    def proj_fm(wb, c0, M, dst, scale=None):
        for t in range(NTT):
            pp = ps.next()
            for kc in range(NK):
                P.mm(pp[0:M, :], wb[:, kc, c0:c0 + M], hT[t][:, kc, :], start=(kc == 0), stop=(kc == NK - 1))
            if scale is None:
                P.copy(dst[0:M, t * 512:(t + 1) * 512], pp[0:M, :], eng=ACT)
            else:
                P.act(dst[0:M, t * 512:(t + 1) * 512], pp[0:M, :], AF.Copy, scale=scale)

    def proj_tm(wb, c0, N, dst, ncol, scale=None):
        for g in range(NCH // 4):
            pp = ps.next()
            for q in range(4):
                n = g * 4 + q
                t, off = n // 4, (n % 4) * 128
                for kc in range(NK):
                    P.mm(pp[:, q * 64:q * 64 + N], hT[t][:, kc, off:off + 128], wb[:, kc, c0:c0 + N],
                         start=(kc == 0), stop=(kc == NK - 1))
            src = pp[:, 0:256].rr("p (q c) -> p q c", q=4)[:, :, 0:N]
            if scale is None:
                P.copy(dst[:, g * 4:(g + 1) * 4, 0:N], src, eng=ACT)
            else:
                P.act(dst[:, g * 4:(g + 1) * 4, 0:N], src, AF.Copy, scale=scale)

    def proj_tm3(wb, dsts):
        for g in range(NCH // 2):
            pp = ps.next()
            for q in range(2):
                n = g * 2 + q
                t, off = n // 4, (n % 4) * 128
                for kc in range(NK):
                    P.mm(pp[:, q * 192:(q + 1) * 192], hT[t][:, kc, off:off + 128], wb[:, kc, 0:192],
                         start=(kc == 0), stop=(kc == NK - 1))
            for j, (dst, scale) in enumerate(dsts):
                src = pp[:, 0:384].rr("p (q c) -> p q c", q=2)[:, :, j * 64:(j + 1) * 64]
                if scale is None:
                    P.copy(dst[:, g * 2:(g + 1) * 2, 0:64], src, eng=ACT)
                else:
                    P.act(dst[:, g * 2:(g + 1) * 2, 0:64], src, AF.Copy, scale=scale)

    rr2 = [0]

    def run_rr(gens, delays=None):
        gens = list(gens)
        delays = list(delays) if delays is not None else [0] * len(gens)
        live = list(range(len(gens)))
        while live:
            nxt = []
            for i in live:
                if delays[i] > 0:
                    delays[i] -= 1
                    nxt.append(i)
                    continue
                try:
                    next(gens[i])
                    nxt.append(i)
                except StopIteration:
                    pass
            live = nxt

    def lin_chunk_g(n, E, WT, qs, ks, lam, S, Vx, o_out, keepcol, sfin_dst, fix=None, bf=False):
        i2 = rr2[0] % 2
        rr2[0] += 1
        if fix is not None:
            i2 = fix
        sfx = "b" if bf else ""
        QTn = mx["QT" + sfx][:, n * 128:(n + 1) * 128]
        KTn = mx["KT" + sfx][:, n * 128:(n + 1) * 128]
        p1 = ps.next()
        P.mm(p1[:, 0:128], KTn, QTn)
        at = mx[("atb%d" if bf else "attn%d") % i2]
        P.tt(at.v(), p1[:, 0:128], WT, ALU.mult)
        kw = mx[("kwb%d" if bf else "kw%d") % i2]
        P.ts(kw.v(), mx["Ktm"][:, n, :], ks, ALU.mult)
        Sr = S[:, 0:E]
        if bf:
            Vx = mx["Vtb"][:, n, 0:E]
            P.copy(Sbf[i2][:, 0:E], S[:, 0:E], eng=ACT)
            Sr = Sbf[i2][:, 0:E]
        yield
        p2 = ps.next()
        P.mm(p2[:, 0:E], at.v(), Vx)
        P.mm(p2[:, 128:128 + E], QTn, Sr)
        o2 = mx["o2_%d" % i2]
        P.copy(o2[:, 0:E], p2[:, 0:E], eng=ACT)
        P.stt(o_out, p2[:, 128:128 + E], qs, o2[:, 0:E], ALU.mult, ALU.add)
        yield
        p3 = ps.next()
        P.mm(p3[0:64, 0:E], kw.v(), Vx)
        P.stt(S[:, 0:E], S[:, 0:E], lam, p3[0:64, 0:E], ALU.mult, ALU.add)
        if sfin_dst is not None:
            sf = Sfin.next()
            P.copy(sf[:, 0:E], S[:, 0:E])
            sfin_dst(sf)
        P.ts(S[:, 0:E], S[:, 0:E], keepcol, ALU.mult)
        yield

    def lin_chunk(*a, **k):
        for _ in lin_chunk_g(*a, **k):
            pass

    def branch_out(m, pair):
        st = brst.next()
        for g in range(NCH // 4):
            pp = ps.next()
            for q in range(4):
                n = g * 4 + q
                P.tr(pp[:, q * 128:(q + 1) * 128], mx["Opair"][:, n, :], ident)
            P.copy(st[:, g * 512:(g + 1) * 512], pp.v())
        P.dma(SP, br_dram[m, pair], st.v())

    def mixer_D(l):
        lgt = smal[:, 0:8]
        qsd = smal[:, 8:16]
        ksd = smal[:, 16:24]
        lamd = smal[:, 24:32]
        P.act(lgt, lpar[l][:, 0:8], AF.Exp, scale=-1.0)
        P.act(lgt, lgt, AF.Ln, bias=1.0)
        P.ts(lgt, lgt, -1.0, ALU.mult)
        for r in range(8):
            d = r // 4
            rel = cst[:, (C_RELF if d == 0 else C_RELB):(C_RELF if d == 0 else C_RELB) + 128]
            P.act(mx["WTd"][:, r, :], rel, AF.Exp, scale=lgt[:, r:r + 1])
            P.act(qsd[:, r:r + 1], cst[:, C_POS + d:C_POS + d + 1], AF.Exp, scale=lgt[:, r:r + 1])
            P.act(ksd[:, r:r + 1], cst[:, C_POS + 2 + d:C_POS + 3 + d], AF.Exp, scale=lgt[:, r:r + 1])
        P.act(lamd, lgt, AF.Exp, scale=128.0)
        for h in range(4):
            wb = load_blk(l, BLK["d_q"])
            proj_fm(wb, h * 64, 64, mx["QTb"])
            wb = load_blk(l, BLK["d_k"])
            proj_fm(wb, h * 64, 64, mx["KTb"], scale=0.125)
            wb = load_blk(l, BLK["d_tm"] + h)
            proj_tm3(wb, [(mx["Ktm"], 0.125), (mx["Vtb"], None), (mx["Gtm"], None)])
            def chain_D(d):
                r = d * 4 + h
                S = Sst[d]
                P.dma(SP, S[:, 0:64], st_ret_in[l, r])
                od = mx["Od%d" % d]
                order = range(NCH) if d == 0 else range(NCH - 1, -1, -1)
                for n in order:
                    isend = (n % 2 == 1) if d == 0 else (n % 2 == 0)
                    dst = None
                    if isend:
                        seg = n // 2
                        dst = (lambda sf, seg=seg, r=r: P.dma(SP, o_ret[l, seg, r], sf[:, 0:64]))
                    kc = cst[0:64, C_KEEP + d * NCH + n:C_KEEP + d * NCH + n + 1]
                    yield from lin_chunk_g(n, 64, mx["WTd"][:, r, :], qsd[:, r:r + 1], ksd[:, r:r + 1],
                                           lamd[0:64, r:r + 1], S, None, od[:, n, 0:64], kc, dst, fix=d, bf=True)
            run_rr([chain_D(0), chain_D(1)])
            o0 = mx["Od0"][:, :, 0:64]
            P.tt(o0, o0, mx["Od1"][:, :, 0:64], ALU.add)
            post_norm_gate(l, o0, 8, AF.Silu, h)
            if h % 2 == 1:
                branch_out(3, h // 2)

    def post_norm_gate(l, o, gcol, gfunc, h):
        sqo = mx["sqo"]
        P.tt(sqo.v(), o, o, ALU.mult)
        ss = smal[:, 32:32 + NCH]
        P.reduce(ss, sqo.v(), ALU.add)
        P.ts(ss, ss, 1.0 / 64, ALU.mult, EPS, ALU.add)
        P.act(ss, ss, AF.Sqrt)
        P.recip(ss, ss)
        P.tt(sqo.v(), o, ss.rr("p (n o) -> p n o", o=1).bc([128, NCH, 64]), ALU.mult)
        gn = lpar[l][:, gcol:gcol + 64].rr("p (o e) -> p o e", o=1).bc([128, NCH, 64])
        P.tt(sqo.v(), sqo.v(), gn, ALU.mult)
        P.act(mx["Gtm"].v(), mx["Gtm"].v(), gfunc)
        P.tt(mx["Opair"][:, :, (h % 2) * 64:(h % 2) * 64 + 64], sqo.v(), mx["Gtm"].v(), ALU.mult)

    def zero_branch(m):
        for pair in range(2):
            st = brst.next()
            P.memset(st.v(), 0.0)
            P.dma(SP, br_dram[m, pair], st.v())

    def merge(l):
        z, zb, brm = mz["z"], mz["zb"], mz["brm"]
        gs = Ring([mz["gsb0"], mz["gsb1"]])
        for t in range(NTT):
            for m in range(4):
                for pair in range(2):
                    P.dma(SP, brm[:, m * 2 + pair, :], br_dram[m, pair][:, t * 512:(t + 1) * 512])
            for m in range(4):
                wbr = wBR.next()
                P.dma(POOL, wbr.v(), wbr_in[l * 4 + m].rr("p (k c) -> p k c", k=2))
                wbv = wbr.v()
                for gb in range(4):
                    wg = load_blk(l, BLK["merge"] + m * 4 + gb)
                    for cc in range(2):
                        dc = gb * 2 + cc
                        pg = ps.next()
                        for kc in range(NK):
                            P.mm(pg.v(), wg[:, kc, cc * 128:(cc + 1) * 128], hT[t][:, kc, :],
                                 start=(kc == 0), stop=(kc == NK - 1))
                        g = gs.next()
                        P.act(g.v(), pg.v(), AF.Sigmoid)
                        pp = ps.next()
                        for kc in range(2):
                            P.mm(pp.v(), wbv[:, kc, dc * 128:(dc + 1) * 128], brm[:, m * 2 + kc, :],
                                 start=(kc == 0), stop=(kc == 1))
                        if m == 0:
                            P.tt(z[:, dc, :], g.v(), pp.v(), ALU.mult)
                        else:
                            P.tt(g.v(), g.v(), pp.v(), ALU.mult)
                            P.tt(z[:, dc, :], z[:, dc, :], g.v(), ALU.add)
            P.copy(zb.v(), z.v(), eng=ACT)
            for ob in range(4):
                wo = wA.next()
                P.dma(POOL, wo.v(), wout_in[l * 4 + ob].rr("p (k c) -> p k c", k=NK))
                for cc in range(2):
                    dc = ob * 2 + cc
                    py = ps.next()
                    for kc in range(NK):
                        P.mm(py.v(), wo[:, kc, cc * 128:(cc + 1) * 128], zb[:, kc, :], start=(kc == 0), stop=(kc == NK - 1))
                    P.stt(xT[t][:, dc, :], py.v(), gate[l][:, 8 + dc:8 + dc + 1], xT[t][:, dc, :], ALU.mult, ALU.add)


    ONESF = cst[:, C_ONES:C_ONES + 128]
    TRIF = cst[:, C_TRIF:C_TRIF + 128]
    TRIB = cst[:, C_TRIB:C_TRIB + 128]
    NEGM = [cst[:, C_NEGF:C_NEGF + 128], cst[:, C_NEGB:C_NEGB + 128]]

    def GTs(i, w=8):
        return mx["GT"][:, i, 0:NCH * 8].rr("p (n r) -> p n r", r=8) if w == 8 else None

    def bc8(col):
        return lpar1[:, col:col + 8].rr("p (o r) -> p o r", o=1).bc([128, NCH, 8])

    def cums(src, dP, dB, dT):
        for lhs, dst in ((TRIF, dP), (TRIB, dB), (ONESF, dT)):
            pp = ps.next()
            P.mm(pp[:, 0:NCH * 8], lhs, src)
            P.copy(dst, pp[:, 0:NCH * 8], eng=ACT)

    def blend_dir(dst, aF, aB):
        P.tt(dst, aF, aB, ALU.subtract)
        P.tt(dst, dst, bc8(L_ISF), ALU.mult)
        P.tt(dst, dst, aB, ALU.add)

    def rows_of(src_slot_view_fn, dst_rows, reduce_max):
        for g in range(NCH // 4):
            pp = ps.next()
            for q in range(4):
                n = g * 4 + q
                P.tr(pp[0:8, q * 128:(q + 1) * 128], src_slot_view_fn(n), ident)
            v = pp[0:8, :].rr("p (q t) -> p q t", q=4)
            if reduce_max:
                P.reduce(dst_rows[:, g * 4:(g + 1) * 4], v, ALU.max)
            else:
                P.copy(dst_rows[:, g * 4:(g + 1) * 4], v[:, :, 0])

    def mixer_C(l):
        G16 = mx["GT"][:, 0:2, :].rr("p a b -> p (a b)").rr("p (n c) -> p n c", c=16)
        IG, LF, bP, bb, TOT, QS, KS, LAM = [GTs(i) for i in range(2, 10)]
        flat = lambda i: mx["GT"][:, i, 0:NCH * 8]
        wb = load_blk(l, BLK["c_g"])
        proj_tm(wb, 0, 16, G16, 16)
        P.tt(IG, G16[:, :, 0:8], bc8(L_MLI), ALU.add)
        P.tt(LF, G16[:, :, 8:16], bc8(L_MLF), ALU.add)
        P.act(LF, LF, AF.Exp, scale=-1.0)
        P.act(LF, LF, AF.Ln, bias=1.0)
        P.ts(LF, LF, -1.0, ALU.mult)
        cums(flat(3), flat(4), flat(7), flat(6))
        blend_dir(bb, bP, QS)
        P.act(QS, bb, AF.Exp)
        P.tt(IG, IG, bb, ALU.subtract)
        P.tt(KS, IG, TOT, ALU.add)
        import os
        DBG = int(os.environ.get("DBG_C", "9"))
        if DBG < 2:
            return
        Xr = rows[:, 0:NCH]
        Tr = rows[:, 16:16 + NCH]
        rows_of(lambda n: KS[:, n, :], Xr, True)
        rows_of(lambda n: TOT[:, n, :], Tr, False)
        X2 = Xr.rr("p (s c) -> p s c", c=2)
        T2 = Tr.rr("p (s c) -> p s c", c=2)
        Bt = rows[:, 32:32 + NSEG]
        ta = rows[:, 48:48 + NSEG]
        mF = rows[:, 64:64 + NSEG]
        mB = rows[:, 80:80 + NSEG]
        mr = rows[:, 96:96 + NSEG]
        P.tt(Bt, T2[:, :, 0], T2[:, :, 1], ALU.add)
        P.tt(ta, X2[:, :, 0], T2[:, :, 1], ALU.add)
        P.tt(mF, Bt, X2[:, :, 1], ALU.max)
        P.tt(mF, mF, ta, ALU.max)
        P.tt(ta, X2[:, :, 1], T2[:, :, 0], ALU.add)
        P.tt(mB, Bt, X2[:, :, 0], ALU.max)
        P.tt(mB, mB, ta, ALU.max)
        P.tt(mr, mF, mB, ALU.subtract)
        P.ts(mr, mr, sel[:, 512:513], ALU.mult)
        P.tt(mr, mr, mB, ALU.add)
        if DBG < 3:
            return
        P.dma(SP, o_m[l], mr)
        if DBG < 4:
            return
        em = rows[:, 112:112 + NSEG]
        P.act(em, mr, AF.Exp, scale=-1.0)
        pp = ps.next()
        for r in range(8):
            P.mm(pp[0:64, r * NSEG:(r + 1) * NSEG], sel[:, r * 64:(r + 1) * 64], em)
        P.copy(embc[:, 0:8 * NSEG], pp[0:64, 0:8 * NSEG])
        P.act(KS, KS, AF.Exp)
        P.act(LAM, TOT, AF.Exp)
        P.dma(SP, em0.v(), st_m_in[l])
        P.act(em0.v(), em0.v(), AF.Exp)
        if DBG < 5:
            return
        for h in range(4):
            wb = load_blk(l, BLK["c_q"])
            proj_fm(wb, h * 64, 64, mx["QTb"], scale=0.125)
            wb = load_blk(l, BLK["c_k"])
            proj_fm(wb, h * 64, 64, mx["KTb"])
            wb = load_blk(l, BLK["c_tm"] + h)
            P.memset(mx["Vtb"][:, :, 64:65], 1.0)
            proj_tm3(wb, [(mx["Ktm"], None), (mx["Vtb"], None), (mx["Gtm"], None)])
            def chain_C(d):
                r = d * 4 + h
                S = Sst[d]
                P.dma(SP, S.v(), st_C_in[l, r])
                P.ts(S.v(), S.v(), em0[0:64, r:r + 1], ALU.mult)
                od = mx["Od%d" % d]
                order = range(NCH) if d == 0 else range(NCH - 1, -1, -1)
                for n in order:
                    dg = mx["dg%d" % d]
                    wt = mx["wt%d" % d]
                    P.ts(dg.v(), ident, bb[:, n, r:r + 1], ALU.mult)
                    pw = ps.next()
                    P.mm(pw[:, 0:128], ONESF, dg.v(), start=True, stop=False)
                    P.mm(pw[:, 0:128], ident, NEGM[d], start=False, stop=True)
                    P.act(wt.v(), pw[:, 0:128], AF.Exp, bias=IG[:, n, r:r + 1])
                    yield
                    isend = (n % 2 == 1) if d == 0 else (n % 2 == 0)
                    dst = None
                    if isend:
                        seg = n // 2

                        def dst(sf, seg=seg, r=r):
                            P.ts(sf.v(), sf.v(), embc[:, r * NSEG + seg:r * NSEG + seg + 1], ALU.mult)
                            P.dma(SP, o_C[l, seg, r], sf.v())
                    kc = cst[0:64, C_KEEP + d * NCH + n:C_KEEP + d * NCH + n + 1]
                    yield from lin_chunk_g(n, 65, wt.v(), QS[:, n, r:r + 1], KS[:, n, r:r + 1], LAM[0:64, n, r:r + 1],
                                           S, None, od[:, n, 0:65], kc, dst, fix=d, bf=True)
            run_rr([chain_C(0), chain_C(1)])
            for d in range(2):
                od = mx["Od%d" % d]
                den = od[:, :, 64:65]
                P.act(den, den, AF.Abs)
                P.ts(den, den, 1.0, ALU.max)
                P.recip(den, den)
                P.tt(od[:, :, 0:64], od[:, :, 0:64], den.bc([128, NCH, 64]), ALU.mult)
            o0 = mx["Od0"][:, :, 0:64]
            P.tt(o0, o0, mx["Od1"][:, :, 0:64], ALU.add)
            post_norm_gate(l, o0, L_MLG, AF.Sigmoid, h)
            if h % 2 == 1:
                branch_out(2, h // 2)


    NOTI = cst[:, C_NOTI:C_NOTI + 128]

    def mixer_A(l):
        G16 = mx["GT"][:, 0:2, :].rr("p a b -> p (a b)").rr("p (n c) -> p n c", c=16)
        SQB, NCG, SQBE, CG, TOT, QS, KS, LAM = [GTs(i) for i in range(2, 10)]
        flat = lambda i: mx["GT"][:, i, 0:NCH * 8]
        wb = load_blk(l, BLK["a_g"])
        proj_tm(wb, 0, 16, G16, 16)
        negA = smal[:, 40:48]
        P.act(negA, lpar1[:, L_DNA:L_DNA + 8], AF.Exp)
        P.ts(negA, negA, -1.0, ALU.mult)
        P.act(SQB, G16[:, :, 0:8], AF.Sigmoid)
        P.act(SQB, SQB, AF.Sqrt)
        P.tt(NCG, G16[:, :, 8:16], bc8(L_DNB), ALU.add)
        P.act(NCG, NCG, AF.Exp)
        P.act(NCG, NCG, AF.Ln, bias=1.0)
        P.tt(NCG, NCG, negA.rr("p (o r) -> p o r", o=1).bc([128, NCH, 8]), ALU.mult)
        cums(flat(3), flat(4), flat(7), flat(6))
        blend_dir(CG, SQBE, QS)
        P.act(QS, CG, AF.Exp)
        P.tt(SQBE, SQB, QS, ALU.mult)
        P.tt(KS, TOT, CG, ALU.subtract)
        P.act(KS, KS, AF.Exp)
        P.act(LAM, TOT, AF.Exp)
        P.ts(NCG, CG, -1.0, ALU.mult)
        I64 = cst[0:64, C_ID:C_ID + 64]
        O64 = cst[0:64, C_ONES:C_ONES + 64]
        for h in range(4):
            for xi, (bn, dstn) in enumerate((("a_q", "QT"), ("a_k", "KT"), ("a_v", None))):
                raw = mx["raw"]
                P.memset(raw[:, 0:1], 0.0)
                P.memset(raw[:, T + 1:T + 2], 0.0)
                wb = load_blk(l, BLK[bn])
                proj_fm(wb, h * 64, 64, raw[:, 1:T + 1])
                cw = lambda j: lpar1[0:64, L_CONV + (xi * 4 + h) * 3 + j:L_CONV + (xi * 4 + h) * 3 + j + 1]
                nw0 = smal[0:64, 48:49]
                nw2 = smal[0:64, 49:50]
                P.ts(nw0, cw(0), cst[0:64, C_PFLAG:C_PFLAG + 1], ALU.mult, -1.0, ALU.mult)
                P.ts(nw2, cw(2), cst[0:64, C_PFLAG:C_PFLAG + 1], ALU.mult, -1.0, ALU.mult)
                for t in range(NTT):
                    off = t * 512
                    st_ = tmpf.next()
                    st = st_[0:64, :]
                    P.ts(st, raw[:, off + 1:off + 513], cw(1), ALU.mult)
                    P.stt(st, raw[:, off:off + 512], cw(0), st, ALU.mult, ALU.add)
                    P.stt(st, raw[:, off + 2:off + 514], cw(2), st, ALU.mult, ALU.add)
                    sv = st.rr("p (a b) -> p a b", b=256)
                    r0 = raw[:, off:off + 512].rr("p (a b) -> p a b", b=256)[:, :, 0]
                    r2 = raw[:, off + 2:off + 514].rr("p (a b) -> p a b", b=256)[:, :, 255]
                    P.stt(sv[:, :, 0], r0, nw0, sv[:, :, 0], ALU.mult, ALU.add)
                    P.stt(sv[:, :, 255], r2, nw2, sv[:, :, 255], ALU.mult, ALU.add)
                    if dstn is not None:
                        dv = mx[dstn][:, off:off + 512]
                        P.act(dv, st, AF.Silu)
                        s2_ = tmpf.next()
                        s2 = s2_[0:64, :]
                        P.act(s2, dv, AF.Square)
                        pn = ps.next()
                        P.mm(pn[0:64, :], O64, s2)
                        P.ts(s2, pn[0:64, :], EPS, ALU.add)
                        P.act(s2, s2, AF.Sqrt)
                        P.recip(s2, s2)
                        if xi == 0:
                            P.stt(dv, dv, 0.125, s2, ALU.mult, ALU.mult)
                        else:
                            P.tt(dv, dv, s2, ALU.mult)
                    else:
                        P.act(st, st, AF.Silu)
                        pt = ps.next()
                        for q in range(4):
                            P.tr(pt[:, q * 64:(q + 1) * 64], st[:, q * 128:(q + 1) * 128], I64)
                        P.copy(mx["Vtm"][:, t * 4:(t + 1) * 4, 0:64], pt[:, 0:256].rr("p (q c) -> p q c", q=4), eng=ACT)
            for g in range(NCH // 4):
                pt = ps.next()
                for q in range(4):
                    n = g * 4 + q
                    P.tr(pt[:, q * 64:(q + 1) * 64], mx["KT"][:, n * 128:(n + 1) * 128], I64)
                P.copy(mx["Ktm"][:, g * 4:(g + 1) * 4, :], pt[:, 0:256].rr("p (q c) -> p q c", q=4), eng=ACT)
            stt_ = {"pre": [0] * (NCH + 2), "scan": 0}
            ident2 = Vw(cst, cst.ap[:, C_ID:C_ID + 128].rearrange("p (o c) -> p o c", o=1).broadcast_to([128, 2, 128]))
            noti2 = Vw(cst, cst.ap[:, C_NOTI:C_NOTI + 128].rearrange("p (o c) -> p o c", o=1).broadcast_to([128, 2, 128]))

            def chunks_of(k):
                return (k, NCH - 1 - k)

            def pre_W(p):
                psp = Ring(ps.tiles[2 * p:2 * p + 2])
                for k in range(p, NCH, 2):
                    while stt_["scan"] < k - 1:
                        yield
                    ns = chunks_of(k)
                    rs = (h, 4 + h)
                    dg, Y, kh, khT, u = [ax["%s_%d" % (nm_, p)] for nm_ in ("dg", "Y", "kh", "khT", "u")]
                    wt = mx["wtW_%d" % p]
                    wT = mx["wTW_%d" % p]
                    for d in range(2):
                        P.ts(dg[:, d, :], ident, CG[:, ns[d], rs[d]:rs[d] + 1], ALU.mult)
                        P.ts(kh[:, d, :], mx["Ktm"][:, ns[d], :], SQB[:, ns[d], rs[d]:rs[d] + 1], ALU.mult)
                    yield
                    pw = psp.next()
                    pk = psp.next()
                    for d in range(2):
                        P.mm(pw[:, d * 128:(d + 1) * 128], ONESF, dg[:, d, :], start=True, stop=False)
                        P.mm(pw[:, d * 128:(d + 1) * 128], ident, NEGM[d], start=False, stop=True)
                        P.tr(pk[0:64, d * 128:(d + 1) * 128], kh[:, d, :], ident)
                    yield
                    for d in range(2):
                        P.act(wt[:, d, :], pw[:, d * 128:(d + 1) * 128], AF.Exp, bias=NCG[:, ns[d], rs[d]:rs[d] + 1])
                    P.copy(khT.v(), pk[0:64, 0:256].rr("p (d c) -> p d c", d=2))
                    for d in range(2):
                        P.ts(Y[:, d, 0:64], mx["Vtm"][:, ns[d], 0:64], SQB[:, ns[d], rs[d]:rs[d] + 1], ALU.mult)
                        P.ts(Y[:, d, 64:128], mx["Ktm"][:, ns[d], :], SQBE[:, ns[d], rs[d]:rs[d] + 1], ALU.mult)
                    yield
                    pg = psp.next()
                    for d in range(2):
                        P.mm(pg[:, d * 128:(d + 1) * 128], khT[:, d, :], khT[:, d, :])
                    yield
                    XT = ax["XTa_%d" % p]
                    X = ax["Xa_%d" % p]
                    P.tt(XT.v(), pg[:, 0:256].rr("p (d c) -> p d c", d=2), wt.v(), ALU.mult)
                    P.tt(XT.v(), XT.v(), noti2, ALU.mult)
                    yield
                    px = psp.next()
                    py = psp.next()
                    for d in range(2):
                        P.tr(px[:, d * 128:(d + 1) * 128], XT[:, d, :], ident)
                        P.mm(py[:, d * 128:(d + 1) * 128], XT[:, d, :], Y[:, d, :])
                    yield
                    P.copy(X.v(), px[:, 0:256].rr("p (d c) -> p d c", d=2), eng=ACT)
                    P.tt(Y.v(), Y.v(), py[:, 0:256].rr("p (d c) -> p d c", d=2), ALU.subtract)
                    yield
                    for lev in range(6):
                        Xn = ax["Xb_%d" % p] if X is ax["Xa_%d" % p] else ax["Xa_%d" % p]
                        XTn = ax["XTb_%d" % p] if XT is ax["XTa_%d" % p] else ax["XTa_%d" % p]
                        p2x = psp.next()
                        p2y = psp.next()
                        for d in range(2):
                            if lev < 5:
                                P.mm(p2x[:, d * 128:(d + 1) * 128], XT[:, d, :], X[:, d, :])
                            P.mm(p2y[:, d * 128:(d + 1) * 128], X[:, d, :], XT[:, d, :])
                        yield
                        if lev < 5:
                            P.copy(Xn.v(), p2x[:, 0:256].rr("p (d c) -> p d c", d=2), eng=ACT)
                        P.copy(XTn.v(), p2y[:, 0:256].rr("p (d c) -> p d c", d=2))
                        X, XT = Xn, XTn
                        yield
                        py = psp.next()
                        for d in range(2):
                            P.mm(py[:, d * 128:(d + 1) * 128], XT[:, d, :], Y[:, d, :])
                        yield
                        P.tt(Y.v(), Y.v(), py[:, 0:256].rr("p (d c) -> p d c", d=2), ALU.add)
                        yield
                    for d in range(2):
                        P.ts(u[:, d, :], Y[:, d, 0:64], SQB[:, ns[d], rs[d]:rs[d] + 1], ALU.mult)
                        P.ts(kh[:, d, :], Y[:, d, 64:128], SQB[:, ns[d], rs[d]:rs[d] + 1], ALU.mult)
                    yield
                    pk2 = psp.next()
                    for d in range(2):
                        P.tr(pk2[0:64, d * 128:(d + 1) * 128], kh[:, d, :], ident)
                    yield
                    P.copy(wT.v(), pk2[0:64, 0:256].rr("p (d c) -> p d c", d=2), eng=ACT)
                    stt_["pre"][k] = 1
                    yield

            def scan_W():
                pss = Ring(ps.tiles[4:6])
                rs = (h, 4 + h)
                Ss = (Sst[0], Sst[1])
                for d in range(2):
                    P.dma(SP, Ss[d][:, 0:64], st_dl_in[l, rs[d]])
                ods = (mx["Od0"], mx["Od1"])
                for k in range(NCH):
                    while not stt_["pre"][k]:
                        yield
                    p = k % 2
                    ns = chunks_of(k)
                    wt = mx["wtW_%d" % p]
                    wT = mx["wTW_%d" % p]
                    u = ax["u_%d" % p]
                    vnew = ax["vnew"]
                    at = mx["atW"]
                    pv = pss.next()
                    p1 = pss.next()
                    for d in range(2):
                        P.mm(pv[:, d * 64:(d + 1) * 64], wT[:, d, :], Ss[d][:, 0:64])
                        P.mm(p1[:, d * 128:(d + 1) * 128], mx["KT"][:, ns[d] * 128:(ns[d] + 1) * 128],
                             mx["QT"][:, ns[d] * 128:(ns[d] + 1) * 128])
                    yield
                    P.tt(vnew.v(), u.v(), pv[:, 0:128].rr("p (d c) -> p d c", d=2), ALU.subtract)
                    P.tt(at.v(), p1[:, 0:256].rr("p (d c) -> p d c", d=2), wt.v(), ALU.mult)
                    kws = (mx["kw0"], mx["kw1"])
                    for d in range(2):
                        P.ts(kws[d].v(), mx["Ktm"][:, ns[d], :], KS[:, ns[d], rs[d]:rs[d] + 1], ALU.mult)
                    yield
                    p2 = pss.next()
                    p3 = pss.next()
                    for d in range(2):
                        P.mm(p2[:, d * 64:(d + 1) * 64], at[:, d, :], vnew[:, d, :])
                        P.mm(p2[:, 128 + d * 64:128 + (d + 1) * 64], mx["QT"][:, ns[d] * 128:(ns[d] + 1) * 128], Ss[d][:, 0:64])
                        P.mm(p3[0:64, d * 64:(d + 1) * 64], kws[d].v(), vnew[:, d, :])
                    yield
                    o2 = mx["attn0"]
                    P.copy(o2[:, 0:128], p2[:, 0:128], eng=ACT)
                    for d in range(2):
                        n = ns[d]
                        r = rs[d]
                        S = Ss[d]
                        P.stt(ods[d][:, n, 0:64], p2[:, 128 + d * 64:128 + (d + 1) * 64], QS[:, n, r:r + 1],
                              o2[:, d * 64:(d + 1) * 64], ALU.mult, ALU.add)
                        P.stt(S[:, 0:64], S[:, 0:64], LAM[0:64, n, r:r + 1], p3[0:64, d * 64:(d + 1) * 64], ALU.mult, ALU.add)
                        isend = (n % 2 == 1) if d == 0 else (n % 2 == 0)
                        if isend:
                            sf = Sfin.next()
                            P.copy(sf[:, 0:64], S[:, 0:64])
                            P.dma(SP, o_dl[l, n // 2, r], sf[:, 0:64])
                        P.ts(S[:, 0:64], S[:, 0:64], cst[0:64, C_KEEP + d * NCH + n:C_KEEP + d * NCH + n + 1], ALU.mult)
                    stt_["scan"] = k + 1
                    yield
            run_rr([pre_W(0), pre_W(1), scan_W()], delays=[0, 2, 0])
            wb = load_blk(l, BLK["a_z"])
            proj_tm(wb, h * 64, 64, mx["Gtm"], 64)
            o0 = mx["Od0"][:, :, 0:64]
            P.tt(o0, o0, mx["Od1"][:, :, 0:64], ALU.add)
            post_norm_gate(l, o0, L_DNG, AF.Silu, h)
            if h % 2 == 1:
                branch_out(0, h // 2)


    def mixer_B(l):
        import math
        lam_init = 0.8 - 0.6 * math.exp(-0.3 * l)
        lt = smal[:, 50:52]
        tmpl = bx["rt0"][:, 0:32]
        P.tt(tmpl, lpar1[:, L_LAM:L_LAM + 32], lpar1[:, L_LAM + 32:L_LAM + 64], ALU.mult)
        P.reduce(lt[:, 0:1], tmpl, ALU.add)
        P.tt(tmpl, lpar1[:, L_LAM + 64:L_LAM + 96], lpar1[:, L_LAM + 96:L_LAM + 128], ALU.mult)
        P.reduce(lt[:, 1:2], tmpl, ALU.add)
        P.act(lt, lt, AF.Exp)
        nlam = smal[:, 52:53]
        P.tt(nlam, lt[:, 1:2], lt[:, 0:1], ALU.subtract)
        P.ts(nlam, nlam, -lam_init, ALU.add)
        P.dma(SP, bx["ropet"].v(), rope_in.v())
        qaugs, kaug, Vext = (bx["qaug0"], bx["qaug1"]), bx["kaug"], bx["Vext"]
        for h in range(4):
            P.memset(kaug.v(), 0.0, eng=POOL)
            P.memset(qaugs[0].v(), 0.0, eng=POOL)
            P.memset(qaugs[1].v(), 0.0, eng=POOL)
            for c in range(2):
                qaug = qaugs[c]
                P.dma(POOL, qaug[c * 64 + 32:c * 64 + 41, :], qmask_in.v())
                P.dma(POOL, kaug[c * 64 + 32:c * 64 + 41, :], kmask_in.v())
                P.dma(POOL, kaug[c * 64:c * 64 + 32, 0:256], ckT_in[l, h, c])
            P.dma(POOL, Vext[:, 0:2, 0:64], cv_in[l, h].rr("(j p) c -> p j c", p=128))
            P.memset(Vext[:, :, 64:65], 1.0)
            wb = load_blk(l, BLK["b_tm"] + h)
            Vf = bx["Vf"]
            proj_tm3(wb, [(Vf, None), (bx["Qb"], None), (bx["Kb"], None)])
            P.dma(SP, o_v[l, h].rr("(n p) c -> p n c", p=128), Vf.v())
            P.copy(Vext[:, 2:NKT, 0:64], Vf.v())
            for xi, (bn, xb, gcol) in enumerate((("b_q", "Qb", L_QNG), ("b_k", "Kb", L_KNG))):
                X = bx[xb]
                sqo = bx["Ob"]
                Xg = X.v().rr("p n (c e) -> p (n c) e", c=2)
                Sg = sqo.v().rr("p n (c e) -> p (n c) e", c=2)
                P.tt(sqo.v(), X.v(), X.v(), ALU.mult)
                ss = rows_b
                P.reduce(ss, Sg, ALU.add)
                P.ts(ss, ss, 1.0 / 32, ALU.mult, EPS, ALU.add)
                P.act(ss, ss, AF.Sqrt)
                P.recip(ss, ss)
                P.tt(Xg, Xg, ss.rr("p (m o) -> p m o", o=1).bc([128, NCH * 2, 32]), ALU.mult)
                P.tt(Xg, Xg, lpar1[:, gcol:gcol + 32].rr("p (o e) -> p o e", o=1).bc([128, NCH * 2, 32]), ALU.mult)
                rp = bx["ropet"]
                cosv = rp[:, 0, :].rr("p (n a f) -> p n a f", a=2, f=8)
                sinv = rp[:, 1, :].rr("p (n a f) -> p n a f", a=2, f=8)
                t1, t2, t3, t4 = [bx["rt%d" % i].v().rr("p (n a f) -> p n a f", a=2, f=8) for i in range(4)]
                for c in range(2):
                    xv = X[:, :, c * 32:(c + 1) * 32].rr("p n (a q f) -> p n a q f", a=2, q=2)
                    x1 = xv[:, :, :, 0, :]
                    x2 = xv[:, :, :, 1, :]
                    P.tt(t1, x1, cosv, ALU.mult)
                    P.tt(t2, x2, sinv, ALU.mult)
                    P.tt(t3, x2, cosv, ALU.mult)
                    P.tt(t4, x1, sinv, ALU.mult)
                    P.tt(x1, t1, t2, ALU.subtract)
                    P.tt(x2, t3, t4, ALU.add)
                if xi == 1:
                    P.dma(SP, o_k[l, h].rr("(n p) c -> p n c", p=128), X.v())
                coff = 0 if xi == 0 else 256
                for c in range(2):
                    dstT = qaugs[c] if xi == 0 else kaug
                    for g in range(NCH // 4):
                        pt = ps.next()
                        for q in range(4):
                            n = g * 4 + q
                            P.tr(pt[0:32, q * 128:(q + 1) * 128], X[:, n, c * 32:(c + 1) * 32], ident)
                        dd = dstT[c * 64:c * 64 + 32, coff + g * 512:coff + (g + 1) * 512]
                        if xi == 0:
                            P.act(dd, pt[0:32, :], AF.Copy, scale=32 ** -0.5)
                        else:
                            P.copy(dd, pt[0:32, :], eng=ACT)
            PTr = Ring([bx["PT%d" % i] for i in range(4)])
            Ob = bx["Ob"]
            for qb in range(NTT):
                for c in range(2):
                    pacc = pacc_r.next()
                    pend = None
                    for kt in range(NKT + 1):
                        cur = None
                        if kt < NKT:
                            psc = ps.next()
                            P.mm(psc.v(), kaug[:, kt * 128:(kt + 1) * 128], qaugs[c][:, qb * 512:(qb + 1) * 512])
                            pt_ = PTr.next()
                            P.act(pt_.v(), psc.v(), AF.Exp, bias=-4.0)
                            cur = (kt, pt_)
                        if pend is not None:
                            k0, p0 = pend
                            P.mm(pacc[0:65, :], Vext[:, k0, :], p0.v(), start=(k0 == 0), stop=(k0 == NKT - 1))
                        pend = cur
                    ocT = bx["ocT"]
                    P.copy(ocT.v(), pacc[0:65, :], eng=ACT)
                    ptr = ps.next()
                    for qs in range(4):
                        P.tr(ptr[:, qs * 65:(qs + 1) * 65], ocT[:, qs * 128:(qs + 1) * 128], cst[0:65, C_ID:C_ID + 65])
                    oc = bx["octmp"] if c == 0 else bx["oc2"]
                    P.copy(oc.v(), ptr[:, 0:260].rr("p (q e) -> p q e", q=4))
                o1, o2 = bx["octmp"], bx["oc2"]
                P.recip(o1[:, :, 64:65], o1[:, :, 64:65])
                P.recip(o2[:, :, 64:65], o2[:, :, 64:65])
                P.ts(o2[:, :, 64:65], o2[:, :, 64:65], nlam, ALU.mult)
                P.tt(o1[:, :, 0:64], o1[:, :, 0:64], o1[:, :, 64:65].bc([128, 4, 64]), ALU.mult)
                P.tt(o2[:, :, 0:64], o2[:, :, 0:64], o2[:, :, 64:65].bc([128, 4, 64]), ALU.mult)
                P.tt(Ob[:, qb * 4:(qb + 1) * 4, :], o1[:, :, 0:64], o2[:, :, 0:64], ALU.add)
            sqo = bx["Qb"]
            P.tt(sqo.v(), Ob.v(), Ob.v(), ALU.mult)
            ss = smal[:, 32:32 + NCH]
            P.reduce(ss, sqo.v(), ALU.add)
            P.ts(ss, ss, 1.0 / 64, ALU.mult, EPS, ALU.add)
            P.act(ss, ss, AF.Sqrt)
            P.recip(ss, ss)
            P.ts(ss, ss, 1.0 - lam_init, ALU.mult)
            P.tt(sqo.v(), Ob.v(), ss.rr("p (n o) -> p n o", o=1).bc([128, NCH, 64]), ALU.mult)
            gn = lpar1[:, L_DAG:L_DAG + 64].rr("p (o e) -> p o e", o=1).bc([128, NCH, 64])
            P.tt(mx["Opair"][:, :, (h % 2) * 64:(h % 2) * 64 + 64], sqo.v(), gn, ALU.mult)
            if h % 2 == 1:
                branch_out(1, h // 2)

    def mixers(l):
        P.dma(SP, lpar1.v(), lp_in[l])
        for t in range(NTT):
            norm_mod(l, 1, t)
        for m in range(4):
            if m not in MIX_IMPL:
                zero_branch(m)
        if 0 in MIX_IMPL:
            mixer_A(l)
        if 1 in MIX_IMPL:
            mixer_B(l)
        if 2 in MIX_IMPL:
            mixer_C(l)
        if 3 in MIX_IMPL:
            mixer_D(l)
        merge(l)

    MIX_IMPL = set(MIXSEL)
    for l in range(NL):
        ffn(l, 0)
        if stage == "ffn1":
            break
        mixers(l)
        if stage == "mix":
            break
        ffn(l, 1)

    yv = y_out.v().rr("(kc p) t -> p kc t", p=128)
    for t in range(NTT):
        P.dma(SP, yv[:, :, t * 512:(t + 1) * 512], xT[t].v())
    P.emit()
    return nc


def kblocks(W, width):
    K, N = W.shape
    kc = K // 128
    nb = N // width
    return np.ascontiguousarray(W.reshape(kc, 128, nb, width).transpose(2, 1, 0, 3).reshape(nb, 128, kc * width))


def host_weights(inp, NL):
    w = {}
    w["wada"] = np.concatenate([kblocks(inp["w_ada"][l], 256) for l in range(NL)], 0)
    w["bada"] = np.stack([np.ascontiguousarray(inp["b_ada"][l].reshape(72, 128).T) for l in range(NL)])
    w["ng"] = np.stack([np.ascontiguousarray(inp["norm_g"][l].reshape(24, 128).T) for l in range(NL)])
    gu = []
    wd = []
    for l in range(NL):
        for i in range(2):
            g = kblocks(inp["ffn_w_gate"][l, i], 128).reshape(NF, 128, NK, 128)
            u = kblocks(inp["ffn_w_up"][l, i], 128).reshape(NF, 128, NK, 128)
            gu.append(np.concatenate([g, u], axis=3).reshape(NF, 128, NK * 256))
            wd.append(kblocks(inp["ffn_w_down"][l, i], 128))
    w["wgu"] = np.ascontiguousarray(np.concatenate(gu, 0))
    w["wd"] = np.ascontiguousarray(np.concatenate(wd, 0))
    return w


IN_SIZES = (256, 256, 256, 256, 8, 8, 256, 256, 256, 256, 256, 256, 256, 8, 8, 256, 256, 256, 256, 4096)


def host_weights2(inp, NL, w):
    offs = np.concatenate([[0], np.cumsum(IN_SIZES)])
    names = ["a_q", "a_k", "a_v", "a_z", "a_beta", "a_alpha", "b_q", "b_k", "b_v", "c_q", "c_k", "c_v", "c_o",
             "c_i", "c_f", "d_q", "d_k", "d_v", "d_g", "merge"]
    col = {n: int(offs[i]) for i, n in enumerate(names)}
    blocks = []
    for l in range(NL):
        W = inp["w_in"][l]

        def blk(c0, n=256):
            b = np.zeros((1024, 256), np.float32)
            b[:, :n] = W[:, c0:c0 + n]
            return b

        def gblk(c0, c1):
            b = np.zeros((1024, 256), np.float32)
            b[:, 0:8] = W[:, c0:c0 + 8]
            b[:, 8:16] = W[:, c1:c1 + 8]
            return b
        bl = [blk(col["a_q"]), blk(col["a_k"]), blk(col["a_v"]), blk(col["a_z"]), gblk(col["a_beta"], col["a_alpha"]),
              blk(col["b_q"]), blk(col["b_k"]), blk(col["b_v"]),
              blk(col["c_q"]), blk(col["c_k"]), blk(col["c_v"]), blk(col["c_o"]), gblk(col["c_i"], col["c_f"]),
              blk(col["d_q"]), blk(col["d_k"]), blk(col["d_v"]), blk(col["d_g"])]
        bl += [blk(col["merge"] + 256 * i) for i in range(16)]

        def tmblk(names, hh):
            b = np.zeros((1024, 256), np.float32)
            for j, nm in enumerate(names):
                b[:, j * 64:(j + 1) * 64] = W[:, col[nm] + hh * 64:col[nm] + (hh + 1) * 64]
            return b
        for names in (("b_v", "b_q", "b_k"), ("c_k", "c_v", "c_o"), ("d_k", "d_v", "d_g")):
            bl += [tmblk(names, hh) for hh in range(4)]
        blocks += [kblocks(b, 256)[0] for b in bl]
    w["win"] = np.ascontiguousarray(np.stack(blocks))
    w["wbr"] = np.ascontiguousarray(np.stack([inp["w_branch"][l, m].reshape(2, 128, 1024).transpose(1, 0, 2).reshape(128, 2048)
                                              for l in range(NL) for m in range(4)]))
    w["wout"] = np.concatenate([kblocks(inp["w_out"][l], 256) for l in range(NL)], 0)
    lp = np.zeros((NL, 128, LP_W), np.float32)
    for l in range(NL):
        lp[l, :, 0:8] = inp["ret_decay_logit"][l].reshape(8)[None, :]
        lp[l, :, 8:72] = inp["ret_norm_g"][l][None, :]
        lp[l, :, L_MLI:L_MLI + 8] = inp["ml_i_bias"][l].reshape(8)[None, :]
        lp[l, :, L_MLF:L_MLF + 8] = inp["ml_f_bias"][l].reshape(8)[None, :]
        lp[l, :, L_MLG:L_MLG + 64] = inp["ml_norm_g"][l][None, :]
        lp[l, :, L_ISF:L_ISF + 4] = 1.0
        lp[l, :, L_QNG:L_QNG + 32] = inp["da_qn_g"][l][None, :]
        lp[l, :, L_KNG:L_KNG + 32] = inp["da_kn_g"][l][None, :]
        lp[l, :, L_DAG:L_DAG + 64] = inp["da_norm_g"][l][None, :]
        lp[l, :, L_LAM:L_LAM + 128] = inp["da_lambda"][l].reshape(128)[None, :]
        lp[l, :, L_DNG:L_DNG + 64] = inp["dn_norm_g"][l][None, :]
        lp[l, :, L_DNA:L_DNA + 8] = inp["dn_a_log"][l].reshape(8)[None, :]
        lp[l, :, L_DNB:L_DNB + 8] = inp["dn_dt_bias"][l].reshape(8)[None, :]
        cw = inp["dn_conv_w"][l]
        for xi in range(3):
            for hh in range(4):
                for jj in range(3):
                    lp[l, 0:64, L_CONV + (xi * 4 + hh) * 3 + jj] = cw[jj, xi * 256 + hh * 64: xi * 256 + hh * 64 + 64]
    w["lpar"] = lp
    return w


def host_consts(T, is_sample):
    NCH = T // 128
    c = np.zeros((128, CST_W), np.float32)
    c[:, C_ID:C_ID + 128] = np.eye(128, dtype=np.float32)
    j = np.arange(128)[:, None]
    i = np.arange(128)[None, :]
    c[:, C_NEGF:C_NEGF + 128] = np.where(j <= i, 0.0, BIGNEG)
    c[:, C_NEGB:C_NEGB + 128] = np.where(j >= i, 0.0, BIGNEG)
    c[:, C_RELF:C_RELF + 128] = np.where(j <= i, (i - j).astype(np.float32), 1e6)
    c[:, C_RELB:C_RELB + 128] = np.where(j >= i, (j - i).astype(np.float32), 1e6)
    p = np.arange(128, dtype=np.float32)
    c[:, C_POS + 0] = p + 1
    c[:, C_POS + 1] = 128 - p
    c[:, C_POS + 2] = 127 - p
    c[:, C_POS + 3] = p
    c[:, C_PFLAG] = 0.0 if is_sample else 1.0
    c[:, C_ONES:C_ONES + 128] = 1.0
    c[:, C_TRIF:C_TRIF + 128] = (j <= i)
    c[:, C_TRIB:C_TRIB + 128] = (j >= i)
    c[:, C_NOTI:C_NOTI + 128] = (j != i)
    for n in range(NCH):
        c[:, C_KEEP + n] = 1.0 if (is_sample or n % 2 == 0) else 0.0
        c[:, C_KEEP + NCH + n] = 1.0 if (is_sample or n % 2 == 1) else 0.0
    return c


def host_sel():
    s = np.zeros((8, 520), np.float32)
    for r in range(8):
        s[r, r * 64:(r + 1) * 64] = 1.0
    s[0:4, 512] = 1.0
    return s


def host_attn_consts(T, is_sample):
    NCH = T // 128
    t = np.arange(T)
    rope = np.zeros((128, 2, NCH, 2, 8), np.float32)
    rope[:, 0] = 1.0
    qmask = np.zeros((9, T), np.float32)
    kmask = np.zeros((9, T + 256), np.float32)
    if is_sample:
        freqs = (10000.0 ** (-np.arange(8, dtype=np.float32) / 8)).astype(np.float32)
        rows = (t // 64).astype(np.float32)
        cols = (t % 64).astype(np.float32)
        for hf, pos in enumerate((rows, cols)):
            ang = (pos[:, None] * freqs[None, :]).astype(np.float32)
            rope[:, 0, :, hf, :] = np.cos(ang).reshape(NCH, 128, 8).transpose(1, 0, 2)
            rope[:, 1, :, hf, :] = np.sin(ang).reshape(NCH, 128, 8).transpose(1, 0, 2)
    else:
        seg = t // 256
        for a in range(8):
            qmask[a] = np.where(seg == a, 0.0, BIGNEG)
            kmask[a, 256:] = (seg == a)
        qmask[8] = BIGNEG
        kmask[8, 0:256] = 1.0
    return rope.reshape(128, 2, NCH * 16), qmask, kmask


T_SLAB = 2048
N_LAYERS = 2
_NC_CACHE = {}


def kernel(**inp):
    inp = {k: np.asarray(v) for k, v in inp.items()}
    NL = N_LAYERS
    T = T_SLAB
    xp = inp["x_prompt"]
    xs = inp["x_sample"]
    w = host_weights(inp, NL)
    host_weights2(inp, NL, w)
    cst_s = host_consts(T, True)
    cst_p = host_consts(T, False)
    att_s = host_attn_consts(T, True)
    att_p = host_attn_consts(T, False)
    w["sel"] = host_sel()
    zf = lambda *s: np.zeros(s, np.float32)
    in_maps = []
    for c in range(8):
        m = dict(w)
        if c < 2:
            x = xs[c]
            cond = inp["c"][c]
            m["cst"] = cst_s
            m["rope"], m["qmask"], m["kmask"] = att_s
            m["st_ret"] = np.ascontiguousarray(inp["state_ret"][c].reshape(NL, 8, 64, 64))
            m["st_delta"] = np.ascontiguousarray(inp["state_delta"][c].reshape(NL, 8, 64, 64))
            m["st_Cn"] = np.ascontiguousarray(np.concatenate(
                [inp["state_mlstm_C"][c].reshape(NL, 8, 64, 64), inp["state_mlstm_n"][c].reshape(NL, 8, 64, 1)], -1))
            m["st_m"] = np.ascontiguousarray(np.broadcast_to(inp["state_mlstm_m"][c].reshape(NL, 1, 8), (NL, 128, 8)))
            m["ckT"] = np.ascontiguousarray(inp["cache_diff_k"][c].reshape(NL, 4, 256, 2, 32).transpose(0, 1, 3, 4, 2))
            m["cv"] = np.ascontiguousarray(inp["cache_diff_v"][c])
        else:
            s0 = (c - 2) * 8
            if s0 < 32:
                x = xp[s0:s0 + 8].reshape(T, D)
            else:
                x = np.zeros((T, D), np.float32)
            cond = inp["c_ctx"]
            m["cst"] = cst_p
            m["rope"], m["qmask"], m["kmask"] = att_p
            m["st_ret"] = zf(NL, 8, 64, 64)
            m["st_delta"] = zf(NL, 8, 64, 64)
            m["st_Cn"] = zf(NL, 8, 64, 65)
            m["st_m"] = zf(NL, 128, 8)
            m["ckT"] = zf(NL, 4, 2, 32, 256)
            m["cv"] = zf(NL, 4, 256, 64)
        m["xT"] = np.ascontiguousarray(x.T)
        m["cond"] = np.ascontiguousarray(cond.reshape(8, 128).T)
        in_maps.append(m)
    if "nc" not in _NC_CACHE:
        _NC_CACHE["nc"] = build(T, NL, "full")
    nc = _NC_CACHE["nc"]
    res = run_bass_kernel_spmd(nc, in_maps, core_ids=list(range(8)))
    R = res.results
    y_prompt = np.concatenate([R[c]["yT"].T.reshape(8, 256, D) for c in range(2, 6)], 0).astype(np.float32)
    y_sample = np.stack([R[c]["yT"].T for c in range(2)]).astype(np.float32)
    B = 32
    pc = range(2, 6)
    cat = lambda f: np.ascontiguousarray(np.concatenate([f(R[c]) for c in pc], 0)).astype(np.float32)
    new_ret = cat(lambda r: r["o_ret"].transpose(1, 0, 2, 3, 4).reshape(8, NL, 2, 4, 64, 64))
    new_delta = cat(lambda r: r["o_delta"].transpose(1, 0, 2, 3, 4).reshape(8, NL, 2, 4, 64, 64))
    new_C = cat(lambda r: r["o_Cn"][..., 0:64].transpose(1, 0, 2, 3, 4).reshape(8, NL, 2, 4, 64, 64))
    new_n = cat(lambda r: r["o_Cn"][..., 64].transpose(1, 0, 2, 3).reshape(8, NL, 2, 4, 64))
    new_m = cat(lambda r: r["o_m"].transpose(2, 0, 1).reshape(8, NL, 2, 4))
    new_k = cat(lambda r: r["o_k"].reshape(NL, 4, 8, 256, 64).transpose(2, 0, 1, 3, 4))
    new_v = cat(lambda r: r["o_v"].reshape(NL, 4, 8, 256, 64).transpose(2, 0, 1, 3, 4))
    return (y_prompt, y_sample, new_k, new_v, new_delta, new_C, new_n, new_m, new_ret)
```

```python
import numpy as np
import concourse.bass as bass
import concourse.mybir as mybir

F32 = mybir.dt.float32
BF16 = mybir.dt.bfloat16
AF = mybir.ActivationFunctionType
ALU = mybir.AluOpType
AX = mybir.AxisListType

PE, ACT, DVE, POOL, SP = "pe", "act", "dve", "pool", "sp"


class Tl:
    def __init__(self, ap, name=""):
        self.ap = ap
        self.name = name
        self.lw = None
        self.rd = []
        self.dram_out = False
        self.writers = []
        self.grp = None

    def __getitem__(self, idx):
        return Vw(self, self.ap[idx])

    def v(self):
        return Vw(self, self.ap)


class Vw:
    def __init__(self, t, ap):
        self.t = t
        self.ap = ap

    def __getitem__(self, idx):
        return Vw(self.t, self.ap[idx])

    def rr(self, s, **kw):
        return Vw(self.t, self.ap.rearrange(s, **kw))

    def bc(self, shape):
        return Vw(self.t, self.ap.broadcast_to(shape))

    def bitcast(self, dt):
        return Vw(self.t, self.ap.bitcast(dt))


class Op:
    __slots__ = ("eng", "fn", "deps", "id", "is_dma", "sig", "cnt", "dsem", "dval", "dprev")

    def __init__(self, eng, fn, deps, is_dma):
        self.eng = eng
        self.fn = fn
        self.deps = deps
        self.is_dma = is_dma
        self.sig = False
        self.cnt = 0
        self.dsem = None
        self.dval = 0
        self.dprev = 0


class Prog:
    def __init__(self, nc, n_dma_sems=20, same_engine_sync=True):
        self.nc = nc
        self.ops = []
        self.same_engine_sync = same_engine_sync
        self.esem = {}
        self.n_dma_sems = n_dma_sems
        self.dma_rr = {}
        self.final_tokens = []
        self.out_tiles = []

    def sb(self, name, shape, dt=F32):
        h = self.nc.alloc_sbuf_tensor(name, list(shape), dt)
        return Tl(h.ap(), name)

    def ps(self, name, shape, dt=F32):
        h = self.nc.alloc_psum_tensor(name, list(shape), dt)
        return Tl(h.ap(), name)

    def wrap(self, ap, name=""):
        return Tl(ap, name)

    def carve(self, arena_tl, specs, precise=True):
        out = {}
        if getattr(arena_tl, "conservative", False):
            precise = False
        if not hasattr(arena_tl, "members"):
            arena_tl.members = []
        for name, off, shape, dt in specs:
            nb = 4 if dt == F32 else 2
            n = 1
            for d in shape[1:]:
                n *= d
            a = arena_tl.ap[0:shape[0], off // 2: off // 2 + n * nb // 2]
            if dt == F32:
                a = a.bitcast(F32)
            if len(shape) == 3:
                a = a.rearrange("p (a b) -> p a b", a=shape[1])
            elif len(shape) == 4:
                a = a.rearrange("p (a b c) -> p a b c", a=shape[1], b=shape[2])
            t = Tl(a, name)
            lo, hi = off, off + n * nb
            t.grp = [t]
            for (ot, olo, ohi) in arena_tl.members:
                if (not precise) or (lo < ohi and olo < hi):
                    t.grp.append(ot)
                    ot.grp.append(t)
            arena_tl.members.append((t, lo, hi))
            out[name] = t
        return out

    def _rec(self, eng, fn, outs, ins, is_dma=False):
        deps = set()
        for v in ins:
            if v is None:
                continue
            for t in (v.t.grp or [v.t]):
                if t.lw is not None:
                    deps.add(t.lw)
        for v in outs:
            if v.t.dram_out:
                continue
            for t in (v.t.grp or [v.t]):
                if t.lw is not None:
                    deps.add(t.lw)
                deps.update(t.rd)
        op = Op(eng, fn, deps, is_dma)
        op.id = len(self.ops)
        self.ops.append(op)
        for v in ins:
            if v is None:
                continue
            v.t.rd.append(op.id)
        for v in outs:
            if v.t.dram_out:
                v.t.writers.append(op.id)
                continue
            v.t.lw = op.id
            v.t.rd = []
        return op

    def op(self, eng, fn, outs, ins):
        return self._rec(eng, fn, outs, ins)

    def dma(self, q, out, in_, **kw):
        o, i = out.ap, in_.ap
        return self._rec(q, lambda e: e.dma_start(out=o, in_=i, **kw), [out], [in_], is_dma=True)

    def mm(self, out, lhsT, rhs, start=True, stop=True, **kw):
        o, l, r = out.ap, lhsT.ap, rhs.ap
        return self._rec(PE, lambda e: e.matmul(o, l, r, start=start, stop=stop, **kw), [out], [lhsT, rhs])

    def tr(self, out, in_, ident):
        o, i, d = out.ap, in_.ap, ident.ap
        return self._rec(PE, lambda e: e.transpose(o, i, d), [out], [in_, ident])

    def act(self, out, in_, func, bias=None, scale=None, accum=None, eng=ACT):
        o, i = out.ap, in_.ap
        kw = {}
        ins = [in_]
        outs = [out]
        if bias is not None:
            if isinstance(bias, Vw):
                kw["bias"] = bias.ap
                ins.append(bias)
            else:
                kw["bias"] = bias
        if scale is not None:
            if isinstance(scale, Vw):
                kw["scale"] = scale.ap
                ins.append(scale)
            else:
                kw["scale"] = scale
        if accum is not None:
            kw["accum_out"] = accum.ap
            outs.append(accum)
        return self._rec(eng, lambda e: e.activation(o, i, func, **kw), outs, ins)

    def tt(self, out, a, b, op, eng=DVE):
        o, x, y = out.ap, a.ap, b.ap
        return self._rec(eng, lambda e: e.tensor_tensor(o, x, y, op), [out], [a, b])

    def ts(self, out, a, s1, op0, s2=None, op1=None, eng=DVE, accum=None):
        o, x = out.ap, a.ap
        ins = [a]
        outs = [out]
        a1 = s1.ap if isinstance(s1, Vw) else s1
        a2 = s2.ap if isinstance(s2, Vw) else s2
        if isinstance(s1, Vw):
            ins.append(s1)
        if isinstance(s2, Vw):
            ins.append(s2)
        kw = {}
        if op1 is not None:
            kw["op1"] = op1
        if accum is not None:
            kw["accum_out"] = accum.ap
            outs.append(accum)
        return self._rec(eng, lambda e: e.tensor_scalar(o, x, a1, a2, op0, **kw), outs, ins)

    def stt(self, out, a, s, b, op0, op1, eng=DVE):
        o, x, y = out.ap, a.ap, b.ap
        ins = [a, b]
        sa = s.ap if isinstance(s, Vw) else s
        if isinstance(s, Vw):
            ins.append(s)
        return self._rec(eng, lambda e: e.scalar_tensor_tensor(o, x, sa, y, op0, op1), [out], ins)

    def copy(self, out, in_, eng=DVE):
        o, i = out.ap, in_.ap
        if eng == ACT:
            return self._rec(eng, lambda e: e.copy(o, i), [out], [in_])
        return self._rec(eng, lambda e: e.tensor_copy(o, i), [out], [in_])

    def memset(self, out, val, eng=DVE):
        o = out.ap
        return self._rec(eng, lambda e: e.memset(o, val), [out], [])

    def reduce(self, out, in_, op, axis=None, eng=DVE):
        o, i = out.ap, in_.ap
        ax = axis if axis is not None else AX.X
        return self._rec(eng, lambda e: e.tensor_reduce(o, i, ax, op), [out], [in_])

    def recip(self, out, in_):
        o, i = out.ap, in_.ap
        return self._rec(DVE, lambda e: e.reciprocal(o, i), [out], [in_])

    def scan(self, out, d0, d1, init, op0, op1, eng=DVE):
        o, a, b = out.ap, d0.ap, d1.ap
        ins = [d0, d1]
        iv = init.ap if isinstance(init, Vw) else init
        if isinstance(init, Vw):
            ins.append(init)
        return self._rec(eng, lambda e: e.tensor_tensor_scan(o, a, b, iv, op0, op1), [out], ins)

    def mark_output(self, tl):
        tl.dram_out = True
        self.out_tiles.append(tl)

    def emit(self):
        nc = self.nc
        ops = self.ops
        for op in ops:
            for d in op.deps:
                p = ops[d]
                if p.is_dma:
                    continue
                if p.eng == op.eng and not op.is_dma:
                    if p.eng == PE:
                        continue
                    if not self.same_engine_sync:
                        continue
                p.sig = True
        final_deps = set()
        for tl in self.out_tiles:
            final_deps.update(tl.writers)
        for d in final_deps:
            if not ops[d].is_dma:
                ops[d].sig = True
        engs = [PE, ACT, DVE, POOL, SP]
        cnt = {e: 0 for e in engs}
        for op in ops:
            if op.is_dma:
                continue
            if op.sig:
                cnt[op.eng] += 1
                op.cnt = cnt[op.eng]
        from contextlib import ExitStack

        with ExitStack() as st:
            for e in engs:
                self.esem[e] = st.enter_context(nc.semaphore("es_" + e))
            qs = sorted(set(op.eng for op in ops if op.is_dma))
            pools = {}
            for q in qs:
                pools[q] = [st.enter_context(nc.semaphore("ds_%s_%d" % (q, i))) for i in range(self.n_dma_sems)]
            rr = {q: 0 for q in qs}
            tot = {}
            for op in ops:
                if not op.is_dma:
                    continue
                pool = pools[op.eng]
                s = pool[rr[op.eng] % len(pool)]
                rr[op.eng] += 1
                key = (op.eng, rr[op.eng] % len(pool) if False else id(s))
                prev = tot.get(id(s), 0)
                op.dsem = s
                op.dprev = prev
                op.dval = prev + 16
                tot[id(s)] = op.dval
            block = st.enter_context(nc.Block())
            by_eng = {e: [op for op in ops if op.eng == e] for e in engs}

            def run(eng_name, e):
                waited = {}

                def wait(sem, val):
                    k = id(sem)
                    if waited.get(k, 0) >= val:
                        return
                    waited[k] = val
                    e.wait_ge(sem, val)

                for op in by_eng[eng_name]:
                    for d in sorted(op.deps):
                        p = ops[d]
                        if p.is_dma:
                            wait(p.dsem, p.dval)
                        else:
                            if p.eng == eng_name and not op.is_dma:
                                if p.eng == PE or not self.same_engine_sync:
                                    continue
                            wait(self.esem[p.eng], p.cnt)
                    if op.is_dma:
                        if op.dprev > 0:
                            wait(op.dsem, op.dprev)
                        ins = op.fn(e)
                        ins.then_inc(op.dsem, 16)
                    else:
                        ins = op.fn(e)
                        if op.sig:
                            ins.then_inc(self.esem[eng_name], 1)
                if eng_name == SP:
                    for d in sorted(final_deps):
                        p = ops[d]
                        if p.is_dma:
                            wait(p.dsem, p.dval)
                        else:
                            wait(self.esem[p.eng], p.cnt)

            @block.tensor
            def _(e):
                run(PE, e)

            @block.scalar
            def _(e):
                run(ACT, e)

            @block.vector
            def _(e):
                run(DVE, e)

            @block.gpsimd
            def _(e):
                run(POOL, e)

            @block.sync
            def _(e):
                run(SP, e)

from concourse.bass_utils import run_bass_kernel_spmd

D = 1024
NK = 8
FF = 2816
NF = 22
NMOD = 9
EPS = 1e-6
WIN_BLOCKS = 45
BLK = dict(a_q=0, a_k=1, a_v=2, a_z=3, a_g=4, b_q=5, b_k=6, b_v=7, c_q=8, c_k=9, c_v=10, c_o=11, c_g=12,
           d_q=13, d_k=14, d_v=15, d_g=16, merge=17, b_tm=33, c_tm=37, d_tm=41)
CST_W = 1216
LP_W = 544
C_ID, C_NEGF, C_NEGB, C_RELF, C_RELB, C_POS, C_KEEP = 0, 128, 256, 384, 512, 640, 644
C_PFLAG = 676
C_ONES, C_TRIF, C_TRIB, C_NOTI = 704, 832, 960, 1088
L_RETL, L_RETG, L_MLI, L_MLF, L_MLG, L_ISF, L_DNG, L_DNA, L_DNB, L_CONV, L_QNG, L_KNG, L_DAG, L_LAM = 0, 8, 72, 80, 88, 152, 160, 224, 232, 240, 276, 308, 340, 404
BIGNEG = -30000.0


class Ring:
    def __init__(self, tiles):
        self.tiles = tiles
        self.i = 0

    def next(self):
        t = self.tiles[self.i % len(self.tiles)]
        self.i += 1
        return t


def build(T, NL, stage="full", MIXSEL=(0, 1, 2, 3)):
    NTT = T // 512
    NH = max(1, NTT // 2)
    TPH = NTT // NH
    nc = bass.Bass("TRN2", target_bir_lowering=False)
    P = Prog(nc)

    def din(name, shape):
        return P.wrap(nc.dram_tensor(name, list(shape), F32, kind="ExternalInput").ap(), name)

    def dout(name, shape):
        t = P.wrap(nc.dram_tensor(name, list(shape), F32, kind="ExternalOutput").ap(), name)
        P.mark_output(t)
        return t

    x_in = din("xT", [D, T])
    cond_in = din("cond", [128, NK])
    wada_in = din("wada", [NL * 36, 128, NK * 256])
    bada_in = din("bada", [NL, 128, 72])
    ng_in = din("ng", [NL, 128, 24])
    wgu_in = din("wgu", [NL * 2 * NF, 128, NK * 256])
    wd_in = din("wd", [NL * 2 * NK, 128, NF * 128])
    y_out = dout("yT", [D, T])
    NCH = T // 128
    win_in = din("win", [NL * WIN_BLOCKS, 128, NK * 256])
    wbr_in = din("wbr", [NL * 4, 128, 2 * 1024])
    wout_in = din("wout", [NL * 4, 128, NK * 256])
    cst_in = din("cst", [128, CST_W])
    lp_in = din("lpar", [NL, 128, LP_W])
    st_ret_in = din("st_ret", [NL, 8, 64, 64])
    o_ret = dout("o_ret", [NL, NCH // 2, 8, 64, 64])
    NSEG = NCH // 2
    sel_in = din("sel", [8, 520])
    st_C_in = din("st_Cn", [NL, 8, 64, 65])
    st_dl_in = din("st_delta", [NL, 8, 64, 64])
    ckT_in = din("ckT", [NL, 4, 2, 32, 256])
    cv_in = din("cv", [NL, 4, 256, 64])
    rope_in = din("rope", [128, 2, NCH * 16])
    qmask_in = din("qmask", [9, T])
    kmask_in = din("kmask", [9, T + 256])
    o_k = dout("o_k", [NL, 4, T, 64])
    o_v = dout("o_v", [NL, 4, T, 64])
    o_dl = dout("o_delta", [NL, NSEG, 8, 64, 64])
    st_m_in = din("st_m", [NL, 128, 8])
    o_C = dout("o_Cn", [NL, NSEG, 8, 64, 65])
    o_m = dout("o_m", [NL, 8, NSEG])

    xT = [P.sb("xT%d" % t, [128, NK, 512]) for t in range(NTT)]
    hT = [P.sb("hT%d" % t, [128, NK, 512], BF16) for t in range(NTT)]
    ARENA_B = 2 * NF * 512 * TPH + 12 * 1024
    arena = P.sb("arena", [128, ARENA_B // 2], BF16)
    import os
    arena.conservative = os.environ.get("A1_CONS", "0") == "1"
    cv = P.carve(arena, [("aT%d" % t, t * NF * 1024, [128, NF, 512], BF16) for t in range(TPH)])
    aT = [cv["aT%d" % t] for t in range(TPH)]
    cva = P.carve(arena, [("wAda%d" % i, i * NK * 1024, [128, NK, 256], F32) for i in range(2)])
    TB = T * 4
    o = 0
    mspec = []
    for nm, shp in [("QT", [64, T]), ("KT", [64, T]), ("Ktm", [128, NCH, 64]), ("Vtm", [128, NCH, 65]),
                    ("Gtm", [128, NCH, 64]), ("Od0", [128, NCH, 65]), ("Od1", [128, NCH, 65]),
                    ("Opair", [128, NCH, 128]), ("sqo", [128, NCH, 64])]:
        n = 4
        for d in shp[1:]:
            n *= d
        mspec.append((nm, o, shp, F32))
        if nm in ("QT", "KT"):
            mspec.append((nm + "b", o, [64, T], BF16))
        if nm == "Vtm":
            mspec.append(("Vtb", o, [128, NCH, 66], BF16))
        if nm == "Gtm":
            mspec.append(("raw", o, [64, T + 2], F32))
            for p_ in range(2):
                mspec.append(("wtW_%d" % p_, o + p_ * 1024, [128, 2, 128], F32))
                mspec.append(("wTW_%d" % p_, o + 2048 + p_ * 1024, [64, 2, 128], F32))
        if nm == "sqo":
            for i in range(2):
                mspec.append(("dg%d" % i, o + i * 1024, [128, 128], F32))
                mspec.append(("wt%d" % i, o + i * 1024 + 512, [128, 128], F32))
            for i, xn in enumerate(["Xa", "Xb", "XTa", "XTb", "Yd", "khT"]):
                mspec.append((xn, o + 1024 + i * 512, [128, 128], F32))
            mspec.append(("atW", o, [128, 2, 128], F32))
            n = max(n, 4096)
        o += n
    for i in range(2):
        mspec.append(("attn%d" % i, o, [128, 128], F32))
        mspec.append(("atb%d" % i, o, [128, 128], BF16)); o += 512
        mspec.append(("kw%d" % i, o, [128, 64], F32))
        mspec.append(("kwb%d" % i, o, [128, 64], BF16)); o += 256
        mspec.append(("o2_%d" % i, o, [128, 65], F32)); o += 260
        mspec.append(("o3_%d" % i, o, [128, 65], F32)); o += 260
    mspec.append(("WTd", o, [128, 8, 128], F32))
    mspec.append(("GT", o, [128, 10, NCH * 8], F32))
    o += max(4096, 10 * NCH * 32)
    assert o <= ARENA_B, (o, ARENA_B)
    mx = P.carve(arena, mspec)
    NKT = NCH + 2
    offs_ = dict((nm_, off_) for nm_, off_, _, _ in mspec)
    bo = 0
    bspec = []
    for nm, shp, dt_ in [("qaug0", [128, T], BF16), ("qaug1", [128, T], BF16), ("kaug", [128, T + 256], BF16),
                         ("Vext", [128, NKT, 65], BF16), ("Qb", [128, NCH, 64], F32),
                         ("Kb", [128, NCH, 64], F32), ("Ob", [128, NCH, 64], F32), ("octmp", [128, 4, 65], F32),
                         ("oc2", [128, 4, 65], F32), ("ocT", [65, 512], F32)]:
        n = 4 if dt_ == F32 else 2
        for d in shp[1:]:
            n *= d
        n = (n + 3) // 4 * 4
        bspec.append((nm, bo, shp, dt_))
        if nm == "Ob":
            bspec.append(("Vf", bo, [128, NCH, 64], F32))
        bo += n
    for i in range(4):
        bspec.append(("PT%d" % i, bo, [128, 512], BF16))
        bo += 1024
    assert (1 not in MIXSEL) or bo <= offs_["Opair"], (bo, offs_["Opair"])
    so = offs_["sqo"]
    bspec.append(("ropet", so, [128, 2, NCH * 16], F32))
    to = offs_["attn0"]
    for i in range(4):
        bspec.append(("rt%d" % i, to + i * NCH * 64, [128, NCH * 16], F32))
    assert to + 4 * NCH * 64 <= ARENA_B
    bx = P.carve(arena, bspec)
    zsp = [("z", 0, [128, NK, 512], F32), ("zb", NK * 2048, [128, NK, 512], BF16),
           ("brm", NK * 2048 + NK * 1024, [128, 8, 512], BF16), ("gsb0", NK * 2048 + 2 * NK * 1024, [128, 512], F32),
           ("gsb1", NK * 2048 + 2 * NK * 1024 + 2048, [128, 512], F32), ("gp0", NK * 2048 + 2 * NK * 1024 + 4096, [128, 512], F32)]
    mz = P.carve(arena, zsp)
    wA = Ring([P.sb("wA%d" % i, [128, NK, 256], BF16) for i in range(3)])
    A2_B = 2 * NF * 256 + 2 * 2048 + 2 * 1024
    arena2 = P.sb("arena2", [128, A2_B // 2], BF16)
    import os
    arena2.conservative = os.environ.get("A2_CONS", "0") == "1"
    c2 = P.carve(arena2, [("wD%d" % i, i * NF * 256, [128, NF, 128], BF16) for i in range(2)]
                 + [("rstd%d" % i, 2 * NF * 256 + i * 2048, [128, 512], F32) for i in range(2)]
                 + [("sq%d" % i, 2 * NF * 256 + 4096 + i * 1024, [128, 512], BF16) for i in range(2)])
    wD = Ring([c2["wD%d" % i] for i in range(2)])
    a2spec = []
    o2_ = 0
    for p_ in range(2):
        for nm_, sz_, shp_ in [("dg", 1024, [128, 2, 128]), ("XTa", 1024, [128, 2, 128]), ("XTb", 1024, [128, 2, 128]),
                               ("Xa", 1024, [128, 2, 128]), ("Xb", 1024, [128, 2, 128]), ("Y", 1024, [128, 2, 128]),
                               ("khT", 1024, [64, 2, 128]), ("kh", 512, [128, 2, 64]), ("u", 512, [128, 2, 64])]:
            a2spec.append(("%s_%d" % (nm_, p_), o2_, shp_, F32))
            o2_ += sz_
    a2spec.append(("vnew", o2_, [128, 2, 64], F32))
    o2_ += 512
    assert o2_ <= A2_B, (o2_, A2_B)
    ax = P.carve(arena2, a2spec)
    wBR = Ring([P.sb("wBR%d" % i, [128, 2, 1024], BF16) for i in range(1)])
    wAda = Ring([cva["wAda%d" % i] for i in range(2)])
    cst = P.sb("cst_sb", [128, CST_W])
    lpar1 = P.sb("lpar1", [128, LP_W])
    lpar = [lpar1 for l in range(NL)]
    sel = P.sb("sel_sb", [8, 520])
    rows = P.sb("rows_sb", [8, 256])
    embc = P.sb("embc", [64, 64])
    rows_b = P.sb("rows_b", [128, 32]).v()
    em0 = P.sb("em0", [128, 8])
    brst = Ring([P.sb("brst%d" % i, [128, T], BF16) for i in range(1)])
    smal = P.sb("smal", [128, 64])
    Sst = [P.sb("Sst%d" % i, [64, 65]) for i in range(2)]
    Sbf = [P.sb("Sbf%d" % i, [64, 66], BF16) for i in range(2)]
    Sfin = Ring([P.sb("Sfin%d" % i, [64, 65]) for i in range(2)])
    br_dram = P.wrap(nc.dram_tensor("br_scratch", [4, 2, 128, T], BF16,
                                    kind=("ExternalOutput" if stage == "mix" else "Internal")).ap(), "br_scratch")
    ps = Ring([P.ps("pb%d" % i, [128, 512]) for i in range(6)])
    pacc_r = Ring([P.ps("pacc%d" % i, [128, 512]) for i in range(2)])
    sq = Ring([c2["sq%d" % i] for i in range(2)])
    tmpf = Ring([P.sb("tmpf%d" % i, [128, 512]) for i in range(2)])
    rstd_r = Ring([c2["rstd%d" % i] for i in range(2)])
    ones_bf = P.sb("ones_bf", [128, 128], BF16)
    scond = P.sb("scond", [128, NK])
    bada = [P.sb("bada%d" % l, [128, 72]) for l in range(NL)]
    ng = [P.sb("ng%d" % l, [128, 24]) for l in range(NL)]
    mod = [P.sb("mod%d" % l, [128, 72]) for l in range(NL)]
    scale = [P.sb("scale%d" % l, [128, 24]) for l in range(NL)]
    gate = [P.sb("gate%d" % l, [128, 24]) for l in range(NL)]

    P.memset(ones_bf.v(), 1.0)

    xv = x_in.v().rr("(kc p) t -> p kc t", p=128)
    for t in range(NTT):
        P.dma(SP, xT[t].v(), xv[:, :, t * 512:(t + 1) * 512])
    P.dma(SP, scond.v(), cond_in.v())
    P.dma(SP, cst.v(), cst_in.v())
    P.dma(SP, sel.v(), sel_in.v())
    for l in range(NL):
        P.dma(SP, bada[l].v(), bada_in[l])
        P.dma(SP, ng[l].v(), ng_in[l])
    P.act(scond.v(), scond.v(), AF.Silu)

    for l in range(NL):
        pm = pacc_r.next()
        for blk in range(36):
            wb = wAda.next()
            P.dma(SP, wb.v(), wada_in[l * 36 + blk].rr("p (k c) -> p k c", k=NK))
            pr = ps.next()
            for kc in range(NK):
                P.mm(pr[0:1, 0:256], scond[:, kc:kc + 1], wb[:, kc, :], start=(kc == 0), stop=(kc == NK - 1))
            rb = rows[0:1, 0:256]
            P.copy(rb, pr[0:1, 0:256], eng=ACT)
            for cc in range(2):
                col = blk * 2 + cc
                P.mm(pm[:, col:col + 1], rb[0:1, cc * 128:(cc + 1) * 128], cst[0:1, C_ONES:C_ONES + 1])
        P.tt(mod[l].v(), pm[:, 0:72], bada[l].v(), ALU.add)
        for i in range(3):
            P.stt(scale[l][:, i * 8:(i + 1) * 8], mod[l][:, (3 * i + 1) * 8:(3 * i + 2) * 8], 1.0,
                  ng[l][:, i * 8:(i + 1) * 8], ALU.add, ALU.mult)
            P.ts(gate[l][:, i * 8:(i + 1) * 8], mod[l][:, (3 * i + 2) * 8:(3 * i + 3) * 8],
                 0.5 if i != 1 else 1.0, ALU.mult)

    def norm_mod(l, i, t):
        pn = ps.next()
        for kc in range(NK):
            s = sq.next()
            P.act(s.v(), xT[t][:, kc, :], AF.Square)
            P.mm(pn.v(), ones_bf.v(), s.v(), start=(kc == 0), stop=(kc == NK - 1))
        r = rstd_r.next()
        P.ts(r.v(), pn.v(), 1.0 / D, ALU.mult, EPS, ALU.add)
        P.act(r.v(), r.v(), AF.Sqrt)
        P.recip(r.v(), r.v())
        for kc in range(NK):
            tm = tmpf.next()
            P.stt(tm.v(), xT[t][:, kc, :], scale[l][:, i * 8 + kc:i * 8 + kc + 1], r.v(), ALU.mult, ALU.mult)
            P.act(hT[t][:, kc, :], tm.v(), AF.Identity, bias=mod[l][:, 3 * i * 8 + kc:3 * i * 8 + kc + 1])

    def ffn(l, i):
        ni = 0 if i == 0 else 2
        for h in range(NH):
            tts = list(range(h * TPH, (h + 1) * TPH))
            for t in tts:
                norm_mod(l, ni, t)
            for f in range(NF):
                wb = wA.next()
                P.dma(POOL, wb.v(), wgu_in[(l * 2 + i) * NF + f].rr("p (k c) -> p k c", k=NK))
                for j, t in enumerate(tts):
                    pg = ps.next()
                    pu = ps.next()
                    for kc in range(NK):
                        P.mm(pg.v(), wb[:, kc, 0:128], hT[t][:, kc, :], start=(kc == 0), stop=(kc == NK - 1))
                    for kc in range(NK):
                        P.mm(pu.v(), wb[:, kc, 128:256], hT[t][:, kc, :], start=(kc == 0), stop=(kc == NK - 1))
                    sg = tmpf.next()
                    P.act(sg.v(), pg.v(), AF.Silu)
                    P.tt(aT[j][:, f, :], sg.v(), pu.v(), ALU.mult)
            for dc in range(NK):
                wdb = wD.next()
                P.dma(POOL, wdb.v(), wd_in[(l * 2 + i) * NK + dc].rr("p (f c) -> p f c", f=NF))
                for j, t in enumerate(tts):
                    py = ps.next()
                    for fc in range(NF):
                        P.mm(py.v(), wdb[:, fc, :], aT[j][:, fc, :], start=(fc == 0), stop=(fc == NF - 1))
                    P.stt(xT[t][:, dc, :], py.v(), gate[l][:, ni * 8 + dc:ni * 8 + dc + 1], xT[t][:, dc, :],
                          ALU.mult, ALU.add)


    ident = cst[:, C_ID:C_ID + 128]

    def load_blk(l, b):
        wb = wA.next()
        P.dma(POOL, wb.v(), win_in[l * WIN_BLOCKS + b].rr("p (k c) -> p k c", k=NK))
        return wb

    def proj_fm(wb, c0, M, dst, scale=None):
        for t in range(NTT):
            pp = ps.next()
            for kc in range(NK):
                P.mm(pp[0:M, :], wb[:, kc, c0:c0 + M], hT[t][:, kc, :], start=(kc == 0), stop=(kc == NK - 1))
            if scale is None:
                P.copy(dst[0:M, t * 512:(t + 1) * 512], pp[0:M, :], eng=ACT)
            else:
                P.act(dst[0:M, t * 512:(t + 1) * 512], pp[0:M, :], AF.Copy, scale=scale)

    def proj_tm(wb, c0, N, dst, ncol, scale=None):
        for g in range(NCH // 4):
            pp = ps.next()
            for q in range(4):
                n = g * 4 + q
                t, off = n // 4, (n % 4) * 128
                for kc in range(NK):
                    P.mm(pp[:, q * 64:q * 64 + N], hT[t][:, kc, off:off + 128], wb[:, kc, c0:c0 + N],
                         start=(kc == 0), stop=(kc == NK - 1))
            src = pp[:, 0:256].rr("p (q c) -> p q c", q=4)[:, :, 0:N]
            if scale is None:
                P.copy(dst[:, g * 4:(g + 1) * 4, 0:N], src, eng=ACT)
            else:
                P.act(dst[:, g * 4:(g + 1) * 4, 0:N], src, AF.Copy, scale=scale)

    def proj_tm3(wb, dsts):
        for g in range(NCH // 2):
            pp = ps.next()
            for q in range(2):
                n = g * 2 + q
                t, off = n // 4, (n % 4) * 128
                for kc in range(NK):
                    P.mm(pp[:, q * 192:(q + 1) * 192], hT[t][:, kc, off:off + 128], wb[:, kc, 0:192],
                         start=(kc == 0), stop=(kc == NK - 1))
            for j, (dst, scale) in enumerate(dsts):
                src = pp[:, 0:384].rr("p (q c) -> p q c", q=2)[:, :, j * 64:(j + 1) * 64]
                if scale is None:
                    P.copy(dst[:, g * 2:(g + 1) * 2, 0:64], src, eng=ACT)
                else:
                    P.act(dst[:, g * 2:(g + 1) * 2, 0:64], src, AF.Copy, scale=scale)

    rr2 = [0]

    def run_rr(gens, delays=None):
        gens = list(gens)
        delays = list(delays) if delays is not None else [0] * len(gens)
        live = list(range(len(gens)))
        while live:
            nxt = []
            for i in live:
                if delays[i] > 0:
                    delays[i] -= 1
                    nxt.append(i)
                    continue
                try:
                    next(gens[i])
                    nxt.append(i)
                except StopIteration:
                    pass
            live = nxt

    def lin_chunk_g(n, E, WT, qs, ks, lam, S, Vx, o_out, keepcol, sfin_dst, fix=None, bf=False):
        i2 = rr2[0] % 2
        rr2[0] += 1
        if fix is not None:
            i2 = fix
        sfx = "b" if bf else ""
        QTn = mx["QT" + sfx][:, n * 128:(n + 1) * 128]
        KTn = mx["KT" + sfx][:, n * 128:(n + 1) * 128]
        p1 = ps.next()
        P.mm(p1[:, 0:128], KTn, QTn)
        at = mx[("atb%d" if bf else "attn%d") % i2]
        P.tt(at.v(), p1[:, 0:128], WT, ALU.mult)
        kw = mx[("kwb%d" if bf else "kw%d") % i2]
        P.ts(kw.v(), mx["Ktm"][:, n, :], ks, ALU.mult)
        Sr = S[:, 0:E]
        if bf:
            Vx = mx["Vtb"][:, n, 0:E]
            P.copy(Sbf[i2][:, 0:E], S[:, 0:E], eng=ACT)
            Sr = Sbf[i2][:, 0:E]
        yield
        p2 = ps.next()
        P.mm(p2[:, 0:E], at.v(), Vx)
        P.mm(p2[:, 128:128 + E], QTn, Sr)
        o2 = mx["o2_%d" % i2]
        P.copy(o2[:, 0:E], p2[:, 0:E], eng=ACT)
        P.stt(o_out, p2[:, 128:128 + E], qs, o2[:, 0:E], ALU.mult, ALU.add)
        yield
        p3 = ps.next()
        P.mm(p3[0:64, 0:E], kw.v(), Vx)
        P.stt(S[:, 0:E], S[:, 0:E], lam, p3[0:64, 0:E], ALU.mult, ALU.add)
        if sfin_dst is not None:
            sf = Sfin.next()
            P.copy(sf[:, 0:E], S[:, 0:E])
            sfin_dst(sf)
        P.ts(S[:, 0:E], S[:, 0:E], keepcol, ALU.mult)
        yield

    def lin_chunk(*a, **k):
        for _ in lin_chunk_g(*a, **k):
            pass

    def branch_out(m, pair):
        st = brst.next()
        for g in range(NCH // 4):
            pp = ps.next()
            for q in range(4):
                n = g * 4 + q
                P.tr(pp[:, q * 128:(q + 1) * 128], mx["Opair"][:, n, :], ident)
            P.copy(st[:, g * 512:(g + 1) * 512], pp.v())
        P.dma(SP, br_dram[m, pair], st.v())

    def mixer_D(l):
        lgt = smal[:, 0:8]
        qsd = smal[:, 8:16]
        ksd = smal[:, 16:24]
        lamd = smal[:, 24:32]
        P.act(lgt, lpar[l][:, 0:8], AF.Exp, scale=-1.0)
        P.act(lgt, lgt, AF.Ln, bias=1.0)
        P.ts(lgt, lgt, -1.0, ALU.mult)
        for r in range(8):
            d = r // 4
            rel = cst[:, (C_RELF if d == 0 else C_RELB):(C_RELF if d == 0 else C_RELB) + 128]
            P.act(mx["WTd"][:, r, :], rel, AF.Exp, scale=lgt[:, r:r + 1])
            P.act(qsd[:, r:r + 1], cst[:, C_POS + d:C_POS + d + 1], AF.Exp, scale=lgt[:, r:r + 1])
            P.act(ksd[:, r:r + 1], cst[:, C_POS + 2 + d:C_POS + 3 + d], AF.Exp, scale=lgt[:, r:r + 1])
        P.act(lamd, lgt, AF.Exp, scale=128.0)
        for h in range(4):
            wb = load_blk(l, BLK["d_q"])
            proj_fm(wb, h * 64, 64, mx["QTb"])
            wb = load_blk(l, BLK["d_k"])
            proj_fm(wb, h * 64, 64, mx["KTb"], scale=0.125)
            wb = load_blk(l, BLK["d_tm"] + h)
            proj_tm3(wb, [(mx["Ktm"], 0.125), (mx["Vtb"], None), (mx["Gtm"], None)])
            def chain_D(d):
                r = d * 4 + h
                S = Sst[d]
                P.dma(SP, S[:, 0:64], st_ret_in[l, r])
                od = mx["Od%d" % d]
                order = range(NCH) if d == 0 else range(NCH - 1, -1, -1)
                for n in order:
                    isend = (n % 2 == 1) if d == 0 else (n % 2 == 0)
                    dst = None
                    if isend:
                        seg = n // 2
                        dst = (lambda sf, seg=seg, r=r: P.dma(SP, o_ret[l, seg, r], sf[:, 0:64]))
                    kc = cst[0:64, C_KEEP + d * NCH + n:C_KEEP + d * NCH + n + 1]
                    yield from lin_chunk_g(n, 64, mx["WTd"][:, r, :], qsd[:, r:r + 1], ksd[:, r:r + 1],
                                           lamd[0:64, r:r + 1], S, None, od[:, n, 0:64], kc, dst, fix=d, bf=True)
            run_rr([chain_D(0), chain_D(1)])
            o0 = mx["Od0"][:, :, 0:64]
            P.tt(o0, o0, mx["Od1"][:, :, 0:64], ALU.add)
            post_norm_gate(l, o0, 8, AF.Silu, h)
            if h % 2 == 1:
                branch_out(3, h // 2)

    def post_norm_gate(l, o, gcol, gfunc, h):
        sqo = mx["sqo"]
        P.tt(sqo.v(), o, o, ALU.mult)
        ss = smal[:, 32:32 + NCH]
        P.reduce(ss, sqo.v(), ALU.add)
        P.ts(ss, ss, 1.0 / 64, ALU.mult, EPS, ALU.add)
        P.act(ss, ss, AF.Sqrt)
        P.recip(ss, ss)
        P.tt(sqo.v(), o, ss.rr("p (n o) -> p n o", o=1).bc([128, NCH, 64]), ALU.mult)
        gn = lpar[l][:, gcol:gcol + 64].rr("p (o e) -> p o e", o=1).bc([128, NCH, 64])
        P.tt(sqo.v(), sqo.v(), gn, ALU.mult)
        P.act(mx["Gtm"].v(), mx["Gtm"].v(), gfunc)
        P.tt(mx["Opair"][:, :, (h % 2) * 64:(h % 2) * 64 + 64], sqo.v(), mx["Gtm"].v(), ALU.mult)

    def zero_branch(m):
        for pair in range(2):
            st = brst.next()
            P.memset(st.v(), 0.0)
            P.dma(SP, br_dram[m, pair], st.v())

    def merge(l):
        z, zb, brm = mz["z"], mz["zb"], mz["brm"]
        gs = Ring([mz["gsb0"], mz["gsb1"]])
        for t in range(NTT):
            for m in range(4):
                for pair in range(2):
                    P.dma(SP, brm[:, m * 2 + pair, :], br_dram[m, pair][:, t * 512:(t + 1) * 512])
            for m in range(4):
                wbr = wBR.next()
                P.dma(POOL, wbr.v(), wbr_in[l * 4 + m].rr("p (k c) -> p k c", k=2))
                wbv = wbr.v()
                for gb in range(4):
                    wg = load_blk(l, BLK["merge"] + m * 4 + gb)
                    for cc in range(2):
                        dc = gb * 2 + cc
                        pg = ps.next()
                        for kc in range(NK):
                            P.mm(pg.v(), wg[:, kc, cc * 128:(cc + 1) * 128], hT[t][:, kc, :],
                                 start=(kc == 0), stop=(kc == NK - 1))
                        g = gs.next()
                        P.act(g.v(), pg.v(), AF.Sigmoid)
                        pp = ps.next()
                        for kc in range(2):
                            P.mm(pp.v(), wbv[:, kc, dc * 128:(dc + 1) * 128], brm[:, m * 2 + kc, :],
                                 start=(kc == 0), stop=(kc == 1))
                        if m == 0:
                            P.tt(z[:, dc, :], g.v(), pp.v(), ALU.mult)
                        else:
                            P.tt(g.v(), g.v(), pp.v(), ALU.mult)
                            P.tt(z[:, dc, :], z[:, dc, :], g.v(), ALU.add)
            P.copy(zb.v(), z.v(), eng=ACT)
            for ob in range(4):
                wo = wA.next()
                P.dma(POOL, wo.v(), wout_in[l * 4 + ob].rr("p (k c) -> p k c", k=NK))
                for cc in range(2):
                    dc = ob * 2 + cc
                    py = ps.next()
                    for kc in range(NK):
                        P.mm(py.v(), wo[:, kc, cc * 128:(cc + 1) * 128], zb[:, kc, :], start=(kc == 0), stop=(kc == NK - 1))
                    P.stt(xT[t][:, dc, :], py.v(), gate[l][:, 8 + dc:8 + dc + 1], xT[t][:, dc, :], ALU.mult, ALU.add)


    ONESF = cst[:, C_ONES:C_ONES + 128]
    TRIF = cst[:, C_TRIF:C_TRIF + 128]
    TRIB = cst[:, C_TRIB:C_TRIB + 128]
    NEGM = [cst[:, C_NEGF:C_NEGF + 128], cst[:, C_NEGB:C_NEGB + 128]]

    def GTs(i, w=8):
        return mx["GT"][:, i, 0:NCH * 8].rr("p (n r) -> p n r", r=8) if w == 8 else None

    def bc8(col):
        return lpar1[:, col:col + 8].rr("p (o r) -> p o r", o=1).bc([128, NCH, 8])

    def cums(src, dP, dB, dT):
        for lhs, dst in ((TRIF, dP), (TRIB, dB), (ONESF, dT)):
            pp = ps.next()
            P.mm(pp[:, 0:NCH * 8], lhs, src)
            P.copy(dst, pp[:, 0:NCH * 8], eng=ACT)

    def blend_dir(dst, aF, aB):
        P.tt(dst, aF, aB, ALU.subtract)
        P.tt(dst, dst, bc8(L_ISF), ALU.mult)
        P.tt(dst, dst, aB, ALU.add)

    def rows_of(src_slot_view_fn, dst_rows, reduce_max):
        for g in range(NCH // 4):
            pp = ps.next()
            for q in range(4):
                n = g * 4 + q
                P.tr(pp[0:8, q * 128:(q + 1) * 128], src_slot_view_fn(n), ident)
            v = pp[0:8, :].rr("p (q t) -> p q t", q=4)
            if reduce_max:
                P.reduce(dst_rows[:, g * 4:(g + 1) * 4], v, ALU.max)
            else:
                P.copy(dst_rows[:, g * 4:(g + 1) * 4], v[:, :, 0])

    def mixer_C(l):
        G16 = mx["GT"][:, 0:2, :].rr("p a b -> p (a b)").rr("p (n c) -> p n c", c=16)
        IG, LF, bP, bb, TOT, QS, KS, LAM = [GTs(i) for i in range(2, 10)]
        flat = lambda i: mx["GT"][:, i, 0:NCH * 8]
        wb = load_blk(l, BLK["c_g"])
        proj_tm(wb, 0, 16, G16, 16)
        P.tt(IG, G16[:, :, 0:8], bc8(L_MLI), ALU.add)
        P.tt(LF, G16[:, :, 8:16], bc8(L_MLF), ALU.add)
        P.act(LF, LF, AF.Exp, scale=-1.0)
        P.act(LF, LF, AF.Ln, bias=1.0)
        P.ts(LF, LF, -1.0, ALU.mult)
        cums(flat(3), flat(4), flat(7), flat(6))
        blend_dir(bb, bP, QS)
        P.act(QS, bb, AF.Exp)
        P.tt(IG, IG, bb, ALU.subtract)
        P.tt(KS, IG, TOT, ALU.add)
        import os
        DBG = int(os.environ.get("DBG_C", "9"))
        if DBG < 2:
            return
        Xr = rows[:, 0:NCH]
        Tr = rows[:, 16:16 + NCH]
        rows_of(lambda n: KS[:, n, :], Xr, True)
        rows_of(lambda n: TOT[:, n, :], Tr, False)
        X2 = Xr.rr("p (s c) -> p s c", c=2)
        T2 = Tr.rr("p (s c) -> p s c", c=2)
        Bt = rows[:, 32:32 + NSEG]
        ta = rows[:, 48:48 + NSEG]
        mF = rows[:, 64:64 + NSEG]
        mB = rows[:, 80:80 + NSEG]
        mr = rows[:, 96:96 + NSEG]
        P.tt(Bt, T2[:, :, 0], T2[:, :, 1], ALU.add)
        P.tt(ta, X2[:, :, 0], T2[:, :, 1], ALU.add)
        P.tt(mF, Bt, X2[:, :, 1], ALU.max)
        P.tt(mF, mF, ta, ALU.max)
        P.tt(ta, X2[:, :, 1], T2[:, :, 0], ALU.add)
        P.tt(mB, Bt, X2[:, :, 0], ALU.max)
        P.tt(mB, mB, ta, ALU.max)
        P.tt(mr, mF, mB, ALU.subtract)
        P.ts(mr, mr, sel[:, 512:513], ALU.mult)
        P.tt(mr, mr, mB, ALU.add)
        if DBG < 3:
            return
        P.dma(SP, o_m[l], mr)
        if DBG < 4:
            return
        em = rows[:, 112:112 + NSEG]
        P.act(em, mr, AF.Exp, scale=-1.0)
        pp = ps.next()
        for r in range(8):
            P.mm(pp[0:64, r * NSEG:(r + 1) * NSEG], sel[:, r * 64:(r + 1) * 64], em)
        P.copy(embc[:, 0:8 * NSEG], pp[0:64, 0:8 * NSEG])
        P.act(KS, KS, AF.Exp)
        P.act(LAM, TOT, AF.Exp)
        P.dma(SP, em0.v(), st_m_in[l])
        P.act(em0.v(), em0.v(), AF.Exp)
        if DBG < 5:
            return
        for h in range(4):
            wb = load_blk(l, BLK["c_q"])
            proj_fm(wb, h * 64, 64, mx["QTb"], scale=0.125)
            wb = load_blk(l, BLK["c_k"])
            proj_fm(wb, h * 64, 64, mx["KTb"])
            wb = load_blk(l, BLK["c_tm"] + h)
            P.memset(mx["Vtb"][:, :, 64:65], 1.0)
            proj_tm3(wb, [(mx["Ktm"], None), (mx["Vtb"], None), (mx["Gtm"], None)])
            def chain_C(d):
                r = d * 4 + h
                S = Sst[d]
                P.dma(SP, S.v(), st_C_in[l, r])
                P.ts(S.v(), S.v(), em0[0:64, r:r + 1], ALU.mult)
                od = mx["Od%d" % d]
                order = range(NCH) if d == 0 else range(NCH - 1, -1, -1)
                for n in order:
                    dg = mx["dg%d" % d]
                    wt = mx["wt%d" % d]
                    P.ts(dg.v(), ident, bb[:, n, r:r + 1], ALU.mult)
                    pw = ps.next()
                    P.mm(pw[:, 0:128], ONESF, dg.v(), start=True, stop=False)
                    P.mm(pw[:, 0:128], ident, NEGM[d], start=False, stop=True)
                    P.act(wt.v(), pw[:, 0:128], AF.Exp, bias=IG[:, n, r:r + 1])
                    yield
                    isend = (n % 2 == 1) if d == 0 else (n % 2 == 0)
                    dst = None
                    if isend:
                        seg = n // 2

                        def dst(sf, seg=seg, r=r):
                            P.ts(sf.v(), sf.v(), embc[:, r * NSEG + seg:r * NSEG + seg + 1], ALU.mult)
                            P.dma(SP, o_C[l, seg, r], sf.v())
                    kc = cst[0:64, C_KEEP + d * NCH + n:C_KEEP + d * NCH + n + 1]
                    yield from lin_chunk_g(n, 65, wt.v(), QS[:, n, r:r + 1], KS[:, n, r:r + 1], LAM[0:64, n, r:r + 1],
                                           S, None, od[:, n, 0:65], kc, dst, fix=d, bf=True)
            run_rr([chain_C(0), chain_C(1)])
            for d in range(2):
                od = mx["Od%d" % d]
                den = od[:, :, 64:65]
                P.act(den, den, AF.Abs)
                P.ts(den, den, 1.0, ALU.max)
                P.recip(den, den)
                P.tt(od[:, :, 0:64], od[:, :, 0:64], den.bc([128, NCH, 64]), ALU.mult)
            o0 = mx["Od0"][:, :, 0:64]
            P.tt(o0, o0, mx["Od1"][:, :, 0:64], ALU.add)
            post_norm_gate(l, o0, L_MLG, AF.Sigmoid, h)
            if h % 2 == 1:
                branch_out(2, h // 2)


    NOTI = cst[:, C_NOTI:C_NOTI + 128]

    def mixer_A(l):
        G16 = mx["GT"][:, 0:2, :].rr("p a b -> p (a b)").rr("p (n c) -> p n c", c=16)
        SQB, NCG, SQBE, CG, TOT, QS, KS, LAM = [GTs(i) for i in range(2, 10)]
        flat = lambda i: mx["GT"][:, i, 0:NCH * 8]
        wb = load_blk(l, BLK["a_g"])
        proj_tm(wb, 0, 16, G16, 16)
        negA = smal[:, 40:48]
        P.act(negA, lpar1[:, L_DNA:L_DNA + 8], AF.Exp)
        P.ts(negA, negA, -1.0, ALU.mult)
        P.act(SQB, G16[:, :, 0:8], AF.Sigmoid)
        P.act(SQB, SQB, AF.Sqrt)
        P.tt(NCG, G16[:, :, 8:16], bc8(L_DNB), ALU.add)
        P.act(NCG, NCG, AF.Exp)
        P.act(NCG, NCG, AF.Ln, bias=1.0)
        P.tt(NCG, NCG, negA.rr("p (o r) -> p o r", o=1).bc([128, NCH, 8]), ALU.mult)
        cums(flat(3), flat(4), flat(7), flat(6))
        blend_dir(CG, SQBE, QS)
        P.act(QS, CG, AF.Exp)
        P.tt(SQBE, SQB, QS, ALU.mult)
        P.tt(KS, TOT, CG, ALU.subtract)
        P.act(KS, KS, AF.Exp)
        P.act(LAM, TOT, AF.Exp)
        P.ts(NCG, CG, -1.0, ALU.mult)
        I64 = cst[0:64, C_ID:C_ID + 64]
        O64 = cst[0:64, C_ONES:C_ONES + 64]
        for h in range(4):
            for xi, (bn, dstn) in enumerate((("a_q", "QT"), ("a_k", "KT"), ("a_v", None))):
                raw = mx["raw"]
                P.memset(raw[:, 0:1], 0.0)
                P.memset(raw[:, T + 1:T + 2], 0.0)
                wb = load_blk(l, BLK[bn])
                proj_fm(wb, h * 64, 64, raw[:, 1:T + 1])
                cw = lambda j: lpar1[0:64, L_CONV + (xi * 4 + h) * 3 + j:L_CONV + (xi * 4 + h) * 3 + j + 1]
                nw0 = smal[0:64, 48:49]
                nw2 = smal[0:64, 49:50]
                P.ts(nw0, cw(0), cst[0:64, C_PFLAG:C_PFLAG + 1], ALU.mult, -1.0, ALU.mult)
                P.ts(nw2, cw(2), cst[0:64, C_PFLAG:C_PFLAG + 1], ALU.mult, -1.0, ALU.mult)
                for t in range(NTT):
                    off = t * 512
                    st_ = tmpf.next()
                    st = st_[0:64, :]
                    P.ts(st, raw[:, off + 1:off + 513], cw(1), ALU.mult)
                    P.stt(st, raw[:, off:off + 512], cw(0), st, ALU.mult, ALU.add)
                    P.stt(st, raw[:, off + 2:off + 514], cw(2), st, ALU.mult, ALU.add)
                    sv = st.rr("p (a b) -> p a b", b=256)
                    r0 = raw[:, off:off + 512].rr("p (a b) -> p a b", b=256)[:, :, 0]
                    r2 = raw[:, off + 2:off + 514].rr("p (a b) -> p a b", b=256)[:, :, 255]
                    P.stt(sv[:, :, 0], r0, nw0, sv[:, :, 0], ALU.mult, ALU.add)
                    P.stt(sv[:, :, 255], r2, nw2, sv[:, :, 255], ALU.mult, ALU.add)
                    if dstn is not None:
                        dv = mx[dstn][:, off:off + 512]
                        P.act(dv, st, AF.Silu)
                        s2_ = tmpf.next()
                        s2 = s2_[0:64, :]
                        P.act(s2, dv, AF.Square)
                        pn = ps.next()
                        P.mm(pn[0:64, :], O64, s2)
                        P.ts(s2, pn[0:64, :], EPS, ALU.add)
                        P.act(s2, s2, AF.Sqrt)
                        P.recip(s2, s2)
                        if xi == 0:
                            P.stt(dv, dv, 0.125, s2, ALU.mult, ALU.mult)
                        else:
                            P.tt(dv, dv, s2, ALU.mult)
                    else:
                        P.act(st, st, AF.Silu)
                        pt = ps.next()
                        for q in range(4):
                            P.tr(pt[:, q * 64:(q + 1) * 64], st[:, q * 128:(q + 1) * 128], I64)
                        P.copy(mx["Vtm"][:, t * 4:(t + 1) * 4, 0:64], pt[:, 0:256].rr("p (q c) -> p q c", q=4), eng=ACT)
            for g in range(NCH // 4):
                pt = ps.next()
                for q in range(4):
                    n = g * 4 + q
                    P.tr(pt[:, q * 64:(q + 1) * 64], mx["KT"][:, n * 128:(n + 1) * 128], I64)
                P.copy(mx["Ktm"][:, g * 4:(g + 1) * 4, :], pt[:, 0:256].rr("p (q c) -> p q c", q=4), eng=ACT)
            stt_ = {"pre": [0] * (NCH + 2), "scan": 0}
            ident2 = Vw(cst, cst.ap[:, C_ID:C_ID + 128].rearrange("p (o c) -> p o c", o=1).broadcast_to([128, 2, 128]))
            noti2 = Vw(cst, cst.ap[:, C_NOTI:C_NOTI + 128].rearrange("p (o c) -> p o c", o=1).broadcast_to([128, 2, 128]))

            def chunks_of(k):
                return (k, NCH - 1 - k)

            def pre_W(p):
                psp = Ring(ps.tiles[2 * p:2 * p + 2])
                for k in range(p, NCH, 2):
                    while stt_["scan"] < k - 1:
                        yield
                    ns = chunks_of(k)
                    rs = (h, 4 + h)
                    dg, Y, kh, khT, u = [ax["%s_%d" % (nm_, p)] for nm_ in ("dg", "Y", "kh", "khT", "u")]
                    wt = mx["wtW_%d" % p]
                    wT = mx["wTW_%d" % p]
                    for d in range(2):
                        P.ts(dg[:, d, :], ident, CG[:, ns[d], rs[d]:rs[d] + 1], ALU.mult)
                        P.ts(kh[:, d, :], mx["Ktm"][:, ns[d], :], SQB[:, ns[d], rs[d]:rs[d] + 1], ALU.mult)
                    yield
                    pw = psp.next()
                    pk = psp.next()
                    for d in range(2):
                        P.mm(pw[:, d * 128:(d + 1) * 128], ONESF, dg[:, d, :], start=True, stop=False)
                        P.mm(pw[:, d * 128:(d + 1) * 128], ident, NEGM[d], start=False, stop=True)
                        P.tr(pk[0:64, d * 128:(d + 1) * 128], kh[:, d, :], ident)
                    yield
                    for d in range(2):
                        P.act(wt[:, d, :], pw[:, d * 128:(d + 1) * 128], AF.Exp, bias=NCG[:, ns[d], rs[d]:rs[d] + 1])
                    P.copy(khT.v(), pk[0:64, 0:256].rr("p (d c) -> p d c", d=2))
                    for d in range(2):
                        P.ts(Y[:, d, 0:64], mx["Vtm"][:, ns[d], 0:64], SQB[:, ns[d], rs[d]:rs[d] + 1], ALU.mult)
                        P.ts(Y[:, d, 64:128], mx["Ktm"][:, ns[d], :], SQBE[:, ns[d], rs[d]:rs[d] + 1], ALU.mult)
                    yield
                    pg = psp.next()
                    for d in range(2):
                        P.mm(pg[:, d * 128:(d + 1) * 128], khT[:, d, :], khT[:, d, :])
                    yield
                    XT = ax["XTa_%d" % p]
                    X = ax["Xa_%d" % p]
                    P.tt(XT.v(), pg[:, 0:256].rr("p (d c) -> p d c", d=2), wt.v(), ALU.mult)
                    P.tt(XT.v(), XT.v(), noti2, ALU.mult)
                    yield
                    px = psp.next()
                    py = psp.next()
                    for d in range(2):
                        P.tr(px[:, d * 128:(d + 1) * 128], XT[:, d, :], ident)
                        P.mm(py[:, d * 128:(d + 1) * 128], XT[:, d, :], Y[:, d, :])
                    yield
                    P.copy(X.v(), px[:, 0:256].rr("p (d c) -> p d c", d=2), eng=ACT)
                    P.tt(Y.v(), Y.v(), py[:, 0:256].rr("p (d c) -> p d c", d=2), ALU.subtract)
                    yield
                    for lev in range(6):
                        Xn = ax["Xb_%d" % p] if X is ax["Xa_%d" % p] else ax["Xa_%d" % p]
                        XTn = ax["XTb_%d" % p] if XT is ax["XTa_%d" % p] else ax["XTa_%d" % p]
                        p2x = psp.next()
                        p2y = psp.next()
                        for d in range(2):
                            if lev < 5:
                                P.mm(p2x[:, d * 128:(d + 1) * 128], XT[:, d, :], X[:, d, :])
                            P.mm(p2y[:, d * 128:(d + 1) * 128], X[:, d, :], XT[:, d, :])
                        yield
                        if lev < 5:
                            P.copy(Xn.v(), p2x[:, 0:256].rr("p (d c) -> p d c", d=2), eng=ACT)
                        P.copy(XTn.v(), p2y[:, 0:256].rr("p (d c) -> p d c", d=2))
                        X, XT = Xn, XTn
                        yield
                        py = psp.next()
                        for d in range(2):
                            P.mm(py[:, d * 128:(d + 1) * 128], XT[:, d, :], Y[:, d, :])
                        yield
                        P.tt(Y.v(), Y.v(), py[:, 0:256].rr("p (d c) -> p d c", d=2), ALU.add)
                        yield
                    for d in range(2):
                        P.ts(u[:, d, :], Y[:, d, 0:64], SQB[:, ns[d], rs[d]:rs[d] + 1], ALU.mult)
                        P.ts(kh[:, d, :], Y[:, d, 64:128], SQB[:, ns[d], rs[d]:rs[d] + 1], ALU.mult)
                    yield
                    pk2 = psp.next()
                    for d in range(2):
                        P.tr(pk2[0:64, d * 128:(d + 1) * 128], kh[:, d, :], ident)
                    yield
                    P.copy(wT.v(), pk2[0:64, 0:256].rr("p (d c) -> p d c", d=2), eng=ACT)
                    stt_["pre"][k] = 1
                    yield

            def scan_W():
                pss = Ring(ps.tiles[4:6])
                rs = (h, 4 + h)
                Ss = (Sst[0], Sst[1])
                for d in range(2):
                    P.dma(SP, Ss[d][:, 0:64], st_dl_in[l, rs[d]])
                ods = (mx["Od0"], mx["Od1"])
                for k in range(NCH):
                    while not stt_["pre"][k]:
                        yield
                    p = k % 2
                    ns = chunks_of(k)
                    wt = mx["wtW_%d" % p]
                    wT = mx["wTW_%d" % p]
                    u = ax["u_%d" % p]
                    vnew = ax["vnew"]
                    at = mx["atW"]
                    pv = pss.next()
                    p1 = pss.next()
                    for d in range(2):
                        P.mm(pv[:, d * 64:(d + 1) * 64], wT[:, d, :], Ss[d][:, 0:64])
                        P.mm(p1[:, d * 128:(d + 1) * 128], mx["KT"][:, ns[d] * 128:(ns[d] + 1) * 128],
                             mx["QT"][:, ns[d] * 128:(ns[d] + 1) * 128])
                    yield
                    P.tt(vnew.v(), u.v(), pv[:, 0:128].rr("p (d c) -> p d c", d=2), ALU.subtract)
                    P.tt(at.v(), p1[:, 0:256].rr("p (d c) -> p d c", d=2), wt.v(), ALU.mult)
                    kws = (mx["kw0"], mx["kw1"])
                    for d in range(2):
                        P.ts(kws[d].v(), mx["Ktm"][:, ns[d], :], KS[:, ns[d], rs[d]:rs[d] + 1], ALU.mult)
                    yield
                    p2 = pss.next()
                    p3 = pss.next()
                    for d in range(2):
                        P.mm(p2[:, d * 64:(d + 1) * 64], at[:, d, :], vnew[:, d, :])
                        P.mm(p2[:, 128 + d * 64:128 + (d + 1) * 64], mx["QT"][:, ns[d] * 128:(ns[d] + 1) * 128], Ss[d][:, 0:64])
                        P.mm(p3[0:64, d * 64:(d + 1) * 64], kws[d].v(), vnew[:, d, :])
                    yield
                    o2 = mx["attn0"]
                    P.copy(o2[:, 0:128], p2[:, 0:128], eng=ACT)
                    for d in range(2):
                        n = ns[d]
                        r = rs[d]
                        S = Ss[d]
                        P.stt(ods[d][:, n, 0:64], p2[:, 128 + d * 64:128 + (d + 1) * 64], QS[:, n, r:r + 1],
                              o2[:, d * 64:(d + 1) * 64], ALU.mult, ALU.add)
                        P.stt(S[:, 0:64], S[:, 0:64], LAM[0:64, n, r:r + 1], p3[0:64, d * 64:(d + 1) * 64], ALU.mult, ALU.add)
                        isend = (n % 2 == 1) if d == 0 else (n % 2 == 0)
                        if isend:
                            sf = Sfin.next()
                            P.copy(sf[:, 0:64], S[:, 0:64])
                            P.dma(SP, o_dl[l, n // 2, r], sf[:, 0:64])
                        P.ts(S[:, 0:64], S[:, 0:64], cst[0:64, C_KEEP + d * NCH + n:C_KEEP + d * NCH + n + 1], ALU.mult)
                    stt_["scan"] = k + 1
                    yield
            run_rr([pre_W(0), pre_W(1), scan_W()], delays=[0, 2, 0])
            wb = load_blk(l, BLK["a_z"])
            proj_tm(wb, h * 64, 64, mx["Gtm"], 64)
            o0 = mx["Od0"][:, :, 0:64]
            P.tt(o0, o0, mx["Od1"][:, :, 0:64], ALU.add)
            post_norm_gate(l, o0, L_DNG, AF.Silu, h)
            if h % 2 == 1:
                branch_out(0, h // 2)


    def mixer_B(l):
        import math
        lam_init = 0.8 - 0.6 * math.exp(-0.3 * l)
        lt = smal[:, 50:52]
        tmpl = bx["rt0"][:, 0:32]
        P.tt(tmpl, lpar1[:, L_LAM:L_LAM + 32], lpar1[:, L_LAM + 32:L_LAM + 64], ALU.mult)
        P.reduce(lt[:, 0:1], tmpl, ALU.add)
        P.tt(tmpl, lpar1[:, L_LAM + 64:L_LAM + 96], lpar1[:, L_LAM + 96:L_LAM + 128], ALU.mult)
        P.reduce(lt[:, 1:2], tmpl, ALU.add)
        P.act(lt, lt, AF.Exp)
        nlam = smal[:, 52:53]
        P.tt(nlam, lt[:, 1:2], lt[:, 0:1], ALU.subtract)
        P.ts(nlam, nlam, -lam_init, ALU.add)
        P.dma(SP, bx["ropet"].v(), rope_in.v())
        qaugs, kaug, Vext = (bx["qaug0"], bx["qaug1"]), bx["kaug"], bx["Vext"]
        P.memset(kaug.v(), 0.0, eng=POOL)
        P.memset(qaugs[0].v(), 0.0, eng=POOL)
        P.memset(qaugs[1].v(), 0.0, eng=POOL)
        for c in range(2):
            P.dma(POOL, qaugs[c][c * 64 + 32:c * 64 + 41, :], qmask_in.v())
            P.dma(POOL, kaug[c * 64 + 32:c * 64 + 41, :], kmask_in.v())
        for h in range(4):
            for c in range(2):
                P.dma(POOL, kaug[c * 64:c * 64 + 32, 0:256], ckT_in[l, h, c])
            P.dma(POOL, Vext[:, 0:2, 0:64], cv_in[l, h].rr("(j p) c -> p j c", p=128))
            P.memset(Vext[:, :, 64:65], 1.0)
            wb = load_blk(l, BLK["b_tm"] + h)
            Vf = bx["Vf"]
            proj_tm3(wb, [(Vf, None), (bx["Qb"], None), (bx["Kb"], None)])
            P.dma(SP, o_v[l, h].rr("(n p) c -> p n c", p=128), Vf.v())
            P.copy(Vext[:, 2:NKT, 0:64], Vf.v())
            for xi, (bn, xb, gcol) in enumerate((("b_q", "Qb", L_QNG), ("b_k", "Kb", L_KNG))):
                X = bx[xb]
                sqo = bx["Ob"]
                Xg = X.v().rr("p n (c e) -> p (n c) e", c=2)
                Sg = sqo.v().rr("p n (c e) -> p (n c) e", c=2)
                P.tt(sqo.v(), X.v(), X.v(), ALU.mult)
                ss = rows_b
                P.reduce(ss, Sg, ALU.add)
                P.ts(ss, ss, 1.0 / 32, ALU.mult, EPS, ALU.add)
                P.act(ss, ss, AF.Sqrt)
                P.recip(ss, ss)
                P.tt(Xg, Xg, ss.rr("p (m o) -> p m o", o=1).bc([128, NCH * 2, 32]), ALU.mult)
                P.tt(Xg, Xg, lpar1[:, gcol:gcol + 32].rr("p (o e) -> p o e", o=1).bc([128, NCH * 2, 32]), ALU.mult)
                rp = bx["ropet"]
                cosv = rp[:, 0, :].rr("p (n a f) -> p n a f", a=2, f=8)
                sinv = rp[:, 1, :].rr("p (n a f) -> p n a f", a=2, f=8)
                t1, t2, t3, t4 = [bx["rt%d" % i].v().rr("p (n a f) -> p n a f", a=2, f=8) for i in range(4)]
                for c in range(2):
                    xv = X[:, :, c * 32:(c + 1) * 32].rr("p n (a q f) -> p n a q f", a=2, q=2)
                    x1 = xv[:, :, :, 0, :]
                    x2 = xv[:, :, :, 1, :]
                    P.tt(t1, x1, cosv, ALU.mult)
                    P.tt(t2, x2, sinv, ALU.mult)
                    P.tt(t3, x2, cosv, ALU.mult)
                    P.tt(t4, x1, sinv, ALU.mult)
                    P.tt(x1, t1, t2, ALU.subtract)
                    P.tt(x2, t3, t4, ALU.add)
                if xi == 1:
                    P.dma(SP, o_k[l, h].rr("(n p) c -> p n c", p=128), X.v())
                coff = 0 if xi == 0 else 256
                for c in range(2):
                    dstT = qaugs[c] if xi == 0 else kaug
                    for g in range(NCH // 4):
                        pt = ps.next()
                        for q in range(4):
                            n = g * 4 + q
                            P.tr(pt[0:32, q * 128:(q + 1) * 128], X[:, n, c * 32:(c + 1) * 32], ident)
                        dd = dstT[c * 64:c * 64 + 32, coff + g * 512:coff + (g + 1) * 512]
                        if xi == 0:
                            P.act(dd, pt[0:32, :], AF.Copy, scale=32 ** -0.5)
                        else:
                            P.copy(dd, pt[0:32, :], eng=ACT)
            PTr = Ring([bx["PT%d" % i] for i in range(4)])
            Ob = bx["Ob"]
            for qb in range(NTT):
                for c in range(2):
                    pacc = pacc_r.next()
                    pend = None
                    for kt in range(NKT + 1):
                        cur = None
                        if kt < NKT:
                            psc = ps.next()
                            P.mm(psc.v(), kaug[:, kt * 128:(kt + 1) * 128], qaugs[c][:, qb * 512:(qb + 1) * 512])
                            pt_ = PTr.next()
                            P.act(pt_.v(), psc.v(), AF.Exp, bias=-4.0)
                            cur = (kt, pt_)
                        if pend is not None:
                            k0, p0 = pend
                            P.mm(pacc[0:65, :], Vext[:, k0, :], p0.v(), start=(k0 == 0), stop=(k0 == NKT - 1))
                        pend = cur
                    ocT = bx["ocT"]
                    P.copy(ocT.v(), pacc[0:65, :], eng=ACT)
                    ptr = ps.next()
                    for qs in range(4):
                        P.tr(ptr[:, qs * 65:(qs + 1) * 65], ocT[:, qs * 128:(qs + 1) * 128], cst[0:65, C_ID:C_ID + 65])
                    oc = bx["octmp"] if c == 0 else bx["oc2"]
                    P.copy(oc.v(), ptr[:, 0:260].rr("p (q e) -> p q e", q=4))
                o1, o2 = bx["octmp"], bx["oc2"]
                P.recip(o1[:, :, 64:65], o1[:, :, 64:65])
                P.recip(o2[:, :, 64:65], o2[:, :, 64:65])
                P.ts(o2[:, :, 64:65], o2[:, :, 64:65], nlam, ALU.mult)
                P.tt(o1[:, :, 0:64], o1[:, :, 0:64], o1[:, :, 64:65].bc([128, 4, 64]), ALU.mult)
                P.tt(o2[:, :, 0:64], o2[:, :, 0:64], o2[:, :, 64:65].bc([128, 4, 64]), ALU.mult)
                P.tt(Ob[:, qb * 4:(qb + 1) * 4, :], o1[:, :, 0:64], o2[:, :, 0:64], ALU.add)
            sqo = bx["Qb"]
            P.tt(sqo.v(), Ob.v(), Ob.v(), ALU.mult)
            ss = smal[:, 32:32 + NCH]
            P.reduce(ss, sqo.v(), ALU.add)
            P.ts(ss, ss, 1.0 / 64, ALU.mult, EPS, ALU.add)
            P.act(ss, ss, AF.Sqrt)
            P.recip(ss, ss)
            P.ts(ss, ss, 1.0 - lam_init, ALU.mult)
            P.tt(sqo.v(), Ob.v(), ss.rr("p (n o) -> p n o", o=1).bc([128, NCH, 64]), ALU.mult)
            gn = lpar1[:, L_DAG:L_DAG + 64].rr("p (o e) -> p o e", o=1).bc([128, NCH, 64])
            P.tt(mx["Opair"][:, :, (h % 2) * 64:(h % 2) * 64 + 64], sqo.v(), gn, ALU.mult)
            if h % 2 == 1:
                branch_out(1, h // 2)

    def mixers(l):
        P.dma(SP, lpar1.v(), lp_in[l])
        for t in range(NTT):
            norm_mod(l, 1, t)
        for m in range(4):
            if m not in MIX_IMPL:
                zero_branch(m)
        if 0 in MIX_IMPL:
            mixer_A(l)
        if 1 in MIX_IMPL:
            mixer_B(l)
        if 2 in MIX_IMPL:
            mixer_C(l)
        if 3 in MIX_IMPL:
            mixer_D(l)
        merge(l)

    MIX_IMPL = set(MIXSEL)
    for l in range(NL):
        ffn(l, 0)
        if stage == "ffn1":
            break
        mixers(l)
        if stage == "mix":
            break
        ffn(l, 1)

    yv = y_out.v().rr("(kc p) t -> p kc t", p=128)
    for t in range(NTT):
        P.dma(SP, yv[:, :, t * 512:(t + 1) * 512], xT[t].v())
    P.emit()
    return nc


def kblocks(W, width):
    K, N = W.shape
    kc = K // 128
    nb = N // width
    return np.ascontiguousarray(W.reshape(kc, 128, nb, width).transpose(2, 1, 0, 3).reshape(nb, 128, kc * width))


def host_weights(inp, NL):
    w = {}
    w["wada"] = np.concatenate([kblocks(inp["w_ada"][l], 256) for l in range(NL)], 0)
    w["bada"] = np.stack([np.ascontiguousarray(inp["b_ada"][l].reshape(72, 128).T) for l in range(NL)])
    w["ng"] = np.stack([np.ascontiguousarray(inp["norm_g"][l].reshape(24, 128).T) for l in range(NL)])
    gu = []
    wd = []
    for l in range(NL):
        for i in range(2):
            g = kblocks(inp["ffn_w_gate"][l, i], 128).reshape(NF, 128, NK, 128)
            u = kblocks(inp["ffn_w_up"][l, i], 128).reshape(NF, 128, NK, 128)
            gu.append(np.concatenate([g, u], axis=3).reshape(NF, 128, NK * 256))
            wd.append(kblocks(inp["ffn_w_down"][l, i], 128))
    w["wgu"] = np.ascontiguousarray(np.concatenate(gu, 0))
    w["wd"] = np.ascontiguousarray(np.concatenate(wd, 0))
    return w


IN_SIZES = (256, 256, 256, 256, 8, 8, 256, 256, 256, 256, 256, 256, 256, 8, 8, 256, 256, 256, 256, 4096)


def host_weights2(inp, NL, w):
    offs = np.concatenate([[0], np.cumsum(IN_SIZES)])
    names = ["a_q", "a_k", "a_v", "a_z", "a_beta", "a_alpha", "b_q", "b_k", "b_v", "c_q", "c_k", "c_v", "c_o",
             "c_i", "c_f", "d_q", "d_k", "d_v", "d_g", "merge"]
    col = {n: int(offs[i]) for i, n in enumerate(names)}
    blocks = []
    for l in range(NL):
        W = inp["w_in"][l]

        def blk(c0, n=256):
            b = np.zeros((1024, 256), np.float32)
            b[:, :n] = W[:, c0:c0 + n]
            return b

        def gblk(c0, c1):
            b = np.zeros((1024, 256), np.float32)
            b[:, 0:8] = W[:, c0:c0 + 8]
            b[:, 8:16] = W[:, c1:c1 + 8]
            return b
        bl = [blk(col["a_q"]), blk(col["a_k"]), blk(col["a_v"]), blk(col["a_z"]), gblk(col["a_beta"], col["a_alpha"]),
              blk(col["b_q"]), blk(col["b_k"]), blk(col["b_v"]),
              blk(col["c_q"]), blk(col["c_k"]), blk(col["c_v"]), blk(col["c_o"]), gblk(col["c_i"], col["c_f"]),
              blk(col["d_q"]), blk(col["d_k"]), blk(col["d_v"]), blk(col["d_g"])]
        bl += [blk(col["merge"] + 256 * i) for i in range(16)]

        def tmblk(names, hh):
            b = np.zeros((1024, 256), np.float32)
            for j, nm in enumerate(names):
                b[:, j * 64:(j + 1) * 64] = W[:, col[nm] + hh * 64:col[nm] + (hh + 1) * 64]
            return b
        for names in (("b_v", "b_q", "b_k"), ("c_k", "c_v", "c_o"), ("d_k", "d_v", "d_g")):
            bl += [tmblk(names, hh) for hh in range(4)]
        blocks += [kblocks(b, 256)[0] for b in bl]
    w["win"] = np.ascontiguousarray(np.stack(blocks))
    w["wbr"] = np.ascontiguousarray(np.stack([inp["w_branch"][l, m].reshape(2, 128, 1024).transpose(1, 0, 2).reshape(128, 2048)
                                              for l in range(NL) for m in range(4)]))
    w["wout"] = np.concatenate([kblocks(inp["w_out"][l], 256) for l in range(NL)], 0)
    lp = np.zeros((NL, 128, LP_W), np.float32)
    for l in range(NL):
        lp[l, :, 0:8] = inp["ret_decay_logit"][l].reshape(8)[None, :]
        lp[l, :, 8:72] = inp["ret_norm_g"][l][None, :]
        lp[l, :, L_MLI:L_MLI + 8] = inp["ml_i_bias"][l].reshape(8)[None, :]
        lp[l, :, L_MLF:L_MLF + 8] = inp["ml_f_bias"][l].reshape(8)[None, :]
        lp[l, :, L_MLG:L_MLG + 64] = inp["ml_norm_g"][l][None, :]
        lp[l, :, L_ISF:L_ISF + 4] = 1.0
        lp[l, :, L_QNG:L_QNG + 32] = inp["da_qn_g"][l][None, :]
        lp[l, :, L_KNG:L_KNG + 32] = inp["da_kn_g"][l][None, :]
        lp[l, :, L_DAG:L_DAG + 64] = inp["da_norm_g"][l][None, :]
        lp[l, :, L_LAM:L_LAM + 128] = inp["da_lambda"][l].reshape(128)[None, :]
        lp[l, :, L_DNG:L_DNG + 64] = inp["dn_norm_g"][l][None, :]
        lp[l, :, L_DNA:L_DNA + 8] = inp["dn_a_log"][l].reshape(8)[None, :]
        lp[l, :, L_DNB:L_DNB + 8] = inp["dn_dt_bias"][l].reshape(8)[None, :]
        cw = inp["dn_conv_w"][l]
        for xi in range(3):
            for hh in range(4):
                for jj in range(3):
                    lp[l, 0:64, L_CONV + (xi * 4 + hh) * 3 + jj] = cw[jj, xi * 256 + hh * 64: xi * 256 + hh * 64 + 64]
    w["lpar"] = lp
    return w


def host_consts(T, is_sample):
    NCH = T // 128
    c = np.zeros((128, CST_W), np.float32)
    c[:, C_ID:C_ID + 128] = np.eye(128, dtype=np.float32)
    j = np.arange(128)[:, None]
    i = np.arange(128)[None, :]
    c[:, C_NEGF:C_NEGF + 128] = np.where(j <= i, 0.0, BIGNEG)
    c[:, C_NEGB:C_NEGB + 128] = np.where(j >= i, 0.0, BIGNEG)
    c[:, C_RELF:C_RELF + 128] = np.where(j <= i, (i - j).astype(np.float32), 1e6)
    c[:, C_RELB:C_RELB + 128] = np.where(j >= i, (j - i).astype(np.float32), 1e6)
    p = np.arange(128, dtype=np.float32)
    c[:, C_POS + 0] = p + 1
    c[:, C_POS + 1] = 128 - p
    c[:, C_POS + 2] = 127 - p
    c[:, C_POS + 3] = p
    c[:, C_PFLAG] = 0.0 if is_sample else 1.0
    c[:, C_ONES:C_ONES + 128] = 1.0
    c[:, C_TRIF:C_TRIF + 128] = (j <= i)
    c[:, C_TRIB:C_TRIB + 128] = (j >= i)
    c[:, C_NOTI:C_NOTI + 128] = (j != i)
    for n in range(NCH):
        c[:, C_KEEP + n] = 1.0 if (is_sample or n % 2 == 0) else 0.0
        c[:, C_KEEP + NCH + n] = 1.0 if (is_sample or n % 2 == 1) else 0.0
    return c


def host_sel():
    s = np.zeros((8, 520), np.float32)
    for r in range(8):
        s[r, r * 64:(r + 1) * 64] = 1.0
    s[0:4, 512] = 1.0
    return s


def host_attn_consts(T, is_sample):
    NCH = T // 128
    t = np.arange(T)
    rope = np.zeros((128, 2, NCH, 2, 8), np.float32)
    rope[:, 0] = 1.0
    qmask = np.zeros((9, T), np.float32)
    kmask = np.zeros((9, T + 256), np.float32)
    if is_sample:
        freqs = (10000.0 ** (-np.arange(8, dtype=np.float32) / 8)).astype(np.float32)
        rows = (t // 64).astype(np.float32)
        cols = (t % 64).astype(np.float32)
        for hf, pos in enumerate((rows, cols)):
            ang = (pos[:, None] * freqs[None, :]).astype(np.float32)
            rope[:, 0, :, hf, :] = np.cos(ang).reshape(NCH, 128, 8).transpose(1, 0, 2)
            rope[:, 1, :, hf, :] = np.sin(ang).reshape(NCH, 128, 8).transpose(1, 0, 2)
    else:
        seg = t // 256
        for a in range(8):
            qmask[a] = np.where(seg == a, 0.0, BIGNEG)
            kmask[a, 256:] = (seg == a)
        qmask[8] = BIGNEG
        kmask[8, 0:256] = 1.0
    return rope.reshape(128, 2, NCH * 16), qmask, kmask


T_SLAB = 2048
N_LAYERS = 2
_NC_CACHE = {}


def kernel(**inp):
    inp = {k: np.asarray(v) for k, v in inp.items()}
    NL = N_LAYERS
    T = T_SLAB
    xp = inp["x_prompt"]
    xs = inp["x_sample"]
    w = host_weights(inp, NL)
    host_weights2(inp, NL, w)
    cst_s = host_consts(T, True)
    cst_p = host_consts(T, False)
    att_s = host_attn_consts(T, True)
    att_p = host_attn_consts(T, False)
    w["sel"] = host_sel()
    zf = lambda *s: np.zeros(s, np.float32)
    in_maps = []
    for c in range(8):
        m = dict(w)
        if c < 2:
            x = xs[c]
            cond = inp["c"][c]
            m["cst"] = cst_s
            m["rope"], m["qmask"], m["kmask"] = att_s
            m["st_ret"] = np.ascontiguousarray(inp["state_ret"][c].reshape(NL, 8, 64, 64))
            m["st_delta"] = np.ascontiguousarray(inp["state_delta"][c].reshape(NL, 8, 64, 64))
            m["st_Cn"] = np.ascontiguousarray(np.concatenate(
                [inp["state_mlstm_C"][c].reshape(NL, 8, 64, 64), inp["state_mlstm_n"][c].reshape(NL, 8, 64, 1)], -1))
            m["st_m"] = np.ascontiguousarray(np.broadcast_to(inp["state_mlstm_m"][c].reshape(NL, 1, 8), (NL, 128, 8)))
            m["ckT"] = np.ascontiguousarray(inp["cache_diff_k"][c].reshape(NL, 4, 256, 2, 32).transpose(0, 1, 3, 4, 2))
            m["cv"] = np.ascontiguousarray(inp["cache_diff_v"][c])
        else:
            s0 = (c - 2) * 8
            if s0 < 32:
                x = xp[s0:s0 + 8].reshape(T, D)
            else:
                x = np.zeros((T, D), np.float32)
            cond = inp["c_ctx"]
            m["cst"] = cst_p
            m["rope"], m["qmask"], m["kmask"] = att_p
            m["st_ret"] = zf(NL, 8, 64, 64)
            m["st_delta"] = zf(NL, 8, 64, 64)
            m["st_Cn"] = zf(NL, 8, 64, 65)
            m["st_m"] = zf(NL, 128, 8)
            m["ckT"] = zf(NL, 4, 2, 32, 256)
            m["cv"] = zf(NL, 4, 256, 64)
        m["xT"] = np.ascontiguousarray(x.T)
        m["cond"] = np.ascontiguousarray(cond.reshape(8, 128).T)
        in_maps.append(m)
    if "nc" not in _NC_CACHE:
        _NC_CACHE["nc"] = build(T, NL, "full")
    nc = _NC_CACHE["nc"]
    res = run_bass_kernel_spmd(nc, in_maps, core_ids=list(range(8)))
    R = res.results
    y_prompt = np.concatenate([R[c]["yT"].T.reshape(8, 256, D) for c in range(2, 6)], 0).astype(np.float32)
    y_sample = np.stack([R[c]["yT"].T for c in range(2)]).astype(np.float32)
    B = 32
    pc = range(2, 6)
    cat = lambda f: np.ascontiguousarray(np.concatenate([f(R[c]) for c in pc], 0)).astype(np.float32)
    new_ret = cat(lambda r: r["o_ret"].transpose(1, 0, 2, 3, 4).reshape(8, NL, 2, 4, 64, 64))
    new_delta = cat(lambda r: r["o_delta"].transpose(1, 0, 2, 3, 4).reshape(8, NL, 2, 4, 64, 64))
    new_C = cat(lambda r: r["o_Cn"][..., 0:64].transpose(1, 0, 2, 3, 4).reshape(8, NL, 2, 4, 64, 64))
    new_n = cat(lambda r: r["o_Cn"][..., 64].transpose(1, 0, 2, 3).reshape(8, NL, 2, 4, 64))
    new_m = cat(lambda r: r["o_m"].transpose(2, 0, 1).reshape(8, NL, 2, 4))
    new_k = cat(lambda r: r["o_k"].reshape(NL, 4, 8, 256, 64).transpose(2, 0, 1, 3, 4))
    new_v = cat(lambda r: r["o_v"].reshape(NL, 4, 8, 256, 64).transpose(2, 0, 1, 3, 4))
    return (y_prompt, y_sample, new_k, new_v, new_delta, new_C, new_n, new_m, new_ret)
```

```python
import numpy as np
import concourse.bass as bass
import concourse.mybir as mybir

F32 = mybir.dt.float32
BF16 = mybir.dt.bfloat16
AF = mybir.ActivationFunctionType
ALU = mybir.AluOpType
AX = mybir.AxisListType

PE, ACT, DVE, POOL, SP = "pe", "act", "dve", "pool", "sp"


class Tl:
    def __init__(self, ap, name=""):
        self.ap = ap
        self.name = name
        self.lw = None
        self.rd = []
        self.dram_out = False
        self.writers = []
        self.grp = None

    def __getitem__(self, idx):
        return Vw(self, self.ap[idx])

    def v(self):
        return Vw(self, self.ap)


class Vw:
    def __init__(self, t, ap):
        self.t = t
        self.ap = ap

    def __getitem__(self, idx):
        return Vw(self.t, self.ap[idx])

    def rr(self, s, **kw):
        return Vw(self.t, self.ap.rearrange(s, **kw))

    def bc(self, shape):
        return Vw(self.t, self.ap.broadcast_to(shape))

    def bitcast(self, dt):
        return Vw(self.t, self.ap.bitcast(dt))


class Op:
    __slots__ = ("eng", "fn", "deps", "id", "is_dma", "sig", "cnt", "dsem", "dval", "dprev")

    def __init__(self, eng, fn, deps, is_dma):
        self.eng = eng
        self.fn = fn
        self.deps = deps
        self.is_dma = is_dma
        self.sig = False
        self.cnt = 0
        self.dsem = None
        self.dval = 0
        self.dprev = 0


class Prog:
    def __init__(self, nc, n_dma_sems=20, same_engine_sync=True):
        self.nc = nc
        self.ops = []
        self.same_engine_sync = same_engine_sync
        self.esem = {}
        self.n_dma_sems = n_dma_sems
        self.dma_rr = {}
        self.final_tokens = []
        self.out_tiles = []

    def sb(self, name, shape, dt=F32):
        h = self.nc.alloc_sbuf_tensor(name, list(shape), dt)
        return Tl(h.ap(), name)

    def ps(self, name, shape, dt=F32):
        h = self.nc.alloc_psum_tensor(name, list(shape), dt)
        return Tl(h.ap(), name)

    def wrap(self, ap, name=""):
        return Tl(ap, name)

    def carve(self, arena_tl, specs, precise=True):
        out = {}
        if getattr(arena_tl, "conservative", False):
            precise = False
        if not hasattr(arena_tl, "members"):
            arena_tl.members = []
        for name, off, shape, dt in specs:
            nb = 4 if dt == F32 else 2
            n = 1
            for d in shape[1:]:
                n *= d
            a = arena_tl.ap[0:shape[0], off // 2: off // 2 + n * nb // 2]
            if dt == F32:
                a = a.bitcast(F32)
            if len(shape) == 3:
                a = a.rearrange("p (a b) -> p a b", a=shape[1])
            elif len(shape) == 4:
                a = a.rearrange("p (a b c) -> p a b c", a=shape[1], b=shape[2])
            t = Tl(a, name)
            lo, hi = off, off + n * nb
            t.grp = [t]
            for (ot, olo, ohi) in arena_tl.members:
                if (not precise) or (lo < ohi and olo < hi):
                    t.grp.append(ot)
                    ot.grp.append(t)
            arena_tl.members.append((t, lo, hi))
            out[name] = t
        return out

    def _rec(self, eng, fn, outs, ins, is_dma=False):
        deps = set()
        for v in ins:
            if v is None:
                continue
            for t in (v.t.grp or [v.t]):
                if t.lw is not None:
                    deps.add(t.lw)
        for v in outs:
            if v.t.dram_out:
                continue
            for t in (v.t.grp or [v.t]):
                if t.lw is not None:
                    deps.add(t.lw)
                deps.update(t.rd)
        op = Op(eng, fn, deps, is_dma)
        op.id = len(self.ops)
        self.ops.append(op)
        for v in ins:
            if v is None:
                continue
            v.t.rd.append(op.id)
        for v in outs:
            if v.t.dram_out:
                v.t.writers.append(op.id)
                continue
            v.t.lw = op.id
            v.t.rd = []
        return op

    def op(self, eng, fn, outs, ins):
        return self._rec(eng, fn, outs, ins)

    def dma(self, q, out, in_, **kw):
        o, i = out.ap, in_.ap
        return self._rec(q, lambda e: e.dma_start(out=o, in_=i, **kw), [out], [in_], is_dma=True)

    def mm(self, out, lhsT, rhs, start=True, stop=True, **kw):
        o, l, r = out.ap, lhsT.ap, rhs.ap
        return self._rec(PE, lambda e: e.matmul(o, l, r, start=start, stop=stop, **kw), [out], [lhsT, rhs])

    def tr(self, out, in_, ident):
        o, i, d = out.ap, in_.ap, ident.ap
        return self._rec(PE, lambda e: e.transpose(o, i, d), [out], [in_, ident])

    def act(self, out, in_, func, bias=None, scale=None, accum=None, eng=ACT):
        o, i = out.ap, in_.ap
        kw = {}
        ins = [in_]
        outs = [out]
        if bias is not None:
            if isinstance(bias, Vw):
                kw["bias"] = bias.ap
                ins.append(bias)
            else:
                kw["bias"] = bias
        if scale is not None:
            if isinstance(scale, Vw):
                kw["scale"] = scale.ap
                ins.append(scale)
            else:
                kw["scale"] = scale
        if accum is not None:
            kw["accum_out"] = accum.ap
            outs.append(accum)
        return self._rec(eng, lambda e: e.activation(o, i, func, **kw), outs, ins)

    def tt(self, out, a, b, op, eng=DVE):
        o, x, y = out.ap, a.ap, b.ap
        return self._rec(eng, lambda e: e.tensor_tensor(o, x, y, op), [out], [a, b])

    def ts(self, out, a, s1, op0, s2=None, op1=None, eng=DVE, accum=None):
        o, x = out.ap, a.ap
        ins = [a]
        outs = [out]
        a1 = s1.ap if isinstance(s1, Vw) else s1
        a2 = s2.ap if isinstance(s2, Vw) else s2
        if isinstance(s1, Vw):
            ins.append(s1)
        if isinstance(s2, Vw):
            ins.append(s2)
        kw = {}
        if op1 is not None:
            kw["op1"] = op1
        if accum is not None:
            kw["accum_out"] = accum.ap
            outs.append(accum)
        return self._rec(eng, lambda e: e.tensor_scalar(o, x, a1, a2, op0, **kw), outs, ins)

    def stt(self, out, a, s, b, op0, op1, eng=DVE):
        o, x, y = out.ap, a.ap, b.ap
        ins = [a, b]
        sa = s.ap if isinstance(s, Vw) else s
        if isinstance(s, Vw):
            ins.append(s)
        return self._rec(eng, lambda e: e.scalar_tensor_tensor(o, x, sa, y, op0, op1), [out], ins)

    def copy(self, out, in_, eng=DVE):
        o, i = out.ap, in_.ap
        if eng == ACT:
            return self._rec(eng, lambda e: e.copy(o, i), [out], [in_])
        return self._rec(eng, lambda e: e.tensor_copy(o, i), [out], [in_])

    def memset(self, out, val, eng=DVE):
        o = out.ap
        return self._rec(eng, lambda e: e.memset(o, val), [out], [])

    def reduce(self, out, in_, op, axis=None, eng=DVE):
        o, i = out.ap, in_.ap
        ax = axis if axis is not None else AX.X
        return self._rec(eng, lambda e: e.tensor_reduce(o, i, ax, op), [out], [in_])

    def recip(self, out, in_):
        o, i = out.ap, in_.ap
        return self._rec(DVE, lambda e: e.reciprocal(o, i), [out], [in_])

    def scan(self, out, d0, d1, init, op0, op1, eng=DVE):
        o, a, b = out.ap, d0.ap, d1.ap
        ins = [d0, d1]
        iv = init.ap if isinstance(init, Vw) else init
        if isinstance(init, Vw):
            ins.append(init)
        return self._rec(eng, lambda e: e.tensor_tensor_scan(o, a, b, iv, op0, op1), [out], ins)

    def mark_output(self, tl):
        tl.dram_out = True
        self.out_tiles.append(tl)

    def emit(self):
        nc = self.nc
        ops = self.ops
        for op in ops:
            for d in op.deps:
                p = ops[d]
                if p.is_dma:
                    continue
                if p.eng == op.eng and not op.is_dma:
                    if p.eng == PE:
                        continue
                    if not self.same_engine_sync:
                        continue
                p.sig = True
        final_deps = set()
        for tl in self.out_tiles:
            final_deps.update(tl.writers)
        for d in final_deps:
            if not ops[d].is_dma:
                ops[d].sig = True
        engs = [PE, ACT, DVE, POOL, SP]
        cnt = {e: 0 for e in engs}
        for op in ops:
            if op.is_dma:
                continue
            if op.sig:
                cnt[op.eng] += 1
                op.cnt = cnt[op.eng]
        from contextlib import ExitStack

        with ExitStack() as st:
            for e in engs:
                self.esem[e] = st.enter_context(nc.semaphore("es_" + e))
            qs = sorted(set(op.eng for op in ops if op.is_dma))
            pools = {}
            for q in qs:
                pools[q] = [st.enter_context(nc.semaphore("ds_%s_%d" % (q, i))) for i in range(self.n_dma_sems)]
            rr = {q: 0 for q in qs}
            tot = {}
            for op in ops:
                if not op.is_dma:
                    continue
                pool = pools[op.eng]
                s = pool[rr[op.eng] % len(pool)]
                rr[op.eng] += 1
                key = (op.eng, rr[op.eng] % len(pool) if False else id(s))
                prev = tot.get(id(s), 0)
                op.dsem = s
                op.dprev = prev
                op.dval = prev + 16
                tot[id(s)] = op.dval
            block = st.enter_context(nc.Block())
            by_eng = {e: [op for op in ops if op.eng == e] for e in engs}

            def run(eng_name, e):
                waited = {}

                def wait(sem, val):
                    k = id(sem)
                    if waited.get(k, 0) >= val:
                        return
                    waited[k] = val
                    e.wait_ge(sem, val)

                for op in by_eng[eng_name]:
                    for d in sorted(op.deps):
                        p = ops[d]
                        if p.is_dma:
                            wait(p.dsem, p.dval)
                        else:
                            if p.eng == eng_name and not op.is_dma:
                                if p.eng == PE or not self.same_engine_sync:
                                    continue
                            wait(self.esem[p.eng], p.cnt)
                    if op.is_dma:
                        if op.dprev > 0:
                            wait(op.dsem, op.dprev)
                        ins = op.fn(e)
                        ins.then_inc(op.dsem, 16)
                    else:
                        ins = op.fn(e)
                        if op.sig:
                            ins.then_inc(self.esem[eng_name], 1)
                if eng_name == SP:
                    for d in sorted(final_deps):
                        p = ops[d]
                        if p.is_dma:
                            wait(p.dsem, p.dval)
                        else:
                            wait(self.esem[p.eng], p.cnt)

            @block.tensor
            def _(e):
                run(PE, e)

            @block.scalar
            def _(e):
                run(ACT, e)

            @block.vector
            def _(e):
                run(DVE, e)

            @block.gpsimd
            def _(e):
                run(POOL, e)

            @block.sync
            def _(e):
                run(SP, e)

from concourse.bass_utils import run_bass_kernel_spmd

D = 1024
NK = 8
FF = 2816
NF = 22
NMOD = 9
EPS = 1e-6
WIN_BLOCKS = 45
BLK = dict(a_q=0, a_k=1, a_v=2, a_z=3, a_g=4, b_q=5, b_k=6, b_v=7, c_q=8, c_k=9, c_v=10, c_o=11, c_g=12,
           d_q=13, d_k=14, d_v=15, d_g=16, merge=17, b_tm=33, c_tm=37, d_tm=41)
CST_W = 1216
LP_W = 544
C_ID, C_NEGF, C_NEGB, C_RELF, C_RELB, C_POS, C_KEEP = 0, 128, 256, 384, 512, 640, 644
C_PFLAG = 676
C_ONES, C_TRIF, C_TRIB, C_NOTI = 704, 832, 960, 1088
L_RETL, L_RETG, L_MLI, L_MLF, L_MLG, L_ISF, L_DNG, L_DNA, L_DNB, L_CONV, L_QNG, L_KNG, L_DAG, L_LAM = 0, 8, 72, 80, 88, 152, 160, 224, 232, 240, 276, 308, 340, 404
BIGNEG = -30000.0


class Ring:
    def __init__(self, tiles):
        self.tiles = tiles
        self.i = 0

    def next(self):
        t = self.tiles[self.i % len(self.tiles)]
        self.i += 1
        return t


def build(T, NL, stage="full", MIXSEL=(0, 1, 2, 3)):
    NTT = T // 512
    NH = max(1, NTT // 2)
    TPH = NTT // NH
    nc = bass.Bass("TRN2", target_bir_lowering=False)
    P = Prog(nc)

    def din(name, shape):
        return P.wrap(nc.dram_tensor(name, list(shape), F32, kind="ExternalInput").ap(), name)

    def dout(name, shape):
        t = P.wrap(nc.dram_tensor(name, list(shape), F32, kind="ExternalOutput").ap(), name)
        P.mark_output(t)
        return t

    x_in = din("xT", [D, T])
    cond_in = din("cond", [128, NK])
    wada_in = din("wada", [NL * 36, 128, NK * 256])
    bada_in = din("bada", [NL, 128, 72])
    ng_in = din("ng", [NL, 128, 24])
    wgu_in = din("wgu", [NL * 2 * NF, 128, NK * 256])
    wd_in = din("wd", [NL * 2 * NK, 128, NF * 128])
    y_out = dout("yT", [D, T])
    NCH = T // 128
    win_in = din("win", [NL * WIN_BLOCKS, 128, NK * 256])
    wbr_in = din("wbr", [NL * 4, 128, 2 * 1024])
    wout_in = din("wout", [NL * 4, 128, NK * 256])
    cst_in = din("cst", [128, CST_W])
    lp_in = din("lpar", [NL, 128, LP_W])
    st_ret_in = din("st_ret", [NL, 8, 64, 64])
    o_ret = dout("o_ret", [NL, NCH // 2, 8, 64, 64])
    NSEG = NCH // 2
    sel_in = din("sel", [8, 520])
    st_C_in = din("st_Cn", [NL, 8, 64, 65])
    st_dl_in = din("st_delta", [NL, 8, 64, 64])
    ckT_in = din("ckT", [NL, 4, 2, 32, 256])
    cv_in = din("cv", [NL, 4, 256, 64])
    rope_in = din("rope", [128, 2, NCH * 16])
    qmask_in = din("qmask", [9, T])
    kmask_in = din("kmask", [9, T + 256])
    o_k = dout("o_k", [NL, 4, T, 64])
    o_v = dout("o_v", [NL, 4, T, 64])
    o_dl = dout("o_delta", [NL, NSEG, 8, 64, 64])
    st_m_in = din("st_m", [NL, 128, 8])
    o_C = dout("o_Cn", [NL, NSEG, 8, 64, 65])
    o_m = dout("o_m", [NL, 8, NSEG])

    xT = [P.sb("xT%d" % t, [128, NK, 512]) for t in range(NTT)]
    hT = [P.sb("hT%d" % t, [128, NK, 512], BF16) for t in range(NTT)]
    ARENA_B = 2 * NF * 512 * TPH + 12 * 1024
    arena = P.sb("arena", [128, ARENA_B // 2], BF16)
    import os
    arena.conservative = os.environ.get("A1_CONS", "0") == "1"
    cv = P.carve(arena, [("aT%d" % t, t * NF * 1024, [128, NF, 512], BF16) for t in range(TPH)])
    aT = [cv["aT%d" % t] for t in range(TPH)]
    cva = P.carve(arena, [("wAda%d" % i, i * NK * 1024, [128, NK, 256], F32) for i in range(2)])
    TB = T * 4
    o = 0
    mspec = []
    for nm, shp in [("QT", [64, T]), ("KT", [64, T]), ("Ktm", [128, NCH, 64]), ("Vtm", [128, NCH, 65]),
                    ("Gtm", [128, NCH, 64]), ("Od0", [128, NCH, 65]), ("Od1", [128, NCH, 65]),
                    ("Opair", [128, NCH, 128]), ("sqo", [128, NCH, 64])]:
        n = 4
        for d in shp[1:]:
            n *= d
        mspec.append((nm, o, shp, F32))
        if nm in ("QT", "KT"):
            mspec.append((nm + "b", o, [64, T], BF16))
        if nm == "Vtm":
            mspec.append(("Vtb", o, [128, NCH, 66], BF16))
        if nm == "Gtm":
            mspec.append(("raw", o, [64, T + 2], F32))
            for p_ in range(2):
                mspec.append(("wtW_%d" % p_, o + p_ * 1024, [128, 2, 128], F32))
                mspec.append(("wTW_%d" % p_, o + 2048 + p_ * 1024, [64, 2, 128], F32))
        if nm == "sqo":
            for i in range(2):
                mspec.append(("dg%d" % i, o + i * 1024, [128, 128], F32))
                mspec.append(("wt%d" % i, o + i * 1024 + 512, [128, 128], F32))
            for i, xn in enumerate(["Xa", "Xb", "XTa", "XTb", "Yd", "khT"]):
                mspec.append((xn, o + 1024 + i * 512, [128, 128], F32))
            mspec.append(("atW", o, [128, 2, 128], F32))
            n = max(n, 4096)
        o += n
    for i in range(2):
        mspec.append(("attn%d" % i, o, [128, 128], F32))
        mspec.append(("atb%d" % i, o, [128, 128], BF16)); o += 512
        mspec.append(("kw%d" % i, o, [128, 64], F32))
        mspec.append(("kwb%d" % i, o, [128, 64], BF16)); o += 256
        mspec.append(("o2_%d" % i, o, [128, 65], F32)); o += 260
        mspec.append(("o3_%d" % i, o, [128, 65], F32)); o += 260
    mspec.append(("WTd", o, [128, 8, 128], F32))
    mspec.append(("GT", o, [128, 10, NCH * 8], F32))
    o += max(4096, 10 * NCH * 32)
    assert o <= ARENA_B, (o, ARENA_B)
    mx = P.carve(arena, mspec)
    NKT = NCH + 2
    offs_ = dict((nm_, off_) for nm_, off_, _, _ in mspec)
    bo = 0
    bspec = []
    for nm, shp, dt_ in [("qaug0", [128, T], BF16), ("qaug1", [128, T], BF16), ("kaug", [128, T + 256], BF16),
                         ("Vext", [128, NKT, 65], BF16), ("Qb", [128, NCH, 64], F32),
                         ("Kb", [128, NCH, 64], F32), ("Ob", [128, NCH, 64], F32), ("octmp", [128, 4, 65], F32),
                         ("oc2", [128, 4, 65], F32), ("ocT", [65, 512], F32)]:
        n = 4 if dt_ == F32 else 2
        for d in shp[1:]:
            n *= d
        n = (n + 3) // 4 * 4
        bspec.append((nm, bo, shp, dt_))
        if nm == "Ob":
            bspec.append(("Vf", bo, [128, NCH, 64], F32))
        bo += n
    for i in range(4):
        bspec.append(("PT%d" % i, bo, [128, 512], BF16))
        bo += 1024
    assert (1 not in MIXSEL) or bo <= offs_["Opair"], (bo, offs_["Opair"])
    so = offs_["sqo"]
    bspec.append(("ropet", so, [128, 2, NCH * 16], F32))
    to = offs_["attn0"]
    for i in range(4):
        bspec.append(("rt%d" % i, to + i * NCH * 64, [128, NCH * 16], F32))
    assert to + 4 * NCH * 64 <= ARENA_B
    bx = P.carve(arena, bspec)
    zsp = [("z", 0, [128, NK, 512], F32), ("zb", NK * 2048, [128, NK, 512], BF16),
           ("brm", NK * 2048 + NK * 1024, [128, 8, 512], BF16), ("gsb0", NK * 2048 + 2 * NK * 1024, [128, 512], F32),
           ("gsb1", NK * 2048 + 2 * NK * 1024 + 2048, [128, 512], F32), ("gp0", NK * 2048 + 2 * NK * 1024 + 4096, [128, 512], F32)]
    mz = P.carve(arena, zsp)
    wA = Ring([P.sb("wA%d" % i, [128, NK, 256], BF16) for i in range(3)])
    A2_B = 2 * NF * 256 + 2 * 2048 + 2 * 1024
    arena2 = P.sb("arena2", [128, A2_B // 2], BF16)
    import os
    arena2.conservative = os.environ.get("A2_CONS", "0") == "1"
    c2 = P.carve(arena2, [("wD%d" % i, i * NF * 256, [128, NF, 128], BF16) for i in range(2)]
                 + [("rstd%d" % i, 2 * NF * 256 + i * 2048, [128, 512], F32) for i in range(2)]
                 + [("sq%d" % i, 2 * NF * 256 + 4096 + i * 1024, [128, 512], BF16) for i in range(2)])
    wD = Ring([c2["wD%d" % i] for i in range(2)])
    a2spec = []
    o2_ = 0
    for p_ in range(2):
        for nm_, sz_, shp_ in [("dg", 1024, [128, 2, 128]), ("XTa", 1024, [128, 2, 128]), ("XTb", 1024, [128, 2, 128]),
                               ("Xa", 1024, [128, 2, 128]), ("Xb", 1024, [128, 2, 128]), ("Y", 1024, [128, 2, 128]),
                               ("khT", 1024, [64, 2, 128]), ("kh", 512, [128, 2, 64]), ("u", 512, [128, 2, 64])]:
            a2spec.append(("%s_%d" % (nm_, p_), o2_, shp_, F32))
            o2_ += sz_
    a2spec.append(("vnew", o2_, [128, 2, 64], F32))
    o2_ += 512
    assert o2_ <= A2_B, (o2_, A2_B)
    ax = P.carve(arena2, a2spec)
    wBR = Ring([P.sb("wBR%d" % i, [128, 2, 1024], BF16) for i in range(1)])
    wAda = Ring([cva["wAda%d" % i] for i in range(2)])
    cst = P.sb("cst_sb", [128, CST_W])
    lpar1 = P.sb("lpar1", [128, LP_W])
    lpar = [lpar1 for l in range(NL)]
    sel = P.sb("sel_sb", [8, 520])
    rows = P.sb("rows_sb", [8, 256])
    embc = P.sb("embc", [64, 64])
    rows_b = P.sb("rows_b", [128, 32]).v()
    em0 = P.sb("em0", [128, 8])
    brst = Ring([P.sb("brst%d" % i, [128, T], BF16) for i in range(1)])
    smal = P.sb("smal", [128, 64])
    Sst = [P.sb("Sst%d" % i, [64, 65]) for i in range(2)]
    Sbf = [P.sb("Sbf%d" % i, [64, 66], BF16) for i in range(2)]
    Sfin = Ring([P.sb("Sfin%d" % i, [64, 65]) for i in range(2)])
    br_dram = P.wrap(nc.dram_tensor("br_scratch", [4, 2, 128, T], BF16,
                                    kind=("ExternalOutput" if stage == "mix" else "Internal")).ap(), "br_scratch")
    ps = Ring([P.ps("pb%d" % i, [128, 512]) for i in range(6)])
    pacc_r = Ring([P.ps("pacc%d" % i, [128, 512]) for i in range(2)])
    sq = Ring([c2["sq%d" % i] for i in range(2)])
    tmpf = Ring([P.sb("tmpf%d" % i, [128, 512]) for i in range(2)])
    rstd_r = Ring([c2["rstd%d" % i] for i in range(2)])
    ones_bf = P.sb("ones_bf", [128, 128], BF16)
    scond = P.sb("scond", [128, NK])
    bada = [P.sb("bada%d" % l, [128, 72]) for l in range(NL)]
    ng = [P.sb("ng%d" % l, [128, 24]) for l in range(NL)]
    mod = [P.sb("mod%d" % l, [128, 72]) for l in range(NL)]
    scale = [P.sb("scale%d" % l, [128, 24]) for l in range(NL)]
    gate = [P.sb("gate%d" % l, [128, 24]) for l in range(NL)]

    P.memset(ones_bf.v(), 1.0)

    xv = x_in.v().rr("(kc p) t -> p kc t", p=128)
    for t in range(NTT):
        P.dma(SP, xT[t].v(), xv[:, :, t * 512:(t + 1) * 512])
    P.dma(SP, scond.v(), cond_in.v())
    P.dma(SP, cst.v(), cst_in.v())
    P.dma(SP, sel.v(), sel_in.v())
    for l in range(NL):
        P.dma(SP, bada[l].v(), bada_in[l])
        P.dma(SP, ng[l].v(), ng_in[l])
    P.act(scond.v(), scond.v(), AF.Silu)

    for l in range(NL):
        pm = pacc_r.next()
        for blk in range(36):
            wb = wAda.next()
            P.dma(SP, wb.v(), wada_in[l * 36 + blk].rr("p (k c) -> p k c", k=NK))
            pr = ps.next()
            for kc in range(NK):
                P.mm(pr[0:1, 0:256], scond[:, kc:kc + 1], wb[:, kc, :], start=(kc == 0), stop=(kc == NK - 1))
            rb = rows[0:1, 0:256]
            P.copy(rb, pr[0:1, 0:256], eng=ACT)
            for cc in range(2):
                col = blk * 2 + cc
                P.mm(pm[:, col:col + 1], rb[0:1, cc * 128:(cc + 1) * 128], cst[0:1, C_ONES:C_ONES + 1])
        P.tt(mod[l].v(), pm[:, 0:72], bada[l].v(), ALU.add)
        for i in range(3):
            P.stt(scale[l][:, i * 8:(i + 1) * 8], mod[l][:, (3 * i + 1) * 8:(3 * i + 2) * 8], 1.0,
                  ng[l][:, i * 8:(i + 1) * 8], ALU.add, ALU.mult)
            P.ts(gate[l][:, i * 8:(i + 1) * 8], mod[l][:, (3 * i + 2) * 8:(3 * i + 3) * 8],
                 0.5 if i != 1 else 1.0, ALU.mult)

    def norm_mod(l, i, t):
        pn = ps.next()
        for kc in range(NK):
            s = sq.next()
            P.act(s.v(), xT[t][:, kc, :], AF.Square)
            P.mm(pn.v(), ones_bf.v(), s.v(), start=(kc == 0), stop=(kc == NK - 1))
        r = rstd_r.next()
        P.ts(r.v(), pn.v(), 1.0 / D, ALU.mult, EPS, ALU.add)
        P.act(r.v(), r.v(), AF.Sqrt)
        P.recip(r.v(), r.v())
        for kc in range(NK):
            tm = tmpf.next()
            P.stt(tm.v(), xT[t][:, kc, :], scale[l][:, i * 8 + kc:i * 8 + kc + 1], r.v(), ALU.mult, ALU.mult)
            P.act(hT[t][:, kc, :], tm.v(), AF.Identity, bias=mod[l][:, 3 * i * 8 + kc:3 * i * 8 + kc + 1])

    def ffn(l, i):
        ni = 0 if i == 0 else 2
        for h in range(NH):
            tts = list(range(h * TPH, (h + 1) * TPH))
            for t in tts:
                norm_mod(l, ni, t)
            for f in range(NF):
                wb = wA.next()
                P.dma(POOL, wb.v(), wgu_in[(l * 2 + i) * NF + f].rr("p (k c) -> p k c", k=NK))
                for j, t in enumerate(tts):
                    pg = ps.next()
                    pu = ps.next()
                    for kc in range(NK):
                        P.mm(pg.v(), wb[:, kc, 0:128], hT[t][:, kc, :], start=(kc == 0), stop=(kc == NK - 1))
                    for kc in range(NK):
                        P.mm(pu.v(), wb[:, kc, 128:256], hT[t][:, kc, :], start=(kc == 0), stop=(kc == NK - 1))
                    sg = tmpf.next()
                    P.act(sg.v(), pg.v(), AF.Silu)
                    P.tt(aT[j][:, f, :], sg.v(), pu.v(), ALU.mult)
            for dc in range(NK):
                wdb = wD.next()
                P.dma(POOL, wdb.v(), wd_in[(l * 2 + i) * NK + dc].rr("p (f c) -> p f c", f=NF))
                for j, t in enumerate(tts):
                    py = ps.next()
                    for fc in range(NF):
                        P.mm(py.v(), wdb[:, fc, :], aT[j][:, fc, :], start=(fc == 0), stop=(fc == NF - 1))
                    P.stt(xT[t][:, dc, :], py.v(), gate[l][:, ni * 8 + dc:ni * 8 + dc + 1], xT[t][:, dc, :],
                          ALU.mult, ALU.add)


    ident = cst[:, C_ID:C_ID + 128]

    def load_blk(l, b):
        wb = wA.next()
        P.dma(POOL, wb.v(), win_in[l * WIN_BLOCKS + b].rr("p (k c) -> p k c", k=NK))
        return wb

    def proj_fm(wb, c0, M, dst, scale=None):
        for t in range(NTT):
            pp = ps.next()
            for kc in range(NK):
                P.mm(pp[0:M, :], wb[:, kc, c0:c0 + M], hT[t][:, kc, :], start=(kc == 0), stop=(kc == NK - 1))
            if scale is None:
                P.copy(dst[0:M, t * 512:(t + 1) * 512], pp[0:M, :], eng=ACT)
            else:
                P.act(dst[0:M, t * 512:(t + 1) * 512], pp[0:M, :], AF.Copy, scale=scale)

    def proj_tm(wb, c0, N, dst, ncol, scale=None):
        for g in range(NCH // 4):
            pp = ps.next()
            for q in range(4):
                n = g * 4 + q
                t, off = n // 4, (n % 4) * 128
                for kc in range(NK):
                    P.mm(pp[:, q * 64:q * 64 + N], hT[t][:, kc, off:off + 128], wb[:, kc, c0:c0 + N],
                         start=(kc == 0), stop=(kc == NK - 1))
            src = pp[:, 0:256].rr("p (q c) -> p q c", q=4)[:, :, 0:N]
            if scale is None:
                P.copy(dst[:, g * 4:(g + 1) * 4, 0:N], src, eng=ACT)
            else:
                P.act(dst[:, g * 4:(g + 1) * 4, 0:N], src, AF.Copy, scale=scale)

    def proj_tm3(wb, dsts):
        for g in range(NCH // 2):
            pp = ps.next()
            for q in range(2):
                n = g * 2 + q
                t, off = n // 4, (n % 4) * 128
                for kc in range(NK):
                    P.mm(pp[:, q * 192:(q + 1) * 192], hT[t][:, kc, off:off + 128], wb[:, kc, 0:192],
                         start=(kc == 0), stop=(kc == NK - 1))
            for j, (dst, scale) in enumerate(dsts):
                src = pp[:, 0:384].rr("p (q c) -> p q c", q=2)[:, :, j * 64:(j + 1) * 64]
                if scale is None:
                    P.copy(dst[:, g * 2:(g + 1) * 2, 0:64], src, eng=ACT)
                else:
                    P.act(dst[:, g * 2:(g + 1) * 2, 0:64], src, AF.Copy, scale=scale)

    rr2 = [0]

    def run_rr(gens, delays=None):
        gens = list(gens)
        delays = list(delays) if delays is not None else [0] * len(gens)
        live = list(range(len(gens)))
        while live:
            nxt = []
            for i in live:
                if delays[i] > 0:
                    delays[i] -= 1
                    nxt.append(i)
                    continue
                try:
                    next(gens[i])
                    nxt.append(i)
                except StopIteration:
                    pass
            live = nxt

    def lin_chunk_g(n, E, WT, qs, ks, lam, S, Vx, o_out, keepcol, sfin_dst, fix=None, bf=False, wcol=None):
        i2 = rr2[0] % 2
        rr2[0] += 1
        if fix is not None:
            i2 = fix
        sfx = "b" if bf else ""
        QTn = mx["QT" + sfx][:, n * 128:(n + 1) * 128]
        KTn = mx["KT" + sfx][:, n * 128:(n + 1) * 128]
        p1 = ps.next()
        P.mm(p1[:, 0:128], KTn, QTn)
        at = mx[("atb%d" if bf else "attn%d") % i2]
        if wcol is None:
            P.tt(at.v(), p1[:, 0:128], WT, ALU.mult)
        else:
            P.stt(at.v(), p1[:, 0:128], wcol, WT, ALU.mult, ALU.mult)
        kw = mx[("kwb%d" if bf else "kw%d") % i2]
        P.ts(kw.v(), mx["Ktm"][:, n, :], ks, ALU.mult)
        Sr = S[:, 0:E]
        if bf:
            Vx = mx["Vtb"][:, n, 0:E]
            P.copy(Sbf[i2][:, 0:E], S[:, 0:E], eng=ACT)
            Sr = Sbf[i2][:, 0:E]
        yield
        p2 = ps.next()
        if wcol is None:
            P.mm(p2[:, 0:E], at.v(), Vx)
            P.mm(p2[:, 128:128 + E], QTn, Sr)
            o2 = mx["o2_%d" % i2]
            P.copy(o2[:, 0:E], p2[:, 0:E], eng=ACT)
            P.stt(o_out, p2[:, 128:128 + E], qs, o2[:, 0:E], ALU.mult, ALU.add)
        else:
            P.mm(p2[:, 0:E], at.v(), Vx, start=True, stop=False)
            P.mm(p2[:, 0:E], QTn, Sr, start=False, stop=True)
            P.ts(o_out, p2[:, 0:E], qs, ALU.mult)
        yield
        p3 = ps.next()
        P.mm(p3[0:64, 0:E], kw.v(), Vx)
        P.stt(S[:, 0:E], S[:, 0:E], lam, p3[0:64, 0:E], ALU.mult, ALU.add)
        if sfin_dst is not None:
            sf = Sfin.next()
            P.copy(sf[:, 0:E], S[:, 0:E])
            sfin_dst(sf)
        P.ts(S[:, 0:E], S[:, 0:E], keepcol, ALU.mult)
        yield

    def lin_chunk(*a, **k):
        for _ in lin_chunk_g(*a, **k):
            pass

    def branch_out(m, pair):
        st = brst.next()
        for g in range(NCH // 4):
            pp = ps.next()
            for q in range(4):
                n = g * 4 + q
                P.tr(pp[:, q * 128:(q + 1) * 128], mx["Opair"][:, n, :], ident)
            P.copy(st[:, g * 512:(g + 1) * 512], pp.v())
        P.dma(SP, br_dram[m, pair], st.v())

    def mixer_D(l):
        lgt = smal[:, 0:8]
        qsd = smal[:, 8:16]
        ksd = smal[:, 16:24]
        lamd = smal[:, 24:32]
        P.act(lgt, lpar[l][:, 0:8], AF.Exp, scale=-1.0)
        P.act(lgt, lgt, AF.Ln, bias=1.0)
        P.ts(lgt, lgt, -1.0, ALU.mult)
        for r in range(8):
            d = r // 4
            rel = cst[:, (C_RELF if d == 0 else C_RELB):(C_RELF if d == 0 else C_RELB) + 128]
            P.act(mx["WTd"][:, r, :], rel, AF.Exp, scale=lgt[:, r:r + 1])
            P.act(qsd[:, r:r + 1], cst[:, C_POS + d:C_POS + d + 1], AF.Exp, scale=lgt[:, r:r + 1])
            P.act(ksd[:, r:r + 1], cst[:, C_POS + 2 + d:C_POS + 3 + d], AF.Exp, scale=lgt[:, r:r + 1])
        P.act(lamd, lgt, AF.Exp, scale=128.0)
        for h in range(4):
            wb = load_blk(l, BLK["d_q"])
            proj_fm(wb, h * 64, 64, mx["QTb"])
            wb = load_blk(l, BLK["d_k"])
            proj_fm(wb, h * 64, 64, mx["KTb"], scale=0.125)
            wb = load_blk(l, BLK["d_tm"] + h)
            proj_tm3(wb, [(mx["Ktm"], 0.125), (mx["Vtb"], None), (mx["Gtm"], None)])
            def chain_D(d):
                r = d * 4 + h
                S = Sst[d]
                P.dma(SP, S[:, 0:64], st_ret_in[l, r])
                od = mx["Od%d" % d]
                order = range(NCH) if d == 0 else range(NCH - 1, -1, -1)
                for n in order:
                    isend = (n % 2 == 1) if d == 0 else (n % 2 == 0)
                    dst = None
                    if isend:
                        seg = n // 2
                        dst = (lambda sf, seg=seg, r=r: P.dma(SP, o_ret[l, seg, r], sf[:, 0:64]))
                    kc = cst[0:64, C_KEEP + d * NCH + n:C_KEEP + d * NCH + n + 1]
                    yield from lin_chunk_g(n, 64, mx["WTd"][:, r, :], qsd[:, r:r + 1], ksd[:, r:r + 1],
                                           lamd[0:64, r:r + 1], S, None, od[:, n, 0:64], kc, dst, fix=d, bf=True)
            run_rr([chain_D(0), chain_D(1)])
            o0 = mx["Od0"][:, :, 0:64]
            P.tt(o0, o0, mx["Od1"][:, :, 0:64], ALU.add)
            post_norm_gate(l, o0, 8, AF.Silu, h)
            if h % 2 == 1:
                branch_out(3, h // 2)

    def post_norm_gate(l, o, gcol, gfunc, h):
        sqo = mx["sqo"]
        P.tt(sqo.v(), o, o, ALU.mult)
        ss = smal[:, 32:32 + NCH]
        P.reduce(ss, sqo.v(), ALU.add)
        P.ts(ss, ss, 1.0 / 64, ALU.mult, EPS, ALU.add)
        P.act(ss, ss, AF.Sqrt)
        P.recip(ss, ss)
        P.tt(sqo.v(), o, ss.rr("p (n o) -> p n o", o=1).bc([128, NCH, 64]), ALU.mult)
        gn = lpar[l][:, gcol:gcol + 64].rr("p (o e) -> p o e", o=1).bc([128, NCH, 64])
        P.tt(sqo.v(), sqo.v(), gn, ALU.mult)
        P.act(mx["Gtm"].v(), mx["Gtm"].v(), gfunc)
        P.tt(mx["Opair"][:, :, (h % 2) * 64:(h % 2) * 64 + 64], sqo.v(), mx["Gtm"].v(), ALU.mult)

    def zero_branch(m):
        for pair in range(2):
            st = brst.next()
            P.memset(st.v(), 0.0)
            P.dma(SP, br_dram[m, pair], st.v())

    def merge(l):
        z, zb, brm = mz["z"], mz["zb"], mz["brm"]
        gs = Ring([mz["gsb0"], mz["gsb1"]])
        for t in range(NTT):
            for m in range(4):
                for pair in range(2):
                    P.dma(SP, brm[:, m * 2 + pair, :], br_dram[m, pair][:, t * 512:(t + 1) * 512])
            for m in range(4):
                wbr = wBR.next()
                P.dma(POOL, wbr.v(), wbr_in[l * 4 + m].rr("p (k c) -> p k c", k=2))
                wbv = wbr.v()
                for gb in range(4):
                    wg = load_blk(l, BLK["merge"] + m * 4 + gb)
                    for cc in range(2):
                        dc = gb * 2 + cc
                        pg = ps.next()
                        for kc in range(NK):
                            P.mm(pg.v(), wg[:, kc, cc * 128:(cc + 1) * 128], hT[t][:, kc, :],
                                 start=(kc == 0), stop=(kc == NK - 1))
                        g = gs.next()
                        P.act(g.v(), pg.v(), AF.Sigmoid)
                        pp = ps.next()
                        for kc in range(2):
                            P.mm(pp.v(), wbv[:, kc, dc * 128:(dc + 1) * 128], brm[:, m * 2 + kc, :],
                                 start=(kc == 0), stop=(kc == 1))
                        if m == 0:
                            P.tt(z[:, dc, :], g.v(), pp.v(), ALU.mult)
                        else:
                            P.tt(g.v(), g.v(), pp.v(), ALU.mult)
                            P.tt(z[:, dc, :], z[:, dc, :], g.v(), ALU.add)
            P.copy(zb.v(), z.v(), eng=ACT)
            for ob in range(4):
                wo = wA.next()
                P.dma(POOL, wo.v(), wout_in[l * 4 + ob].rr("p (k c) -> p k c", k=NK))
                for cc in range(2):
                    dc = ob * 2 + cc
                    py = ps.next()
                    for kc in range(NK):
                        P.mm(py.v(), wo[:, kc, cc * 128:(cc + 1) * 128], zb[:, kc, :], start=(kc == 0), stop=(kc == NK - 1))
                    P.stt(xT[t][:, dc, :], py.v(), gate[l][:, 8 + dc:8 + dc + 1], xT[t][:, dc, :], ALU.mult, ALU.add)


    ONESF = cst[:, C_ONES:C_ONES + 128]
    TRIF = cst[:, C_TRIF:C_TRIF + 128]
    TRIB = cst[:, C_TRIB:C_TRIB + 128]
    NEGM = [cst[:, C_NEGF:C_NEGF + 128], cst[:, C_NEGB:C_NEGB + 128]]

    def GTs(i, w=8):
        return mx["GT"][:, i, 0:NCH * 8].rr("p (n r) -> p n r", r=8) if w == 8 else None

    def bc8(col):
        return lpar1[:, col:col + 8].rr("p (o r) -> p o r", o=1).bc([128, NCH, 8])

    def cums(src, dP, dB, dT):
        for lhs, dst in ((TRIF, dP), (TRIB, dB), (ONESF, dT)):
            pp = ps.next()
            P.mm(pp[:, 0:NCH * 8], lhs, src)
            P.copy(dst, pp[:, 0:NCH * 8], eng=ACT)

    def blend_dir(dst, aF, aB):
        P.tt(dst, aF, aB, ALU.subtract)
        P.tt(dst, dst, bc8(L_ISF), ALU.mult)
        P.tt(dst, dst, aB, ALU.add)

    def rows_of(src_slot_view_fn, dst_rows, reduce_max):
        for g in range(NCH // 4):
            pp = ps.next()
            for q in range(4):
                n = g * 4 + q
                P.tr(pp[0:8, q * 128:(q + 1) * 128], src_slot_view_fn(n), ident)
            v = pp[0:8, :].rr("p (q t) -> p q t", q=4)
            if reduce_max:
                P.reduce(dst_rows[:, g * 4:(g + 1) * 4], v, ALU.max)
            else:
                P.copy(dst_rows[:, g * 4:(g + 1) * 4], v[:, :, 0])

    def mixer_C(l):
        G16 = mx["GT"][:, 0:2, :].rr("p a b -> p (a b)").rr("p (n c) -> p n c", c=16)
        IG, LF, bP, bb, TOT, QS, KS, LAM = [GTs(i) for i in range(2, 10)]
        flat = lambda i: mx["GT"][:, i, 0:NCH * 8]
        wb = load_blk(l, BLK["c_g"])
        proj_tm(wb, 0, 16, G16, 16)
        P.tt(IG, G16[:, :, 0:8], bc8(L_MLI), ALU.add)
        P.tt(LF, G16[:, :, 8:16], bc8(L_MLF), ALU.add)
        P.act(LF, LF, AF.Exp, scale=-1.0)
        P.act(LF, LF, AF.Ln, bias=1.0)
        P.ts(LF, LF, -1.0, ALU.mult)
        cums(flat(3), flat(4), flat(7), flat(6))
        blend_dir(bb, bP, QS)
        P.act(QS, bb, AF.Exp)
        P.tt(IG, IG, bb, ALU.subtract)
        P.tt(KS, IG, TOT, ALU.add)
        import os
        DBG = int(os.environ.get("DBG_C", "9"))
        if DBG < 2:
            return
        Xr = rows[:, 0:NCH]
        Tr = rows[:, 16:16 + NCH]
        rows_of(lambda n: KS[:, n, :], Xr, True)
        rows_of(lambda n: TOT[:, n, :], Tr, False)
        X2 = Xr.rr("p (s c) -> p s c", c=2)
        T2 = Tr.rr("p (s c) -> p s c", c=2)
        Bt = rows[:, 32:32 + NSEG]
        ta = rows[:, 48:48 + NSEG]
        mF = rows[:, 64:64 + NSEG]
        mB = rows[:, 80:80 + NSEG]
        mr = rows[:, 96:96 + NSEG]
        P.tt(Bt, T2[:, :, 0], T2[:, :, 1], ALU.add)
        P.tt(ta, X2[:, :, 0], T2[:, :, 1], ALU.add)
        P.tt(mF, Bt, X2[:, :, 1], ALU.max)
        P.tt(mF, mF, ta, ALU.max)
        P.tt(ta, X2[:, :, 1], T2[:, :, 0], ALU.add)
        P.tt(mB, Bt, X2[:, :, 0], ALU.max)
        P.tt(mB, mB, ta, ALU.max)
        P.tt(mr, mF, mB, ALU.subtract)
        P.ts(mr, mr, sel[:, 512:513], ALU.mult)
        P.tt(mr, mr, mB, ALU.add)
        if DBG < 3:
            return
        P.dma(SP, o_m[l], mr)
        if DBG < 4:
            return
        em = rows[:, 112:112 + NSEG]
        P.act(em, mr, AF.Exp, scale=-1.0)
        pp = ps.next()
        for r in range(8):
            P.mm(pp[0:64, r * NSEG:(r + 1) * NSEG], sel[:, r * 64:(r + 1) * 64], em)
        P.copy(embc[:, 0:8 * NSEG], pp[0:64, 0:8 * NSEG])
        P.act(KS, KS, AF.Exp)
        P.act(LAM, TOT, AF.Exp)
        P.act(LF, IG, AF.Exp)
        P.dma(SP, em0.v(), st_m_in[l])
        P.act(em0.v(), em0.v(), AF.Exp)
        if DBG < 5:
            return
        for h in range(4):
            wb = load_blk(l, BLK["c_q"])
            proj_fm(wb, h * 64, 64, mx["QTb"], scale=0.125)
            wb = load_blk(l, BLK["c_k"])
            proj_fm(wb, h * 64, 64, mx["KTb"])
            wb = load_blk(l, BLK["c_tm"] + h)
            P.memset(mx["Vtb"][:, :, 64:65], 1.0)
            proj_tm3(wb, [(mx["Ktm"], None), (mx["Vtb"], None), (mx["Gtm"], None)])
            def chain_C(d):
                r = d * 4 + h
                S = Sst[d]
                P.dma(SP, S.v(), st_C_in[l, r])
                P.ts(S.v(), S.v(), em0[0:64, r:r + 1], ALU.mult)
                od = mx["Od%d" % d]
                order = range(NCH) if d == 0 else range(NCH - 1, -1, -1)
                for n in order:
                    isend = (n % 2 == 1) if d == 0 else (n % 2 == 0)
                    dst = None
                    if isend:
                        seg = n // 2

                        def dst(sf, seg=seg, r=r):
                            P.ts(sf.v(), sf.v(), embc[:, r * NSEG + seg:r * NSEG + seg + 1], ALU.mult)
                            P.dma(SP, o_C[l, seg, r], sf.v())
                    kc = cst[0:64, C_KEEP + d * NCH + n:C_KEEP + d * NCH + n + 1]
                    yield from lin_chunk_g(n, 65, (TRIF if d == 0 else TRIB), QS[:, n, r:r + 1], KS[:, n, r:r + 1],
                                           LAM[0:64, n, r:r + 1], S, None, od[:, n, 0:65], kc, dst, fix=d, bf=True,
                                           wcol=LF[:, n, r:r + 1])
            run_rr([chain_C(0), chain_C(1)])
            for d in range(2):
                od = mx["Od%d" % d]
                den = od[:, :, 64:65]
                P.act(den, den, AF.Abs)
                P.ts(den, den, 1.0, ALU.max)
                P.recip(den, den)
                P.tt(od[:, :, 0:64], od[:, :, 0:64], den.bc([128, NCH, 64]), ALU.mult)
            o0 = mx["Od0"][:, :, 0:64]
            P.tt(o0, o0, mx["Od1"][:, :, 0:64], ALU.add)
            post_norm_gate(l, o0, L_MLG, AF.Sigmoid, h)
            if h % 2 == 1:
                branch_out(2, h // 2)


    NOTI = cst[:, C_NOTI:C_NOTI + 128]

    def mixer_A(l):
        G16 = mx["GT"][:, 0:2, :].rr("p a b -> p (a b)").rr("p (n c) -> p n c", c=16)
        SQB, NCG, SQBE, CG, TOT, QS, KS, LAM = [GTs(i) for i in range(2, 10)]
        flat = lambda i: mx["GT"][:, i, 0:NCH * 8]
        wb = load_blk(l, BLK["a_g"])
        proj_tm(wb, 0, 16, G16, 16)
        negA = smal[:, 40:48]
        P.act(negA, lpar1[:, L_DNA:L_DNA + 8], AF.Exp)
        P.ts(negA, negA, -1.0, ALU.mult)
        P.act(SQB, G16[:, :, 0:8], AF.Sigmoid)
        P.act(SQB, SQB, AF.Sqrt)
        P.tt(NCG, G16[:, :, 8:16], bc8(L_DNB), ALU.add)
        P.act(NCG, NCG, AF.Exp)
        P.act(NCG, NCG, AF.Ln, bias=1.0)
        P.tt(NCG, NCG, negA.rr("p (o r) -> p o r", o=1).bc([128, NCH, 8]), ALU.mult)
        cums(flat(3), flat(4), flat(7), flat(6))
        blend_dir(CG, SQBE, QS)
        P.act(QS, CG, AF.Exp)
        P.tt(SQBE, SQB, QS, ALU.mult)
        P.tt(KS, TOT, CG, ALU.subtract)
        P.act(KS, KS, AF.Exp)
        P.act(LAM, TOT, AF.Exp)
        P.ts(NCG, CG, -1.0, ALU.mult)
        I64 = cst[0:64, C_ID:C_ID + 64]
        O64 = cst[0:64, C_ONES:C_ONES + 64]
        for h in range(4):
            for xi, (bn, dstn) in enumerate((("a_q", "QT"), ("a_k", "KT"), ("a_v", None))):
                raw = mx["raw"]
                P.memset(raw[:, 0:1], 0.0)
                P.memset(raw[:, T + 1:T + 2], 0.0)
                wb = load_blk(l, BLK[bn])
                proj_fm(wb, h * 64, 64, raw[:, 1:T + 1])
                cw = lambda j: lpar1[0:64, L_CONV + (xi * 4 + h) * 3 + j:L_CONV + (xi * 4 + h) * 3 + j + 1]
                nw0 = smal[0:64, 48:49]
                nw2 = smal[0:64, 49:50]
                P.ts(nw0, cw(0), cst[0:64, C_PFLAG:C_PFLAG + 1], ALU.mult, -1.0, ALU.mult)
                P.ts(nw2, cw(2), cst[0:64, C_PFLAG:C_PFLAG + 1], ALU.mult, -1.0, ALU.mult)
                for t in range(NTT):
                    off = t * 512
                    st_ = tmpf.next()
                    st = st_[0:64, :]
                    P.ts(st, raw[:, off + 1:off + 513], cw(1), ALU.mult)
                    P.stt(st, raw[:, off:off + 512], cw(0), st, ALU.mult, ALU.add)
                    P.stt(st, raw[:, off + 2:off + 514], cw(2), st, ALU.mult, ALU.add)
                    sv = st.rr("p (a b) -> p a b", b=256)
                    r0 = raw[:, off:off + 512].rr("p (a b) -> p a b", b=256)[:, :, 0]
                    r2 = raw[:, off + 2:off + 514].rr("p (a b) -> p a b", b=256)[:, :, 255]
                    P.stt(sv[:, :, 0], r0, nw0, sv[:, :, 0], ALU.mult, ALU.add)
                    P.stt(sv[:, :, 255], r2, nw2, sv[:, :, 255], ALU.mult, ALU.add)
                    if dstn is not None:
                        dv = mx[dstn][:, off:off + 512]
                        P.act(dv, st, AF.Silu)
                        s2_ = tmpf.next()
                        s2 = s2_[0:64, :]
                        P.act(s2, dv, AF.Square)
                        pn = ps.next()
                        P.mm(pn[0:64, :], O64, s2)
                        P.ts(s2, pn[0:64, :], EPS, ALU.add)
                        P.act(s2, s2, AF.Sqrt)
                        P.recip(s2, s2)
                        if xi == 0:
                            P.stt(dv, dv, 0.125, s2, ALU.mult, ALU.mult)
                        else:
                            P.tt(dv, dv, s2, ALU.mult)
                    else:
                        P.act(st, st, AF.Silu)
                        pt = ps.next()
                        for q in range(4):
                            P.tr(pt[:, q * 64:(q + 1) * 64], st[:, q * 128:(q + 1) * 128], I64)
                        P.copy(mx["Vtm"][:, t * 4:(t + 1) * 4, 0:64], pt[:, 0:256].rr("p (q c) -> p q c", q=4), eng=ACT)
            for g in range(NCH // 4):
                pt = ps.next()
                for q in range(4):
                    n = g * 4 + q
                    P.tr(pt[:, q * 64:(q + 1) * 64], mx["KT"][:, n * 128:(n + 1) * 128], I64)
                P.copy(mx["Ktm"][:, g * 4:(g + 1) * 4, :], pt[:, 0:256].rr("p (q c) -> p q c", q=4), eng=ACT)
            stt_ = {"pre": [0] * (NCH + 2), "scan": 0}
            ident2 = Vw(cst, cst.ap[:, C_ID:C_ID + 128].rearrange("p (o c) -> p o c", o=1).broadcast_to([128, 2, 128]))
            noti2 = Vw(cst, cst.ap[:, C_NOTI:C_NOTI + 128].rearrange("p (o c) -> p o c", o=1).broadcast_to([128, 2, 128]))

            def chunks_of(k):
                return (k, NCH - 1 - k)

            def pre_W(p):
                psp = Ring(ps.tiles[2 * p:2 * p + 2])
                for k in range(p, NCH, 2):
                    while stt_["scan"] < k - 1:
                        yield
                    ns = chunks_of(k)
                    rs = (h, 4 + h)
                    dg, Y, kh, khT, u = [ax["%s_%d" % (nm_, p)] for nm_ in ("dg", "Y", "kh", "khT", "u")]
                    wt = mx["wtW_%d" % p]
                    wT = mx["wTW_%d" % p]
                    for d in range(2):
                        P.ts(dg[:, d, :], ident, CG[:, ns[d], rs[d]:rs[d] + 1], ALU.mult)
                        P.ts(kh[:, d, :], mx["Ktm"][:, ns[d], :], SQB[:, ns[d], rs[d]:rs[d] + 1], ALU.mult)
                    yield
                    pw = psp.next()
                    pk = psp.next()
                    for d in range(2):
                        P.mm(pw[:, d * 128:(d + 1) * 128], ONESF, dg[:, d, :], start=True, stop=False)
                        P.mm(pw[:, d * 128:(d + 1) * 128], ident, NEGM[d], start=False, stop=True)
                        P.tr(pk[0:64, d * 128:(d + 1) * 128], kh[:, d, :], ident)
                    yield
                    for d in range(2):
                        P.act(wt[:, d, :], pw[:, d * 128:(d + 1) * 128], AF.Exp, bias=NCG[:, ns[d], rs[d]:rs[d] + 1])
                    P.copy(khT.v(), pk[0:64, 0:256].rr("p (d c) -> p d c", d=2))
                    for d in range(2):
                        P.ts(Y[:, d, 0:64], mx["Vtm"][:, ns[d], 0:64], SQB[:, ns[d], rs[d]:rs[d] + 1], ALU.mult)
                        P.ts(Y[:, d, 64:128], mx["Ktm"][:, ns[d], :], SQBE[:, ns[d], rs[d]:rs[d] + 1], ALU.mult)
                    yield
                    pg = psp.next()
                    for d in range(2):
                        P.mm(pg[:, d * 128:(d + 1) * 128], khT[:, d, :], khT[:, d, :])
                    yield
                    XT = ax["XTa_%d" % p]
                    X = ax["Xa_%d" % p]
                    P.tt(XT.v(), pg[:, 0:256].rr("p (d c) -> p d c", d=2), wt.v(), ALU.mult)
                    P.tt(XT.v(), XT.v(), noti2, ALU.mult)
                    yield
                    px = psp.next()
                    py = psp.next()
                    for d in range(2):
                        P.tr(px[:, d * 128:(d + 1) * 128], XT[:, d, :], ident)
                        P.mm(py[:, d * 128:(d + 1) * 128], XT[:, d, :], Y[:, d, :])
                    yield
                    P.copy(X.v(), px[:, 0:256].rr("p (d c) -> p d c", d=2), eng=ACT)
                    P.tt(Y.v(), Y.v(), py[:, 0:256].rr("p (d c) -> p d c", d=2), ALU.subtract)
                    yield
                    for lev in range(6):
                        Xn = ax["Xb_%d" % p] if X is ax["Xa_%d" % p] else ax["Xa_%d" % p]
                        XTn = ax["XTb_%d" % p] if XT is ax["XTa_%d" % p] else ax["XTa_%d" % p]
                        p2x = psp.next()
                        p2y = psp.next()
                        for d in range(2):
                            if lev < 5:
                                P.mm(p2x[:, d * 128:(d + 1) * 128], XT[:, d, :], X[:, d, :])
                            P.mm(p2y[:, d * 128:(d + 1) * 128], X[:, d, :], XT[:, d, :])
                        yield
                        if lev < 5:
                            P.copy(Xn.v(), p2x[:, 0:256].rr("p (d c) -> p d c", d=2), eng=ACT)
                        P.copy(XTn.v(), p2y[:, 0:256].rr("p (d c) -> p d c", d=2))
                        X, XT = Xn, XTn
                        yield
                        py = psp.next()
                        for d in range(2):
                            P.mm(py[:, d * 128:(d + 1) * 128], XT[:, d, :], Y[:, d, :])
                        yield
                        P.tt(Y.v(), Y.v(), py[:, 0:256].rr("p (d c) -> p d c", d=2), ALU.add)
                        yield
                    for d in range(2):
                        P.ts(u[:, d, :], Y[:, d, 0:64], SQB[:, ns[d], rs[d]:rs[d] + 1], ALU.mult)
                        P.ts(kh[:, d, :], Y[:, d, 64:128], SQB[:, ns[d], rs[d]:rs[d] + 1], ALU.mult)
                    yield
                    pk2 = psp.next()
                    for d in range(2):
                        P.tr(pk2[0:64, d * 128:(d + 1) * 128], kh[:, d, :], ident)
                    yield
                    P.copy(wT.v(), pk2[0:64, 0:256].rr("p (d c) -> p d c", d=2), eng=ACT)
                    stt_["pre"][k] = 1
                    yield

            def scan_W():
                pss = Ring(ps.tiles[4:6])
                rs = (h, 4 + h)
                Ss = (Sst[0], Sst[1])
                for d in range(2):
                    P.dma(SP, Ss[d][:, 0:64], st_dl_in[l, rs[d]])
                ods = (mx["Od0"], mx["Od1"])
                for k in range(NCH):
                    while not stt_["pre"][k]:
                        yield
                    p = k % 2
                    ns = chunks_of(k)
                    wt = mx["wtW_%d" % p]
                    wT = mx["wTW_%d" % p]
                    u = ax["u_%d" % p]
                    vnew = ax["vnew"]
                    at = mx["atW"]
                    pv = pss.next()
                    p1 = pss.next()
                    for d in range(2):
                        P.mm(pv[:, d * 64:(d + 1) * 64], wT[:, d, :], Ss[d][:, 0:64])
                        P.mm(p1[:, d * 128:(d + 1) * 128], mx["KT"][:, ns[d] * 128:(ns[d] + 1) * 128],
                             mx["QT"][:, ns[d] * 128:(ns[d] + 1) * 128])
                    yield
                    P.tt(vnew.v(), u.v(), pv[:, 0:128].rr("p (d c) -> p d c", d=2), ALU.subtract)
                    P.tt(at.v(), p1[:, 0:256].rr("p (d c) -> p d c", d=2), wt.v(), ALU.mult)
                    kws = (mx["kw0"], mx["kw1"])
                    for d in range(2):
                        P.ts(kws[d].v(), mx["Ktm"][:, ns[d], :], KS[:, ns[d], rs[d]:rs[d] + 1], ALU.mult)
                    yield
                    p2 = pss.next()
                    p3 = pss.next()
                    for d in range(2):
                        P.mm(p2[:, d * 64:(d + 1) * 64], at[:, d, :], vnew[:, d, :])
                        P.mm(p2[:, 128 + d * 64:128 + (d + 1) * 64], mx["QT"][:, ns[d] * 128:(ns[d] + 1) * 128], Ss[d][:, 0:64])
                        P.mm(p3[0:64, d * 64:(d + 1) * 64], kws[d].v(), vnew[:, d, :])
                    yield
                    o2 = mx["attn0"]
                    P.copy(o2[:, 0:128], p2[:, 0:128], eng=ACT)
                    for d in range(2):
                        n = ns[d]
                        r = rs[d]
                        S = Ss[d]
                        P.stt(ods[d][:, n, 0:64], p2[:, 128 + d * 64:128 + (d + 1) * 64], QS[:, n, r:r + 1],
                              o2[:, d * 64:(d + 1) * 64], ALU.mult, ALU.add)
                        P.stt(S[:, 0:64], S[:, 0:64], LAM[0:64, n, r:r + 1], p3[0:64, d * 64:(d + 1) * 64], ALU.mult, ALU.add)
                        isend = (n % 2 == 1) if d == 0 else (n % 2 == 0)
                        if isend:
                            sf = Sfin.next()
                            P.copy(sf[:, 0:64], S[:, 0:64])
                            P.dma(SP, o_dl[l, n // 2, r], sf[:, 0:64])
                        P.ts(S[:, 0:64], S[:, 0:64], cst[0:64, C_KEEP + d * NCH + n:C_KEEP + d * NCH + n + 1], ALU.mult)
                    stt_["scan"] = k + 1
                    yield
            run_rr([pre_W(0), pre_W(1), scan_W()], delays=[0, 2, 0])
            wb = load_blk(l, BLK["a_z"])
            proj_tm(wb, h * 64, 64, mx["Gtm"], 64)
            o0 = mx["Od0"][:, :, 0:64]
            P.tt(o0, o0, mx["Od1"][:, :, 0:64], ALU.add)
            post_norm_gate(l, o0, L_DNG, AF.Silu, h)
            if h % 2 == 1:
                branch_out(0, h // 2)


    def mixer_B(l):
        import math
        lam_init = 0.8 - 0.6 * math.exp(-0.3 * l)
        lt = smal[:, 50:52]
        tmpl = bx["rt0"][:, 0:32]
        P.tt(tmpl, lpar1[:, L_LAM:L_LAM + 32], lpar1[:, L_LAM + 32:L_LAM + 64], ALU.mult)
        P.reduce(lt[:, 0:1], tmpl, ALU.add)
        P.tt(tmpl, lpar1[:, L_LAM + 64:L_LAM + 96], lpar1[:, L_LAM + 96:L_LAM + 128], ALU.mult)
        P.reduce(lt[:, 1:2], tmpl, ALU.add)
        P.act(lt, lt, AF.Exp)
        nlam = smal[:, 52:53]
        P.tt(nlam, lt[:, 1:2], lt[:, 0:1], ALU.subtract)
        P.ts(nlam, nlam, -lam_init, ALU.add)
        P.dma(SP, bx["ropet"].v(), rope_in.v())
        qaugs, kaug, Vext = (bx["qaug0"], bx["qaug1"]), bx["kaug"], bx["Vext"]
        P.memset(kaug.v(), 0.0, eng=POOL)
        P.memset(qaugs[0].v(), 0.0, eng=POOL)
        P.memset(qaugs[1].v(), 0.0, eng=POOL)
        for c in range(2):
            P.dma(POOL, qaugs[c][c * 64 + 32:c * 64 + 41, :], qmask_in.v())
            P.dma(POOL, kaug[c * 64 + 32:c * 64 + 41, :], kmask_in.v())
        for h in range(4):
            for c in range(2):
                P.dma(POOL, kaug[c * 64:c * 64 + 32, 0:256], ckT_in[l, h, c])
            P.dma(POOL, Vext[:, 0:2, 0:64], cv_in[l, h].rr("(j p) c -> p j c", p=128))
            P.memset(Vext[:, :, 64:65], 1.0)
            wb = load_blk(l, BLK["b_tm"] + h)
            Vf = bx["Vf"]
            proj_tm3(wb, [(Vf, None), (bx["Qb"], None), (bx["Kb"], None)])
            P.dma(SP, o_v[l, h].rr("(n p) c -> p n c", p=128), Vf.v())
            P.copy(Vext[:, 2:NKT, 0:64], Vf.v())
            for xi, (bn, xb, gcol) in enumerate((("b_q", "Qb", L_QNG), ("b_k", "Kb", L_KNG))):
                X = bx[xb]
                sqo = bx["Ob"]
                Xg = X.v().rr("p n (c e) -> p (n c) e", c=2)
                Sg = sqo.v().rr("p n (c e) -> p (n c) e", c=2)
                P.tt(sqo.v(), X.v(), X.v(), ALU.mult)
                ss = rows_b
                P.reduce(ss, Sg, ALU.add)
                P.ts(ss, ss, 1.0 / 32, ALU.mult, EPS, ALU.add)
                P.act(ss, ss, AF.Sqrt)
                P.recip(ss, ss)
                P.tt(Xg, Xg, ss.rr("p (m o) -> p m o", o=1).bc([128, NCH * 2, 32]), ALU.mult)
                P.tt(Xg, Xg, lpar1[:, gcol:gcol + 32].rr("p (o e) -> p o e", o=1).bc([128, NCH * 2, 32]), ALU.mult)
                rp = bx["ropet"]
                cosv = rp[:, 0, :].rr("p (n a f) -> p n a f", a=2, f=8)
                sinv = rp[:, 1, :].rr("p (n a f) -> p n a f", a=2, f=8)
                t1, t2, t3, t4 = [bx["rt%d" % i].v().rr("p (n a f) -> p n a f", a=2, f=8) for i in range(4)]
                for c in range(2):
                    xv = X[:, :, c * 32:(c + 1) * 32].rr("p n (a q f) -> p n a q f", a=2, q=2)
                    x1 = xv[:, :, :, 0, :]
                    x2 = xv[:, :, :, 1, :]
                    P.tt(t1, x1, cosv, ALU.mult)
                    P.tt(t2, x2, sinv, ALU.mult)
                    P.tt(t3, x2, cosv, ALU.mult)
                    P.tt(t4, x1, sinv, ALU.mult)
                    P.tt(x1, t1, t2, ALU.subtract)
                    P.tt(x2, t3, t4, ALU.add)
                if xi == 1:
                    P.dma(SP, o_k[l, h].rr("(n p) c -> p n c", p=128), X.v())
                coff = 0 if xi == 0 else 256
                for c in range(2):
                    dstT = qaugs[c] if xi == 0 else kaug
                    for g in range(NCH // 4):
                        pt = ps.next()
                        for q in range(4):
                            n = g * 4 + q
                            P.tr(pt[0:32, q * 128:(q + 1) * 128], X[:, n, c * 32:(c + 1) * 32], ident)
                        dd = dstT[c * 64:c * 64 + 32, coff + g * 512:coff + (g + 1) * 512]
                        if xi == 0:
                            P.act(dd, pt[0:32, :], AF.Copy, scale=32 ** -0.5)
                        else:
                            P.copy(dd, pt[0:32, :], eng=ACT)
            PTr = Ring([bx["PT%d" % i] for i in range(4)])
            Ob = bx["Ob"]
            for qb in range(NTT):
                for c in range(2):
                    pacc = pacc_r.next()
                    pend = None
                    for kt in range(NKT + 1):
                        cur = None
                        if kt < NKT:
                            psc = ps.next()
                            P.mm(psc.v(), kaug[:, kt * 128:(kt + 1) * 128], qaugs[c][:, qb * 512:(qb + 1) * 512])
                            pt_ = PTr.next()
                            P.act(pt_.v(), psc.v(), AF.Exp, bias=-4.0)
                            cur = (kt, pt_)
                        if pend is not None:
                            k0, p0 = pend
                            P.mm(pacc[0:65, :], Vext[:, k0, :], p0.v(), start=(k0 == 0), stop=(k0 == NKT - 1))
                        pend = cur
                    ocT = bx["ocT"]
                    P.copy(ocT.v(), pacc[0:65, :], eng=ACT)
                    ptr = ps.next()
                    for qs in range(4):
                        P.tr(ptr[:, qs * 65:(qs + 1) * 65], ocT[:, qs * 128:(qs + 1) * 128], cst[0:65, C_ID:C_ID + 65])
                    oc = bx["octmp"] if c == 0 else bx["oc2"]
                    P.copy(oc.v(), ptr[:, 0:260].rr("p (q e) -> p q e", q=4))
                o1, o2 = bx["octmp"], bx["oc2"]
                P.recip(o1[:, :, 64:65], o1[:, :, 64:65])
                P.recip(o2[:, :, 64:65], o2[:, :, 64:65])
                P.ts(o2[:, :, 64:65], o2[:, :, 64:65], nlam, ALU.mult)
                P.tt(o1[:, :, 0:64], o1[:, :, 0:64], o1[:, :, 64:65].bc([128, 4, 64]), ALU.mult)
                P.tt(o2[:, :, 0:64], o2[:, :, 0:64], o2[:, :, 64:65].bc([128, 4, 64]), ALU.mult)
                P.tt(Ob[:, qb * 4:(qb + 1) * 4, :], o1[:, :, 0:64], o2[:, :, 0:64], ALU.add)
            sqo = bx["Qb"]
            P.tt(sqo.v(), Ob.v(), Ob.v(), ALU.mult)
            ss = smal[:, 32:32 + NCH]
            P.reduce(ss, sqo.v(), ALU.add)
            P.ts(ss, ss, 1.0 / 64, ALU.mult, EPS, ALU.add)
            P.act(ss, ss, AF.Sqrt)
            P.recip(ss, ss)
            P.ts(ss, ss, 1.0 - lam_init, ALU.mult)
            P.tt(sqo.v(), Ob.v(), ss.rr("p (n o) -> p n o", o=1).bc([128, NCH, 64]), ALU.mult)
            gn = lpar1[:, L_DAG:L_DAG + 64].rr("p (o e) -> p o e", o=1).bc([128, NCH, 64])
            P.tt(mx["Opair"][:, :, (h % 2) * 64:(h % 2) * 64 + 64], sqo.v(), gn, ALU.mult)
            if h % 2 == 1:
                branch_out(1, h // 2)

    def mixers(l):
        P.dma(SP, lpar1.v(), lp_in[l])
        for t in range(NTT):
            norm_mod(l, 1, t)
        for m in range(4):
            if m not in MIX_IMPL:
                zero_branch(m)
        if 0 in MIX_IMPL:
            mixer_A(l)
        if 1 in MIX_IMPL:
            mixer_B(l)
        if 2 in MIX_IMPL:
            mixer_C(l)
        if 3 in MIX_IMPL:
            mixer_D(l)
        merge(l)

    MIX_IMPL = set(MIXSEL)
    for l in range(NL):
        ffn(l, 0)
        if stage == "ffn1":
            break
        mixers(l)
        if stage == "mix":
            break
        ffn(l, 1)

    yv = y_out.v().rr("(kc p) t -> p kc t", p=128)
    for t in range(NTT):
        P.dma(SP, yv[:, :, t * 512:(t + 1) * 512], xT[t].v())
    P.emit()
    return nc


def kblocks(W, width):
    K, N = W.shape
    kc = K // 128
    nb = N // width
    return np.ascontiguousarray(W.reshape(kc, 128, nb, width).transpose(2, 1, 0, 3).reshape(nb, 128, kc * width))


def host_weights(inp, NL):
    w = {}
    w["wada"] = np.concatenate([kblocks(inp["w_ada"][l], 256) for l in range(NL)], 0)
    w["bada"] = np.stack([np.ascontiguousarray(inp["b_ada"][l].reshape(72, 128).T) for l in range(NL)])
    w["ng"] = np.stack([np.ascontiguousarray(inp["norm_g"][l].reshape(24, 128).T) for l in range(NL)])
    gu = []
    wd = []
    for l in range(NL):
        for i in range(2):
            g = kblocks(inp["ffn_w_gate"][l, i], 128).reshape(NF, 128, NK, 128)
            u = kblocks(inp["ffn_w_up"][l, i], 128).reshape(NF, 128, NK, 128)
            gu.append(np.concatenate([g, u], axis=3).reshape(NF, 128, NK * 256))
            wd.append(kblocks(inp["ffn_w_down"][l, i], 128))
    w["wgu"] = np.ascontiguousarray(np.concatenate(gu, 0))
    w["wd"] = np.ascontiguousarray(np.concatenate(wd, 0))
    return w


IN_SIZES = (256, 256, 256, 256, 8, 8, 256, 256, 256, 256, 256, 256, 256, 8, 8, 256, 256, 256, 256, 4096)


def host_weights2(inp, NL, w):
    offs = np.concatenate([[0], np.cumsum(IN_SIZES)])
    names = ["a_q", "a_k", "a_v", "a_z", "a_beta", "a_alpha", "b_q", "b_k", "b_v", "c_q", "c_k", "c_v", "c_o",
             "c_i", "c_f", "d_q", "d_k", "d_v", "d_g", "merge"]
    col = {n: int(offs[i]) for i, n in enumerate(names)}
    blocks = []
    for l in range(NL):
        W = inp["w_in"][l]

        def blk(c0, n=256):
            b = np.zeros((1024, 256), np.float32)
            b[:, :n] = W[:, c0:c0 + n]
            return b

        def gblk(c0, c1):
            b = np.zeros((1024, 256), np.float32)
            b[:, 0:8] = W[:, c0:c0 + 8]
            b[:, 8:16] = W[:, c1:c1 + 8]
            return b
        bl = [blk(col["a_q"]), blk(col["a_k"]), blk(col["a_v"]), blk(col["a_z"]), gblk(col["a_beta"], col["a_alpha"]),
              blk(col["b_q"]), blk(col["b_k"]), blk(col["b_v"]),
              blk(col["c_q"]), blk(col["c_k"]), blk(col["c_v"]), blk(col["c_o"]), gblk(col["c_i"], col["c_f"]),
              blk(col["d_q"]), blk(col["d_k"]), blk(col["d_v"]), blk(col["d_g"])]
        bl += [blk(col["merge"] + 256 * i) for i in range(16)]

        def tmblk(names, hh):
            b = np.zeros((1024, 256), np.float32)
            for j, nm in enumerate(names):
                b[:, j * 64:(j + 1) * 64] = W[:, col[nm] + hh * 64:col[nm] + (hh + 1) * 64]
            return b
        for names in (("b_v", "b_q", "b_k"), ("c_k", "c_v", "c_o"), ("d_k", "d_v", "d_g")):
            bl += [tmblk(names, hh) for hh in range(4)]
        blocks += [kblocks(b, 256)[0] for b in bl]
    w["win"] = np.ascontiguousarray(np.stack(blocks))
    w["wbr"] = np.ascontiguousarray(np.stack([inp["w_branch"][l, m].reshape(2, 128, 1024).transpose(1, 0, 2).reshape(128, 2048)
                                              for l in range(NL) for m in range(4)]))
    w["wout"] = np.concatenate([kblocks(inp["w_out"][l], 256) for l in range(NL)], 0)
    lp = np.zeros((NL, 128, LP_W), np.float32)
    for l in range(NL):
        lp[l, :, 0:8] = inp["ret_decay_logit"][l].reshape(8)[None, :]
        lp[l, :, 8:72] = inp["ret_norm_g"][l][None, :]
        lp[l, :, L_MLI:L_MLI + 8] = inp["ml_i_bias"][l].reshape(8)[None, :]
        lp[l, :, L_MLF:L_MLF + 8] = inp["ml_f_bias"][l].reshape(8)[None, :]
        lp[l, :, L_MLG:L_MLG + 64] = inp["ml_norm_g"][l][None, :]
        lp[l, :, L_ISF:L_ISF + 4] = 1.0
        lp[l, :, L_QNG:L_QNG + 32] = inp["da_qn_g"][l][None, :]
        lp[l, :, L_KNG:L_KNG + 32] = inp["da_kn_g"][l][None, :]
        lp[l, :, L_DAG:L_DAG + 64] = inp["da_norm_g"][l][None, :]
        lp[l, :, L_LAM:L_LAM + 128] = inp["da_lambda"][l].reshape(128)[None, :]
        lp[l, :, L_DNG:L_DNG + 64] = inp["dn_norm_g"][l][None, :]
        lp[l, :, L_DNA:L_DNA + 8] = inp["dn_a_log"][l].reshape(8)[None, :]
        lp[l, :, L_DNB:L_DNB + 8] = inp["dn_dt_bias"][l].reshape(8)[None, :]
        cw = inp["dn_conv_w"][l]
        for xi in range(3):
            for hh in range(4):
                for jj in range(3):
                    lp[l, 0:64, L_CONV + (xi * 4 + hh) * 3 + jj] = cw[jj, xi * 256 + hh * 64: xi * 256 + hh * 64 + 64]
    w["lpar"] = lp
    return w


def host_consts(T, is_sample):
    NCH = T // 128
    c = np.zeros((128, CST_W), np.float32)
    c[:, C_ID:C_ID + 128] = np.eye(128, dtype=np.float32)
    j = np.arange(128)[:, None]
    i = np.arange(128)[None, :]
    c[:, C_NEGF:C_NEGF + 128] = np.where(j <= i, 0.0, BIGNEG)
    c[:, C_NEGB:C_NEGB + 128] = np.where(j >= i, 0.0, BIGNEG)
    c[:, C_RELF:C_RELF + 128] = np.where(j <= i, (i - j).astype(np.float32), 1e6)
    c[:, C_RELB:C_RELB + 128] = np.where(j >= i, (j - i).astype(np.float32), 1e6)
    p = np.arange(128, dtype=np.float32)
    c[:, C_POS + 0] = p + 1
    c[:, C_POS + 1] = 128 - p
    c[:, C_POS + 2] = 127 - p
    c[:, C_POS + 3] = p
    c[:, C_PFLAG] = 0.0 if is_sample else 1.0
    c[:, C_ONES:C_ONES + 128] = 1.0
    c[:, C_TRIF:C_TRIF + 128] = (j <= i)
    c[:, C_TRIB:C_TRIB + 128] = (j >= i)
    c[:, C_NOTI:C_NOTI + 128] = (j != i)
    for n in range(NCH):
        c[:, C_KEEP + n] = 1.0 if (is_sample or n % 2 == 0) else 0.0
        c[:, C_KEEP + NCH + n] = 1.0 if (is_sample or n % 2 == 1) else 0.0
    return c


def host_sel():
    s = np.zeros((8, 520), np.float32)
    for r in range(8):
        s[r, r * 64:(r + 1) * 64] = 1.0
    s[0:4, 512] = 1.0
    return s


def host_attn_consts(T, is_sample):
    NCH = T // 128
    t = np.arange(T)
    rope = np.zeros((128, 2, NCH, 2, 8), np.float32)
    rope[:, 0] = 1.0
    qmask = np.zeros((9, T), np.float32)
    kmask = np.zeros((9, T + 256), np.float32)
    if is_sample:
        freqs = (10000.0 ** (-np.arange(8, dtype=np.float32) / 8)).astype(np.float32)
        rows = (t // 64).astype(np.float32)
        cols = (t % 64).astype(np.float32)
        for hf, pos in enumerate((rows, cols)):
            ang = (pos[:, None] * freqs[None, :]).astype(np.float32)
            rope[:, 0, :, hf, :] = np.cos(ang).reshape(NCH, 128, 8).transpose(1, 0, 2)
            rope[:, 1, :, hf, :] = np.sin(ang).reshape(NCH, 128, 8).transpose(1, 0, 2)
    else:
        seg = t // 256
        for a in range(8):
            qmask[a] = np.where(seg == a, 0.0, BIGNEG)
            kmask[a, 256:] = (seg == a)
        qmask[8] = BIGNEG
        kmask[8, 0:256] = 1.0
    return rope.reshape(128, 2, NCH * 16), qmask, kmask


T_SLAB = 2048
N_LAYERS = 2
_NC_CACHE = {}


def kernel(**inp):
    inp = {k: np.asarray(v) for k, v in inp.items()}
    NL = N_LAYERS
    T = T_SLAB
    xp = inp["x_prompt"]
    xs = inp["x_sample"]
    w = host_weights(inp, NL)
    host_weights2(inp, NL, w)
    cst_s = host_consts(T, True)
    cst_p = host_consts(T, False)
    att_s = host_attn_consts(T, True)
    att_p = host_attn_consts(T, False)
    w["sel"] = host_sel()
    zf = lambda *s: np.zeros(s, np.float32)
    in_maps = []
    for c in range(8):
        m = dict(w)
        if c < 2:
            x = xs[c]
            cond = inp["c"][c]
            m["cst"] = cst_s
            m["rope"], m["qmask"], m["kmask"] = att_s
            m["st_ret"] = np.ascontiguousarray(inp["state_ret"][c].reshape(NL, 8, 64, 64))
            m["st_delta"] = np.ascontiguousarray(inp["state_delta"][c].reshape(NL, 8, 64, 64))
            m["st_Cn"] = np.ascontiguousarray(np.concatenate(
                [inp["state_mlstm_C"][c].reshape(NL, 8, 64, 64), inp["state_mlstm_n"][c].reshape(NL, 8, 64, 1)], -1))
            m["st_m"] = np.ascontiguousarray(np.broadcast_to(inp["state_mlstm_m"][c].reshape(NL, 1, 8), (NL, 128, 8)))
            m["ckT"] = np.ascontiguousarray(inp["cache_diff_k"][c].reshape(NL, 4, 256, 2, 32).transpose(0, 1, 3, 4, 2))
            m["cv"] = np.ascontiguousarray(inp["cache_diff_v"][c])
        else:
            s0 = (c - 2) * 8
            if s0 < 32:
                x = xp[s0:s0 + 8].reshape(T, D)
            else:
                x = np.zeros((T, D), np.float32)
            cond = inp["c_ctx"]
            m["cst"] = cst_p
            m["rope"], m["qmask"], m["kmask"] = att_p
            m["st_ret"] = zf(NL, 8, 64, 64)
            m["st_delta"] = zf(NL, 8, 64, 64)
            m["st_Cn"] = zf(NL, 8, 64, 65)
            m["st_m"] = zf(NL, 128, 8)
            m["ckT"] = zf(NL, 4, 2, 32, 256)
            m["cv"] = zf(NL, 4, 256, 64)
        m["xT"] = np.ascontiguousarray(x.T)
        m["cond"] = np.ascontiguousarray(cond.reshape(8, 128).T)
        in_maps.append(m)
    if "nc" not in _NC_CACHE:
        _NC_CACHE["nc"] = build(T, NL, "full")
    nc = _NC_CACHE["nc"]
    res = run_bass_kernel_spmd(nc, in_maps, core_ids=list(range(8)))
    R = res.results
    y_prompt = np.concatenate([R[c]["yT"].T.reshape(8, 256, D) for c in range(2, 6)], 0).astype(np.float32)
    y_sample = np.stack([R[c]["yT"].T for c in range(2)]).astype(np.float32)
    B = 32
    pc = range(2, 6)
    cat = lambda f: np.ascontiguousarray(np.concatenate([f(R[c]) for c in pc], 0)).astype(np.float32)
    new_ret = cat(lambda r: r["o_ret"].transpose(1, 0, 2, 3, 4).reshape(8, NL, 2, 4, 64, 64))
    new_delta = cat(lambda r: r["o_delta"].transpose(1, 0, 2, 3, 4).reshape(8, NL, 2, 4, 64, 64))
    new_C = cat(lambda r: r["o_Cn"][..., 0:64].transpose(1, 0, 2, 3, 4).reshape(8, NL, 2, 4, 64, 64))
    new_n = cat(lambda r: r["o_Cn"][..., 64].transpose(1, 0, 2, 3).reshape(8, NL, 2, 4, 64))
    new_m = cat(lambda r: r["o_m"].transpose(2, 0, 1).reshape(8, NL, 2, 4))
    new_k = cat(lambda r: r["o_k"].reshape(NL, 4, 8, 256, 64).transpose(2, 0, 1, 3, 4))
    new_v = cat(lambda r: r["o_v"].reshape(NL, 4, 8, 256, 64).transpose(2, 0, 1, 3, 4))
    return (y_prompt, y_sample, new_k, new_v, new_delta, new_C, new_n, new_m, new_ret)
```

```python
import numpy as np
import concourse.bass as bass
import concourse.mybir as mybir

F32 = mybir.dt.float32
BF16 = mybir.dt.bfloat16
AF = mybir.ActivationFunctionType
ALU = mybir.AluOpType
AX = mybir.AxisListType

PE, ACT, DVE, POOL, SP = "pe", "act", "dve", "pool", "sp"


class Tl:
    def __init__(self, ap, name=""):
        self.ap = ap
        self.name = name
        self.lw = None
        self.rd = []
        self.dram_out = False
        self.writers = []
        self.grp = None

    def __getitem__(self, idx):
        return Vw(self, self.ap[idx])

    def v(self):
        return Vw(self, self.ap)


class Vw:
    def __init__(self, t, ap):
        self.t = t
        self.ap = ap

    def __getitem__(self, idx):
        return Vw(self.t, self.ap[idx])

    def rr(self, s, **kw):
        return Vw(self.t, self.ap.rearrange(s, **kw))

    def bc(self, shape):
        return Vw(self.t, self.ap.broadcast_to(shape))

    def bitcast(self, dt):
        return Vw(self.t, self.ap.bitcast(dt))


class Op:
    __slots__ = ("eng", "fn", "deps", "id", "is_dma", "sig", "cnt", "dsem", "dval", "dprev")

    def __init__(self, eng, fn, deps, is_dma):
        self.eng = eng
        self.fn = fn
        self.deps = deps
        self.is_dma = is_dma
        self.sig = False
        self.cnt = 0
        self.dsem = None
        self.dval = 0
        self.dprev = 0


class Prog:
    def __init__(self, nc, n_dma_sems=20, same_engine_sync=True):
        self.nc = nc
        self.ops = []
        self.same_engine_sync = same_engine_sync
        self.esem = {}
        self.n_dma_sems = n_dma_sems
        self.dma_rr = {}
        self.final_tokens = []
        self.out_tiles = []

    def sb(self, name, shape, dt=F32):
        h = self.nc.alloc_sbuf_tensor(name, list(shape), dt)
        return Tl(h.ap(), name)

    def ps(self, name, shape, dt=F32):
        h = self.nc.alloc_psum_tensor(name, list(shape), dt)
        return Tl(h.ap(), name)

    def wrap(self, ap, name=""):
        return Tl(ap, name)

    def carve(self, arena_tl, specs, precise=True):
        out = {}
        if getattr(arena_tl, "conservative", False):
            precise = False
        if not hasattr(arena_tl, "members"):
            arena_tl.members = []
        for name, off, shape, dt in specs:
            nb = 4 if dt == F32 else 2
            n = 1
            for d in shape[1:]:
                n *= d
            a = arena_tl.ap[0:shape[0], off // 2: off // 2 + n * nb // 2]
            if dt == F32:
                a = a.bitcast(F32)
            if len(shape) == 3:
                a = a.rearrange("p (a b) -> p a b", a=shape[1])
            elif len(shape) == 4:
                a = a.rearrange("p (a b c) -> p a b c", a=shape[1], b=shape[2])
            t = Tl(a, name)
            lo, hi = off, off + n * nb
            t.grp = [t]
            for (ot, olo, ohi) in arena_tl.members:
                if (not precise) or (lo < ohi and olo < hi):
                    t.grp.append(ot)
                    ot.grp.append(t)
            arena_tl.members.append((t, lo, hi))
            out[name] = t
        return out

    def _rec(self, eng, fn, outs, ins, is_dma=False):
        deps = set()
        for v in ins:
            if v is None:
                continue
            for t in (v.t.grp or [v.t]):
                if t.lw is not None:
                    deps.add(t.lw)
        for v in outs:
            if v.t.dram_out:
                continue
            for t in (v.t.grp or [v.t]):
                if t.lw is not None:
                    deps.add(t.lw)
                deps.update(t.rd)
        op = Op(eng, fn, deps, is_dma)
        op.id = len(self.ops)
        self.ops.append(op)
        for v in ins:
            if v is None:
                continue
            v.t.rd.append(op.id)
        for v in outs:
            if v.t.dram_out:
                v.t.writers.append(op.id)
                continue
            v.t.lw = op.id
            v.t.rd = []
        return op

    def op(self, eng, fn, outs, ins):
        return self._rec(eng, fn, outs, ins)

    def dma(self, q, out, in_, **kw):
        o, i = out.ap, in_.ap
        return self._rec(q, lambda e: e.dma_start(out=o, in_=i, **kw), [out], [in_], is_dma=True)

    def mm(self, out, lhsT, rhs, start=True, stop=True, **kw):
        o, l, r = out.ap, lhsT.ap, rhs.ap
        return self._rec(PE, lambda e: e.matmul(o, l, r, start=start, stop=stop, **kw), [out], [lhsT, rhs])

    def tr(self, out, in_, ident):
        o, i, d = out.ap, in_.ap, ident.ap
        return self._rec(PE, lambda e: e.transpose(o, i, d), [out], [in_, ident])

    def act(self, out, in_, func, bias=None, scale=None, accum=None, eng=ACT):
        o, i = out.ap, in_.ap
        kw = {}
        ins = [in_]
        outs = [out]
        if bias is not None:
            if isinstance(bias, Vw):
                kw["bias"] = bias.ap
                ins.append(bias)
            else:
                kw["bias"] = bias
        if scale is not None:
            if isinstance(scale, Vw):
                kw["scale"] = scale.ap
                ins.append(scale)
            else:
                kw["scale"] = scale
        if accum is not None:
            kw["accum_out"] = accum.ap
            outs.append(accum)
        return self._rec(eng, lambda e: e.activation(o, i, func, **kw), outs, ins)

    def tt(self, out, a, b, op, eng=DVE):
        o, x, y = out.ap, a.ap, b.ap
        return self._rec(eng, lambda e: e.tensor_tensor(o, x, y, op), [out], [a, b])

    def ts(self, out, a, s1, op0, s2=None, op1=None, eng=DVE, accum=None):
        o, x = out.ap, a.ap
        ins = [a]
        outs = [out]
        a1 = s1.ap if isinstance(s1, Vw) else s1
        a2 = s2.ap if isinstance(s2, Vw) else s2
        if isinstance(s1, Vw):
            ins.append(s1)
        if isinstance(s2, Vw):
            ins.append(s2)
        kw = {}
        if op1 is not None:
            kw["op1"] = op1
        if accum is not None:
            kw["accum_out"] = accum.ap
            outs.append(accum)
        return self._rec(eng, lambda e: e.tensor_scalar(o, x, a1, a2, op0, **kw), outs, ins)

    def stt(self, out, a, s, b, op0, op1, eng=DVE):
        o, x, y = out.ap, a.ap, b.ap
        ins = [a, b]
        sa = s.ap if isinstance(s, Vw) else s
        if isinstance(s, Vw):
            ins.append(s)
        return self._rec(eng, lambda e: e.scalar_tensor_tensor(o, x, sa, y, op0, op1), [out], ins)

    def copy(self, out, in_, eng=DVE):
        o, i = out.ap, in_.ap
        if eng == ACT:
            return self._rec(eng, lambda e: e.copy(o, i), [out], [in_])
        return self._rec(eng, lambda e: e.tensor_copy(o, i), [out], [in_])

    def memset(self, out, val, eng=DVE):
        o = out.ap
        return self._rec(eng, lambda e: e.memset(o, val), [out], [])

    def reduce(self, out, in_, op, axis=None, eng=DVE):
        o, i = out.ap, in_.ap
        ax = axis if axis is not None else AX.X
        return self._rec(eng, lambda e: e.tensor_reduce(o, i, ax, op), [out], [in_])

    def recip(self, out, in_):
        o, i = out.ap, in_.ap
        return self._rec(DVE, lambda e: e.reciprocal(o, i), [out], [in_])

    def scan(self, out, d0, d1, init, op0, op1, eng=DVE):
        o, a, b = out.ap, d0.ap, d1.ap
        ins = [d0, d1]
        iv = init.ap if isinstance(init, Vw) else init
        if isinstance(init, Vw):
            ins.append(init)
        return self._rec(eng, lambda e: e.tensor_tensor_scan(o, a, b, iv, op0, op1), [out], ins)

    def mark_output(self, tl):
        tl.dram_out = True
        self.out_tiles.append(tl)

    def emit(self):
        nc = self.nc
        ops = self.ops
        for op in ops:
            for d in op.deps:
                p = ops[d]
                if p.is_dma:
                    continue
                if p.eng == op.eng and not op.is_dma:
                    if p.eng == PE:
                        continue
                    if not self.same_engine_sync:
                        continue
                p.sig = True
        final_deps = set()
        for tl in self.out_tiles:
            final_deps.update(tl.writers)
        for d in final_deps:
            if not ops[d].is_dma:
                ops[d].sig = True
        engs = [PE, ACT, DVE, POOL, SP]
        cnt = {e: 0 for e in engs}
        for op in ops:
            if op.is_dma:
                continue
            if op.sig:
                cnt[op.eng] += 1
                op.cnt = cnt[op.eng]
        from contextlib import ExitStack

        with ExitStack() as st:
            for e in engs:
                self.esem[e] = st.enter_context(nc.semaphore("es_" + e))
            qs = sorted(set(op.eng for op in ops if op.is_dma))
            pools = {}
            for q in qs:
                pools[q] = [st.enter_context(nc.semaphore("ds_%s_%d" % (q, i))) for i in range(self.n_dma_sems)]
            rr = {q: 0 for q in qs}
            tot = {}
            for op in ops:
                if not op.is_dma:
                    continue
                pool = pools[op.eng]
                s = pool[rr[op.eng] % len(pool)]
                rr[op.eng] += 1
                key = (op.eng, rr[op.eng] % len(pool) if False else id(s))
                prev = tot.get(id(s), 0)
                op.dsem = s
                op.dprev = prev
                op.dval = prev + 16
                tot[id(s)] = op.dval
            block = st.enter_context(nc.Block())
            by_eng = {e: [op for op in ops if op.eng == e] for e in engs}

            def run(eng_name, e):
                waited = {}

                def wait(sem, val):
                    k = id(sem)
                    if waited.get(k, 0) >= val:
                        return
                    waited[k] = val
                    e.wait_ge(sem, val)

                for op in by_eng[eng_name]:
                    for d in sorted(op.deps):
                        p = ops[d]
                        if p.is_dma:
                            wait(p.dsem, p.dval)
                        else:
                            if p.eng == eng_name and not op.is_dma:
                                if p.eng == PE or not self.same_engine_sync:
                                    continue
                            wait(self.esem[p.eng], p.cnt)
                    if op.is_dma:
                        if op.dprev > 0:
                            wait(op.dsem, op.dprev)
                        ins = op.fn(e)
                        ins.then_inc(op.dsem, 16)
                    else:
                        ins = op.fn(e)
                        if op.sig:
                            ins.then_inc(self.esem[eng_name], 1)
                if eng_name == SP:
                    for d in sorted(final_deps):
                        p = ops[d]
                        if p.is_dma:
                            wait(p.dsem, p.dval)
                        else:
                            wait(self.esem[p.eng], p.cnt)

            @block.tensor
            def _(e):
                run(PE, e)

            @block.scalar
            def _(e):
                run(ACT, e)

            @block.vector
            def _(e):
                run(DVE, e)

            @block.gpsimd
            def _(e):
                run(POOL, e)

            @block.sync
            def _(e):
                run(SP, e)

from concourse.bass_utils import run_bass_kernel_spmd

D = 1024
NK = 8
FF = 2816
NF = 22
NMOD = 9
EPS = 1e-6
WIN_BLOCKS = 45
BLK = dict(a_q=0, a_k=1, a_v=2, a_z=3, a_g=4, b_q=5, b_k=6, b_v=7, c_q=8, c_k=9, c_v=10, c_o=11, c_g=12,
           d_q=13, d_k=14, d_v=15, d_g=16, merge=17, b_tm=33, c_tm=37, d_tm=41)
CST_W = 1216
LP_W = 544
C_ID, C_NEGF, C_NEGB, C_RELF, C_RELB, C_POS, C_KEEP = 0, 128, 256, 384, 512, 640, 644
C_PFLAG = 676
C_ONES, C_TRIF, C_TRIB, C_NOTI = 704, 832, 960, 1088
L_RETL, L_RETG, L_MLI, L_MLF, L_MLG, L_ISF, L_DNG, L_DNA, L_DNB, L_CONV, L_QNG, L_KNG, L_DAG, L_LAM = 0, 8, 72, 80, 88, 152, 160, 224, 232, 240, 276, 308, 340, 404
BIGNEG = -30000.0


class Ring:
    def __init__(self, tiles):
        self.tiles = tiles
        self.i = 0

    def next(self):
        t = self.tiles[self.i % len(self.tiles)]
        self.i += 1
        return t


def build(T, NL, stage="full", MIXSEL=(0, 1, 2, 3)):
    NTT = T // 512
    NH = max(1, NTT // 2)
    TPH = NTT // NH
    nc = bass.Bass("TRN2", target_bir_lowering=False)
    P = Prog(nc)

    def din(name, shape):
        return P.wrap(nc.dram_tensor(name, list(shape), F32, kind="ExternalInput").ap(), name)

    def dout(name, shape):
        t = P.wrap(nc.dram_tensor(name, list(shape), F32, kind="ExternalOutput").ap(), name)
        P.mark_output(t)
        return t

    x_in = din("xT", [D, T])
    cond_in = din("cond", [128, NK])
    wada_in = din("wada", [NL * 36, 128, NK * 256])
    bada_in = din("bada", [NL, 128, 72])
    ng_in = din("ng", [NL, 128, 24])
    wgu_in = din("wgu", [NL * 2 * NF, 128, NK * 256])
    wd_in = din("wd", [NL * 2 * NK, 128, NF * 128])
    y_out = dout("yT", [D, T])
    NCH = T // 128
    win_in = din("win", [NL * WIN_BLOCKS, 128, NK * 256])
    wbr_in = din("wbr", [NL * 4, 128, 2 * 1024])
    wout_in = din("wout", [NL * 4, 128, NK * 256])
    cst_in = din("cst", [128, CST_W])
    lp_in = din("lpar", [NL, 128, LP_W])
    st_ret_in = din("st_ret", [NL, 8, 64, 64])
    o_ret = dout("o_ret", [NL, NCH // 2, 8, 64, 64])
    NSEG = NCH // 2
    sel_in = din("sel", [8, 520])
    st_C_in = din("st_Cn", [NL, 8, 64, 65])
    st_dl_in = din("st_delta", [NL, 8, 64, 64])
    ckT_in = din("ckT", [NL, 4, 2, 32, 256])
    cv_in = din("cv", [NL, 4, 256, 64])
    rope_in = din("rope", [128, 2, NCH * 16])
    qmask_in = din("qmask", [9, T])
    kmask_in = din("kmask", [9, T + 256])
    o_k = dout("o_k", [NL, 4, T, 64])
    o_v = dout("o_v", [NL, 4, T, 64])
    o_dl = dout("o_delta", [NL, NSEG, 8, 64, 64])
    st_m_in = din("st_m", [NL, 128, 8])
    o_C = dout("o_Cn", [NL, NSEG, 8, 64, 65])
    o_m = dout("o_m", [NL, 8, NSEG])

    xT = [P.sb("xT%d" % t, [128, NK, 512]) for t in range(NTT)]
    hT = [P.sb("hT%d" % t, [128, NK, 512], BF16) for t in range(NTT)]
    ARENA_B = 2 * NF * 512 * TPH + 12 * 1024
    arena = P.sb("arena", [128, ARENA_B // 2], BF16)
    import os
    arena.conservative = os.environ.get("A1_CONS", "0") == "1"
    cv = P.carve(arena, [("aT%d" % t, t * NF * 1024, [128, NF, 512], BF16) for t in range(TPH)])
    aT = [cv["aT%d" % t] for t in range(TPH)]
    cva = P.carve(arena, [("wAda%d" % i, i * NK * 1024, [128, NK, 256], F32) for i in range(2)])
    TB = T * 4
    o = 0
    mspec = []
    for nm, shp in [("QT", [64, T]), ("KT", [64, T]), ("Ktm", [128, NCH, 64]), ("Vtm", [128, NCH, 65]),
                    ("Gtm", [128, NCH, 64]), ("Od0", [128, NCH, 65]), ("Od1", [128, NCH, 65]),
                    ("Opair", [128, NCH, 128]), ("sqo", [128, NCH, 64])]:
        n = 4
        for d in shp[1:]:
            n *= d
        mspec.append((nm, o, shp, F32))
        if nm in ("QT", "KT"):
            mspec.append((nm + "b", o, [64, T], BF16))
        if nm == "Vtm":
            mspec.append(("Vtb", o, [128, NCH, 66], BF16))
        if nm == "Gtm":
            mspec.append(("raw", o, [64, T + 2], F32))
            for p_ in range(2):
                mspec.append(("wtW_%d" % p_, o + p_ * 1024, [128, 2, 128], F32))
                mspec.append(("wTW_%d" % p_, o + 2048 + p_ * 1024, [64, 2, 128], F32))
        if nm == "sqo":
            for i in range(2):
                mspec.append(("dg%d" % i, o + i * 1024, [128, 128], F32))
                mspec.append(("wt%d" % i, o + i * 1024 + 512, [128, 128], F32))
            for i, xn in enumerate(["Xa", "Xb", "XTa", "XTb", "Yd", "khT"]):
                mspec.append((xn, o + 1024 + i * 512, [128, 128], F32))
            mspec.append(("atW", o, [128, 2, 128], F32))
            n = max(n, 4096)
        o += n
    for i in range(2):
        mspec.append(("attn%d" % i, o, [128, 128], F32))
        mspec.append(("atb%d" % i, o, [128, 128], BF16)); o += 512
        mspec.append(("kw%d" % i, o, [128, 64], F32))
        mspec.append(("kwb%d" % i, o, [128, 64], BF16)); o += 256
        mspec.append(("o2_%d" % i, o, [128, 65], F32)); o += 260
        mspec.append(("o3_%d" % i, o, [128, 65], F32)); o += 260
    mspec.append(("WTd", o, [128, 8, 128], F32))
    mspec.append(("GT", o, [128, 10, NCH * 8], F32))
    o += max(4096, 10 * NCH * 32)
    assert o <= ARENA_B, (o, ARENA_B)
    mx = P.carve(arena, mspec)
    NKT = NCH + 2
    offs_ = dict((nm_, off_) for nm_, off_, _, _ in mspec)
    bo = 0
    bspec = []
    for nm, shp, dt_ in [("qaug0", [128, T], BF16), ("qaug1", [128, T], BF16), ("kaug", [128, T + 256], BF16),
                         ("Vext", [128, NKT, 65], BF16), ("Qb", [128, NCH, 64], F32),
                         ("Kb", [128, NCH, 64], F32), ("Ob", [128, NCH, 64], F32), ("octmp", [128, 4, 65], F32),
                         ("oc2", [128, 4, 65], F32), ("ocT", [65, 512], F32)]:
        n = 4 if dt_ == F32 else 2
        for d in shp[1:]:
            n *= d
        n = (n + 3) // 4 * 4
        bspec.append((nm, bo, shp, dt_))
        if nm == "Ob":
            bspec.append(("Vf", bo, [128, NCH, 64], F32))
        bo += n
    for i in range(4):
        bspec.append(("PT%d" % i, bo, [128, 512], BF16))
        bo += 1024
    assert (1 not in MIXSEL) or bo <= offs_["Opair"], (bo, offs_["Opair"])
    so = offs_["sqo"]
    bspec.append(("ropet", so, [128, 2, NCH * 16], F32))
    to = offs_["attn0"]
    for i in range(4):
        bspec.append(("rt%d" % i, to + i * NCH * 64, [128, NCH * 16], F32))
    assert to + 4 * NCH * 64 <= ARENA_B
    bx = P.carve(arena, bspec)
    zsp = [("z", 0, [128, NK, 512], F32), ("zb", NK * 2048, [128, NK, 512], BF16),
           ("brm", NK * 2048 + NK * 1024, [128, 8, 512], BF16), ("gsb0", NK * 2048 + 2 * NK * 1024, [128, 512], F32),
           ("gsb1", NK * 2048 + 2 * NK * 1024 + 2048, [128, 512], F32), ("gp0", NK * 2048 + 2 * NK * 1024 + 4096, [128, 512], F32)]
    mz = P.carve(arena, zsp)
    wA = Ring([P.sb("wA%d" % i, [128, NK, 256], BF16) for i in range(3)])
    A2_B = 2 * NF * 256 + 2 * 2048 + 2 * 1024
    arena2 = P.sb("arena2", [128, A2_B // 2], BF16)
    import os
    arena2.conservative = os.environ.get("A2_CONS", "0") == "1"
    c2 = P.carve(arena2, [("wD%d" % i, i * NF * 256, [128, NF, 128], BF16) for i in range(2)]
                 + [("rstd%d" % i, 2 * NF * 256 + i * 2048, [128, 512], F32) for i in range(2)]
                 + [("sq%d" % i, 2 * NF * 256 + 4096 + i * 1024, [128, 512], BF16) for i in range(2)])
    wD = Ring([c2["wD%d" % i] for i in range(2)])
    a2spec = []
    o2_ = 0
    for p_ in range(2):
        for nm_, sz_, shp_ in [("dg", 1024, [128, 2, 128]), ("XTa", 1024, [128, 2, 128]), ("XTb", 1024, [128, 2, 128]),
                               ("Xa", 1024, [128, 2, 128]), ("Xb", 1024, [128, 2, 128]), ("Y", 1024, [128, 2, 128]),
                               ("khT", 1024, [64, 2, 128]), ("kh", 512, [128, 2, 64]), ("u", 512, [128, 2, 64])]:
            a2spec.append(("%s_%d" % (nm_, p_), o2_, shp_, F32))
            o2_ += sz_
    a2spec.append(("vnew", o2_, [128, 2, 64], F32))
    o2_ += 512
    assert o2_ <= A2_B, (o2_, A2_B)
    ax = P.carve(arena2, a2spec)
    wBR = Ring([P.sb("wBR%d" % i, [128, 2, 1024], BF16) for i in range(1)])
    wAda = Ring([cva["wAda%d" % i] for i in range(2)])
    cst = P.sb("cst_sb", [128, CST_W])
    lpar1 = P.sb("lpar1", [128, LP_W])
    lpar = [lpar1 for l in range(NL)]
    sel = P.sb("sel_sb", [8, 520])
    rows = P.sb("rows_sb", [8, 256])
    embc = P.sb("embc", [64, 64])
    rows_b = P.sb("rows_b", [128, 32]).v()
    em0 = P.sb("em0", [128, 8])
    brst = Ring([P.sb("brst%d" % i, [128, T], BF16) for i in range(1)])
    smal = P.sb("smal", [128, 64])
    Sst = [P.sb("Sst%d" % i, [64, 65]) for i in range(2)]
    Sbf = [P.sb("Sbf%d" % i, [64, 66], BF16) for i in range(2)]
    Sfin = Ring([P.sb("Sfin%d" % i, [64, 65]) for i in range(2)])
    br_dram = P.wrap(nc.dram_tensor("br_scratch", [4, 2, 128, T], BF16,
                                    kind=("ExternalOutput" if stage == "mix" else "Internal")).ap(), "br_scratch")
    ps = Ring([P.ps("pb%d" % i, [128, 512]) for i in range(6)])
    pacc_r = Ring([P.ps("pacc%d" % i, [128, 512]) for i in range(2)])
    sq = Ring([c2["sq%d" % i] for i in range(2)])
    tmpf = Ring([P.sb("tmpf%d" % i, [128, 512]) for i in range(2)])
    rstd_r = Ring([c2["rstd%d" % i] for i in range(2)])
    ones_bf = P.sb("ones_bf", [128, 128], BF16)
    scond = P.sb("scond", [128, NK])
    bada = [P.sb("bada%d" % l, [128, 72]) for l in range(NL)]
    ng = [P.sb("ng%d" % l, [128, 24]) for l in range(NL)]
    mod = [P.sb("mod%d" % l, [128, 72]) for l in range(NL)]
    scale = [P.sb("scale%d" % l, [128, 24]) for l in range(NL)]
    gate = [P.sb("gate%d" % l, [128, 24]) for l in range(NL)]

    P.memset(ones_bf.v(), 1.0)

    xv = x_in.v().rr("(kc p) t -> p kc t", p=128)
    for t in range(NTT):
        P.dma(SP, xT[t].v(), xv[:, :, t * 512:(t + 1) * 512])
    P.dma(SP, scond.v(), cond_in.v())
    P.dma(SP, cst.v(), cst_in.v())
    P.dma(SP, sel.v(), sel_in.v())
    for l in range(NL):
        P.dma(SP, bada[l].v(), bada_in[l])
        P.dma(SP, ng[l].v(), ng_in[l])
    P.act(scond.v(), scond.v(), AF.Silu)

    for l in range(NL):
        pm = pacc_r.next()
        for blk in range(36):
            wb = wAda.next()
            P.dma(SP, wb.v(), wada_in[l * 36 + blk].rr("p (k c) -> p k c", k=NK))
            pr = ps.next()
            for kc in range(NK):
                P.mm(pr[0:1, 0:256], scond[:, kc:kc + 1], wb[:, kc, :], start=(kc == 0), stop=(kc == NK - 1))
            rb = rows[0:1, 0:256]
            P.copy(rb, pr[0:1, 0:256], eng=ACT)
            for cc in range(2):
                col = blk * 2 + cc
                P.mm(pm[:, col:col + 1], rb[0:1, cc * 128:(cc + 1) * 128], cst[0:1, C_ONES:C_ONES + 1])
        P.tt(mod[l].v(), pm[:, 0:72], bada[l].v(), ALU.add)
        for i in range(3):
            P.stt(scale[l][:, i * 8:(i + 1) * 8], mod[l][:, (3 * i + 1) * 8:(3 * i + 2) * 8], 1.0,
                  ng[l][:, i * 8:(i + 1) * 8], ALU.add, ALU.mult)
            P.ts(gate[l][:, i * 8:(i + 1) * 8], mod[l][:, (3 * i + 2) * 8:(3 * i + 3) * 8],
                 0.5 if i != 1 else 1.0, ALU.mult)

    def norm_mod(l, i, t):
        pn = ps.next()
        for kc in range(NK):
            s = sq.next()
            P.act(s.v(), xT[t][:, kc, :], AF.Square)
            P.mm(pn.v(), ones_bf.v(), s.v(), start=(kc == 0), stop=(kc == NK - 1))
        r = rstd_r.next()
        P.ts(r.v(), pn.v(), 1.0 / D, ALU.mult, EPS, ALU.add)
        P.act(r.v(), r.v(), AF.Sqrt)
        P.recip(r.v(), r.v())
        for kc in range(NK):
            tm = tmpf.next()
            P.stt(tm.v(), xT[t][:, kc, :], scale[l][:, i * 8 + kc:i * 8 + kc + 1], r.v(), ALU.mult, ALU.mult)
            P.act(hT[t][:, kc, :], tm.v(), AF.Identity, bias=mod[l][:, 3 * i * 8 + kc:3 * i * 8 + kc + 1])

    def ffn(l, i):
        ni = 0 if i == 0 else 2
        for h in range(NH):
            tts = list(range(h * TPH, (h + 1) * TPH))
            for t in tts:
                norm_mod(l, ni, t)
            for f in range(NF):
                wb = wA.next()
                P.dma(POOL, wb.v(), wgu_in[(l * 2 + i) * NF + f].rr("p (k c) -> p k c", k=NK))
                for j, t in enumerate(tts):
                    pg = ps.next()
                    pu = ps.next()
                    for kc in range(NK):
                        P.mm(pg.v(), wb[:, kc, 0:128], hT[t][:, kc, :], start=(kc == 0), stop=(kc == NK - 1))
                    for kc in range(NK):
                        P.mm(pu.v(), wb[:, kc, 128:256], hT[t][:, kc, :], start=(kc == 0), stop=(kc == NK - 1))
                    sg = tmpf.next()
                    P.act(sg.v(), pg.v(), AF.Silu)
                    P.tt(aT[j][:, f, :], sg.v(), pu.v(), ALU.mult)
            for dc in range(NK):
                wdb = wD.next()
                P.dma(POOL, wdb.v(), wd_in[(l * 2 + i) * NK + dc].rr("p (f c) -> p f c", f=NF))
                for j, t in enumerate(tts):
                    py = ps.next()
                    for fc in range(NF):
                        P.mm(py.v(), wdb[:, fc, :], aT[j][:, fc, :], start=(fc == 0), stop=(fc == NF - 1))
                    P.stt(xT[t][:, dc, :], py.v(), gate[l][:, ni * 8 + dc:ni * 8 + dc + 1], xT[t][:, dc, :],
                          ALU.mult, ALU.add)


    ident = cst[:, C_ID:C_ID + 128]

    def load_blk(l, b):
        wb = wA.next()
        P.dma(POOL, wb.v(), win_in[l * WIN_BLOCKS + b].rr("p (k c) -> p k c", k=NK))
        return wb

    def proj_fm(wb, c0, M, dst, scale=None):
        for t in range(NTT):
            pp = ps.next()
            for kc in range(NK):
                P.mm(pp[0:M, :], wb[:, kc, c0:c0 + M], hT[t][:, kc, :], start=(kc == 0), stop=(kc == NK - 1))
            if scale is None:
                P.copy(dst[0:M, t * 512:(t + 1) * 512], pp[0:M, :], eng=ACT)
            else:
                P.act(dst[0:M, t * 512:(t + 1) * 512], pp[0:M, :], AF.Copy, scale=scale)

    def proj_tm(wb, c0, N, dst, ncol, scale=None):
        for g in range(NCH // 4):
            pp = ps.next()
            for q in range(4):
                n = g * 4 + q
                t, off = n // 4, (n % 4) * 128
                for kc in range(NK):
                    P.mm(pp[:, q * 64:q * 64 + N], hT[t][:, kc, off:off + 128], wb[:, kc, c0:c0 + N],
                         start=(kc == 0), stop=(kc == NK - 1))
            src = pp[:, 0:256].rr("p (q c) -> p q c", q=4)[:, :, 0:N]
            if scale is None:
                P.copy(dst[:, g * 4:(g + 1) * 4, 0:N], src, eng=ACT)
            else:
                P.act(dst[:, g * 4:(g + 1) * 4, 0:N], src, AF.Copy, scale=scale)

    def proj_tm3(wb, dsts):
        for g in range(NCH // 2):
            pp = ps.next()
            for q in range(2):
                n = g * 2 + q
                t, off = n // 4, (n % 4) * 128
                for kc in range(NK):
                    P.mm(pp[:, q * 192:(q + 1) * 192], hT[t][:, kc, off:off + 128], wb[:, kc, 0:192],
                         start=(kc == 0), stop=(kc == NK - 1))
            for j, (dst, scale) in enumerate(dsts):
                src = pp[:, 0:384].rr("p (q c) -> p q c", q=2)[:, :, j * 64:(j + 1) * 64]
                if scale is None:
                    P.copy(dst[:, g * 2:(g + 1) * 2, 0:64], src, eng=ACT)
                else:
                    P.act(dst[:, g * 2:(g + 1) * 2, 0:64], src, AF.Copy, scale=scale)

    rr2 = [0]

    def run_rr(gens, delays=None):
        gens = list(gens)
        delays = list(delays) if delays is not None else [0] * len(gens)
        live = list(range(len(gens)))
        while live:
            nxt = []
            for i in live:
                if delays[i] > 0:
                    delays[i] -= 1
                    nxt.append(i)
                    continue
                try:
                    next(gens[i])
                    nxt.append(i)
                except StopIteration:
                    pass
            live = nxt

    def lin_chunk_g(n, E, WT, qs, ks, lam, S, Vx, o_out, keepcol, sfin_dst, fix=None, bf=False, wcol=None):
        i2 = rr2[0] % 2
        rr2[0] += 1
        if fix is not None:
            i2 = fix
        sfx = "b" if bf else ""
        QTn = mx["QT" + sfx][:, n * 128:(n + 1) * 128]
        KTn = mx["KT" + sfx][:, n * 128:(n + 1) * 128]
        p1 = ps.next()
        P.mm(p1[:, 0:128], KTn, QTn)
        at = mx[("atb%d" if bf else "attn%d") % i2]
        if wcol is None:
            P.tt(at.v(), p1[:, 0:128], WT, ALU.mult)
        else:
            P.stt(at.v(), p1[:, 0:128], wcol, WT, ALU.mult, ALU.mult)
        kw = mx[("kwb%d" if bf else "kw%d") % i2]
        P.ts(kw.v(), mx["Ktm"][:, n, :], ks, ALU.mult)
        Sr = S[:, 0:E]
        if bf:
            Vx = mx["Vtb"][:, n, 0:E]
            P.copy(Sbf[i2][:, 0:E], S[:, 0:E], eng=ACT)
            Sr = Sbf[i2][:, 0:E]
        yield
        p2 = ps.next()
        if wcol is None:
            P.mm(p2[:, 0:E], at.v(), Vx)
            P.mm(p2[:, 128:128 + E], QTn, Sr)
            o2 = mx["o2_%d" % i2]
            P.copy(o2[:, 0:E], p2[:, 0:E], eng=ACT)
            P.stt(o_out, p2[:, 128:128 + E], qs, o2[:, 0:E], ALU.mult, ALU.add)
        else:
            P.mm(p2[:, 0:E], at.v(), Vx, start=True, stop=False)
            P.mm(p2[:, 0:E], QTn, Sr, start=False, stop=True)
            P.ts(o_out, p2[:, 0:E], qs, ALU.mult)
        yield
        p3 = ps.next()
        P.mm(p3[0:64, 0:E], kw.v(), Vx)
        P.stt(S[:, 0:E], S[:, 0:E], lam, p3[0:64, 0:E], ALU.mult, ALU.add)
        if sfin_dst is not None:
            sf = Sfin.next()
            P.copy(sf[:, 0:E], S[:, 0:E])
            sfin_dst(sf)
        P.ts(S[:, 0:E], S[:, 0:E], keepcol, ALU.mult)
        yield

    def lin_chunk(*a, **k):
        for _ in lin_chunk_g(*a, **k):
            pass

    def branch_out(m, pair):
        st = brst.next()
        for g in range(NCH // 4):
            pp = ps.next()
            for q in range(4):
                n = g * 4 + q
                P.tr(pp[:, q * 128:(q + 1) * 128], mx["Opair"][:, n, :], ident)
            P.copy(st[:, g * 512:(g + 1) * 512], pp.v())
        P.dma(SP, br_dram[m, pair], st.v())

    def mixer_D(l):
        lgt = smal[:, 0:8]
        qsd = smal[:, 8:16]
        ksd = smal[:, 16:24]
        lamd = smal[:, 24:32]
        P.act(lgt, lpar[l][:, 0:8], AF.Exp, scale=-1.0)
        P.act(lgt, lgt, AF.Ln, bias=1.0)
        P.ts(lgt, lgt, -1.0, ALU.mult)
        nlg = smal[:, 54:62]
        wcd = rows_b[:, 0:8]
        P.ts(nlg, lgt, -1.0, ALU.mult)
        for r in range(8):
            d = r // 4
            P.act(wcd[:, r:r + 1], cst[:, C_POS + d:C_POS + d + 1], AF.Exp, scale=nlg[:, r:r + 1])
            P.act(qsd[:, r:r + 1], cst[:, C_POS + d:C_POS + d + 1], AF.Exp, scale=lgt[:, r:r + 1])
            P.act(ksd[:, r:r + 1], cst[:, C_POS + 2 + d:C_POS + 3 + d], AF.Exp, scale=lgt[:, r:r + 1])
        P.act(lamd, lgt, AF.Exp, scale=128.0)
        for h in range(4):
            wb = load_blk(l, BLK["d_q"])
            proj_fm(wb, h * 64, 64, mx["QTb"])
            wb = load_blk(l, BLK["d_k"])
            proj_fm(wb, h * 64, 64, mx["KTb"], scale=0.125)
            wb = load_blk(l, BLK["d_tm"] + h)
            proj_tm3(wb, [(mx["Ktm"], 0.125), (mx["Vtb"], None), (mx["Gtm"], None)])
            def chain_D(d):
                r = d * 4 + h
                S = Sst[d]
                P.dma(SP, S[:, 0:64], st_ret_in[l, r])
                od = mx["Od%d" % d]
                order = range(NCH) if d == 0 else range(NCH - 1, -1, -1)
                for n in order:
                    isend = (n % 2 == 1) if d == 0 else (n % 2 == 0)
                    dst = None
                    if isend:
                        seg = n // 2
                        dst = (lambda sf, seg=seg, r=r: P.dma(SP, o_ret[l, seg, r], sf[:, 0:64]))
                    kc = cst[0:64, C_KEEP + d * NCH + n:C_KEEP + d * NCH + n + 1]
                    yield from lin_chunk_g(n, 64, (TRIF if d == 0 else TRIB), qsd[:, r:r + 1], ksd[:, r:r + 1],
                                           lamd[0:64, r:r + 1], S, None, od[:, n, 0:64], kc, dst, fix=d, bf=True,
                                           wcol=wcd[:, r:r + 1])
            run_rr([chain_D(0), chain_D(1)])
            o0 = mx["Od0"][:, :, 0:64]
            P.tt(o0, o0, mx["Od1"][:, :, 0:64], ALU.add)
            post_norm_gate(l, o0, 8, AF.Silu, h)
            if h % 2 == 1:
                branch_out(3, h // 2)

    def post_norm_gate(l, o, gcol, gfunc, h):
        sqo = mx["sqo"]
        P.tt(sqo.v(), o, o, ALU.mult)
        ss = smal[:, 32:32 + NCH]
        P.reduce(ss, sqo.v(), ALU.add)
        P.ts(ss, ss, 1.0 / 64, ALU.mult, EPS, ALU.add)
        P.act(ss, ss, AF.Sqrt)
        P.recip(ss, ss)
        P.tt(sqo.v(), o, ss.rr("p (n o) -> p n o", o=1).bc([128, NCH, 64]), ALU.mult)
        gn = lpar[l][:, gcol:gcol + 64].rr("p (o e) -> p o e", o=1).bc([128, NCH, 64])
        P.tt(sqo.v(), sqo.v(), gn, ALU.mult)
        P.act(mx["Gtm"].v(), mx["Gtm"].v(), gfunc)
        P.tt(mx["Opair"][:, :, (h % 2) * 64:(h % 2) * 64 + 64], sqo.v(), mx["Gtm"].v(), ALU.mult)

    def zero_branch(m):
        for pair in range(2):
            st = brst.next()
            P.memset(st.v(), 0.0)
            P.dma(SP, br_dram[m, pair], st.v())

    def merge(l):
        z, zb, brm = mz["z"], mz["zb"], mz["brm"]
        gs = Ring([mz["gsb0"], mz["gsb1"]])
        for t in range(NTT):
            for m in range(4):
                for pair in range(2):
                    P.dma(SP, brm[:, m * 2 + pair, :], br_dram[m, pair][:, t * 512:(t + 1) * 512])
            for m in range(4):
                wbr = wBR.next()
                P.dma(POOL, wbr.v(), wbr_in[l * 4 + m].rr("p (k c) -> p k c", k=2))
                wbv = wbr.v()
                for gb in range(4):
                    wg = load_blk(l, BLK["merge"] + m * 4 + gb)
                    for cc in range(2):
                        dc = gb * 2 + cc
                        pg = ps.next()
                        for kc in range(NK):
                            P.mm(pg.v(), wg[:, kc, cc * 128:(cc + 1) * 128], hT[t][:, kc, :],
                                 start=(kc == 0), stop=(kc == NK - 1))
                        g = gs.next()
                        P.act(g.v(), pg.v(), AF.Sigmoid)
                        pp = ps.next()
                        for kc in range(2):
                            P.mm(pp.v(), wbv[:, kc, dc * 128:(dc + 1) * 128], brm[:, m * 2 + kc, :],
                                 start=(kc == 0), stop=(kc == 1))
                        if m == 0:
                            P.tt(z[:, dc, :], g.v(), pp.v(), ALU.mult)
                        else:
                            P.tt(g.v(), g.v(), pp.v(), ALU.mult)
                            P.tt(z[:, dc, :], z[:, dc, :], g.v(), ALU.add)
            P.copy(zb.v(), z.v(), eng=ACT)
            for ob in range(4):
                wo = wA.next()
                P.dma(POOL, wo.v(), wout_in[l * 4 + ob].rr("p (k c) -> p k c", k=NK))
                for cc in range(2):
                    dc = ob * 2 + cc
                    py = ps.next()
                    for kc in range(NK):
                        P.mm(py.v(), wo[:, kc, cc * 128:(cc + 1) * 128], zb[:, kc, :], start=(kc == 0), stop=(kc == NK - 1))
                    P.stt(xT[t][:, dc, :], py.v(), gate[l][:, 8 + dc:8 + dc + 1], xT[t][:, dc, :], ALU.mult, ALU.add)


    ONESF = cst[:, C_ONES:C_ONES + 128]
    TRIF = cst[:, C_TRIF:C_TRIF + 128]
    TRIB = cst[:, C_TRIB:C_TRIB + 128]
    NEGM = [cst[:, C_NEGF:C_NEGF + 128], cst[:, C_NEGB:C_NEGB + 128]]

    def GTs(i, w=8):
        return mx["GT"][:, i, 0:NCH * 8].rr("p (n r) -> p n r", r=8) if w == 8 else None

    def bc8(col):
        return lpar1[:, col:col + 8].rr("p (o r) -> p o r", o=1).bc([128, NCH, 8])

    def cums(src, dP, dB, dT):
        for lhs, dst in ((TRIF, dP), (TRIB, dB), (ONESF, dT)):
            pp = ps.next()
            P.mm(pp[:, 0:NCH * 8], lhs, src)
            P.copy(dst, pp[:, 0:NCH * 8], eng=ACT)

    def blend_dir(dst, aF, aB):
        P.tt(dst, aF, aB, ALU.subtract)
        P.tt(dst, dst, bc8(L_ISF), ALU.mult)
        P.tt(dst, dst, aB, ALU.add)

    def rows_of(src_slot_view_fn, dst_rows, reduce_max):
        for g in range(NCH // 4):
            pp = ps.next()
            for q in range(4):
                n = g * 4 + q
                P.tr(pp[0:8, q * 128:(q + 1) * 128], src_slot_view_fn(n), ident)
            v = pp[0:8, :].rr("p (q t) -> p q t", q=4)
            if reduce_max:
                P.reduce(dst_rows[:, g * 4:(g + 1) * 4], v, ALU.max)
            else:
                P.copy(dst_rows[:, g * 4:(g + 1) * 4], v[:, :, 0])

    def mixer_C(l):
        G16 = mx["GT"][:, 0:2, :].rr("p a b -> p (a b)").rr("p (n c) -> p n c", c=16)
        IG, LF, bP, bb, TOT, QS, KS, LAM = [GTs(i) for i in range(2, 10)]
        flat = lambda i: mx["GT"][:, i, 0:NCH * 8]
        wb = load_blk(l, BLK["c_g"])
        proj_tm(wb, 0, 16, G16, 16)
        P.tt(IG, G16[:, :, 0:8], bc8(L_MLI), ALU.add)
        P.tt(LF, G16[:, :, 8:16], bc8(L_MLF), ALU.add)
        P.act(LF, LF, AF.Exp, scale=-1.0)
        P.act(LF, LF, AF.Ln, bias=1.0)
        P.ts(LF, LF, -1.0, ALU.mult)
        cums(flat(3), flat(4), flat(7), flat(6))
        blend_dir(bb, bP, QS)
        P.act(QS, bb, AF.Exp)
        P.tt(IG, IG, bb, ALU.subtract)
        P.tt(KS, IG, TOT, ALU.add)
        import os
        DBG = int(os.environ.get("DBG_C", "9"))
        if DBG < 2:
            return
        Xr = rows[:, 0:NCH]
        Tr = rows[:, 16:16 + NCH]
        rows_of(lambda n: KS[:, n, :], Xr, True)
        rows_of(lambda n: TOT[:, n, :], Tr, False)
        X2 = Xr.rr("p (s c) -> p s c", c=2)
        T2 = Tr.rr("p (s c) -> p s c", c=2)
        Bt = rows[:, 32:32 + NSEG]
        ta = rows[:, 48:48 + NSEG]
        mF = rows[:, 64:64 + NSEG]
        mB = rows[:, 80:80 + NSEG]
        mr = rows[:, 96:96 + NSEG]
        P.tt(Bt, T2[:, :, 0], T2[:, :, 1], ALU.add)
        P.tt(ta, X2[:, :, 0], T2[:, :, 1], ALU.add)
        P.tt(mF, Bt, X2[:, :, 1], ALU.max)
        P.tt(mF, mF, ta, ALU.max)
        P.tt(ta, X2[:, :, 1], T2[:, :, 0], ALU.add)
        P.tt(mB, Bt, X2[:, :, 0], ALU.max)
        P.tt(mB, mB, ta, ALU.max)
        P.tt(mr, mF, mB, ALU.subtract)
        P.ts(mr, mr, sel[:, 512:513], ALU.mult)
        P.tt(mr, mr, mB, ALU.add)
        if DBG < 3:
            return
        P.dma(SP, o_m[l], mr)
        if DBG < 4:
            return
        em = rows[:, 112:112 + NSEG]
        P.act(em, mr, AF.Exp, scale=-1.0)
        pp = ps.next()
        for r in range(8):
            P.mm(pp[0:64, r * NSEG:(r + 1) * NSEG], sel[:, r * 64:(r + 1) * 64], em)
        P.copy(embc[:, 0:8 * NSEG], pp[0:64, 0:8 * NSEG])
        P.act(KS, KS, AF.Exp)
        P.act(LAM, TOT, AF.Exp)
        P.act(LF, IG, AF.Exp)
        P.dma(SP, em0.v(), st_m_in[l])
        P.act(em0.v(), em0.v(), AF.Exp)
        if DBG < 5:
            return
        for h in range(4):
            wb = load_blk(l, BLK["c_q"])
            proj_fm(wb, h * 64, 64, mx["QTb"], scale=0.125)
            wb = load_blk(l, BLK["c_k"])
            proj_fm(wb, h * 64, 64, mx["KTb"])
            wb = load_blk(l, BLK["c_tm"] + h)
            P.memset(mx["Vtb"][:, :, 64:65], 1.0)
            proj_tm3(wb, [(mx["Ktm"], None), (mx["Vtb"], None), (mx["Gtm"], None)])
            def chain_C(d):
                r = d * 4 + h
                S = Sst[d]
                P.dma(SP, S.v(), st_C_in[l, r])
                P.ts(S.v(), S.v(), em0[0:64, r:r + 1], ALU.mult)
                od = mx["Od%d" % d]
                order = range(NCH) if d == 0 else range(NCH - 1, -1, -1)
                for n in order:
                    isend = (n % 2 == 1) if d == 0 else (n % 2 == 0)
                    dst = None
                    if isend:
                        seg = n // 2

                        def dst(sf, seg=seg, r=r):
                            P.ts(sf.v(), sf.v(), embc[:, r * NSEG + seg:r * NSEG + seg + 1], ALU.mult)
                            P.dma(SP, o_C[l, seg, r], sf.v())
                    kc = cst[0:64, C_KEEP + d * NCH + n:C_KEEP + d * NCH + n + 1]
                    yield from lin_chunk_g(n, 65, (TRIF if d == 0 else TRIB), QS[:, n, r:r + 1], KS[:, n, r:r + 1],
                                           LAM[0:64, n, r:r + 1], S, None, od[:, n, 0:65], kc, dst, fix=d, bf=True,
                                           wcol=LF[:, n, r:r + 1])
            run_rr([chain_C(0), chain_C(1)])
            for d in range(2):
                od = mx["Od%d" % d]
                den = od[:, :, 64:65]
                P.act(den, den, AF.Abs)
                P.ts(den, den, 1.0, ALU.max)
                P.recip(den, den)
                P.tt(od[:, :, 0:64], od[:, :, 0:64], den.bc([128, NCH, 64]), ALU.mult)
            o0 = mx["Od0"][:, :, 0:64]
            P.tt(o0, o0, mx["Od1"][:, :, 0:64], ALU.add)
            post_norm_gate(l, o0, L_MLG, AF.Sigmoid, h)
            if h % 2 == 1:
                branch_out(2, h // 2)


    NOTI = cst[:, C_NOTI:C_NOTI + 128]

    def mixer_A(l):
        G16 = mx["GT"][:, 0:2, :].rr("p a b -> p (a b)").rr("p (n c) -> p n c", c=16)
        SQB, NCG, SQBE, CG, TOT, QS, KS, LAM = [GTs(i) for i in range(2, 10)]
        flat = lambda i: mx["GT"][:, i, 0:NCH * 8]
        wb = load_blk(l, BLK["a_g"])
        proj_tm(wb, 0, 16, G16, 16)
        negA = smal[:, 40:48]
        P.act(negA, lpar1[:, L_DNA:L_DNA + 8], AF.Exp)
        P.ts(negA, negA, -1.0, ALU.mult)
        P.act(SQB, G16[:, :, 0:8], AF.Sigmoid)
        P.act(SQB, SQB, AF.Sqrt)
        P.tt(NCG, G16[:, :, 8:16], bc8(L_DNB), ALU.add)
        P.act(NCG, NCG, AF.Exp)
        P.act(NCG, NCG, AF.Ln, bias=1.0)
        P.tt(NCG, NCG, negA.rr("p (o r) -> p o r", o=1).bc([128, NCH, 8]), ALU.mult)
        cums(flat(3), flat(4), flat(7), flat(6))
        blend_dir(CG, SQBE, QS)
        P.act(QS, CG, AF.Exp)
        P.tt(SQBE, SQB, QS, ALU.mult)
        P.tt(KS, TOT, CG, ALU.subtract)
        P.act(KS, KS, AF.Exp)
        P.act(LAM, TOT, AF.Exp)
        P.ts(NCG, CG, -1.0, ALU.mult)
        I64 = cst[0:64, C_ID:C_ID + 64]
        O64 = cst[0:64, C_ONES:C_ONES + 64]
        for h in range(4):
            for xi, (bn, dstn) in enumerate((("a_q", "QT"), ("a_k", "KT"), ("a_v", None))):
                raw = mx["raw"]
                P.memset(raw[:, 0:1], 0.0)
                P.memset(raw[:, T + 1:T + 2], 0.0)
                wb = load_blk(l, BLK[bn])
                proj_fm(wb, h * 64, 64, raw[:, 1:T + 1])
                cw = lambda j: lpar1[0:64, L_CONV + (xi * 4 + h) * 3 + j:L_CONV + (xi * 4 + h) * 3 + j + 1]
                nw0 = smal[0:64, 48:49]
                nw2 = smal[0:64, 49:50]
                P.ts(nw0, cw(0), cst[0:64, C_PFLAG:C_PFLAG + 1], ALU.mult, -1.0, ALU.mult)
                P.ts(nw2, cw(2), cst[0:64, C_PFLAG:C_PFLAG + 1], ALU.mult, -1.0, ALU.mult)
                for t in range(NTT):
                    off = t * 512
                    st_ = tmpf.next()
                    st = st_[0:64, :]
                    P.ts(st, raw[:, off + 1:off + 513], cw(1), ALU.mult)
                    P.stt(st, raw[:, off:off + 512], cw(0), st, ALU.mult, ALU.add)
                    P.stt(st, raw[:, off + 2:off + 514], cw(2), st, ALU.mult, ALU.add)
                    sv = st.rr("p (a b) -> p a b", b=256)
                    r0 = raw[:, off:off + 512].rr("p (a b) -> p a b", b=256)[:, :, 0]
                    r2 = raw[:, off + 2:off + 514].rr("p (a b) -> p a b", b=256)[:, :, 255]
                    P.stt(sv[:, :, 0], r0, nw0, sv[:, :, 0], ALU.mult, ALU.add)
                    P.stt(sv[:, :, 255], r2, nw2, sv[:, :, 255], ALU.mult, ALU.add)
                    if dstn is not None:
                        dv = mx[dstn][:, off:off + 512]
                        P.act(dv, st, AF.Silu)
                        s2_ = tmpf.next()
                        s2 = s2_[0:64, :]
                        P.act(s2, dv, AF.Square)
                        pn = ps.next()
                        P.mm(pn[0:64, :], O64, s2)
                        P.ts(s2, pn[0:64, :], EPS, ALU.add)
                        P.act(s2, s2, AF.Sqrt)
                        P.recip(s2, s2)
                        if xi == 0:
                            P.stt(dv, dv, 0.125, s2, ALU.mult, ALU.mult)
                        else:
                            P.tt(dv, dv, s2, ALU.mult)
                    else:
                        P.act(st, st, AF.Silu)
                        pt = ps.next()
                        for q in range(4):
                            P.tr(pt[:, q * 64:(q + 1) * 64], st[:, q * 128:(q + 1) * 128], I64)
                        P.copy(mx["Vtm"][:, t * 4:(t + 1) * 4, 0:64], pt[:, 0:256].rr("p (q c) -> p q c", q=4), eng=ACT)
            for g in range(NCH // 4):
                pt = ps.next()
                for q in range(4):
                    n = g * 4 + q
                    P.tr(pt[:, q * 64:(q + 1) * 64], mx["KT"][:, n * 128:(n + 1) * 128], I64)
                P.copy(mx["Ktm"][:, g * 4:(g + 1) * 4, :], pt[:, 0:256].rr("p (q c) -> p q c", q=4), eng=ACT)
            stt_ = {"pre": [0] * (NCH + 2), "scan": 0}
            ident2 = Vw(cst, cst.ap[:, C_ID:C_ID + 128].rearrange("p (o c) -> p o c", o=1).broadcast_to([128, 2, 128]))
            noti2 = Vw(cst, cst.ap[:, C_NOTI:C_NOTI + 128].rearrange("p (o c) -> p o c", o=1).broadcast_to([128, 2, 128]))

            def chunks_of(k):
                return (k, NCH - 1 - k)

            def pre_W(p):
                psp = Ring(ps.tiles[2 * p:2 * p + 2])
                for k in range(p, NCH, 2):
                    while stt_["scan"] < k - 1:
                        yield
                    ns = chunks_of(k)
                    rs = (h, 4 + h)
                    dg, Y, kh, khT, u = [ax["%s_%d" % (nm_, p)] for nm_ in ("dg", "Y", "kh", "khT", "u")]
                    wt = mx["wtW_%d" % p]
                    wT = mx["wTW_%d" % p]
                    for d in range(2):
                        P.ts(dg[:, d, :], ident, CG[:, ns[d], rs[d]:rs[d] + 1], ALU.mult)
                        P.ts(kh[:, d, :], mx["Ktm"][:, ns[d], :], SQB[:, ns[d], rs[d]:rs[d] + 1], ALU.mult)
                    yield
                    pw = psp.next()
                    pk = psp.next()
                    for d in range(2):
                        P.mm(pw[:, d * 128:(d + 1) * 128], ONESF, dg[:, d, :], start=True, stop=False)
                        P.mm(pw[:, d * 128:(d + 1) * 128], ident, NEGM[d], start=False, stop=True)
                        P.tr(pk[0:64, d * 128:(d + 1) * 128], kh[:, d, :], ident)
                    yield
                    for d in range(2):
                        P.act(wt[:, d, :], pw[:, d * 128:(d + 1) * 128], AF.Exp, bias=NCG[:, ns[d], rs[d]:rs[d] + 1])
                    P.copy(khT.v(), pk[0:64, 0:256].rr("p (d c) -> p d c", d=2))
                    for d in range(2):
                        P.ts(Y[:, d, 0:64], mx["Vtm"][:, ns[d], 0:64], SQB[:, ns[d], rs[d]:rs[d] + 1], ALU.mult)
                        P.ts(Y[:, d, 64:128], mx["Ktm"][:, ns[d], :], SQBE[:, ns[d], rs[d]:rs[d] + 1], ALU.mult)
                    yield
                    pg = psp.next()
                    for d in range(2):
                        P.mm(pg[:, d * 128:(d + 1) * 128], khT[:, d, :], khT[:, d, :])
                    yield
                    XT = ax["XTa_%d" % p]
                    X = ax["Xa_%d" % p]
                    P.tt(XT.v(), pg[:, 0:256].rr("p (d c) -> p d c", d=2), wt.v(), ALU.mult)
                    P.tt(XT.v(), XT.v(), noti2, ALU.mult)
                    yield
                    px = psp.next()
                    py = psp.next()
                    for d in range(2):
                        P.tr(px[:, d * 128:(d + 1) * 128], XT[:, d, :], ident)
                        P.mm(py[:, d * 128:(d + 1) * 128], XT[:, d, :], Y[:, d, :])
                    yield
                    P.copy(X.v(), px[:, 0:256].rr("p (d c) -> p d c", d=2), eng=ACT)
                    P.tt(Y.v(), Y.v(), py[:, 0:256].rr("p (d c) -> p d c", d=2), ALU.subtract)
                    yield
                    for lev in range(6):
                        Xn = ax["Xb_%d" % p] if X is ax["Xa_%d" % p] else ax["Xa_%d" % p]
                        XTn = ax["XTb_%d" % p] if XT is ax["XTa_%d" % p] else ax["XTa_%d" % p]
                        p2x = psp.next()
                        p2y = psp.next()
                        for d in range(2):
                            if lev < 5:
                                P.mm(p2x[:, d * 128:(d + 1) * 128], XT[:, d, :], X[:, d, :])
                            P.mm(p2y[:, d * 128:(d + 1) * 128], X[:, d, :], XT[:, d, :])
                        yield
                        if lev < 5:
                            P.copy(Xn.v(), p2x[:, 0:256].rr("p (d c) -> p d c", d=2), eng=ACT)
                        P.copy(XTn.v(), p2y[:, 0:256].rr("p (d c) -> p d c", d=2))
                        X, XT = Xn, XTn
                        yield
                        py = psp.next()
                        for d in range(2):
                            P.mm(py[:, d * 128:(d + 1) * 128], XT[:, d, :], Y[:, d, :])
                        yield
                        P.tt(Y.v(), Y.v(), py[:, 0:256].rr("p (d c) -> p d c", d=2), ALU.add)
                        yield
                    for d in range(2):
                        P.ts(u[:, d, :], Y[:, d, 0:64], SQB[:, ns[d], rs[d]:rs[d] + 1], ALU.mult)
                        P.ts(kh[:, d, :], Y[:, d, 64:128], SQB[:, ns[d], rs[d]:rs[d] + 1], ALU.mult)
                    yield
                    pk2 = psp.next()
                    for d in range(2):
                        P.tr(pk2[0:64, d * 128:(d + 1) * 128], kh[:, d, :], ident)
                    yield
                    P.copy(wT.v(), pk2[0:64, 0:256].rr("p (d c) -> p d c", d=2), eng=ACT)
                    stt_["pre"][k] = 1
                    yield

            def scan_W():
                pss = Ring(ps.tiles[4:6])
                rs = (h, 4 + h)
                Ss = (Sst[0], Sst[1])
                for d in range(2):
                    P.dma(SP, Ss[d][:, 0:64], st_dl_in[l, rs[d]])
                ods = (mx["Od0"], mx["Od1"])
                for k in range(NCH):
                    while not stt_["pre"][k]:
                        yield
                    p = k % 2
                    ns = chunks_of(k)
                    wt = mx["wtW_%d" % p]
                    wT = mx["wTW_%d" % p]
                    u = ax["u_%d" % p]
                    vnew = ax["vnew"]
                    at = mx["atW"]
                    pv = pss.next()
                    p1 = pss.next()
                    for d in range(2):
                        P.mm(pv[:, d * 64:(d + 1) * 64], wT[:, d, :], Ss[d][:, 0:64])
                        P.mm(p1[:, d * 128:(d + 1) * 128], mx["KT"][:, ns[d] * 128:(ns[d] + 1) * 128],
                             mx["QT"][:, ns[d] * 128:(ns[d] + 1) * 128])
                    yield
                    P.tt(vnew.v(), u.v(), pv[:, 0:128].rr("p (d c) -> p d c", d=2), ALU.subtract)
                    P.tt(at.v(), p1[:, 0:256].rr("p (d c) -> p d c", d=2), wt.v(), ALU.mult)
                    kws = (mx["kw0"], mx["kw1"])
                    for d in range(2):
                        P.ts(kws[d].v(), mx["Ktm"][:, ns[d], :], KS[:, ns[d], rs[d]:rs[d] + 1], ALU.mult)
                    yield
                    p2 = pss.next()
                    p3 = pss.next()
                    for d in range(2):
                        P.mm(p2[:, d * 64:(d + 1) * 64], at[:, d, :], vnew[:, d, :])
                        P.mm(p2[:, 128 + d * 64:128 + (d + 1) * 64], mx["QT"][:, ns[d] * 128:(ns[d] + 1) * 128], Ss[d][:, 0:64])
                        P.mm(p3[0:64, d * 64:(d + 1) * 64], kws[d].v(), vnew[:, d, :])
                    yield
                    o2 = mx["attn0"]
                    P.copy(o2[:, 0:128], p2[:, 0:128], eng=ACT)
                    for d in range(2):
                        n = ns[d]
                        r = rs[d]
                        S = Ss[d]
                        P.stt(ods[d][:, n, 0:64], p2[:, 128 + d * 64:128 + (d + 1) * 64], QS[:, n, r:r + 1],
                              o2[:, d * 64:(d + 1) * 64], ALU.mult, ALU.add)
                        P.stt(S[:, 0:64], S[:, 0:64], LAM[0:64, n, r:r + 1], p3[0:64, d * 64:(d + 1) * 64], ALU.mult, ALU.add)
                        isend = (n % 2 == 1) if d == 0 else (n % 2 == 0)
                        if isend:
                            sf = Sfin.next()
                            P.copy(sf[:, 0:64], S[:, 0:64])
                            P.dma(SP, o_dl[l, n // 2, r], sf[:, 0:64])
                        P.ts(S[:, 0:64], S[:, 0:64], cst[0:64, C_KEEP + d * NCH + n:C_KEEP + d * NCH + n + 1], ALU.mult)
                    stt_["scan"] = k + 1
                    yield
            run_rr([pre_W(0), pre_W(1), scan_W()], delays=[0, 2, 0])
            wb = load_blk(l, BLK["a_z"])
            proj_tm(wb, h * 64, 64, mx["Gtm"], 64)
            o0 = mx["Od0"][:, :, 0:64]
            P.tt(o0, o0, mx["Od1"][:, :, 0:64], ALU.add)
            post_norm_gate(l, o0, L_DNG, AF.Silu, h)
            if h % 2 == 1:
                branch_out(0, h // 2)


    def mixer_B(l):
        import math
        lam_init = 0.8 - 0.6 * math.exp(-0.3 * l)
        lt = smal[:, 50:52]
        tmpl = bx["rt0"][:, 0:32]
        P.tt(tmpl, lpar1[:, L_LAM:L_LAM + 32], lpar1[:, L_LAM + 32:L_LAM + 64], ALU.mult)
        P.reduce(lt[:, 0:1], tmpl, ALU.add)
        P.tt(tmpl, lpar1[:, L_LAM + 64:L_LAM + 96], lpar1[:, L_LAM + 96:L_LAM + 128], ALU.mult)
        P.reduce(lt[:, 1:2], tmpl, ALU.add)
        P.act(lt, lt, AF.Exp)
        nlam = smal[:, 52:53]
        P.tt(nlam, lt[:, 1:2], lt[:, 0:1], ALU.subtract)
        P.ts(nlam, nlam, -lam_init, ALU.add)
        P.dma(SP, bx["ropet"].v(), rope_in.v())
        qaugs, kaug, Vext = (bx["qaug0"], bx["qaug1"]), bx["kaug"], bx["Vext"]
        P.memset(kaug.v(), 0.0, eng=POOL)
        P.memset(qaugs[0].v(), 0.0, eng=POOL)
        P.memset(qaugs[1].v(), 0.0, eng=POOL)
        for c in range(2):
            P.dma(POOL, qaugs[c][c * 64 + 32:c * 64 + 41, :], qmask_in.v())
            P.dma(POOL, kaug[c * 64 + 32:c * 64 + 41, :], kmask_in.v())
        for h in range(4):
            for c in range(2):
                P.dma(POOL, kaug[c * 64:c * 64 + 32, 0:256], ckT_in[l, h, c])
            P.dma(POOL, Vext[:, 0:2, 0:64], cv_in[l, h].rr("(j p) c -> p j c", p=128))
            P.memset(Vext[:, :, 64:65], 1.0)
            wb = load_blk(l, BLK["b_tm"] + h)
            Vf = bx["Vf"]
            proj_tm3(wb, [(Vf, None), (bx["Qb"], None), (bx["Kb"], None)])
            P.dma(SP, o_v[l, h].rr("(n p) c -> p n c", p=128), Vf.v())
            P.copy(Vext[:, 2:NKT, 0:64], Vf.v())
            for xi, (bn, xb, gcol) in enumerate((("b_q", "Qb", L_QNG), ("b_k", "Kb", L_KNG))):
                X = bx[xb]
                sqo = bx["Ob"]
                Xg = X.v().rr("p n (c e) -> p (n c) e", c=2)
                Sg = sqo.v().rr("p n (c e) -> p (n c) e", c=2)
                P.tt(sqo.v(), X.v(), X.v(), ALU.mult)
                ss = rows_b
                P.reduce(ss, Sg, ALU.add)
                P.ts(ss, ss, 1.0 / 32, ALU.mult, EPS, ALU.add)
                P.act(ss, ss, AF.Sqrt)
                P.recip(ss, ss)
                P.tt(Xg, Xg, ss.rr("p (m o) -> p m o", o=1).bc([128, NCH * 2, 32]), ALU.mult)
                P.tt(Xg, Xg, lpar1[:, gcol:gcol + 32].rr("p (o e) -> p o e", o=1).bc([128, NCH * 2, 32]), ALU.mult)
                rp = bx["ropet"]
                cosv = rp[:, 0, :].rr("p (n a f) -> p n a f", a=2, f=8)
                sinv = rp[:, 1, :].rr("p (n a f) -> p n a f", a=2, f=8)
                t1, t2, t3, t4 = [bx["rt%d" % i].v().rr("p (n a f) -> p n a f", a=2, f=8) for i in range(4)]
                for c in range(2):
                    xv = X[:, :, c * 32:(c + 1) * 32].rr("p n (a q f) -> p n a q f", a=2, q=2)
                    x1 = xv[:, :, :, 0, :]
                    x2 = xv[:, :, :, 1, :]
                    P.tt(t1, x1, cosv, ALU.mult)
                    P.tt(t2, x2, sinv, ALU.mult)
                    P.tt(t3, x2, cosv, ALU.mult)
                    P.tt(t4, x1, sinv, ALU.mult)
                    P.tt(x1, t1, t2, ALU.subtract)
                    P.tt(x2, t3, t4, ALU.add)
                if xi == 1:
                    P.dma(SP, o_k[l, h].rr("(n p) c -> p n c", p=128), X.v())
                coff = 0 if xi == 0 else 256
                for c in range(2):
                    dstT = qaugs[c] if xi == 0 else kaug
                    for g in range(NCH // 4):
                        pt = ps.next()
                        for q in range(4):
                            n = g * 4 + q
                            P.tr(pt[0:32, q * 128:(q + 1) * 128], X[:, n, c * 32:(c + 1) * 32], ident)
                        dd = dstT[c * 64:c * 64 + 32, coff + g * 512:coff + (g + 1) * 512]
                        if xi == 0:
                            P.act(dd, pt[0:32, :], AF.Copy, scale=32 ** -0.5)
                        else:
                            P.copy(dd, pt[0:32, :], eng=ACT)
            PTr = Ring([bx["PT%d" % i] for i in range(4)])
            Ob = bx["Ob"]
            for qb in range(NTT):
                for c in range(2):
                    pacc = pacc_r.next()
                    pend = None
                    for kt in range(NKT + 1):
                        cur = None
                        if kt < NKT:
                            psc = ps.next()
                            P.mm(psc.v(), kaug[:, kt * 128:(kt + 1) * 128], qaugs[c][:, qb * 512:(qb + 1) * 512])
                            pt_ = PTr.next()
                            P.act(pt_.v(), psc.v(), AF.Exp, bias=-4.0)
                            cur = (kt, pt_)
                        if pend is not None:
                            k0, p0 = pend
                            P.mm(pacc[0:65, :], Vext[:, k0, :], p0.v(), start=(k0 == 0), stop=(k0 == NKT - 1))
                        pend = cur
                    ocT = bx["ocT"]
                    P.copy(ocT.v(), pacc[0:65, :], eng=ACT)
                    ptr = ps.next()
                    for qs in range(4):
                        P.tr(ptr[:, qs * 65:(qs + 1) * 65], ocT[:, qs * 128:(qs + 1) * 128], cst[0:65, C_ID:C_ID + 65])
                    oc = bx["octmp"] if c == 0 else bx["oc2"]
                    P.copy(oc.v(), ptr[:, 0:260].rr("p (q e) -> p q e", q=4))
                o1, o2 = bx["octmp"], bx["oc2"]
                P.recip(o1[:, :, 64:65], o1[:, :, 64:65])
                P.recip(o2[:, :, 64:65], o2[:, :, 64:65])
                P.ts(o2[:, :, 64:65], o2[:, :, 64:65], nlam, ALU.mult)
                P.tt(o1[:, :, 0:64], o1[:, :, 0:64], o1[:, :, 64:65].bc([128, 4, 64]), ALU.mult)
                P.tt(o2[:, :, 0:64], o2[:, :, 0:64], o2[:, :, 64:65].bc([128, 4, 64]), ALU.mult)
                P.tt(Ob[:, qb * 4:(qb + 1) * 4, :], o1[:, :, 0:64], o2[:, :, 0:64], ALU.add)
            sqo = bx["Qb"]
            P.tt(sqo.v(), Ob.v(), Ob.v(), ALU.mult)
            ss = smal[:, 32:32 + NCH]
            P.reduce(ss, sqo.v(), ALU.add)
            P.ts(ss, ss, 1.0 / 64, ALU.mult, EPS, ALU.add)
            P.act(ss, ss, AF.Sqrt)
            P.recip(ss, ss)
            P.ts(ss, ss, 1.0 - lam_init, ALU.mult)
            P.tt(sqo.v(), Ob.v(), ss.rr("p (n o) -> p n o", o=1).bc([128, NCH, 64]), ALU.mult)
            gn = lpar1[:, L_DAG:L_DAG + 64].rr("p (o e) -> p o e", o=1).bc([128, NCH, 64])
            P.tt(mx["Opair"][:, :, (h % 2) * 64:(h % 2) * 64 + 64], sqo.v(), gn, ALU.mult)
            if h % 2 == 1:
                branch_out(1, h // 2)

    def mixers(l):
        P.dma(SP, lpar1.v(), lp_in[l])
        for t in range(NTT):
            norm_mod(l, 1, t)
        for m in range(4):
            if m not in MIX_IMPL:
                zero_branch(m)
        if 0 in MIX_IMPL:
            mixer_A(l)
        if 1 in MIX_IMPL:
            mixer_B(l)
        if 2 in MIX_IMPL:
            mixer_C(l)
        if 3 in MIX_IMPL:
            mixer_D(l)
        merge(l)

    MIX_IMPL = set(MIXSEL)
    for l in range(NL):
        ffn(l, 0)
        if stage == "ffn1":
            break
        mixers(l)
        if stage == "mix":
            break
        ffn(l, 1)

    yv = y_out.v().rr("(kc p) t -> p kc t", p=128)
    for t in range(NTT):
        P.dma(SP, yv[:, :, t * 512:(t + 1) * 512], xT[t].v())
    P.emit()
    return nc


def kblocks(W, width):
    K, N = W.shape
    kc = K // 128
    nb = N // width
    return np.ascontiguousarray(W.reshape(kc, 128, nb, width).transpose(2, 1, 0, 3).reshape(nb, 128, kc * width))


def host_weights(inp, NL):
    w = {}
    w["wada"] = np.concatenate([kblocks(inp["w_ada"][l], 256) for l in range(NL)], 0)
    w["bada"] = np.stack([np.ascontiguousarray(inp["b_ada"][l].reshape(72, 128).T) for l in range(NL)])
    w["ng"] = np.stack([np.ascontiguousarray(inp["norm_g"][l].reshape(24, 128).T) for l in range(NL)])
    gu = []
    wd = []
    for l in range(NL):
        for i in range(2):
            g = kblocks(inp["ffn_w_gate"][l, i], 128).reshape(NF, 128, NK, 128)
            u = kblocks(inp["ffn_w_up"][l, i], 128).reshape(NF, 128, NK, 128)
            gu.append(np.concatenate([g, u], axis=3).reshape(NF, 128, NK * 256))
            wd.append(kblocks(inp["ffn_w_down"][l, i], 128))
    w["wgu"] = np.ascontiguousarray(np.concatenate(gu, 0))
    w["wd"] = np.ascontiguousarray(np.concatenate(wd, 0))
    return w


IN_SIZES = (256, 256, 256, 256, 8, 8, 256, 256, 256, 256, 256, 256, 256, 8, 8, 256, 256, 256, 256, 4096)


def host_weights2(inp, NL, w):
    offs = np.concatenate([[0], np.cumsum(IN_SIZES)])
    names = ["a_q", "a_k", "a_v", "a_z", "a_beta", "a_alpha", "b_q", "b_k", "b_v", "c_q", "c_k", "c_v", "c_o",
             "c_i", "c_f", "d_q", "d_k", "d_v", "d_g", "merge"]
    col = {n: int(offs[i]) for i, n in enumerate(names)}
    blocks = []
    for l in range(NL):
        W = inp["w_in"][l]

        def blk(c0, n=256):
            b = np.zeros((1024, 256), np.float32)
            b[:, :n] = W[:, c0:c0 + n]
            return b

        def gblk(c0, c1):
            b = np.zeros((1024, 256), np.float32)
            b[:, 0:8] = W[:, c0:c0 + 8]
            b[:, 8:16] = W[:, c1:c1 + 8]
            return b
        bl = [blk(col["a_q"]), blk(col["a_k"]), blk(col["a_v"]), blk(col["a_z"]), gblk(col["a_beta"], col["a_alpha"]),
              blk(col["b_q"]), blk(col["b_k"]), blk(col["b_v"]),
              blk(col["c_q"]), blk(col["c_k"]), blk(col["c_v"]), blk(col["c_o"]), gblk(col["c_i"], col["c_f"]),
              blk(col["d_q"]), blk(col["d_k"]), blk(col["d_v"]), blk(col["d_g"])]
        bl += [blk(col["merge"] + 256 * i) for i in range(16)]

        def tmblk(names, hh):
            b = np.zeros((1024, 256), np.float32)
            for j, nm in enumerate(names):
                b[:, j * 64:(j + 1) * 64] = W[:, col[nm] + hh * 64:col[nm] + (hh + 1) * 64]
            return b
        for names in (("b_v", "b_q", "b_k"), ("c_k", "c_v", "c_o"), ("d_k", "d_v", "d_g")):
            bl += [tmblk(names, hh) for hh in range(4)]
        blocks += [kblocks(b, 256)[0] for b in bl]
    w["win"] = np.ascontiguousarray(np.stack(blocks))
    w["wbr"] = np.ascontiguousarray(np.stack([inp["w_branch"][l, m].reshape(2, 128, 1024).transpose(1, 0, 2).reshape(128, 2048)
                                              for l in range(NL) for m in range(4)]))
    w["wout"] = np.concatenate([kblocks(inp["w_out"][l], 256) for l in range(NL)], 0)
    lp = np.zeros((NL, 128, LP_W), np.float32)
    for l in range(NL):
        lp[l, :, 0:8] = inp["ret_decay_logit"][l].reshape(8)[None, :]
        lp[l, :, 8:72] = inp["ret_norm_g"][l][None, :]
        lp[l, :, L_MLI:L_MLI + 8] = inp["ml_i_bias"][l].reshape(8)[None, :]
        lp[l, :, L_MLF:L_MLF + 8] = inp["ml_f_bias"][l].reshape(8)[None, :]
        lp[l, :, L_MLG:L_MLG + 64] = inp["ml_norm_g"][l][None, :]
        lp[l, :, L_ISF:L_ISF + 4] = 1.0
        lp[l, :, L_QNG:L_QNG + 32] = inp["da_qn_g"][l][None, :]
        lp[l, :, L_KNG:L_KNG + 32] = inp["da_kn_g"][l][None, :]
        lp[l, :, L_DAG:L_DAG + 64] = inp["da_norm_g"][l][None, :]
        lp[l, :, L_LAM:L_LAM + 128] = inp["da_lambda"][l].reshape(128)[None, :]
        lp[l, :, L_DNG:L_DNG + 64] = inp["dn_norm_g"][l][None, :]
        lp[l, :, L_DNA:L_DNA + 8] = inp["dn_a_log"][l].reshape(8)[None, :]
        lp[l, :, L_DNB:L_DNB + 8] = inp["dn_dt_bias"][l].reshape(8)[None, :]
        cw = inp["dn_conv_w"][l]
        for xi in range(3):
            for hh in range(4):
                for jj in range(3):
                    lp[l, 0:64, L_CONV + (xi * 4 + hh) * 3 + jj] = cw[jj, xi * 256 + hh * 64: xi * 256 + hh * 64 + 64]
    w["lpar"] = lp
    return w


def host_consts(T, is_sample):
    NCH = T // 128
    c = np.zeros((128, CST_W), np.float32)
    c[:, C_ID:C_ID + 128] = np.eye(128, dtype=np.float32)
    j = np.arange(128)[:, None]
    i = np.arange(128)[None, :]
    c[:, C_NEGF:C_NEGF + 128] = np.where(j <= i, 0.0, BIGNEG)
    c[:, C_NEGB:C_NEGB + 128] = np.where(j >= i, 0.0, BIGNEG)
    c[:, C_RELF:C_RELF + 128] = np.where(j <= i, (i - j).astype(np.float32), 1e6)
    c[:, C_RELB:C_RELB + 128] = np.where(j >= i, (j - i).astype(np.float32), 1e6)
    p = np.arange(128, dtype=np.float32)
    c[:, C_POS + 0] = p + 1
    c[:, C_POS + 1] = 128 - p
    c[:, C_POS + 2] = 127 - p
    c[:, C_POS + 3] = p
    c[:, C_PFLAG] = 0.0 if is_sample else 1.0
    c[:, C_ONES:C_ONES + 128] = 1.0
    c[:, C_TRIF:C_TRIF + 128] = (j <= i)
    c[:, C_TRIB:C_TRIB + 128] = (j >= i)
    c[:, C_NOTI:C_NOTI + 128] = (j != i)
    for n in range(NCH):
        c[:, C_KEEP + n] = 1.0 if (is_sample or n % 2 == 0) else 0.0
        c[:, C_KEEP + NCH + n] = 1.0 if (is_sample or n % 2 == 1) else 0.0
    return c


def host_sel():
    s = np.zeros((8, 520), np.float32)
    for r in range(8):
        s[r, r * 64:(r + 1) * 64] = 1.0
    s[0:4, 512] = 1.0
    return s


def host_attn_consts(T, is_sample):
    NCH = T // 128
    t = np.arange(T)
    rope = np.zeros((128, 2, NCH, 2, 8), np.float32)
    rope[:, 0] = 1.0
    qmask = np.zeros((9, T), np.float32)
    kmask = np.zeros((9, T + 256), np.float32)
    if is_sample:
        freqs = (10000.0 ** (-np.arange(8, dtype=np.float32) / 8)).astype(np.float32)
        rows = (t // 64).astype(np.float32)
        cols = (t % 64).astype(np.float32)
        for hf, pos in enumerate((rows, cols)):
            ang = (pos[:, None] * freqs[None, :]).astype(np.float32)
            rope[:, 0, :, hf, :] = np.cos(ang).reshape(NCH, 128, 8).transpose(1, 0, 2)
            rope[:, 1, :, hf, :] = np.sin(ang).reshape(NCH, 128, 8).transpose(1, 0, 2)
    else:
        seg = t // 256
        for a in range(8):
            qmask[a] = np.where(seg == a, 0.0, BIGNEG)
            kmask[a, 256:] = (seg == a)
        qmask[8] = BIGNEG
        kmask[8, 0:256] = 1.0
    return rope.reshape(128, 2, NCH * 16), qmask, kmask


T_SLAB = 2048
N_LAYERS = 2
_NC_CACHE = {}


def kernel(**inp):
    inp = {k: np.asarray(v) for k, v in inp.items()}
    NL = N_LAYERS
    T = T_SLAB
    xp = inp["x_prompt"]
    xs = inp["x_sample"]
    w = host_weights(inp, NL)
    host_weights2(inp, NL, w)
    cst_s = host_consts(T, True)
    cst_p = host_consts(T, False)
    att_s = host_attn_consts(T, True)
    att_p = host_attn_consts(T, False)
    w["sel"] = host_sel()
    zf = lambda *s: np.zeros(s, np.float32)
    in_maps = []
    for c in range(8):
        m = dict(w)
        if c < 2:
            x = xs[c]
            cond = inp["c"][c]
            m["cst"] = cst_s
            m["rope"], m["qmask"], m["kmask"] = att_s
            m["st_ret"] = np.ascontiguousarray(inp["state_ret"][c].reshape(NL, 8, 64, 64))
            m["st_delta"] = np.ascontiguousarray(inp["state_delta"][c].reshape(NL, 8, 64, 64))
            m["st_Cn"] = np.ascontiguousarray(np.concatenate(
                [inp["state_mlstm_C"][c].reshape(NL, 8, 64, 64), inp["state_mlstm_n"][c].reshape(NL, 8, 64, 1)], -1))
            m["st_m"] = np.ascontiguousarray(np.broadcast_to(inp["state_mlstm_m"][c].reshape(NL, 1, 8), (NL, 128, 8)))
            m["ckT"] = np.ascontiguousarray(inp["cache_diff_k"][c].reshape(NL, 4, 256, 2, 32).transpose(0, 1, 3, 4, 2))
            m["cv"] = np.ascontiguousarray(inp["cache_diff_v"][c])
        else:
            s0 = (c - 2) * 8
            if s0 < 32:
                x = xp[s0:s0 + 8].reshape(T, D)
            else:
                x = np.zeros((T, D), np.float32)
            cond = inp["c_ctx"]
            m["cst"] = cst_p
            m["rope"], m["qmask"], m["kmask"] = att_p
            m["st_ret"] = zf(NL, 8, 64, 64)
            m["st_delta"] = zf(NL, 8, 64, 64)
            m["st_Cn"] = zf(NL, 8, 64, 65)
            m["st_m"] = zf(NL, 128, 8)
            m["ckT"] = zf(NL, 4, 2, 32, 256)
            m["cv"] = zf(NL, 4, 256, 64)
        m["xT"] = np.ascontiguousarray(x.T)
        m["cond"] = np.ascontiguousarray(cond.reshape(8, 128).T)
        in_maps.append(m)
    if "nc" not in _NC_CACHE:
        _NC_CACHE["nc"] = build(T, NL, "full")
    nc = _NC_CACHE["nc"]
    res = run_bass_kernel_spmd(nc, in_maps, core_ids=list(range(8)))
    R = res.results
    y_prompt = np.concatenate([R[c]["yT"].T.reshape(8, 256, D) for c in range(2, 6)], 0).astype(np.float32)
    y_sample = np.stack([R[c]["yT"].T for c in range(2)]).astype(np.float32)
    B = 32
    pc = range(2, 6)
    cat = lambda f: np.ascontiguousarray(np.concatenate([f(R[c]) for c in pc], 0)).astype(np.float32)
    new_ret = cat(lambda r: r["o_ret"].transpose(1, 0, 2, 3, 4).reshape(8, NL, 2, 4, 64, 64))
    new_delta = cat(lambda r: r["o_delta"].transpose(1, 0, 2, 3, 4).reshape(8, NL, 2, 4, 64, 64))
    new_C = cat(lambda r: r["o_Cn"][..., 0:64].transpose(1, 0, 2, 3, 4).reshape(8, NL, 2, 4, 64, 64))
    new_n = cat(lambda r: r["o_Cn"][..., 64].transpose(1, 0, 2, 3).reshape(8, NL, 2, 4, 64))
    new_m = cat(lambda r: r["o_m"].transpose(2, 0, 1).reshape(8, NL, 2, 4))
    new_k = cat(lambda r: r["o_k"].reshape(NL, 4, 8, 256, 64).transpose(2, 0, 1, 3, 4))
    new_v = cat(lambda r: r["o_v"].reshape(NL, 4, 8, 256, 64).transpose(2, 0, 1, 3, 4))
    return (y_prompt, y_sample, new_k, new_v, new_delta, new_C, new_n, new_m, new_ret)
```
